# Optimizing a Trainium2 kernel written in Bass

```python
import math
import jax, jax.numpy as jnp
from jax import lax
import numpy as np

D_MODEL = 1024
BATCH = 8
SEQ = 8192
DEPTH = 2

PLE_DIM = 256
RMS_EPS = 1e-6
ROPE_THETA = 10000.0
RW_HEADS = 4
RW_HEAD_DIM = 64
RW_WIDTH = RW_HEADS * RW_HEAD_DIM
RW_DECAY_LORA = 64
RW_AAA_LORA = 64
RW_GATE_LORA = 128
RW_GN_EPS = 64e-5
RW_IN = 3 * RW_WIDTH + RW_DECAY_LORA + RW_AAA_LORA + RW_GATE_LORA
RW_SPLITS = (RW_WIDTH, 2 * RW_WIDTH, 3 * RW_WIDTH, 3 * RW_WIDTH + RW_DECAY_LORA,
             3 * RW_WIDTH + RW_DECAY_LORA + RW_AAA_LORA)
POOL_GROUPS = 4
POOL_GROUP_DIM = 64
POOL_WIDTH = POOL_GROUPS * POOL_GROUP_DIM
POOL_WINDOWS = (2, 4, 8, 16)
NSA_Q_HEADS = 8
NSA_KV_HEADS = 2
NSA_HEAD_DIM = 64
NSA_GQA = NSA_Q_HEADS // NSA_KV_HEADS
NSA_WIDTH = NSA_Q_HEADS * NSA_HEAD_DIM
NSA_KV = NSA_KV_HEADS * NSA_HEAD_DIM
NSA_IN = NSA_WIDTH + 6 * NSA_KV + 3 * NSA_Q_HEADS
NSA_SPLITS = tuple(NSA_WIDTH + i * NSA_KV for i in range(7))
CMP_BLOCK = 32
CMP_STRIDE = 16
SEL_BLOCK = 64
SEL_TOPN = 16
WINDOW = 512
Q_BLOCK = 128
NEG = -1e30
FORCE = 1e9
IN_WIDTH = RW_IN + POOL_WIDTH + NSA_IN
MIX_WIDTH = RW_WIDTH + POOL_WIDTH + NSA_WIDTH
D_FF = 2816
CONV_W = 3

kernel_name = "hybrid_rwkv7_pool_nsa_convffn_ple"


def rmsnorm(x, g):
    xf = x.astype(jnp.float32)
    y = xf * lax.rsqrt(jnp.mean(xf * xf, axis=-1, keepdims=True) + RMS_EPS)
    return (y * g.astype(jnp.float32)).astype(x.dtype)


def token_shift(x):
    return jnp.pad(x, ((0, 0), (1, 0), (0, 0)))[:, :-1]


def rope(x, pos):
    half = x.shape[-1] // 2
    inv = ROPE_THETA ** (-jnp.arange(half, dtype=jnp.float32) / half)
    ang = pos.astype(jnp.float32)[..., None] * inv
    cos = jnp.cos(ang)[:, :, None, :]
    sin = jnp.sin(ang)[:, :, None, :]
    xf = x.astype(jnp.float32)
    x1, x2 = xf[..., :half], xf[..., half:]
    return jnp.concatenate([x1 * cos - x2 * sin, x2 * cos + x1 * sin], axis=-1)


def rwkv7_mixer(z, mu, w0, w_up, a0, a_up, g_up, k_k, k_a, r_k, gn_g, gn_b):
    B, T, _ = z.shape
    zf = z.astype(jnp.float32)
    zf = zf + (token_shift(zf) - zf) * mu.astype(jnp.float32)
    r, k, v, zw, za, zg = jnp.split(zf, RW_SPLITS, axis=-1)
    w = -jax.nn.softplus(-(w0 + jnp.tanh(zw) @ w_up)) - 0.5
    decay = jnp.exp(-jnp.exp(w))
    a = jax.nn.sigmoid(a0 + za @ a_up)
    g = jax.nn.sigmoid(zg) @ g_up
    kk = k * k_k
    k = k * (1.0 + (a - 1.0) * k_a)
    hs = lambda t: t.reshape(B, T, RW_HEADS, RW_HEAD_DIM).astype(jnp.float32)
    r, k, v, a, decay, kk = hs(r), hs(k), hs(v), hs(a), hs(decay), hs(kk)
    kk = kk / jnp.maximum(jnp.linalg.norm(kk, axis=-1, keepdims=True), 1e-12)

    def step(S, inp):
        r_t, w_t, k_t, v_t, kk_t, a_t = inp
        sa = jnp.einsum('bhvk,bhk->bhv', S, -kk_t)
        S = (S * w_t[:, :, None, :] + sa[..., None] * (kk_t * a_t)[:, :, None, :]
             + v_t[..., None] * k_t[:, :, None, :])
        return S, jnp.einsum('bhvk,bhk->bhv', S, r_t)

    xs = tuple(jnp.moveaxis(t, 1, 0) for t in (r, decay, k, v, kk, a))
    S0 = jnp.zeros((B, RW_HEADS, RW_HEAD_DIM, RW_HEAD_DIM), jnp.float32)
    _, y = lax.scan(step, S0, xs)
    y = jnp.moveaxis(y, 0, 1)
    mean = jnp.mean(y, axis=-1, keepdims=True)
    var = jnp.mean(jnp.square(y - mean), axis=-1, keepdims=True)
    y = ((y - mean) * lax.rsqrt(var + RW_GN_EPS)).reshape(B, T, RW_WIDTH) * gn_g + gn_b
    bonus = jnp.sum(r * k * r_k.astype(jnp.float32), axis=-1, keepdims=True) * v
    out = (y + bonus.reshape(B, T, RW_WIDTH)) * g
    return out.astype(z.dtype)


def pool_mixer(z, w_pool, scale):
    B, T, _ = z.shape
    zf = z.astype(jnp.float32).reshape(B, T, POOL_GROUPS, POOL_GROUP_DIM)
    c = jnp.pad(jnp.cumsum(zf, axis=1), ((0, 0), (1, 0), (0, 0), (0, 0)))
    t_idx = jnp.arange(T)
    outs = []
    for gi, win in enumerate(POOL_WINDOWS):
        lo = jnp.maximum(t_idx + 1 - win, 0)
        s = c[:, 1:, gi] - c[:, lo, gi]
        cnt = jnp.minimum(t_idx + 1, win).astype(jnp.float32)[:, None]
        outs.append(s / cnt - zf[:, :, gi])
    pooled = jnp.stack(outs, axis=2)
    y = jnp.einsum('btgc,gcd->btgd', pooled, w_pool.astype(jnp.float32))
    return (y.reshape(B, T, POOL_WIDTH) * scale).astype(z.dtype)


def nsa_mixer(z, pos, pe_k, pe_v, w_ck, w_cv):
    B, T, _ = z.shape
    f32 = jnp.float32
    q, kc, vc, ks, vs, kw, vw, gates = jnp.split(z, NSA_SPLITS, axis=-1)
    kvh = lambda t: t.reshape(B, T, NSA_KV_HEADS, NSA_HEAD_DIM)
    q = rope(q.reshape(B, T, NSA_Q_HEADS, NSA_HEAD_DIM), pos) * (NSA_HEAD_DIM ** -0.5)
    qg = q.reshape(B, T, NSA_KV_HEADS, NSA_GQA, NSA_HEAD_DIM)
    kc, ks, kw = rope(kvh(kc), pos), rope(kvh(ks), pos), rope(kvh(kw), pos)
    vc, vs, vw = kvh(vc).astype(f32), kvh(vs).astype(f32), kvh(vw).astype(f32)
    gates = jax.nn.sigmoid(gates.astype(f32)).reshape(B, T, NSA_KV_HEADS, NSA_GQA, 3)

    n_cmp = (T - CMP_BLOCK) // CMP_STRIDE + 1
    cmp_start = CMP_STRIDE * jnp.arange(n_cmp)
    idx = cmp_start[:, None] + jnp.arange(CMP_BLOCK)[None]
    k_cmp = jnp.einsum('bnlhd,lde->bnhe', kc[:, idx] + pe_k.astype(f32)[:, None, :], w_ck.astype(f32))
    v_cmp = jnp.einsum('bnlhd,lde->bnhe', vc[:, idx] + pe_v.astype(f32)[:, None, :], w_cv.astype(f32))
    cmp_end = cmp_start + CMP_BLOCK - 1

    n_sel = T // SEL_BLOCK
    top_n = min(SEL_TOPN, n_sel)
    sel_start = SEL_BLOCK * jnp.arange(n_sel)
    overlap = (jnp.minimum(cmp_end[:, None], sel_start[None] + SEL_BLOCK - 1)
               - jnp.maximum(cmp_start[:, None], sel_start[None]) + 1)
    m_cs = jnp.clip(overlap, 0, CMP_BLOCK).astype(f32) / CMP_BLOCK
    ks_blk = ks.reshape(B, n_sel, SEL_BLOCK, NSA_KV_HEADS, NSA_HEAD_DIM).transpose(0, 3, 1, 2, 4)
    vs_blk = vs.reshape(B, n_sel, SEL_BLOCK, NSA_KV_HEADS, NSA_HEAD_DIM).transpose(0, 3, 1, 2, 4)
    bi = jnp.arange(B)[:, None, None, None]
    hi = jnp.arange(NSA_KV_HEADS)[None, :, None, None]
    jj = jnp.arange(n_sel)

    kw_pad = jnp.pad(kw, ((0, 0), (WINDOW, 0), (0, 0), (0, 0)))
    vw_pad = jnp.pad(vw, ((0, 0), (WINDOW, 0), (0, 0), (0, 0)))

    def q_block(jb):
        s = jb * Q_BLOCK
        t = s + jnp.arange(Q_BLOCK)
        qb = lax.dynamic_slice_in_dim(qg, s, Q_BLOCK, axis=1)
        ok_c = cmp_end[None, :] <= t[:, None]
        sc = jnp.where(ok_c, jnp.einsum('bqhgd,bnhd->bhgqn', qb, k_cmp), NEG)
        pc = jax.nn.softmax(sc, axis=-1) * jnp.any(ok_c, axis=-1)[:, None].astype(f32)
        o_cmp = jnp.einsum('bhgqn,bnhd->bqhgd', pc, v_cmp)
        imp = jnp.einsum('bhqn,nj->bhqj', jnp.sum(pc, axis=2), m_cs)
        jc = t // SEL_BLOCK
        causal = jj[None] <= jc[:, None]
        force = (jj[None] == 0) | (jj[None] == jc[:, None]) | (jj[None] == jc[:, None] - 1)
        imp = jnp.where(causal, jnp.where(force, FORCE, imp), NEG)
        _, sel = lax.top_k(imp, top_n)
        kg = ks_blk[bi, hi, sel]
        vg = vs_blk[bi, hi, sel]
        kpos = sel[..., None] * SEL_BLOCK + jnp.arange(SEL_BLOCK)
        ok_s = (sel[..., None] <= jc[None, None, :, None, None]) & (kpos <= t[None, None, :, None, None])
        ss = jnp.where(ok_s[:, :, None], jnp.einsum('bqhgd,bhqnld->bhgqnl', qb, kg), NEG)
        ps = jax.nn.softmax(ss, axis=(-2, -1))
        o_sel = jnp.einsum('bhgqnl,bhqnld->bqhgd', ps, vg)
        kwb = lax.dynamic_slice_in_dim(kw_pad, s, WINDOW + Q_BLOCK, axis=1)
        vwb = lax.dynamic_slice_in_dim(vw_pad, s, WINDOW + Q_BLOCK, axis=1)
        kidx = s - WINDOW + jnp.arange(WINDOW + Q_BLOCK)
        ok_w = (kidx[None] <= t[:, None]) & (kidx[None] > t[:, None] - WINDOW) & (kidx[None] >= 0)
        sw = jnp.where(ok_w, jnp.einsum('bqhgd,bmhd->bhgqm', qb, kwb), NEG)
        o_win = jnp.einsum('bhgqm,bmhd->bqhgd', jax.nn.softmax(sw, axis=-1), vwb)
        gb = lax.dynamic_slice_in_dim(gates, s, Q_BLOCK, axis=1)
        o = gb[..., 0:1] * o_cmp + gb[..., 1:2] * o_sel + gb[..., 2:3] * o_win
        return o.reshape(B, Q_BLOCK, NSA_WIDTH)

    out = lax.map(q_block, jnp.arange(T // Q_BLOCK))
    return jnp.moveaxis(out, 0, 1).reshape(B, T, NSA_WIDTH).astype(z.dtype)


def conv_ffn(h, w_up, conv_w, conv_b, w_down):
    u = h @ w_up
    c = lax.conv_general_dilated(u, conv_w.astype(u.dtype), window_strides=(1,),
                                 padding=[(CONV_W - 1, 0)],
                                 dimension_numbers=('NWC', 'WIO', 'NWC'),
                                 feature_group_count=u.shape[-1]) + conv_b
    gate, val = jnp.split(c, 2, axis=-1)
    return (jax.nn.gelu(gate) * val) @ w_down


def setup_inputs(seed: int = 0) -> dict:
    key = jax.random.key(seed)
    ks = jax.random.split(key, 40)
    f32 = jnp.float32
    nrm = lambda k, shape, s: jax.random.normal(k, shape, f32) * s
    gain = lambda k, shape: 1.0 + nrm(k, shape, 0.02)
    L = DEPTH
    positions = (jax.random.randint(ks[2], (BATCH, 1), 0, 4096, jnp.int32)
                 + jnp.arange(SEQ, dtype=jnp.int32)[None])
    return {
        "x": nrm(ks[0], (BATCH, SEQ, D_MODEL), 1.0),
        "p": nrm(ks[1], (DEPTH, BATCH, SEQ, PLE_DIM), 1.0),
        "positions": positions,
        "g_mix": gain(ks[3], (L, D_MODEL)),
        "w_in": nrm(ks[4], (L, D_MODEL, IN_WIDTH), D_MODEL ** -0.5),
        "rw_mu": jax.random.uniform(ks[5], (L, RW_IN), f32, 0.2, 0.8),
        "rw_w0": jax.random.uniform(ks[6], (L, RW_WIDTH), f32, -6.0, -1.0),
        "rw_w_up": nrm(ks[7], (L, RW_DECAY_LORA, RW_WIDTH), 0.1),
        "rw_a0": nrm(ks[8], (L, RW_WIDTH), 0.1),
        "rw_a_up": nrm(ks[9], (L, RW_AAA_LORA, RW_WIDTH), 0.5 * RW_AAA_LORA ** -0.5),
        "rw_g_up": nrm(ks[10], (L, RW_GATE_LORA, RW_WIDTH), RW_GATE_LORA ** -0.5),
        "rw_k_k": 0.85 + nrm(ks[11], (L, RW_WIDTH), 0.02),
        "rw_k_a": gain(ks[12], (L, RW_WIDTH)),
        "rw_r_k": nrm(ks[13], (L, RW_HEADS, RW_HEAD_DIM), 0.1),
        "rw_gn_g": gain(ks[14], (L, RW_WIDTH)),
        "rw_gn_b": nrm(ks[15], (L, RW_WIDTH), 0.02),
        "pool_w": nrm(ks[16], (L, POOL_GROUPS, POOL_GROUP_DIM, POOL_GROUP_DIM), POOL_GROUP_DIM ** -0.5),
        "pool_scale": 1.0 + nrm(ks[17], (L, POOL_WIDTH), 0.1),
        "nsa_pe_k": nrm(ks[18], (L, CMP_BLOCK, NSA_HEAD_DIM), 0.02),
        "nsa_pe_v": nrm(ks[19], (L, CMP_BLOCK, NSA_HEAD_DIM), 0.02),
        "nsa_w_ck": nrm(ks[20], (L, CMP_BLOCK, NSA_HEAD_DIM, NSA_HEAD_DIM), (CMP_BLOCK * NSA_HEAD_DIM) ** -0.5),
        "nsa_w_cv": nrm(ks[21], (L, CMP_BLOCK, NSA_HEAD_DIM, NSA_HEAD_DIM), (CMP_BLOCK * NSA_HEAD_DIM) ** -0.5),
        "w_out": nrm(ks[22], (L, MIX_WIDTH, D_MODEL), MIX_WIDTH ** -0.5),
        "g_ffn": gain(ks[23], (L, D_MODEL)),
        "ffn_w_up": nrm(ks[24], (L, D_MODEL, 2 * D_FF), D_MODEL ** -0.5),
        "ffn_conv_w": nrm(ks[25], (L, CONV_W, 1, 2 * D_FF), CONV_W ** -0.5),
        "ffn_conv_b": nrm(ks[26], (L, 2 * D_FF), 0.02),
        "ffn_w_down": nrm(ks[27], (L, D_FF, D_MODEL), D_FF ** -0.5),
        "g_ple": gain(ks[28], (L, D_MODEL)),
        "ple_w_gate": nrm(ks[29], (L, D_MODEL, D_MODEL), D_MODEL ** -0.5),
        "ple_w_proj": nrm(ks[30], (L, PLE_DIM, D_MODEL), PLE_DIM ** -0.5),
        "g_final": gain(ks[31], (D_MODEL,)),
    }


def reference(x, p, positions, g_mix, w_in, rw_mu, rw_w0, rw_w_up, rw_a0, rw_a_up, rw_g_up,
              rw_k_k, rw_k_a, rw_r_k, rw_gn_g, rw_gn_b, pool_w, pool_scale,
              nsa_pe_k, nsa_pe_v, nsa_w_ck, nsa_w_cv, w_out, g_ffn, ffn_w_up, ffn_conv_w,
              ffn_conv_b, ffn_w_down, g_ple, ple_w_gate, ple_w_proj, g_final):
    for i in range(DEPTH):
        h = rmsnorm(x, g_mix[i])
        z = h @ w_in[i]
        z_a, z_b, z_c = jnp.split(z, (RW_IN, RW_IN + POOL_WIDTH), axis=-1)
        y_a = rwkv7_mixer(z_a, rw_mu[i], rw_w0[i], rw_w_up[i], rw_a0[i], rw_a_up[i], rw_g_up[i],
                          rw_k_k[i], rw_k_a[i], rw_r_k[i], rw_gn_g[i], rw_gn_b[i])
        y_b = pool_mixer(z_b, pool_w[i], pool_scale[i])
        y_c = nsa_mixer(z_c, positions, nsa_pe_k[i], nsa_pe_v[i], nsa_w_ck[i], nsa_w_cv[i])
        x = x + jnp.concatenate([y_a, y_b, y_c], axis=-1) @ w_out[i]
        x = x + conv_ffn(rmsnorm(x, g_ffn[i]), ffn_w_up[i], ffn_conv_w[i], ffn_conv_b[i], ffn_w_down[i])
        gate = jax.nn.sigmoid(rmsnorm(x, g_ple[i]) @ ple_w_gate[i])
        x = x + (p[i] @ ple_w_proj[i]) * gate
    return rmsnorm(x, g_final)
```

```python
import os
import numpy as np
from contextlib import ExitStack
import concourse.bass as bass
import concourse.mybir as mybir
from concourse.bass_utils import run_bass_kernel_spmd

F32 = mybir.dt.float32
BF16 = mybir.dt.bfloat16
I32 = mybir.dt.int32
AF = mybir.ActivationFunctionType
ALU = mybir.AluOpType
AX = mybir.AxisListType

DM = 1024
PI = float(np.pi)
BIG = 30000.0
NFEAT = 3200
NTOKC = 280
NEXT = NFEAT + NTOKC
DFF = 2816


class Buf:
    __slots__ = ("w", "rs", "rd", "excl")

    def __init__(self, excl=False):
        self.w = None
        self.rs = {}
        self.rd = []
        self.excl = excl


class Sched:
    EPOCH = 30000
    NDMA = 6

    def __init__(self, nc, stack):
        self.nc = nc
        self.stack = stack
        self.engs = {"pe": nc.tensor, "act": nc.scalar, "dve": nc.vector, "pool": nc.gpsimd, "sp": nc.sync}
        self.nsem = 0
        self.sem = {}
        self.cnt = {}
        self.seen = {k: {} for k in self.engs}
        for k in self.engs:
            self.sem[k] = self._newsem(k)
            self.cnt[k] = 0
        self.dsem = {}
        self.dpos = {}
        for k in ("sp", "act", "pool"):
            self.dsem[k] = [[self._newsem("d" + k), 0] for _ in range(self.NDMA)]
            self.dpos[k] = 0
        self.last = {}
        self.ninstr = 0
        self.store_q = "pool"

    def _newsem(self, name):
        self.nsem += 1
        return self.stack.enter_context(self.nc.semaphore(f"s_{name}_{self.nsem}"))

    def _wait(self, ek, tok):
        sem, val, src = tok
        d = self.seen[ek]
        key = id(sem)
        if d.get(key, 0) >= val:
            return
        self.engs[ek].wait_ge(sem, val)
        d[key] = val

    def _deps(self, ek, reads, writes):
        toks = []
        same_ok = (ek == "pe")
        for b in reads:
            if b.w is not None and not (b.w[2] == ek and same_ok):
                toks.append(b.w)
            if b.excl:
                for e, t in b.rs.items():
                    if e != ek:
                        toks.append(t)
        for b in writes:
            if b.w is not None and not (b.w[2] == ek and same_ok):
                toks.append(b.w)
            for e, t in b.rs.items():
                if not (e == ek and same_ok):
                    toks.append(t)
            toks.extend(b.rd)
        for t in toks:
            self._wait(ek, t)

    def _record(self, tok, reads, writes, is_dma):
        for b in reads:
            if is_dma:
                b.rd.append(tok)
                if len(b.rd) > 24:
                    del b.rd[0]
            else:
                b.rs[tok[2]] = tok
        for b in writes:
            b.w = tok
            b.rs = {}
            b.rd = []

    def op(self, ek, fn, reads=(), writes=()):
        self._deps(ek, reads, writes)
        ins = fn(self.engs[ek])
        self.cnt[ek] += 1
        ins.then_inc(self.sem[ek], 1)
        tok = (self.sem[ek], self.cnt[ek], ek)
        self.last[ek] = tok
        self._record(tok, reads, writes, False)
        self.ninstr += 1
        if self.cnt[ek] >= self.EPOCH:
            self.sem[ek] = self._newsem(ek)
            self.cnt[ek] = 0
        return tok

    def dma(self, qk, out, in_, reads=(), writes=(), **kw):
        if qk == "sp" and len(writes) == 0 and self.store_q is not None:
            qk = self.store_q
        self._deps(qk, reads, writes)
        slots = self.dsem[qk]
        i = self.dpos[qk]
        self.dpos[qk] = (i + 1) % len(slots)
        sem, val = slots[i]
        if val > 0:
            self._wait(qk, (sem, val, "dma"))
        if val + 16 > 60000:
            sem = self._newsem("d" + qk)
            val = 0
            slots[i][0] = sem
        ins = self.engs[qk].dma_start(out=out, in_=in_, **kw)
        val += 16
        ins.then_inc(sem, 16)
        slots[i][1] = val
        tok = (sem, val, "dma")
        self._record(tok, reads, writes, True)
        self.ninstr += 1
        return tok

    def barrier(self):
        toks = list(self.last.values())
        for qk in self.dsem:
            for sem, val in self.dsem[qk]:
                if val > 0:
                    toks.append((sem, val, "dma"))
        for ek in self.engs:
            for t in toks:
                self._wait(ek, t)


def inproj_cols():
    qb = 1280
    rot = lambda base, nh: [base + h * 64 + ((d + 32) % 64) for h in range(nh) for d in range(64)]
    cols = list(range(0, 1280))
    cols += list(range(qb, qb + 512)) + rot(qb, 8)
    for off in (512, 768, 1024):
        cols += list(range(qb + off, qb + off + 128))
    for off in (512, 768, 1024):
        cols += rot(qb + off, 2)
    cols += list(range(qb + 640, qb + 768))
    assert len(cols) == NFEAT
    cols += list(range(qb + 896, qb + 1024)) + list(range(qb + 1152, qb + 1280))
    cols += list(range(qb + 1280, qb + 1304))
    assert len(cols) == NEXT
    return np.array(cols)


class G:
    uid = 0
    debug = False
    use_f32r = True

    def dbg(self, name, ap, buf, shape):
        if not self.debug or name in self.D:
            return
        self.D[name] = self.nc.dram_tensor(name, list(shape), F32, kind="ExternalOutput").ap()
        self.S.dma("sp", self.D[name], ap, reads=[buf])

    def nm(self, name):
        self.uid += 1
        return f"{name}_{self.uid}"


def cp(S, ek, out, in_, reads, writes):
    if ek == "act":
        return S.op("act", lambda e: e.copy(out=out, in_=in_), reads=reads, writes=writes)
    return S.op(ek, lambda e: e.tensor_copy(out=out, in_=in_), reads=reads, writes=writes)


def load_w_bf16(g, dst, b_dst, src, KC, N, stage, b_stage, rows=128):
    S = g.S
    CH = stage[0].shape[-1]
    srcv = src.rearrange("(c p) n -> p c n", p=rows)
    for c in range(KC):
        for n0 in range(0, N, CH):
            n1 = min(N, n0 + CH)
            i = g.wctr % len(stage)
            g.wctr += 1
            S.dma("sp", stage[i][0:rows, 0:n1 - n0], srcv[:, c, n0:n1], writes=[b_stage[i]])
            ek = ("act", "dve", "pool")[g.wctr % 3]
            cp(S, ek, dst[0:rows, c, n0:n1], stage[i][0:rows, 0:n1 - n0], [b_stage[i]], [b_dst])


def rmsnorm_tile(g, xt, b_x, hT, b_h, gcol, b_g, N, R):
    S = g.S
    S.op("act", lambda e: e.activation(out=R["sq"][:, :, 0:N], in_=xt[:, :, 0:N], func=AF.Square), reads=[b_x], writes=[R["b_sq"]])
    for c in range(8):
        S.op("pe", lambda e: e.matmul(R["p_rms"][:, 0:N], lhsT=R["ones"][:], rhs=R["sq"][:, c, 0:N], start=(c == 0), stop=(c == 7)),
             reads=[R["b_ones"], R["b_sq"]], writes=[R["b_prms"]])
    S.op("act", lambda e: e.activation(out=R["rstd"][:, 0:N], in_=R["p_rms"][:, 0:N], func=AF.Sqrt, bias=R["eps"][:, 0:1], scale=1.0 / DM),
         reads=[R["b_prms"], R["b_eps"]], writes=[R["b_rstd"]])
    S.op("dve", lambda e: e.reciprocal(out=R["rstd"][:, 0:N], in_=R["rstd"][:, 0:N]), reads=[R["b_rstd"]], writes=[R["b_rstd"]])
    for c in range(8):
        S.op("dve", lambda e: e.scalar_tensor_tensor(out=hT[:, c, 0:N], in0=xt[:, c, 0:N], scalar=gcol[:, c:c + 1], in1=R["rstd"][:, 0:N],
                                                       op0=ALU.mult, op1=ALU.mult),
             reads=[b_x, b_g, R["b_rstd"]], writes=[b_h])


def rms_shared(g, sbf, psf, N):
    S = g.S
    R = {}
    R["sq"] = sbf("rsq", [128, 8, N], BF16); R["b_sq"] = Buf()
    R["rstd"] = sbf("rstd", [128, N]); R["b_rstd"] = Buf()
    R["p_rms"] = psf("p_rms", [128, 512]); R["b_prms"] = Buf(excl=True)
    R["ones"] = sbf("ones", [128, 128], BF16); R["b_ones"] = Buf()
    R["eps"] = sbf("epsb", [128, 1]); R["b_eps"] = Buf()
    S.op("pool", lambda e: e.memset(R["ones"][:], 1.0), writes=[R["b_ones"]])
    S.op("pool", lambda e: e.memset(R["eps"][:], 1e-6), writes=[R["b_eps"]])
    return R


def phase_rope(g):
    nc, S, D, T = g.nc, g.S, g.D, g.T
    with ExitStack() as st:
        sbf = lambda name, shape, dt=F32: st.enter_context(nc.sbuf_tensor(g.nm(name), list(shape), dt))
        CH = min(T, 2048)
        posi = sbf("posi", [128, CH], I32); b_posi = Buf()
        posf = sbf("posf", [128, CH]); b_posf = Buf()
        ang = sbf("ang", [128, CH]); b_ang = Buf()
        tab = sbf("tab", [128, CH]); b_tab = Buf()
        ki = sbf("ki", [128, CH], I32); b_ki = Buf()
        kf = sbf("kf", [128, CH]); b_kf = Buf()
        inv_sb = sbf("inv_sb", [128, 1]); b_inv = Buf()
        S.dma("sp", inv_sb[:], D["invf"], writes=[b_inv])
        C1 = 6.28125
        C2 = 2 * np.pi - 6.28125
        for c0 in range(0, T, CH):
            S.dma("sp", posi[:], D["pos"][:, c0:c0 + CH].partition_broadcast(128), writes=[b_posi])
            S.op("dve", lambda e: e.tensor_copy(out=posf[:], in_=posi[:]), reads=[b_posi], writes=[b_posf])
            S.op("dve", lambda e: e.tensor_scalar(out=posf[:], in0=posf[:], scalar1=inv_sb[:, 0:1], scalar2=None, op0=ALU.mult),
                 reads=[b_posf, b_inv], writes=[b_posf])
            S.op("dve", lambda e: e.tensor_scalar(out=kf[:], in0=posf[:], scalar1=float(1.0 / (2 * np.pi)), scalar2=None, op0=ALU.mult),
                 reads=[b_posf], writes=[b_kf])
            S.op("dve", lambda e: e.tensor_copy(out=ki[:], in_=kf[:]), reads=[b_kf], writes=[b_ki])
            S.op("dve", lambda e: e.tensor_copy(out=kf[:], in_=ki[:]), reads=[b_ki], writes=[b_kf])
            S.op("dve", lambda e: e.scalar_tensor_tensor(out=posf[:], in0=kf[:], scalar=-C1, in1=posf[:], op0=ALU.mult, op1=ALU.add),
                 reads=[b_kf, b_posf], writes=[b_posf])
            S.op("dve", lambda e: e.scalar_tensor_tensor(out=posf[:], in0=kf[:], scalar=-C2, in1=posf[:], op0=ALU.mult, op1=ALU.add),
                 reads=[b_kf, b_posf], writes=[b_posf])
            for which, shift, dst in (("sin", 0.0, D["sinT"]), ("cos", PI / 2, D["cosT"])):
                S.op("dve", lambda e: e.tensor_scalar(out=ang[:], in0=posf[:], scalar1=shift, scalar2=None, op0=ALU.add),
                     reads=[b_posf], writes=[b_ang])
                S.op("dve", lambda e: e.tensor_scalar(out=kf[:], in0=ang[:], scalar1=PI, scalar2=-2 * PI, op0=ALU.is_gt, op1=ALU.mult),
                     reads=[b_ang], writes=[b_kf])
                S.op("dve", lambda e: e.tensor_tensor(out=ang[:], in0=ang[:], in1=kf[:], op=ALU.add), reads=[b_ang, b_kf], writes=[b_ang])
                S.op("dve", lambda e: e.tensor_scalar(out=ang[:], in0=ang[:], scalar1=3.141592, scalar2=-3.141592, op0=ALU.min, op1=ALU.max),
                     reads=[b_ang], writes=[b_ang])
                S.op("act", lambda e: e.activation(out=tab[:], in_=ang[:], func=AF.Sin), reads=[b_ang], writes=[b_tab])
                if which == "sin":
                    for base in (0, 64):
                        S.op("dve", lambda e: e.tensor_scalar(out=tab[base:base + 32, :], in0=tab[base:base + 32, :], scalar1=-1.0, scalar2=None, op0=ALU.mult),
                             reads=[b_tab], writes=[b_tab])
                S.dma("sp", dst[:, c0:c0 + CH], tab[:], reads=[b_tab])
        S.barrier()


def phase_inproj(g, l, xin):
    nc, S, D, T = g.nc, g.S, g.D, g.T
    with ExitStack() as st:
        sbf = lambda name, shape, dt=F32: st.enter_context(nc.sbuf_tensor(g.nm(name), list(shape), dt))
        psf = lambda name, shape, dt=F32: st.enter_context(nc.psum_tensor(g.nm(name), list(shape), dt))
        Wb = sbf("Wb", [128, 8, NEXT], BF16); b_W = Buf()
        stage = [sbf(f"wst{i}", [128, 1740]) for i in range(2)]; b_stage = [Buf() for _ in range(2)]
        load_w_bf16(g, Wb, b_W, D["w_in"][l], 8, NEXT, stage, b_stage)
        sm = sbf("sm", [128, 8]); b_sm = Buf()
        S.dma("sp", sm[:], D["smalls"][l][:, 0:8], writes=[b_sm])
        R = rms_shared(g, sbf, psf, 512)
        xt = [sbf(f"xt{i}", [128, 8, 512]) for i in range(2)]; b_xt = [Buf() for _ in range(2)]
        hT = sbf("hT", [128, 8, 512], BF16); b_h = Buf()
        cs = [sbf(f"cs{i}", [128, 512]) for i in range(2)]; b_cs = [Buf() for _ in range(2)]
        sn = [sbf(f"sn{i}", [128, 512]) for i in range(2)]; b_sn = [Buf() for _ in range(2)]
        NEV = 4
        ev = [sbf(f"ev{i}", [128, 512]) for i in range(NEV)]; b_ev = [Buf() for _ in range(NEV)]
        evb = [sbf(f"evb{i}", [128, 512], BF16) for i in range(NEV)]; b_evb = [Buf() for _ in range(NEV)]
        t1 = [sbf(f"t1_{i}", [128, 512]) for i in range(2)]; b_t1 = [Buf() for _ in range(2)]
        t2 = [sbf(f"t2_{i}", [128, 512]) for i in range(2)]; b_t2 = [Buf() for _ in range(2)]
        vt = [sbf(f"vt{i}", [128, 256], BF16) for i in range(2)]; b_vt = [Buf() for _ in range(2)]
        gt = [sbf(f"gt{i}", [128, 24]) for i in range(2)]; b_gt = [Buf() for _ in range(2)]
        NPF = 5
        p_f = [psf(f"p_f{i}", [128, 512]) for i in range(NPF)]; b_pf = [Buf(excl=True) for _ in range(NPF)]
        p_t = [psf(f"p_t{i}", [128, 512]) for i in range(2)]; b_pt = [Buf(excl=True) for _ in range(2)]
        xv = xin.rearrange("(c p) t -> p c t", p=128)
        evc = 0
        pfc = 0

        def mm_feat(f, pf, bpf):
            for c in range(8):
                S.op("pe", lambda e: e.matmul(pf[:], lhsT=Wb[:, c, f * 128:(f + 1) * 128], rhs=hT[:, c, :], start=(c == 0), stop=(c == 7)),
                     reads=[b_W, b_h], writes=[bpf])
        for tt in range(T // 512):
            t0 = tt * 512
            xi = tt % 2
            S.dma("sp", xt[xi][:], xv[:, :, t0:t0 + 512], writes=[b_xt[xi]])
            S.dma("sp", cs[xi][:], D["cosT"][:, t0:t0 + 512], writes=[b_cs[xi]])
            S.dma("sp", sn[xi][:], D["sinT"][:, t0:t0 + 512], writes=[b_sn[xi]])
            rmsnorm_tile(g, xt[xi], b_xt[xi], hT, b_h, sm, b_sm, 512, R)
            plain = [(f, D["zaT"][f * 128:(f + 1) * 128, t0:t0 + 512], False) for f in range(8)]
            plain += [(8 + f, D["zbT"][f * 128:(f + 1) * 128, t0:t0 + 512], False) for f in range(2)]
            plain += [(24, D["vcT"][:, t0:t0 + 512], True)]
            for k_, (f, dst, isb) in enumerate(plain):
                pi = pfc % NPF; pfc += 1
                mm_feat(f, p_f[pi], b_pf[pi])
                ei = evc % NEV; evc += 1
                ek = "act" if k_ % 2 == 0 else "dve"
                if isb:
                    cp(S, ek, evb[ei][:], p_f[pi][:], [b_pf[pi]], [b_evb[ei]])
                    S.dma("sp", dst, evb[ei][:], reads=[b_evb[ei]])
                else:
                    cp(S, ek, ev[ei][:], p_f[pi][:], [b_pf[pi]], [b_ev[ei]])
                    S.dma("sp", dst, ev[ei][:], reads=[b_ev[ei]])
            for j in range(7):
                if j < 4:
                    f_a, f_b = 10 + j, 14 + j
                    dst = D["qT"][j * 128:(j + 1) * 128, t0:t0 + 512]
                else:
                    f_a, f_b = 18 + (j - 4), 21 + (j - 4)
                    dst = D["kT"][(j - 4) * 128:(j - 3) * 128, t0:t0 + 512]
                pa = pfc % NPF; pfc += 1
                mm_feat(f_a, p_f[pa], b_pf[pa])
                pb = pfc % NPF; pfc += 1
                mm_feat(f_b, p_f[pb], b_pf[pb])
                ti = j % 2
                S.op("dve", lambda e: e.tensor_tensor(out=t1[ti][:], in0=p_f[pa][:], in1=cs[xi][:], op=ALU.mult),
                     reads=[b_pf[pa], b_cs[xi]], writes=[b_t1[ti]])
                S.op("dve", lambda e: e.tensor_tensor(out=t2[ti][:], in0=p_f[pb][:], in1=sn[xi][:], op=ALU.mult),
                     reads=[b_pf[pb], b_sn[xi]], writes=[b_t2[ti]])
                ei = evc % NEV; evc += 1
                S.op("pool", lambda e: e.tensor_tensor(out=evb[ei][:], in0=t1[ti][:], in1=t2[ti][:], op=ALU.add),
                     reads=[b_t1[ti], b_t2[ti]], writes=[b_evb[ei]])
                S.dma("sp", dst, evb[ei][:], reads=[b_evb[ei]])
            for s4 in range(4):
                pi = s4 % 2
                for c in range(8):
                    S.op("pe", lambda e: e.matmul(p_t[pi][:, 0:NTOKC], lhsT=hT[:, c, s4 * 128:(s4 + 1) * 128], rhs=Wb[:, c, NFEAT:NEXT],
                                                  start=(c == 0), stop=(c == 7)),
                         reads=[b_W, b_h], writes=[b_pt[pi]])
                S.op("dve", lambda e: e.tensor_copy(out=vt[pi][:], in_=p_t[pi][:, 0:256]), reads=[b_pt[pi]], writes=[b_vt[pi]])
                S.op("act", lambda e: e.activation(out=gt[pi][:], in_=p_t[pi][:, 256:280], func=AF.Sigmoid), reads=[b_pt[pi]], writes=[b_gt[pi]])
                S.dma("sp", D["vtok"][t0 + s4 * 128:t0 + (s4 + 1) * 128, :], vt[pi][:], reads=[b_vt[pi]])
                S.dma("sp", D["gates"][t0 + s4 * 128:t0 + (s4 + 1) * 128, :], gt[pi][:], reads=[b_gt[pi]])
        S.barrier()


def phase_pool(g, l):
    nc, S, D, T = g.nc, g.S, g.D, g.T
    with ExitStack() as st:
        sbf = lambda name, shape, dt=F32: st.enter_context(nc.sbuf_tensor(g.nm(name), list(shape), dt))
        psf = lambda name, shape, dt=F32: st.enter_context(nc.psum_tensor(g.nm(name), list(shape), dt))
        CH = 512
        PAD = 16
        wp = sbf("wp", [128, 2, 128]); b_wp = Buf()
        S.dma("sp", wp[:], D["pool_wbd"][l], writes=[b_wp])
        sm = sbf("smp", [128, 2]); b_sm = Buf()
        S.dma("sp", sm[:], D["smalls"][l][:, 8:10], writes=[b_sm])
        fix = sbf("fix", [128, 2, 16]); b_fix = Buf()
        S.dma("sp", fix[:], D["pool_fix"], writes=[b_fix])
        z = [[sbf(f"pz{i}_{f}", [128, PAD + CH]) for f in range(2)] for i in range(2)]
        b_z = [[Buf() for f in range(2)] for i in range(2)]
        s_a = sbf("ps_a", [128, PAD + CH]); b_sa = Buf()
        s_b = sbf("ps_b", [128, PAD + CH]); b_sb = Buf()
        pl = sbf("ppl", [128, CH]); b_pl = Buf()
        ob = [sbf(f"pob{i}", [128, CH], BF16) for i in range(2)]; b_ob = [Buf() for _ in range(2)]
        pp = [psf(f"ppp{i}", [128, 512]) for i in range(2)]; b_pp = [Buf(excl=True) for _ in range(2)]
        k = 0
        for tt in range(T // CH):
            t0 = tt * CH
            zi = tt % 2
            for f in range(2):
                zt, bz = z[zi][f], b_z[zi][f]
                if tt == 0:
                    S.op("pool", lambda e: e.memset(zt[:, 0:PAD], 0.0), writes=[bz])
                    S.dma("sp", zt[:, PAD:PAD + CH], D["zbT"][f * 128:(f + 1) * 128, 0:CH], writes=[bz])
                else:
                    S.dma("sp", zt[:], D["zbT"][f * 128:(f + 1) * 128, t0 - PAD:t0 + CH], writes=[bz])
                W = PAD + CH
                S.op("dve", lambda e: e.tensor_tensor(out=s_a[:, 1:W], in0=zt[:, 1:W], in1=zt[:, 0:W - 1], op=ALU.add), reads=[bz], writes=[b_sa])
                S.op("dve", lambda e: e.tensor_tensor(out=s_b[:, 3:W], in0=s_a[:, 3:W], in1=s_a[:, 1:W - 2], op=ALU.add), reads=[b_sa], writes=[b_sb])
                if f == 0:
                    lo, hi = s_a, s_b
                    blo, bhi = b_sa, b_sb
                    wl, wh = 2, 4
                else:
                    S.op("dve", lambda e: e.tensor_tensor(out=s_a[:, 7:W], in0=s_b[:, 7:W], in1=s_b[:, 3:W - 4], op=ALU.add), reads=[b_sb], writes=[b_sa])
                    S.op("dve", lambda e: e.tensor_tensor(out=s_b[:, 15:W], in0=s_a[:, 15:W], in1=s_a[:, 7:W - 8], op=ALU.add), reads=[b_sa], writes=[b_sb])
                    lo, hi = s_a, s_b
                    blo, bhi = b_sa, b_sb
                    wl, wh = 8, 16
                S.op("dve", lambda e: e.scalar_tensor_tensor(out=pl[0:64, :], in0=lo[0:64, PAD:W], scalar=1.0 / wl, in1=zt[0:64, PAD:W], op0=ALU.mult, op1=ALU.subtract),
                     reads=[blo, bz], writes=[b_pl])
                S.op("dve", lambda e: e.scalar_tensor_tensor(out=pl[64:128, :], in0=hi[64:128, PAD:W], scalar=1.0 / wh, in1=zt[64:128, PAD:W], op0=ALU.mult, op1=ALU.subtract),
                     reads=[bhi, bz], writes=[b_pl])
                if tt == 0:
                    S.op("dve", lambda e: e.tensor_tensor(out=pl[0:64, 0:16], in0=lo[0:64, PAD:PAD + 16], in1=fix[0:64, f, :], op=ALU.mult), reads=[blo, b_fix], writes=[b_pl])
                    S.op("dve", lambda e: e.tensor_tensor(out=pl[64:128, 0:16], in0=hi[64:128, PAD:PAD + 16], in1=fix[64:128, f, :], op=ALU.mult), reads=[bhi, b_fix], writes=[b_pl])
                    S.op("dve", lambda e: e.tensor_tensor(out=pl[:, 0:16], in0=pl[:, 0:16], in1=zt[:, PAD:PAD + 16], op=ALU.subtract), reads=[b_pl, bz], writes=[b_pl])
                pi = k % 2; k += 1
                S.op("pe", lambda e: e.matmul(pp[pi][:, 0:CH], lhsT=wp[:, f, :], rhs=pl[:], start=True, stop=True), reads=[b_wp, b_pl], writes=[b_pp[pi]])
                S.op("act", lambda e: e.activation(out=ob[pi][:], in_=pp[pi][:, 0:CH], func=AF.Copy, scale=sm[:, f:f + 1]), reads=[b_pp[pi], b_sm], writes=[b_ob[pi]])
                S.dma("sp", D["ymixT"][256 + f * 128:256 + (f + 1) * 128, t0:t0 + CH], ob[pi][:], reads=[b_ob[pi]])
        S.barrier()


def phase_outproj(g, l, xin, xout):
    nc, S, D, T = g.nc, g.S, g.D, g.T
    with ExitStack() as st:
        sbf = lambda name, shape, dt=F32: st.enter_context(nc.sbuf_tensor(g.nm(name), list(shape), dt))
        psf = lambda name, shape, dt=F32: st.enter_context(nc.psum_tensor(g.nm(name), list(shape), dt))
        Wo = sbf("Wo", [128, 8, DM], BF16); b_W = Buf()
        stage = [sbf(f"wst{i}", [128, 1024]) for i in range(2)]; b_stage = [Buf() for _ in range(2)]
        load_w_bf16(g, Wo, b_W, D["w_out"][l], 8, DM, stage, b_stage)
        xt = [sbf(f"oxt{i}", [128, 8, 512]) for i in range(2)]; b_xt = [Buf() for _ in range(2)]
        ym = [sbf(f"oym{i}", [128, 8, 512], BF16) for i in range(2)]; b_ym = [Buf() for _ in range(2)]
        xo = [sbf(f"oxo{i}", [128, 8, 512]) for i in range(2)]; b_xo = [Buf() for _ in range(2)]
        pq = [psf(f"opq{i}", [128, 512]) for i in range(4)]; b_pq = [Buf(excl=True) for _ in range(4)]
        xv = xin.rearrange("(c p) t -> p c t", p=128)
        xov = xout.rearrange("(c p) t -> p c t", p=128)
        yv = D["ymixT"].rearrange("(c p) t -> p c t", p=128)
        k = 0
        for tt in range(T // 512):
            t0 = tt * 512
            xi = tt % 2
            S.dma("sp", xt[xi][:], xv[:, :, t0:t0 + 512], writes=[b_xt[xi]])
            S.dma("sp", ym[xi][:], yv[:, :, t0:t0 + 512], writes=[b_ym[xi]])
            for j in range(8):
                pi = k % 4; k += 1
                for c in range(8):
                    S.op("pe", lambda e: e.matmul(pq[pi][:], lhsT=Wo[:, c, j * 128:(j + 1) * 128], rhs=ym[xi][:, c, :], start=(c == 0), stop=(c == 7)),
                         reads=[b_W, b_ym[xi]], writes=[b_pq[pi]])
                S.op("dve", lambda e: e.tensor_tensor(out=xo[xi][:, j, :], in0=pq[pi][:], in1=xt[xi][:, j, :], op=ALU.add),
                     reads=[b_pq[pi], b_xt[xi]], writes=[b_xo[xi]])
            S.dma("sp", xov[:, :, t0:t0 + 512], xo[xi][:], reads=[b_xo[xi]])
        S.barrier()


def phase_ffn(g, l, xin, xout):
    nc, S, D, T = g.nc, g.S, g.D, g.T
    N = 512
    with ExitStack() as st:
        sbf = lambda name, shape, dt=F32: st.enter_context(nc.sbuf_tensor(g.nm(name), list(shape), dt))
        psf = lambda name, shape, dt=F32: st.enter_context(nc.psum_tensor(g.nm(name), list(shape), dt))
        Wu = sbf("Wu", [128, 8, 2 * DFF], BF16); b_Wu = Buf()
        Wd = sbf("Wd", [128, 22, DM], BF16); b_Wd = Buf()
        with ExitStack() as st2:
            stage = [st2.enter_context(nc.sbuf_tensor(g.nm(f"wst{i}"), [128, 1408], F32)) for i in range(2)]; b_stage = [Buf() for _ in range(2)]
            load_w_bf16(g, Wu, b_Wu, D["ffn_w_up"][l], 8, 2 * DFF, stage, b_stage)
            load_w_bf16(g, Wd, b_Wd, D["ffn_w_down"][l], 22, DM, stage, b_stage)
            S.barrier()
        sm = sbf("smf", [128, 8]); b_sm = Buf()
        S.dma("sp", sm[:], D["smalls"][l][:, 16:24], writes=[b_sm])
        cw = sbf("cw", [128, 44, 4]); b_cw = Buf()
        S.dma("sp", cw[:], D["convp"][l], writes=[b_cw])
        gated = sbf("gated", [128, 22, N], BF16); b_gt = Buf()
        R = {}
        R["sq"] = gated[:, 0:8, :]; R["b_sq"] = b_gt
        R["rstd"] = sbf("rstd", [128, N]); R["b_rstd"] = Buf()
        R["p_rms"] = psf("p_rms", [128, 512]); R["b_prms"] = Buf(excl=True)
        R["ones"] = sbf("ones", [128, 128], BF16); R["b_ones"] = Buf()
        R["eps"] = sbf("epsb", [128, 1]); R["b_eps"] = Buf()
        S.op("pool", lambda e: e.memset(R["ones"][:], 1.0), writes=[R["b_ones"]])
        S.op("pool", lambda e: e.memset(R["eps"][:], 1e-6), writes=[R["b_eps"]])
        xt = sbf("fxt", [128, 8, N]); b_xt = Buf()
        hT = sbf("fhT", [128, 8, N], BF16); b_h = Buf()
        carry = sbf("carry", [128, 44, 2]); b_carry = Buf()
        S.op("pool", lambda e: e.memset(carry[:], 0.0), writes=[b_carry])
        NU = 3
        U = [sbf(f"U{i}", [128, N + 2]) for i in range(NU)]; b_U = [Buf() for _ in range(NU)]
        cg = [sbf(f"cg{i}", [128, N]) for i in range(2)]; b_cg = [Buf() for _ in range(2)]
        cv = [sbf(f"cv{i}", [128, N]) for i in range(2)]; b_cv = [Buf() for _ in range(2)]
        gi = [sbf(f"gi{i}", [128, N]) for i in range(2)]; b_gi = [Buf() for _ in range(2)]
        pu = [psf(f"fpu{i}", [128, 512]) for i in range(4)]; b_pu = [Buf(excl=True) for _ in range(4)]
        pd = [psf(f"fpd{i}", [128, 512]) for i in range(2)]; b_pd = [Buf(excl=True) for _ in range(2)]
        xv = xin.rearrange("(c p) t -> p c t", p=128)
        xov = xout.rearrange("(c p) t -> p c t", p=128)
        uc = 0
        pc_ = 0
        for tt in range(T // N):
            t0 = tt * N
            S.dma("sp", xt[:], xv[:, :, t0:t0 + N], writes=[b_xt])
            rmsnorm_tile(g, xt, b_xt, hT, b_h, sm, b_sm, N, R)
            for i in range(22):
                k2 = i % 2
                for which, ch in ((0, i), (1, 22 + i)):
                    ui = uc % NU; uc += 1
                    pi = pc_ % 4; pc_ += 1
                    for c in range(8):
                        S.op("pe", lambda e: e.matmul(pu[pi][:, 0:N], lhsT=Wu[:, c, ch * 128:(ch + 1) * 128], rhs=hT[:, c, :], start=(c == 0), stop=(c == 7)),
                             reads=[b_Wu, b_h], writes=[b_pu[pi]])
                    S.op("act", lambda e: e.copy(out=U[ui][:, 2:N + 2], in_=pu[pi][:, 0:N]), reads=[b_pu[pi]], writes=[b_U[ui]])
                    S.op("act", lambda e: e.copy(out=U[ui][:, 0:2], in_=carry[:, ch, :]), reads=[b_carry], writes=[b_U[ui]])
                    dst, bd_ = (cg[k2], b_cg[k2]) if which == 0 else (cv[k2], b_cv[k2])
                    S.op("dve", lambda e: e.tensor_scalar(out=dst[:], in0=U[ui][:, 0:N], scalar1=cw[:, ch, 0:1], scalar2=cw[:, ch, 3:4], op0=ALU.mult, op1=ALU.add),
                         reads=[b_U[ui], b_cw], writes=[bd_])
                    S.op("dve", lambda e: e.scalar_tensor_tensor(out=dst[:], in0=U[ui][:, 1:N + 1], scalar=cw[:, ch, 1:2], in1=dst[:], op0=ALU.mult, op1=ALU.add),
                         reads=[b_U[ui], b_cw, bd_], writes=[bd_])
                    S.op("dve", lambda e: e.scalar_tensor_tensor(out=dst[:], in0=U[ui][:, 2:N + 2], scalar=cw[:, ch, 2:3], in1=dst[:], op0=ALU.mult, op1=ALU.add),
                         reads=[b_U[ui], b_cw, bd_], writes=[bd_])
                    S.op("act", lambda e: e.copy(out=carry[:, ch, :], in_=U[ui][:, N:N + 2]), reads=[b_U[ui]], writes=[b_carry])
                S.op("pool", lambda e: e.tensor_tensor(out=gi[k2][:], in0=cg[k2][:], in1=cg[k2][:], op=ALU.mult), reads=[b_cg[k2]], writes=[b_gi[k2]])
                S.op("pool", lambda e: e.tensor_scalar(out=gi[k2][:], in0=gi[k2][:], scalar1=0.044715, scalar2=1.0, op0=ALU.mult, op1=ALU.add), reads=[b_gi[k2]], writes=[b_gi[k2]])
                S.op("pool", lambda e: e.tensor_tensor(out=gi[k2][:], in0=gi[k2][:], in1=cg[k2][:], op=ALU.mult), reads=[b_gi[k2], b_cg[k2]], writes=[b_gi[k2]])
                S.op("act", lambda e: e.activation(out=gi[k2][:], in_=gi[k2][:], func=AF.Sigmoid, scale=1.5957691216057308), reads=[b_gi[k2]], writes=[b_gi[k2]])
                S.op("pool", lambda e: e.tensor_tensor(out=gi[k2][:], in0=gi[k2][:], in1=cg[k2][:], op=ALU.mult), reads=[b_gi[k2], b_cg[k2]], writes=[b_gi[k2]])
                S.op("dve", lambda e: e.tensor_tensor(out=gated[:, i, :], in0=gi[k2][:], in1=cv[k2][:], op=ALU.mult), reads=[b_gi[k2], b_cv[k2]], writes=[b_gt])
            for j in range(8):
                pi = j % 2
                for i in range(22):
                    S.op("pe", lambda e: e.matmul(pd[pi][:, 0:N], lhsT=Wd[:, i, j * 128:(j + 1) * 128], rhs=gated[:, i, :], start=(i == 0), stop=(i == 21)),
                         reads=[b_Wd, b_gt], writes=[b_pd[pi]])
                S.op("dve", lambda e: e.tensor_tensor(out=xt[:, j, :], in0=pd[pi][:, 0:N], in1=xt[:, j, :], op=ALU.add),
                     reads=[b_pd[pi], b_xt], writes=[b_xt])
            S.dma("sp", xov[:, :, t0:t0 + N], xt[:], reads=[b_xt])
        S.barrier()


def phase_ple(g, l, xin, xout, final):
    nc, S, D, T = g.nc, g.S, g.D, g.T
    N = 512
    with ExitStack() as st:
        sbf = lambda name, shape, dt=F32: st.enter_context(nc.sbuf_tensor(g.nm(name), list(shape), dt))
        psf = lambda name, shape, dt=F32: st.enter_context(nc.psum_tensor(g.nm(name), list(shape), dt))
        Wg = sbf("Wg", [128, 8, DM], BF16); b_Wg = Buf()
        Wp = sbf("Wp", [128, 2, DM], BF16); b_Wp = Buf()
        stage = [sbf(f"wst{i}", [128, 1024]) for i in range(2)]; b_stage = [Buf() for _ in range(2)]
        load_w_bf16(g, Wg, b_Wg, D["ple_w_gate"][l], 8, DM, stage, b_stage)
        load_w_bf16(g, Wp, b_Wp, D["ple_w_proj"][l], 2, DM, stage, b_stage)
        sm = sbf("smq", [128, 16]); b_sm = Buf()
        S.dma("sp", sm[:, 0:8], D["smalls"][l][:, 24:32], writes=[b_sm])
        S.dma("sp", sm[:, 8:16], D["smalls"][l][:, 32:40], writes=[b_sm])
        R = rms_shared(g, sbf, psf, N)
        xt = [sbf(f"pxt{i}", [128, 8, N]) for i in range(2)]; b_xt = [Buf() for _ in range(2)]
        pt = [sbf(f"ppt{i}", [128, 2, N]) for i in range(2)]; b_pt = [Buf() for _ in range(2)]
        ptb = sbf("pptb", [128, 2, N], BF16); b_ptb = Buf()
        hT = sbf("phT", [128, 8, N], BF16); b_h = Buf()
        xo = sbf("pxo", [128, 8, N]); b_xo = Buf()
        xf = sbf("pxf", [128, 8, N]); b_xf = Buf()
        gs = [sbf(f"pgs{i}", [128, N]) for i in range(2)]; b_gs = [Buf() for _ in range(2)]
        pg = [psf(f"ppg{i}", [128, 512]) for i in range(2)]; b_pg = [Buf(excl=True) for _ in range(2)]
        pq = [psf(f"ppq{i}", [128, 512]) for i in range(2)]; b_pq = [Buf(excl=True) for _ in range(2)]
        xv = xin.rearrange("(c p) t -> p c t", p=128)
        xov = xout.rearrange("(c p) t -> p c t", p=128)
        pv = D["pT"][l].rearrange("(c p) t -> p c t", p=128)
        for tt in range(T // N):
            t0 = tt * N
            xi = tt % 2
            S.dma("sp", xt[xi][:], xv[:, :, t0:t0 + N], writes=[b_xt[xi]])
            S.dma("sp", pt[xi][:], pv[:, :, t0:t0 + N], writes=[b_pt[xi]])
            S.op("pool", lambda e: e.tensor_copy(out=ptb[:], in_=pt[xi][:]), reads=[b_pt[xi]], writes=[b_ptb])
            rmsnorm_tile(g, xt[xi], b_xt[xi], hT, b_h, sm, b_sm, N, R)
            for j in range(8):
                pi = j % 2
                for c in range(8):
                    S.op("pe", lambda e: e.matmul(pg[pi][:], lhsT=Wg[:, c, j * 128:(j + 1) * 128], rhs=hT[:, c, :], start=(c == 0), stop=(c == 7)),
                         reads=[b_Wg, b_h], writes=[b_pg[pi]])
                for c in range(2):
                    S.op("pe", lambda e: e.matmul(pq[pi][:], lhsT=Wp[:, c, j * 128:(j + 1) * 128], rhs=ptb[:, c, :], start=(c == 0), stop=(c == 1)),
                         reads=[b_Wp, b_ptb], writes=[b_pq[pi]])
                S.op("act", lambda e: e.activation(out=gs[pi][:], in_=pg[pi][:], func=AF.Sigmoid), reads=[b_pg[pi]], writes=[b_gs[pi]])
                S.op("dve", lambda e: e.tensor_tensor(out=gs[pi][:], in0=pq[pi][:], in1=gs[pi][:], op=ALU.mult), reads=[b_pq[pi], b_gs[pi]], writes=[b_gs[pi]])
                S.op("pool", lambda e: e.tensor_tensor(out=xo[:, j, :], in0=gs[pi][:], in1=xt[xi][:, j, :], op=ALU.add), reads=[b_gs[pi], b_xt[xi]], writes=[b_xo])
            if not final:
                S.dma("sp", xov[:, :, t0:t0 + N], xo[:], reads=[b_xo])
            else:
                S.op("act", lambda e: e.activation(out=R["sq"][:], in_=xo[:], func=AF.Square), reads=[b_xo], writes=[R["b_sq"]])
                for c in range(8):
                    S.op("pe", lambda e: e.matmul(R["p_rms"][:], lhsT=R["ones"][:], rhs=R["sq"][:, c, :], start=(c == 0), stop=(c == 7)),
                         reads=[R["b_ones"], R["b_sq"]], writes=[R["b_prms"]])
                S.op("act", lambda e: e.activation(out=R["rstd"][:], in_=R["p_rms"][:], func=AF.Sqrt, bias=R["eps"][:, 0:1], scale=1.0 / DM),
                     reads=[R["b_prms"], R["b_eps"]], writes=[R["b_rstd"]])
                S.op("dve", lambda e: e.reciprocal(out=R["rstd"][:], in_=R["rstd"][:]), reads=[R["b_rstd"]], writes=[R["b_rstd"]])
                for c in range(8):
                    S.op("dve", lambda e: e.scalar_tensor_tensor(out=xf[:, c, :], in0=xo[:, c, :], scalar=sm[:, 8 + c:9 + c], in1=R["rstd"][:],
                                                                   op0=ALU.mult, op1=ALU.mult),
                         reads=[b_xo, b_sm, R["b_rstd"]], writes=[b_xf])
                S.dma("sp", xov[:, :, t0:t0 + N], xf[:], reads=[b_xf])
        S.barrier()


def nsa_consts(T):
    import ml_dtypes
    bf = ml_dtypes.bfloat16
    f = np.float32
    c = {}
    nl = np.arange(128)[:, None]
    ql = np.arange(128)[None, :]
    masks = np.zeros((19, 128, 128), f)
    for i in range(17):
        masks[i] = np.where(16 * nl + 31 - ql <= 128 * i, 0.0, -BIG)
    masks[17] = np.where(nl <= ql, 0.0, -BIG)
    masks[18] = np.where(nl > ql, 0.0, -BIG)
    m4 = np.tile(masks, (1, 1, 4))
    c["nsa_masks"] = np.ascontiguousarray(np.transpose(m4, (1, 0, 2))).astype(bf)
    c["identb"] = np.eye(128, dtype=f).astype(bf)
    c["identf"] = np.eye(128, dtype=f)
    key = np.arange(T)[None, :]
    c["E_all"] = (key // 64 == np.arange(128)[:, None]).astype(f).astype(bf)
    n_cmp = (T - 32) // 16 + 1
    ntc = (n_cmp + 127) // 128
    cs = 16 * np.arange(n_cmp)
    ce = cs + 31
    ss = 64 * np.arange(128)
    ov = np.minimum(ce[:, None], ss[None] + 63) - np.maximum(cs[:, None], ss[None]) + 1
    mcs = np.zeros((ntc * 128, 128), f)
    mcs[:n_cmp] = np.clip(ov, 0, 32).astype(f) / 32
    c["mcs"] = np.ascontiguousarray(mcs.reshape(ntc, 128, 128).transpose(1, 0, 2)).astype(bf)
    keep = np.zeros((128, 256), f)
    add = np.zeros((128, 256), f)
    for q in range(128):
        jc = 126 if q < 64 else 127
        cc = np.arange(256)
        keep[q] = (cc < jc - 1)
        add[q] = np.where(cc == jc - 1, 1.1e9, np.where(cc == jc, 1.2e9, np.where(cc > jc, -1e30, 0.0)))
    c["keepw"] = keep
    c["addw"] = add
    return c


def phase_nsa(g, l):
    nc, S, D, T = g.nc, g.S, g.D, g.T
    NQB = T // 128
    NCMP = (T - 32) // 16 + 1
    NTC = (NCMP + 127) // 128
    SK = dict(skip_group_check=True)
    with ExitStack() as st:
        sbf = lambda name, shape, dt=F32: st.enter_context(nc.sbuf_tensor(g.nm(name), list(shape), dt))
        psf = lambda name, shape, dt=F32: st.enter_context(nc.psum_tensor(g.nm(name), list(shape), dt))
        stp = [psf(f"nst{i}", [128, 512]) for i in range(3)]; b_stp = [Buf(excl=True) for _ in range(3)]
        acc = [psf(f"nacc{i}", [128, 512]) for i in range(3)]; b_acc = [Buf(excl=True) for _ in range(3)]
        stp.append(acc[2]); b_stp.append(b_acc[2])
        imp = psf("nimp", [128, 512]); b_imp = Buf(excl=True)
        msc = psf("nmsc", [128, 512]); b_msc = Buf(excl=True)
        mscb = msc[:, 384:448].bitcast(BF16); b_mscb = b_msc
        KcT = sbf("KcT", [64, 2, NTC * 128], BF16); b_Kc = Buf()
        Vc = sbf("Vc", [128, NTC, 2, 128], BF16); b_Vc = Buf()
        S.op("pool", lambda e: e.memset(KcT[:], 0.0), writes=[b_Kc])
        S.op("pool", lambda e: e.memset(Vc[:], 0.0), writes=[b_Vc])
        with ExitStack() as st2:
            sb2 = lambda name, shape, dt=F32: st2.enter_context(nc.sbuf_tensor(g.nm(name), list(shape), dt))
            kc = sb2("kc", [64, 2, T], BF16); b_kc = Buf()
            vc = sb2("vc", [64, 2, T], BF16); b_vc = Buf()
            S.dma("sp", kc[:], D["kT"][0:128, :].rearrange("(h d) t -> d h t", d=64), writes=[b_kc])
            S.dma("sp", vc[:], D["vcT"].rearrange("(h d) t -> d h t", d=64), writes=[b_vc])
            wst = sb2("wckst", [64, 32, 64]); b_wst = Buf()
            wck = sb2("wck", [64, 32, 64], BF16); b_wck = Buf()
            wcv = sb2("wcv", [64, 32, 64], BF16); b_wcv = Buf()
            S.dma("sp", wst[:], D["nsa_w_ck"][l].rearrange("l d e -> d l e"), writes=[b_wst])
            cp(S, "dve", wck[:], wst[:], [b_wst], [b_wck])
            S.dma("sp", wst[:], D["nsa_w_cv"][l].rearrange("l d e -> d l e"), writes=[b_wst])
            cp(S, "dve", wcv[:], wst[:], [b_wst], [b_wcv])
            pest = sb2("pest", [64, 2, 32]); b_pest = Buf()
            peb = sb2("peb", [64, 2, 32], BF16); b_peb = Buf()
            S.dma("sp", pest[:], D["nsa_peT"][l].rearrange("w d l -> d w l"), writes=[b_pest])
            cp(S, "dve", peb[:], pest[:], [b_pest], [b_peb])
            biask = sb2("biask", [64, 1]); b_bk = Buf()
            biasv = sb2("biasv", [1, 64], BF16); b_bv = Buf()
            onesr = sb2("onesr", [1, 128], BF16); b_or = Buf()
            S.op("pool", lambda e: e.memset(onesr[:], 1.0), writes=[b_or])
            for i_ in range(32):
                S.op("pe", lambda e: e.matmul(msc[0:64, 0:1], lhsT=wck[:, i_, :], rhs=peb[:, 0, i_:i_ + 1], start=(i_ == 0), stop=(i_ == 31)),
                     reads=[b_wck, b_peb], writes=[b_msc])
            cp(S, "dve", biask[:], msc[0:64, 0:1], [b_msc], [b_bk])
            for i_ in range(32):
                S.op("pe", lambda e: e.matmul(msc[0:1, 0:64], lhsT=peb[:, 1, i_:i_ + 1], rhs=wcv[:, i_, :], start=(i_ == 0), stop=(i_ == 31)),
                     reads=[b_wcv, b_peb], writes=[b_msc])
            cp(S, "dve", biasv[:], msc[0:1, 0:64], [b_msc], [b_bv])
            span = 16 * (NCMP - 1) + 1
            for h in range(2):
                pk = stp[h]
                for i_ in range(32):
                    S.op("pe", lambda e: e.matmul(pk[0:64, 0:NCMP], lhsT=wck[:, i_, :], rhs=kc[:, h, i_:i_ + span:16], start=(i_ == 0), stop=(i_ == 31)),
                         reads=[b_wck, b_kc], writes=[b_stp[h]])
                S.op("act", lambda e: e.activation(out=KcT[:, h, 0:NCMP], in_=pk[0:64, 0:NCMP], func=AF.Identity, bias=biask[:, 0:1], scale=1.0),
                     reads=[b_stp[h], b_bk], writes=[b_Kc])
            k_ = 0
            for h in range(2):
                for nt in range(NTC):
                    nn = min(128, NCMP - nt * 128)
                    pv_ = acc[k_ % 3]; bpv = b_acc[k_ % 3]; k_ += 1
                    base = 16 * 128 * nt
                    sp_ = 16 * (nn - 1) + 1
                    for i_ in range(32):
                        S.op("pe", lambda e: e.matmul(pv_[0:nn, 0:64], lhsT=vc[:, h, base + i_:base + i_ + sp_:16], rhs=wcv[:, i_, :], start=(i_ == 0), stop=False),
                             reads=[b_wcv, b_vc], writes=[bpv])
                    S.op("pe", lambda e: e.matmul(pv_[0:nn, 0:64], lhsT=onesr[0:1, 0:nn], rhs=biasv[0:1, :], start=False, stop=True),
                         reads=[b_or, b_bv], writes=[bpv])
                    cp(S, "dve", Vc[0:nn, nt, h, 0:64], pv_[0:nn, 0:64], [bpv], [b_Vc])
                    S.op("dve", lambda e: e.memset(Vc[0:nn, nt, h, 64:65], 1.0), writes=[b_Vc])
            S.barrier()
        masks = sbf("masks", [128, 19, 512], BF16); b_masks = Buf()
        S.dma("sp", masks[:], D["nsa_masks"], writes=[b_masks])
        identb = sbf("identb", [128, 128], BF16); b_idb = Buf()
        S.dma("sp", identb[:], D["identb"], writes=[b_idb])
        identf = sbf("identf", [128, 128]); b_idf = Buf()
        S.dma("sp", identf[:], D["identf"], writes=[b_idf])
        MCS = sbf("MCS", [128, NTC, 128], BF16); b_mcs = Buf()
        S.dma("sp", MCS[:], D["mcs"], writes=[b_mcs])
        keepw = sbf("keepw", [128, 256]); b_kw_ = Buf()
        S.dma("sp", keepw[:], D["keepw"], writes=[b_kw_])
        addw = sbf("addw", [128, 256]); b_aw = Buf()
        S.dma("sp", addw[:], D["addw"], writes=[b_aw])
        LH = sbf("LH", [128, 2, T], BF16); b_LH = Buf()
        KwT = sbf("KwT", [64, 2, T], BF16); b_Kw = Buf()
        TH = min(T, 4096)
        ksv = D["kT"][128:256, :].rearrange("(h d) t -> d h t", d=64)
        S.dma("sp", LH[64:128, :, 0:TH], ksv[:, :, 0:TH], writes=[b_LH])
        for h_ in range(2):
            S.dma("sp", LH[0:64, h_, 0:TH], D["E_all"][0:64, 0:TH], writes=[b_LH])
        if T > TH:
            S.dma("sp", LH[0:64, :, TH:T], ksv[:, :, TH:T], writes=[b_LH])
            for h_ in range(2):
                S.dma("sp", LH[64:128, h_, TH:T], D["E_all"][64:128, TH:T], writes=[b_LH])
        S.dma("sp", KwT[:], D["kT"][256:384, :].rearrange("(h d) t -> d h t", d=64), writes=[b_Kw])
        VW = 128
        Vs = sbf("Vs", [128, NQB, 2, VW], BF16); b_Vs = Buf()
        Vw = sbf("Vw", [128, NQB, 2, VW], BF16); b_Vw = Buf()
        S.op("pool", lambda e: e.memset(Vs[:], 0.0), writes=[b_Vs])
        S.op("pool", lambda e: e.memset(Vw[:], 0.0), writes=[b_Vw])
        S.op("pool", lambda e: e.memset(Vs[:, :, :, 64:65], 1.0), writes=[b_Vs])
        S.op("pool", lambda e: e.memset(Vw[:, :, :, 64:65], 1.0), writes=[b_Vw])
        vtv = D["vtok"].rearrange("(kt p) (w h d) -> p kt w h d", p=128, w=2, h=2)
        for h in range(2):
            S.dma("sp", Vs[:, :, h, 0:64], vtv[:, :, 0, h, :], writes=[b_Vs])
            S.dma("sp", Vw[:, :, h, 0:64], vtv[:, :, 1, h, :], writes=[b_Vw])
        Qg = [sbf(f"Qg{i}", [64, 4, 128], BF16) for i in range(2)]; b_Qg = [Buf() for _ in range(2)]
        gt = [sbf(f"ngt{i}", [128, 24]) for i in range(2)]; b_gt = [Buf() for _ in range(2)]
        Pc = [sbf(f"Pc{i}", [128, 512], BF16) for i in range(max(NTC, 1))]; b_Pc = [Buf() for _ in range(max(NTC, 1))]
        NP = 3
        Pb = [sbf(f"Pb{i}", [128, 512], BF16) for i in range(NP)]; b_Pb = [Buf() for _ in range(NP)]
        zz = sbf("zz", [128, 3, 4]); b_zz = Buf()
        coef = sbf("coef", [128, 3, 4]); b_coef = Buf()
        impS = sbf("impS", [128, 128]); b_impS = Buf()
        imp2 = sbf("imp2", [128, 128]); b_imp2 = Buf()
        mx = sbf("mx", [128, 16]); b_mx = Buf()
        thr = sbf("thr", [128, 1]); b_thr = Buf()
        MBf = sbf("MBf", [128, 128]); b_MBf = Buf()
        MBb = sbf("MBb", [128, 128], BF16); b_MBb = Buf()
        MBT4 = sbf("MBT4", [128, 4, 128], BF16); b_MBT4 = Buf()
        yc = [sbf(f"yc{i}", [128, 512]) for i in range(2)]; b_yc = [Buf() for _ in range(2)]
        ycT = [sbf(f"ycT{i}", [128, 4, 128], BF16) for i in range(2)]; b_ycT = [Buf() for _ in range(2)]
        stc = 0
        pbc = 0
        qc = 0
        qv = D["qT"].rearrange("(hq d) t -> d hq t", d=64)
        ymv = D["ymixT"][512:1024, :].rearrange("(c p) t -> p c t", p=128)
        aS = sbf("aS", [65, 512]); b_aS = Buf()
        R0 = [sbf(f"R0_{i}", [128, 512], BF16) for i in range(2)]; b_R0 = [Buf() for _ in range(2)]
        R1 = [sbf(f"R1_{i}", [128, 512], BF16) for i in range(2)]; b_R1 = [Buf() for _ in range(2)]

        def score_tile(KT, bK, h, kt, Q, bQ, extra, fixed=None):
            nonlocal stc
            if fixed is not None:
                si = fixed
            else:
                si = stc % 3; stc += 1
            n_mm = 1 + len(extra)
            rhs_ap = Q[:].rearrange("d g q -> d (g q)") if len(Q.shape) == 3 else Q[:]
            S.op("pe", lambda e: e.matmul(stp[si][:], lhsT=KT[:, h, kt * 128:(kt + 1) * 128], rhs=rhs_ap, start=True, stop=(n_mm == 1)),
                 reads=[bK, bQ], writes=[b_stp[si]])
            for i_, (la, ra, bufs) in enumerate(extra):
                S.op("pe", lambda e: e.matmul(stp[si][:], lhsT=la, rhs=ra, start=False, stop=(i_ == len(extra) - 1)),
                     reads=bufs, writes=[b_stp[si]])
            return si

        def run_branch(br, tiles, h, Q, bQ, V, bV, Pbufs=None, after_exp=None, tick=None, fixed=None):
            nonlocal pbc
            n = len(tiles)
            tiles = [tl if len(tl) == 6 else tl + (Q, bQ) for tl in tiles]
            DEPTH = 2 if fixed is None else 1
            issued = []
            nxt = 0
            for i_ in range(n):
                while nxt < n and nxt <= i_ + DEPTH - 1 + (0 if i_ else 0):
                    KT2, bK2, kt2, extra2, Qx2, bQx2 = tiles[nxt]
                    issued.append(score_tile(KT2, bK2, h, kt2, Qx2, bQx2, extra2, fixed))
                    nxt += 1
                si = issued[i_]
                kt = tiles[i_][2]
                if Pbufs is None:
                    pi = pbc % NP; pbc += 1
                    P, bP = Pb[pi], b_Pb[pi]
                else:
                    P, bP = Pbufs[i_]
                S.op("act", lambda e: e.activation(out=P[:], in_=stp[si][:], func=AF.Exp, scale=0.125), reads=[b_stp[si]], writes=[bP])
                if nxt < n and fixed is None:
                    KT2, bK2, kt2, extra2, Qx2, bQx2 = tiles[nxt]
                    issued.append(score_tile(KT2, bK2, h, kt2, Qx2, bQx2, extra2))
                    nxt += 1
                S.op("pe", lambda e: e.matmul(acc[br][:, :], lhsT=V[:, kt, h, :], rhs=P[:], start=(i_ == 0), stop=(i_ == n - 1)),
                     reads=[bP, bV], writes=[b_acc[br]])
                if after_exp is not None:
                    after_exp(i_, P, bP)
                if tick is not None:
                    tick()

        def combine(br, h, yi):
            cp(S, "act", aS[:], acc[br][0:65, :], [b_acc[br]], [b_aS])
            for gq in range(4):
                S.op("pe", lambda e: e.transpose(out=msc[:, gq * 65:(gq + 1) * 65], in_=aS[0:65, gq * 128:(gq + 1) * 128], identity=identf[0:65, 0:65]),
                     reads=[b_aS, b_idf], writes=[b_msc])
            a3 = msc[:, 0:260].rearrange("p (g c) -> p g c", c=65)
            g3 = gt[yi][:].rearrange("p (hg b) -> p hg b", b=3)
            S.op("dve", lambda e: e.tensor_scalar(out=zz[:, br, :], in0=a3[:, :, 64], scalar1=1e-30, scalar2=None, op0=ALU.max), reads=[b_msc], writes=[b_zz])
            S.op("dve", lambda e: e.reciprocal(out=zz[:, br, :], in_=zz[:, br, :]), reads=[b_zz], writes=[b_zz])
            S.op("dve", lambda e: e.tensor_tensor(out=coef[:, br, :], in0=zz[:, br, :], in1=g3[:, h * 4:(h + 1) * 4, br], op=ALU.mult), reads=[b_zz, b_gt[yi]], writes=[b_coef])
            for gq in range(4):
                o_ = yc[yi][:, (h * 4 + gq) * 64:(h * 4 + gq + 1) * 64]
                if br == 0:
                    S.op("dve", lambda e: e.tensor_scalar(out=o_, in0=a3[:, gq, 0:64], scalar1=coef[:, br, gq:gq + 1], scalar2=None, op0=ALU.mult),
                         reads=[b_msc, b_coef], writes=[b_yc[yi]])
                else:
                    S.op("dve", lambda e: e.scalar_tensor_tensor(out=o_, in0=a3[:, gq, 0:64], scalar=coef[:, br, gq:gq + 1], in1=o_, op0=ALU.mult, op1=ALU.add),
                         reads=[b_msc, b_coef, b_yc[yi]], writes=[b_yc[yi]])

        items = [(qb, h) for qb in range(NQB) for h in range(2)]
        qis = {}

        def stage1(qb, h):
            nonlocal qc
            q0 = qb * 128
            yi = qb % 2
            if h == 0:
                S.dma("sp", gt[yi][:], D["gates"][q0:q0 + 128, :], writes=[b_gt[yi]])
            qi = qc % 2; qc += 1
            qis[(qb, h)] = qi
            Q, bQ = Qg[qi], b_Qg[qi]
            S.dma("sp", Q[:], qv[:, h * 4:(h + 1) * 4, q0:q0 + 128], writes=[bQ])
            S.dma("sp", R0[qi][64:128, :].rearrange("d (g q) -> d g q", g=4), qv[:, h * 4:(h + 1) * 4, q0:q0 + 128], writes=[b_R0[qi]])
            if qb >= 32:
                S.dma("sp", R1[qi][0:64, :].rearrange("d (g q) -> d g q", g=4), qv[:, h * 4:(h + 1) * 4, q0:q0 + 128], writes=[b_R1[qi]])
            yield
            ntc = min(NTC, (8 * qb + 6) // 128 + 1)
            S.op("dve", lambda e: e.memset(imp[:], 0.0), writes=[b_imp])
            tiles = []
            for nt in range(ntc):
                delta = 128 * qb - 2048 * nt
                extra = []
                if delta < 2064:
                    extra.append((identb[:], masks[:, delta // 128, :], [b_idb, b_masks]))
                tiles.append((KcT, b_Kc, nt, extra))

            def imp_mm(i_, P, bP):
                for gq in range(4):
                    S.op("pe", lambda e: e.matmul(imp[:, gq * 128:(gq + 1) * 128], lhsT=P[:, gq * 128:(gq + 1) * 128], rhs=MCS[:, i_, :], start=False, stop=(i_ == ntc - 1), **SK),
                         reads=[bP, b_mcs], writes=[b_imp])
            run_branch(0, tiles, h, Q, bQ, Vc, b_Vc, Pbufs=[(Pc[i_], b_Pc[i_]) for i_ in range(ntc)], after_exp=imp_mm, fixed=3)
            yield
            combine(0, h, yi)
            yield
            S.op("dve", lambda e: e.tensor_scalar(out=impS[:], in0=imp[:, 0:128], scalar1=zz[:, 0, 0:1], scalar2=None, op0=ALU.mult), reads=[b_imp, b_zz], writes=[b_impS])
            for gq in range(1, 4):
                S.op("dve", lambda e: e.scalar_tensor_tensor(out=impS[:], in0=imp[:, gq * 128:(gq + 1) * 128], scalar=zz[:, 0, gq:gq + 1], in1=impS[:], op0=ALU.mult, op1=ALU.add),
                     reads=[b_imp, b_zz, b_impS], writes=[b_impS])
            yield
            c0 = 126 - 2 * qb
            S.op("dve", lambda e: e.tensor_tensor(out=impS[:], in0=impS[:], in1=keepw[:, c0:c0 + 128], op=ALU.mult), reads=[b_impS, b_kw_], writes=[b_impS])
            S.op("dve", lambda e: e.tensor_tensor(out=impS[:], in0=impS[:], in1=addw[:, c0:c0 + 128], op=ALU.add), reads=[b_impS, b_aw], writes=[b_impS])
            S.op("dve", lambda e: e.memset(impS[:, 0:1], 1.0e9), writes=[b_impS])
            yield
            S.op("dve", lambda e: e.max(out=mx[:, 0:8], in_=impS[:]), reads=[b_impS], writes=[b_mx])
            S.op("dve", lambda e: e.match_replace(out=imp2[:], in_to_replace=mx[:, 0:8], in_values=impS[:], imm_value=-3.0e38), reads=[b_mx, b_impS], writes=[b_imp2])
            S.op("dve", lambda e: e.max(out=mx[:, 8:16], in_=imp2[:]), reads=[b_imp2], writes=[b_mx])
            S.op("dve", lambda e: e.tensor_reduce(out=thr[:], in_=mx[:, 8:16], axis=AX.X, op=ALU.min), reads=[b_mx], writes=[b_thr])
            yield
            S.op("dve", lambda e: e.tensor_scalar(out=MBf[:], in0=impS[:], scalar1=thr[:, 0:1], scalar2=None, op0=ALU.is_ge), reads=[b_impS, b_thr], writes=[b_MBf])
            S.op("dve", lambda e: e.tensor_scalar(out=MBb[:], in0=MBf[:], scalar1=1.0, scalar2=BIG, op0=ALU.subtract, op1=ALU.mult), reads=[b_MBf], writes=[b_MBb])
            yield
            S.op("pe", lambda e: e.transpose(out=mscb[:], in_=MBb[:], identity=identb[:]), reads=[b_MBb, b_idb], writes=[b_mscb])
            for gq in range(4):
                cp(S, "act" if gq % 2 else "dve", R0[qi][0:64, gq * 128:(gq + 1) * 128], mscb[0:64, :], [b_mscb], [b_R0[qi]])
                if qb >= 32:
                    cp(S, "dve" if gq % 2 else "act", R1[qi][64:128, gq * 128:(gq + 1) * 128], mscb[64:128, :], [b_mscb], [b_R1[qi]])

        def stage2(qb, h, tick=None):
            q0 = qb * 128
            yi = qb % 2
            qi = qis[(qb, h)]
            Q, bQ = Qg[qi], b_Qg[qi]
            tiles = []
            for kt in range(max(0, qb - 4), qb + 1):
                extra = []
                if kt == qb:
                    extra.append((identb[:], masks[:, 17, :], [b_idb, b_masks]))
                elif kt == qb - 4:
                    extra.append((identb[:], masks[:, 18, :], [b_idb, b_masks]))
                tiles.append((KwT, b_Kw, kt, extra))
            run_branch(2, tiles, h, Q, bQ, Vw, b_Vw)
            combine(2, h, yi)
            tiles = []
            for kt in range(qb + 1):
                extra = []
                if kt == qb:
                    extra.append((identb[:], masks[:, 17, :], [b_idb, b_masks]))
                if kt < 32:
                    tiles.append((LH, b_LH, kt, extra, R0[qi], b_R0[qi]))
                else:
                    tiles.append((LH, b_LH, kt, extra, R1[qi], b_R1[qi]))
            run_branch(1, tiles, h, Q, bQ, Vs, b_Vs, tick=tick)
            combine(1, h, yi)
            if h == 1:
                for c in range(4):
                    S.op("pe", lambda e: e.transpose(out=msc[:, c * 128:(c + 1) * 128], in_=yc[yi][:, c * 128:(c + 1) * 128], identity=identf[:]),
                         reads=[b_yc[yi], b_idf], writes=[b_msc])
                cp(S, "act", ycT[yi][:].rearrange("p c q -> p (c q)"), msc[:], [b_msc], [b_ycT[yi]])
                S.dma("sp", ymv[:, :, q0:q0 + 128], ycT[yi][:], reads=[b_ycT[yi]])

        for _ in stage1(*items[0]):
            pass
        for n_ in range(len(items)):
            gen = stage1(*items[n_ + 1]) if n_ + 1 < len(items) else None
            if gen is not None:
                next(gen, None)
            stage2(*items[n_], tick=(lambda gg=gen: next(gg, None)) if gen is not None else None)
            if gen is not None:
                for _ in gen:
                    pass
        S.barrier()


def rwkv_consts():
    f = np.float32
    c = {}
    hs = np.arange(128) // 64
    tt = np.arange(128) % 64
    same = hs[:, None] == hs[None, :]
    c["rw_msu"] = (same & (tt[:, None] < tt[None, :])).astype(f)
    c["rw_mu"] = (same & (tt[:, None] <= tt[None, :])).astype(f)
    c["rw_msl"] = (same & (tt[:, None] > tt[None, :])).astype(f)
    il = np.zeros((64, 128), f); il[np.arange(64), np.arange(64)] = 1
    ir = np.zeros((64, 128), f); ir[np.arange(64), 64 + np.arange(64)] = 1
    c["rw_il"] = il
    c["rw_ir"] = ir
    return c


def phase_rwkv(g, l):
    nc, S, D, T = g.nc, g.S, g.D, g.T
    TB = 256
    NCH = TB // 64
    SK = dict(skip_group_check=True)
    with ExitStack() as st:
        sbf = lambda name, shape, dt=F32: st.enter_context(nc.sbuf_tensor(g.nm(name), list(shape), dt))
        psf = lambda name, shape, dt=F32: st.enter_context(nc.psum_tensor(g.nm(name), list(shape), dt))
        NB = 8
        bank = [psf(f"rb{i}", [128, 512]) for i in range(NB)]; b_bank = [Buf(excl=True) for _ in range(NB)]
        bctr = [0]

        busy = [False] * NB
        F32R = mybir.dt.float32r
        MT = F32R if g.use_f32r else F32
        RR = lambda ap: ap
        AS32 = (lambda ap: ap.bitcast(F32)) if g.use_f32r else (lambda ap: ap)

        def nb():
            for k_ in range(NB):
                i = (bctr[0] + k_) % NB
                if not busy[i]:
                    bctr[0] = i + 1
                    busy[i] = True
                    return bank[i], b_bank[i]
            raise AssertionError("rwkv: no free PSUM bank (too many live tiles across a yield)")

        def rel(bbuf):
            busy[b_bank.index(bbuf)] = False

        def nbx():
            i = bctr[0] % NB
            bctr[0] += 1
            return bank[i], b_bank[i]
        def const(name, shape, src):
            t_ = sbf(name, shape); b_ = Buf()
            S.dma("sp", t_[:], src, writes=[b_])
            return t_, b_
        msu, b_msu = const("msu", [128, 128], D["rw_msu"])
        mu_, b_mu = const("mu", [128, 128], D["rw_mu"])
        msl, b_msl = const("msl", [128, 128], D["rw_msl"])
        il32, b_il32 = const("il32", [64, 128], D["rw_il"])
        ir32, b_ir32 = const("ir32", [64, 128], D["rw_ir"])
        idf, b_idf = const("idf", [128, 128], D["identf"])
        il = sbf("il", [64, 128], MT); b_il = Buf()
        ir = sbf("ir", [64, 128], MT); b_ir = Buf()
        cp(S, "dve", il[:], il32[:], [b_il32], [b_il])
        cp(S, "dve", ir[:], ir32[:], [b_ir32], [b_ir])
        rp, b_rp = const("rp", [128, 64], D["rwp"][l])
        wup32, b_wup32 = const("wup32", [64, 256], D["rw_w_up"][l])
        aup32, b_aup32 = const("aup32", [64, 256], D["rw_a_up"][l])
        gup32, b_gup32 = const("gup32", [128, 256], D["rw_g_up"][l])
        wup = sbf("wup", [64, 256], MT); b_wup = Buf()
        aup = sbf("aup", [64, 256], MT); b_aup = Buf()
        gup = sbf("gup", [128, 256], MT); b_gup = Buf()
        cp(S, "dve", wup[:], wup32[:], [b_wup32], [b_wup])
        cp(S, "dve", aup[:], aup32[:], [b_aup32], [b_aup])
        cp(S, "dve", gup[:], gup32[:], [b_gup32], [b_gup])
        ones32 = sbf("ones32", [64, 64]); b_o32 = Buf()
        S.op("pool", lambda e: e.memset(ones32[:], 1.0), writes=[b_o32])
        ones64 = sbf("ones64", [64, 64], MT); b_o64 = Buf()
        cp(S, "dve", ones64[:], ones32[:], [b_o32], [b_o64])
        omka = sbf("omka", [64, 4]); b_omka = Buf()
        S.op("dve", lambda e: e.tensor_scalar(out=omka[:], in0=rp[0:64, 28:32], scalar1=-1.0, scalar2=1.0, op0=ALU.mult, op1=ALU.add), reads=[b_rp], writes=[b_omka])
        cst = sbf("rcst", [128, 2]); b_cst = Buf()
        S.op("pool", lambda e: e.memset(cst[:, 0:1], 64e-5), writes=[b_cst])
        ST = [[sbf(f"ST{p}_{i}", [128, 64], MT) for i in range(2)] for p in range(2)]
        b_ST = [[Buf() for i in range(2)] for p in range(2)]
        for p in range(2):
            S.op("pool", lambda e: e.memset(AS32(ST[p][0][:]), 0.0), writes=[b_ST[p][0]])
        sidx = [0, 0]
        def arr(name, shape=None, dt=F32):
            return sbf(name, shape or [64, 4, TB], dt), Buf()
        Z3, b_Z3 = arr("Z3", [64, 12, TB + 1])
        ZL, b_ZL = arr("ZL", [64, 2, TB + 1])
        ZG, b_ZG = arr("ZG", [128, TB + 1])
        X3, b_X3 = arr("X3", [64, 12, TB])
        XL, b_XL = arr("XL", [64, 2, TB], MT)
        XG, b_XG = arr("XG", [128, TB], MT)
        Dt, b_Dt = arr("Dt", [128, 12, TB])
        lw, b_lw = arr("lw")
        cl2, b_cl2 = arr("cl2")
        aa, b_aa = arr("aa")
        kkn, b_kkn = arr("kkn")
        tmp, b_tmp = arr("tmp", None, MT)
        kfin, b_kfin = arr("kfin")
        epos, b_epos = arr("epos")
        eneg, b_eneg = arr("eneg")
        eprev, b_eprev = arr("eprev")
        eC, b_eC = arr("eC")
        AR, b_AR = arr("AR", [64, NCH, 2, 2, 2, 64], MT)
        Bt, b_Bt = arr("Bt", [64, NCH, 4, 64], MT)
        Kt, b_Kt = arr("Kt", [64, NCH, 4, 64], MT)
        Bh, b_Bh = arr("Bh", [64, NCH, 4, 64])
        Kh, b_Kh = arr("Kh", [64, NCH, 4, 64])
        Vc_, b_Vc_ = arr("Vcm", [64, NCH, 4, 64])
        hm = lambda a: a[:].rearrange("k h (c t) -> k h c t", t=64)
        cm = lambda a: a[:].rearrange("k c h t -> k h c t")
        arv = lambda ty: AR[:, :, :, ty, :, :].rearrange("k c p hh t -> k p hh c t")
        hm5 = lambda a: a[:].rearrange("k (p hh) (c t) -> k p hh c t", hh=2, t=64)
        bv, b_bv = arr("bv")
        gT, b_gT = arr("gT")
        YN, b_YN = arr("YN")
        PCf, b_PCf = arr("PCf", [64, 4, NCH], MT)
        PCc = sbf("PCc", [128, 2, NCH]); b_PCc = Buf()
        yo = sbf("yo", [64, 4, TB], BF16); b_yo = Buf()
        NTMP = 100
        tm = [sbf(f"tm{i}", [128, 128], MT) for i in range(NTMP)]; b_tm = [Buf() for _ in range(NTMP)]
        NTF = 16
        tf = [sbf(f"tf{i}", [128, 128]) for i in range(NTF)]; b_tf = [Buf() for _ in range(NTF)]
        fctr = [0]

        def ntf_():
            i = fctr[0] % NTF
            fctr[0] += 1
            return tf[i], b_tf[i]
        tctr = [0]

        def nt_():
            i = tctr[0] % NTMP
            tctr[0] += 1
            return tm[i], b_tm[i]
        NBD = 5
        bd = [[sbf(f"bd{k_}_{i}", [128, 128], MT) for i in range(NBD)] for k_ in range(3)]
        b_bd = [[Buf() for i in range(NBD)] for k_ in range(3)]
        for k_ in range(3):
            for i in range(NBD):
                S.op("pool", lambda e: e.memset(AS32(bd[k_][i][:]), 0.0), writes=[b_bd[k_][i]])
        bdc = [0]
        zav = D["zaT"]
        ev_ctr = [0]

        def evac(out, in_, reads, writes):
            ek = "act" if ev_ctr[0] % 2 == 0 else "dve"
            ev_ctr[0] += 1
            cp(S, ek, out, in_, reads, writes)

        for tt in range(T // TB):
            t0 = tt * TB
            if tt == 0:
                S.op("pool", lambda e: e.memset(Z3[:, :, 0:1], 0.0), writes=[b_Z3])
                S.op("pool", lambda e: e.memset(ZL[:, :, 0:1], 0.0), writes=[b_ZL])
                S.op("pool", lambda e: e.memset(ZG[:, 0:1], 0.0), writes=[b_ZG])
                S.dma("sp", Z3[:, :, 1:TB + 1], zav[0:768, 0:TB].rearrange("(gh k) t -> k gh t", k=64), writes=[b_Z3])
                S.dma("sp", ZL[:, :, 1:TB + 1], zav[768:896, 0:TB].rearrange("(g j) t -> j g t", j=64), writes=[b_ZL])
                S.dma("sp", ZG[:, 1:TB + 1], zav[896:1024, 0:TB], writes=[b_ZG])
            else:
                S.dma("sp", Z3[:], zav[0:768, t0 - 1:t0 + TB].rearrange("(gh k) t -> k gh t", k=64), writes=[b_Z3])
                S.dma("sp", ZL[:], zav[768:896, t0 - 1:t0 + TB].rearrange("(g j) t -> j g t", j=64), writes=[b_ZL])
                S.dma("sp", ZG[:], zav[896:1024, t0 - 1:t0 + TB], writes=[b_ZG])
            S.op("dve", lambda e: e.tensor_tensor(out=Dt[0:64, :, :], in0=Z3[:, :, 0:TB], in1=Z3[:, :, 1:TB + 1], op=ALU.subtract), reads=[b_Z3], writes=[b_Dt])
            for j in range(12):
                if j % 3 == 2:
                    S.op("act", lambda e: e.activation(out=Dt[0:64, j, :], in_=Dt[0:64, j, :], func=AF.Copy, scale=rp[0:64, j:j + 1]), reads=[b_Dt, b_rp], writes=[b_Dt])
                else:
                    S.op("dve", lambda e: e.tensor_scalar(out=Dt[0:64, j, :], in0=Dt[0:64, j, :], scalar1=rp[0:64, j:j + 1], scalar2=None, op0=ALU.mult), reads=[b_Dt, b_rp], writes=[b_Dt])
            S.op("dve", lambda e: e.tensor_tensor(out=X3[:], in0=Dt[0:64, :, :], in1=Z3[:, :, 1:TB + 1], op=ALU.add), reads=[b_Dt, b_Z3], writes=[b_X3])
            S.op("dve", lambda e: e.tensor_tensor(out=Dt[0:64, 0:2, :], in0=ZL[:, :, 0:TB], in1=ZL[:, :, 1:TB + 1], op=ALU.subtract), reads=[b_ZL], writes=[b_Dt])
            for j in range(2):
                S.op("dve", lambda e: e.scalar_tensor_tensor(out=XL[:, j, :], in0=Dt[0:64, j, :], scalar=rp[0:64, 12 + j:13 + j], in1=ZL[:, j, 1:TB + 1], op0=ALU.mult, op1=ALU.add),
                     reads=[b_Dt, b_rp, b_ZL], writes=[b_XL])
            S.op("dve", lambda e: e.tensor_tensor(out=Dt[:, 2, :], in0=ZG[:, 0:TB], in1=ZG[:, 1:TB + 1], op=ALU.subtract), reads=[b_ZG], writes=[b_Dt])
            S.op("dve", lambda e: e.scalar_tensor_tensor(out=XG[:], in0=Dt[:, 2, :], scalar=rp[:, 14:15], in1=ZG[:, 1:TB + 1], op0=ALU.mult, op1=ALU.add),
                 reads=[b_Dt, b_rp, b_ZG], writes=[b_XG])
            r_ = lambda h: X3[:, h, :]
            k_ = lambda h: X3[:, 4 + h, :]
            v_ = lambda h: X3[:, 8 + h, :]
            S.op("act", lambda e: e.activation(out=XL[:, 0, :], in_=XL[:, 0, :], func=AF.Tanh), reads=[b_XL], writes=[b_XL])
            S.op("act", lambda e: e.activation(out=XG[:], in_=XG[:], func=AF.Sigmoid), reads=[b_XG], writes=[b_XG])
            for h in range(4):
                pb, bpb = nbx()
                S.op("pe", lambda e: e.matmul(pb[0:64, 0:TB], lhsT=wup[:, h * 64:(h + 1) * 64], rhs=XL[:, 0, :], start=True, stop=True), reads=[b_wup, b_XL], writes=[bpb])
                S.op("act", lambda e: e.activation(out=lw[:, h, :], in_=pb[0:64, 0:TB], func=AF.Sigmoid, bias=rp[0:64, 16 + h:17 + h], scale=1.0), reads=[bpb, b_rp], writes=[b_lw])
                pb, bpb = nbx()
                S.op("pe", lambda e: e.matmul(pb[0:64, 0:TB], lhsT=aup[:, h * 64:(h + 1) * 64], rhs=XL[:, 1, :], start=True, stop=True), reads=[b_aup, b_XL], writes=[bpb])
                S.op("act", lambda e: e.activation(out=aa[:, h, :], in_=pb[0:64, 0:TB], func=AF.Sigmoid, bias=rp[0:64, 20 + h:21 + h], scale=1.0), reads=[bpb, b_rp], writes=[b_aa])
                pb, bpb = nbx()
                S.op("pe", lambda e: e.matmul(pb[0:64, 0:TB], lhsT=gup[:, h * 64:(h + 1) * 64], rhs=XG[:], start=True, stop=True), reads=[b_gup, b_XG], writes=[bpb])
                evac(gT[:, h, :], pb[0:64, 0:TB], [bpb], [b_gT])
            S.op("dve", lambda e: e.tensor_scalar(out=lw[:], in0=lw[:], scalar1=-0.6065306597126334, scalar2=None, op0=ALU.mult), reads=[b_lw], writes=[b_lw])
            for h in range(4):
                S.op("dve", lambda e: e.tensor_scalar(out=kkn[:, h, :], in0=k_(h), scalar1=rp[0:64, 24 + h:25 + h], scalar2=None, op0=ALU.mult), reads=[b_X3, b_rp], writes=[b_kkn])
            S.op("act", lambda e: e.activation(out=tmp[:], in_=kkn[:], func=AF.Square), reads=[b_kkn], writes=[b_tmp])
            for h in range(4):
                pb, bpb = nbx()
                S.op("pe", lambda e: e.matmul(pb[0:64, 0:TB], lhsT=ones64[:], rhs=tmp[:, h, :], start=True, stop=True), reads=[b_o64, b_tmp], writes=[bpb])
                S.op("act", lambda e: e.activation(out=eC[:, h, :], in_=pb[0:64, 0:TB], func=AF.Sqrt), reads=[bpb], writes=[b_eC])
            S.op("dve", lambda e: e.tensor_scalar(out=eC[:], in0=eC[:], scalar1=1e-12, scalar2=None, op0=ALU.max), reads=[b_eC], writes=[b_eC])
            S.op("dve", lambda e: e.reciprocal(out=eC[:], in_=eC[:]), reads=[b_eC], writes=[b_eC])
            S.op("dve", lambda e: e.tensor_tensor(out=kkn[:], in0=kkn[:], in1=eC[:], op=ALU.mult), reads=[b_kkn, b_eC], writes=[b_kkn])
            for h in range(4):
                S.op("dve", lambda e: e.tensor_scalar(out=tmp[:, h, :], in0=aa[:, h, :], scalar1=rp[0:64, 28 + h:29 + h], scalar2=omka[:, h:h + 1], op0=ALU.mult, op1=ALU.add),
                     reads=[b_aa, b_rp, b_omka], writes=[b_tmp])
            S.op("dve", lambda e: e.tensor_tensor(out=kfin[:], in0=X3[:, 4:8, :], in1=tmp[:], op=ALU.mult), reads=[b_X3, b_tmp], writes=[b_kfin])
            for h in range(4):
                S.op("dve", lambda e: e.scalar_tensor_tensor(out=tmp[:, h, :], in0=r_(h), scalar=rp[0:64, 32 + h:33 + h], in1=kfin[:, h, :], op0=ALU.mult, op1=ALU.mult),
                     reads=[b_X3, b_kfin, b_rp], writes=[b_tmp])
                pb, bpb = nbx()
                S.op("pe", lambda e: e.matmul(pb[0:64, 0:TB], lhsT=ones64[:], rhs=tmp[:, h, :], start=True, stop=True), reads=[b_o64, b_tmp], writes=[bpb])
                S.op("dve", lambda e: e.tensor_tensor(out=bv[:, h, :], in0=pb[0:64, 0:TB], in1=v_(h), op=ALU.mult), reads=[bpb, b_X3], writes=[b_bv])
            src, bsrc, dst, bdst = lw, b_lw, cl2, b_cl2
            cp(S, "act", eprev[:], lw[:], [b_lw], [b_eprev])
            for sft in (1, 2, 4, 8, 16, 32):
                s5 = src[:].rearrange("k h (c t) -> k h c t", t=64)
                d5 = dst[:].rearrange("k h (c t) -> k h c t", t=64)
                S.op("dve", lambda e: e.tensor_tensor(out=d5[:, :, :, sft:64], in0=s5[:, :, :, sft:64], in1=s5[:, :, :, 0:64 - sft], op=ALU.add), reads=[bsrc], writes=[bdst])
                cp(S, "act", d5[:, :, :, 0:sft], s5[:, :, :, 0:sft], [bsrc], [bdst])
                src, bsrc, dst, bdst = dst, bdst, src, bsrc
            cl, b_cl = src, bsrc
            S.op("act", lambda e: e.activation(out=epos[:], in_=cl[:], func=AF.Exp), reads=[b_cl], writes=[b_epos])
            S.op("act", lambda e: e.activation(out=eneg[:], in_=cl[:], func=AF.Exp, scale=-1.0), reads=[b_cl], writes=[b_eneg])
            S.op("dve", lambda e: e.tensor_tensor(out=eprev[:], in0=cl[:], in1=eprev[:], op=ALU.subtract), reads=[b_cl, b_eprev], writes=[b_eprev])
            S.op("act", lambda e: e.activation(out=eprev[:], in_=eprev[:], func=AF.Exp), reads=[b_eprev], writes=[b_eprev])
            ep5 = epos[:].rearrange("k h (c t) -> k h c t", t=64)
            S.op("dve", lambda e: e.tensor_copy(out=PCf[:], in_=ep5[:, :, :, 63]), reads=[b_epos], writes=[b_PCf])
            en5 = eneg[:].rearrange("k h (c t) -> k h c t", t=64)
            ec5 = eC[:].rearrange("k h (c t) -> k h c t", t=64)
            for h in range(4):
                for c in range(NCH):
                    if (h + c) % 2:
                        S.op("dve", lambda e: e.tensor_scalar(out=ec5[:, h, c, :], in0=en5[:, h, c, :], scalar1=PCf[:, h, c:c + 1], scalar2=None, op0=ALU.mult),
                             reads=[b_eneg, b_PCf], writes=[b_eC])
                    else:
                        S.op("act", lambda e: e.activation(out=ec5[:, h, c, :], in_=en5[:, h, c, :], func=AF.Copy, scale=AS32(PCf[:, h, c:c + 1])),
                             reads=[b_eneg, b_PCf], writes=[b_eC])
            for h in range(4):
                S.op("dve", lambda e: e.scalar_tensor_tensor(out=AR[:, :, h // 2, 0, h % 2, :], in0=hm(kkn)[:, h], scalar=-1.0, in1=hm(eprev)[:, h], op0=ALU.mult, op1=ALU.mult), reads=[b_kkn, b_eprev], writes=[b_AR])
                S.op("dve", lambda e: e.tensor_tensor(out=AR[:, :, h // 2, 1, h % 2, :], in0=X3[:, h, :].rearrange("k (c t) -> k c t", t=64), in1=hm(epos)[:, h], op=ALU.mult), reads=[b_X3, b_epos], writes=[b_AR])
            S.op("dve", lambda e: e.tensor_tensor(out=tmp[:], in0=kkn[:], in1=aa[:], op=ALU.mult), reads=[b_kkn, b_aa], writes=[b_tmp])
            for h in range(4):
                S.op("dve", lambda e: e.tensor_tensor(out=cm(Bt)[:, h], in0=hm(tmp)[:, h], in1=hm(eneg)[:, h], op=ALU.mult), reads=[b_tmp, b_eneg], writes=[b_Bt])
                S.op("pool", lambda e: e.tensor_tensor(out=cm(Bh)[:, h], in0=hm(tmp)[:, h], in1=hm(eC)[:, h], op=ALU.mult), reads=[b_tmp, b_eC], writes=[b_Bh])
                S.op("dve", lambda e: e.tensor_tensor(out=cm(Kt)[:, h], in0=hm(kfin)[:, h], in1=hm(eneg)[:, h], op=ALU.mult), reads=[b_kfin, b_eneg], writes=[b_Kt])
                S.op("pool", lambda e: e.tensor_tensor(out=cm(Kh)[:, h], in0=hm(kfin)[:, h], in1=hm(eC)[:, h], op=ALU.mult), reads=[b_kfin, b_eC], writes=[b_Kh])
                cp(S, "act", cm(Vc_)[:, h], X3[:, 8 + h, :].rearrange("k (c t) -> k c t", t=64), [b_X3], [b_Vc_])
            for p in range(2):
                pb, bpb = nbx()
                S.op("pe", lambda e: e.matmul(pb[:, 0:NCH], lhsT=il[:], rhs=PCf[:, 2 * p, :], start=True, stop=False), reads=[b_il, b_PCf], writes=[bpb])
                S.op("pe", lambda e: e.matmul(pb[:, 0:NCH], lhsT=ir[:], rhs=PCf[:, 2 * p + 1, :], start=False, stop=True), reads=[b_ir, b_PCf], writes=[bpb])
                evac(PCc[:, p, :], pb[:, 0:NCH], [bpb], [b_PCc])
            if tt == 0:
                g.dbg("d_X3", X3[:], b_X3, [64, 12, TB]); g.dbg("d_cl", cl[:], b_cl, [64, 4, TB]); g.dbg("d_aa", aa[:], b_aa, [64, 4, TB])
                g.dbg("d_kkn", kkn[:], b_kkn, [64, 4, TB]); g.dbg("d_kfin", kfin[:], b_kfin, [64, 4, TB]); g.dbg("d_bv", bv[:], b_bv, [64, 4, TB])
                g.dbg("d_gT", gT[:], b_gT, [64, 4, TB]); g.dbg("d_eC", eC[:], b_eC, [64, 4, TB])
                g.dbg("d_PCc", PCc[:], b_PCc, [128, 2, NCH])
            def unit(c, p):
                tc = slice(c * 64, (c + 1) * 64)
                hp = slice(2 * p, 2 * p + 2)
                fl = lambda ap: ap.rearrange("k h t -> k (h t)")
                At_ = fl(AR[:, c, p, 0, :, :]); Bt_ = fl(Bt[:, c, hp, :]); Kt_ = fl(Kt[:, c, hp, :])
                ARp = AR[:, c, p, :, :, :].rearrange("k a h t -> k (a h t)")
                p1, bp1 = nb(); p2, bp2 = nb()
                S.op("pe", lambda e: e.matmul(p1[:, 0:256], lhsT=RR(Bt_), rhs=RR(ARp), start=True, stop=True), reads=[b_Bt, b_AR], writes=[bp1])
                S.op("pe", lambda e: e.matmul(p2[:, 0:256], lhsT=RR(Kt_), rhs=RR(ARp), start=True, stop=True), reads=[b_Kt, b_AR], writes=[bp2])
                yield
                N0, bN0 = nt_(); ArbT, bArbT = nt_(); AakT, bAakT = nt_(); ArkT, bArkT = nt_()
                S.op("dve", lambda e: e.tensor_tensor(out=N0[:], in0=p1[:, 0:128], in1=msu[:], op=ALU.mult), reads=[bp1, b_msu], writes=[bN0])
                S.op("dve", lambda e: e.tensor_tensor(out=ArbT[:], in0=p1[:, 128:256], in1=mu_[:], op=ALU.mult), reads=[bp1, b_mu], writes=[bArbT])
                rel(bp1)
                S.op("dve", lambda e: e.tensor_tensor(out=AakT[:], in0=p2[:, 0:128], in1=msu[:], op=ALU.mult), reads=[bp2, b_msu], writes=[bAakT])
                S.op("dve", lambda e: e.tensor_tensor(out=ArkT[:], in0=p2[:, 128:256], in1=mu_[:], op=ALU.mult), reads=[bp2, b_mu], writes=[bArkT])
                rel(bp2)
                p3, bp3 = nb()
                S.op("pe", lambda e: e.matmul(p3[:, 0:128], lhsT=RR(At_), rhs=RR(Bt_), start=True, stop=True), reads=[b_AR, b_Bt], writes=[bp3])
                ptr, bptr = nb()
                srcs = [(At_, b_AR), (fl(Vc_[:, c, hp, :]), b_Vc_), (fl(Bh[:, c, hp, :]), b_Bh), (fl(Kh[:, c, hp, :]), b_Kh)]
                for i_, (sap, sb_) in enumerate(srcs):
                    S.op("pe", lambda e: e.transpose(out=ptr[:, i_ * 64:(i_ + 1) * 64], in_=AS32(sap) if i_ == 0 else sap, identity=idf[0:64, 0:64]), reads=[sb_, b_idf], writes=[bptr])
                yield
                NT0, bNT0 = nt_()
                S.op("dve", lambda e: e.tensor_tensor(out=NT0[:], in0=p3[:, 0:128], in1=msl[:], op=ALU.mult), reads=[bp3, b_msl], writes=[bNT0])
                rel(bp3)
                Z, bZ = nt_()
                S.op("pool", lambda e: e.tensor_tensor(out=Z[:], in0=N0[:], in1=idf[:], op=ALU.add), reads=[bN0, b_idf], writes=[bZ])
                TA, bTA = nt_()
                Vt, bVt = nt_()
                cp(S, "act", TA[:, 0:64], ptr[:, 0:64], [bptr], [bTA])
                cp(S, "act", Vt[:, 0:64], ptr[:, 64:128], [bptr], [bVt])
                bi = bdc[0] % NBD; bdc[0] += 1
                Bbd, bBbd = bd[0][bi], b_bd[0][bi]
                Kbd, bKbd = bd[1][bi], b_bd[1][bi]
                Apb, bApb = bd[2][bi], b_bd[2][bi]
                for hh in range(2):
                    rs = slice(hh * 64, (hh + 1) * 64)
                    cp(S, "act", Bbd[rs, rs], ptr[rs, 128:192], [bptr], [bBbd])
                    cp(S, "act", Kbd[rs, rs], ptr[rs, 192:256], [bptr], [bKbd])
                rel(bptr)
                yield
                X, bX, XT, bXT = N0, bN0, NT0, bNT0
                pw, bpw = nb()
                S.op("pe", lambda e: e.matmul(pw[:, 0:64], lhsT=RR(AakT[:]), rhs=RR(Vt[:, 0:64]), start=True, stop=True), reads=[bAakT, bVt], writes=[bpw])
                yield
                cp(S, "act", TA[:, 64:128], pw[:, 0:64], [bpw], [bTA])
                rel(bpw)
                for j in range(1, 6):
                    if j <= 4:
                        px, bpx = nb()
                        S.op("pe", lambda e: e.matmul(px[:, 0:128], lhsT=RR(XT[:]), rhs=RR(X[:]), start=True, stop=True), reads=[bXT, bX], writes=[bpx])
                    pxt, bpxt = nb()
                    S.op("pe", lambda e: e.matmul(pxt[:, 0:128], lhsT=RR(X[:]), rhs=RR(XT[:]), start=True, stop=True), reads=[bXT, bX], writes=[bpxt])
                    yield
                    if j <= 4:
                        Xn, bXn = nt_()
                        cp(S, "act", Xn[:], px[:, 0:128], [bpx], [bXn])
                        rel(bpx)
                    XTn, bXTn = nt_()
                    cp(S, "act" if j > 4 else "dve", XTn[:], pxt[:, 0:128], [bpxt], [bXTn])
                    rel(bpxt)
                    pz, bpz = nb()
                    S.op("pe", lambda e: e.matmul(pz[:, 0:128], lhsT=RR(XTn[:]), rhs=RR(Z[:]), start=True, stop=True), reads=[bXTn, bZ], writes=[bpz])
                    yield
                    Zn, bZn = nt_()
                    S.op("dve", lambda e: e.tensor_tensor(out=Zn[:], in0=pz[:, 0:128], in1=Z[:], op=ALU.add), reads=[bpz, bZ], writes=[bZn])
                    rel(bpz)
                    Z, bZ = Zn, bZn
                    if j <= 4:
                        X, bX = Xn, bXn
                    XT, bXT = XTn, bXTn
                pu, bpu = nb()
                S.op("pe", lambda e: e.matmul(pu[:, 0:128], lhsT=RR(Z[:]), rhs=RR(TA[:]), start=True, stop=True), reads=[bZ, bTA], writes=[bpu])
                yield
                U0, bU0 = nt_()
                cp(S, "dve", U0[:, 0:64], pu[:, 64:128], [bpu], [bU0])
                for hh in range(2):
                    rs = slice(hh * 64, (hh + 1) * 64)
                    cp(S, "act", Apb[rs, rs], pu[rs, 0:64], [bpu], [bApb])
                rel(bpu)
                pg, bpg = nb()
                S.op("pe", lambda e: e.matmul(pg[:, 0:64], lhsT=RR(Bbd[:]), rhs=RR(U0[:, 0:64]), start=True, stop=False), reads=[bBbd, bU0], writes=[bpg])
                S.op("pe", lambda e: e.matmul(pg[:, 0:64], lhsT=RR(Kbd[:]), rhs=RR(Vt[:, 0:64]), start=False, stop=True), reads=[bKbd, bVt], writes=[bpg])
                pf, bpf = nb()
                S.op("pe", lambda e: e.matmul(pf[:, 0:128], lhsT=RR(Apb[:]), rhs=RR(Bbd[:]), start=True, stop=True), reads=[bApb, bBbd], writes=[bpf])
                yield
                Gs, bGs = ntf_()
                cp(S, "act", Gs[:, 0:64], pg[:, 0:64], [bpg], [bGs])
                rel(bpg)
                PhiT, bPhiT = nt_()
                S.op("dve", lambda e: e.scalar_tensor_tensor(out=PhiT[:], in0=idf[:], scalar=PCc[:, p, c:c + 1], in1=pf[:, 0:128], op0=ALU.mult, op1=ALU.add),
                     reads=[b_idf, b_PCc, bpf], writes=[bPhiT])
                rel(bpf)
                pr, bpr = nb()
                S.op("pe", lambda e: e.matmul(pr[:, 0:64], lhsT=RR(il[:]), rhs=RR(AR[:, c, p, 1, 0, :]), start=True, stop=False, **SK), reads=[b_il, b_AR], writes=[bpr])
                S.op("pe", lambda e: e.matmul(pr[:, 64:128], lhsT=RR(ir[:]), rhs=RR(AR[:, c, p, 1, 1, :]), start=False, stop=False, **SK), reads=[b_ir, b_AR], writes=[bpr])
                S.op("pe", lambda e: e.matmul(pr[:, 0:128], lhsT=RR(Apb[:]), rhs=RR(ArbT[:]), start=False, stop=True, **SK), reads=[bApb, bArbT], writes=[bpr])
                yield
                RpT, bRpT = nt_()
                cp(S, "act", RpT[:], pr[:, 0:128], [bpr], [bRpT])
                rel(bpr)
                Scur, bScur = ST[p][sidx[p]], b_ST[p][sidx[p]]
                py, bpy = nb()
                S.op("pe", lambda e: e.matmul(py[:, 0:64], lhsT=RR(ArbT[:]), rhs=RR(U0[:, 0:64]), start=True, stop=False), reads=[bArbT, bU0], writes=[bpy])
                S.op("pe", lambda e: e.matmul(py[:, 0:64], lhsT=RR(ArkT[:]), rhs=RR(Vt[:, 0:64]), start=False, stop=False), reads=[bArkT, bVt], writes=[bpy])
                S.op("pe", lambda e: e.matmul(py[:, 0:64], lhsT=RR(RpT[:]), rhs=RR(Scur[:]), start=False, stop=True), reads=[bRpT, bScur], writes=[bpy])
                ps_, bps = nb()
                S.op("pe", lambda e: e.matmul(ps_[:, 0:64], lhsT=RR(PhiT[:]), rhs=RR(Scur[:]), start=True, stop=True), reads=[bPhiT, bScur], writes=[bps])
                sidx[p] ^= 1
                Snew, bSnew = ST[p][sidx[p]], b_ST[p][sidx[p]]
                S.op("dve", lambda e: e.tensor_tensor(out=Snew[:], in0=ps_[:, 0:64], in1=Gs[:, 0:64], op=ALU.add), reads=[bps, bGs], writes=[bSnew])
                rel(bps)
                Yt, bYt = ntf_()
                st_, bst = ntf_()
                cp(S, "act", Yt[:, 0:64], py[:, 0:64], [bpy], [bYt])
                rel(bpy)
                yield
                S.op("dve", lambda e: e.bn_stats(out=st_[:, 0:6], in_=Yt[:, 0:64]), reads=[bYt], writes=[bst])
                yield
                S.op("dve", lambda e: e.bn_aggr(out=st_[:, 8:10], in_=st_[:, 0:6]), reads=[bst], writes=[bst])
                yield
                S.op("act", lambda e: e.activation(out=st_[:, 10:11], in_=st_[:, 9:10], func=AF.Sqrt, bias=cst[:, 0:1], scale=1.0), reads=[bst, b_cst], writes=[bst])
                yield
                S.op("dve", lambda e: e.reciprocal(out=st_[:, 11:12], in_=st_[:, 10:11]), reads=[bst], writes=[bst])
                yield
                S.op("dve", lambda e: e.tensor_scalar(out=Yt[:, 0:64], in0=Yt[:, 0:64], scalar1=st_[:, 8:9], scalar2=st_[:, 11:12], op0=ALU.subtract, op1=ALU.mult),
                     reads=[bYt, bst], writes=[bYt])
                pyt, bpyt = nb()
                S.op("pe", lambda e: e.transpose(out=pyt[0:64, 0:128], in_=Yt[:, 0:64], identity=idf[:]), reads=[bYt, b_idf], writes=[bpyt])
                yield
                cp(S, "act", YN[:, hp, tc], pyt[0:64, 0:128].rearrange("v (h t) -> v h t", h=2), [bpyt], [b_YN])
                rel(bpyt)

            GRP = 2
            for c0_ in range(0, NCH, GRP):
                for k_ in range(NB):
                    busy[k_] = False
                gens = [unit(c_, p_) for c_ in range(c0_, min(NCH, c0_ + GRP)) for p_ in range(2)]
                alive = list(gens)
                while alive:
                    nxt = []
                    for gn_ in alive:
                        try:
                            next(gn_)
                            nxt.append(gn_)
                        except StopIteration:
                            pass
                    alive = nxt
            if tt == 0:
                g.dbg("d_YN", YN[:], b_YN, [64, 4, TB])
            for h in range(4):
                S.op("dve", lambda e: e.tensor_scalar(out=YN[:, h, :], in0=YN[:, h, :], scalar1=rp[0:64, 36 + h:37 + h], scalar2=rp[0:64, 40 + h:41 + h], op0=ALU.mult, op1=ALU.add),
                     reads=[b_YN, b_rp], writes=[b_YN])
            S.op("dve", lambda e: e.tensor_tensor(out=YN[:], in0=YN[:], in1=bv[:], op=ALU.add), reads=[b_YN, b_bv], writes=[b_YN])
            S.op("dve", lambda e: e.tensor_tensor(out=yo[:], in0=YN[:], in1=gT[:], op=ALU.mult), reads=[b_YN, b_gT], writes=[b_yo])
            S.dma("sp", D["ymixT"][0:256, t0:t0 + TB].rearrange("(h v) t -> v h t", v=64), yo[:], reads=[b_yo])
        S.barrier()


def build(T=8192, debug=False, phases=None, nlayers=2):
    nc = bass.Bass("TRN2", target_bir_lowering=False)
    g = G()
    g.nc, g.T, g.wctr = nc, T, 0
    g.debug = debug
    D = {}
    g.D = D

    def din(name, shape, dt=F32):
        D[name] = nc.dram_tensor(name, list(shape), dt, kind="ExternalInput").ap()

    def dscr(name, shape, dt=F32, out=False):
        D[name] = nc.dram_tensor(name, list(shape), dt, kind=("ExternalOutput" if (out or debug) else "Internal")).ap()

    din("xT", [DM, T]); din("pT", [2, 256, T]); din("pos", [1, T], I32); din("invf", [128, 1])
    din("w_in", [2, DM, NEXT]); din("smalls", [2, 128, 64]); din("w_out", [2, DM, DM])
    din("ffn_w_up", [2, DM, 2 * DFF]); din("ffn_w_down", [2, DFF, DM]); din("convp", [2, 128, 44, 4])
    din("ple_w_gate", [2, DM, DM]); din("ple_w_proj", [2, 256, DM])
    din("pool_wbd", [2, 128, 2, 128]); din("pool_fix", [128, 2, 16])
    for nm, shp, dt in g_extra_inputs(T):
        din(nm, shp, dt)
    dscr("cosT", [128, T]); dscr("sinT", [128, T])
    dscr("zaT", [1024, T]); dscr("zbT", [256, T]); dscr("qT", [512, T], BF16); dscr("kT", [384, T], BF16)
    dscr("vcT", [128, T], BF16); dscr("vtok", [T, 256], BF16); dscr("gates", [T, 24])
    dscr("ymixT", [1024, T], BF16)
    for nm, shp, dt in g_extra_scratch(T):
        dscr(nm, shp, dt)
    dscr("xs0", [DM, T]); dscr("xs1", [DM, T]); dscr("xs2", [DM, T])
    dscr("outT", [DM, T], out=True)
    with ExitStack() as stack:
        g.S = Sched(nc, stack)
        if phases is None:
            phases = ("rope", "inproj", "rwkv", "pool", "nsa", "outproj", "ffn", "ple")
        if "rope" in phases:
            phase_rope(g)
        xcur = D["xT"]
        for l in range(nlayers):
            if "inproj" in phases:
                phase_inproj(g, l, xcur)
            if "rwkv" in phases:
                phase_rwkv(g, l)
            if "pool" in phases:
                phase_pool(g, l)
            if "nsa" in phases:
                phase_nsa(g, l)
            if "outproj" in phases:
                phase_outproj(g, l, xcur, D["xs0"])
            if "ffn" in phases:
                phase_ffn(g, l, D["xs0"], D["xs1"])
            if "ple" in phases:
                last = (l == nlayers - 1)
                phase_ple(g, l, D["xs1"], D["outT"] if last else D["xs2"], last)
            xcur = D["xs2"]
        g.S.barrier()
        g.ninstr = g.S.ninstr
    return nc, g


def g_extra_inputs(T):
    NCMP = (T - 32) // 16 + 1
    NTC = (NCMP + 127) // 128
    return [("nsa_masks", [128, 19, 512], BF16), ("identb", [128, 128], BF16), ("identf", [128, 128], F32),
            ("E_all", [128, T], BF16), ("mcs", [128, NTC, 128], BF16), ("keepw", [128, 256], F32), ("addw", [128, 256], F32),
            ("rw_msu", [128, 128], F32), ("rw_mu", [128, 128], F32), ("rw_msl", [128, 128], F32), ("rw_il", [64, 128], F32), ("rw_ir", [64, 128], F32),
            ("rwp", [2, 128, 64], F32), ("rw_w_up", [2, 64, 256], F32), ("rw_a_up", [2, 64, 256], F32), ("rw_g_up", [2, 128, 256], F32),
            ("nsa_w_ck", [2, 32, 64, 64], F32), ("nsa_w_cv", [2, 32, 64, 64], F32), ("nsa_peT", [2, 2, 64, 32], F32)]


def g_extra_scratch(T):
    return []


def host_prep(inp, T=8192):
    f = np.float32
    cols = inproj_cols()
    shared = {}
    shared["w_in"] = np.ascontiguousarray(inp["w_in"][:, :, cols])
    sm = np.zeros((2, 128, 64), f)
    for l in range(2):
        sm[l, :, 0:8] = inp["g_mix"][l].reshape(8, 128).T
        sm[l, :, 8:10] = inp["pool_scale"][l].reshape(2, 128).T
        sm[l, :, 16:24] = inp["g_ffn"][l].reshape(8, 128).T
        sm[l, :, 24:32] = inp["g_ple"][l].reshape(8, 128).T
        sm[l, :, 32:40] = inp["g_final"].reshape(8, 128).T
    shared["smalls"] = sm
    shared["w_out"] = np.ascontiguousarray(inp["w_out"])
    shared["ffn_w_up"] = np.ascontiguousarray(inp["ffn_w_up"])
    shared["ffn_w_down"] = np.ascontiguousarray(inp["ffn_w_down"])
    cp_ = np.zeros((2, 128, 44, 4), f)
    for l in range(2):
        cw = inp["ffn_conv_w"][l][:, 0, :]
        for i in range(3):
            cp_[l, :, :, i] = cw[i].reshape(44, 128).T
        cp_[l, :, :, 3] = inp["ffn_conv_b"][l].reshape(44, 128).T
    shared["convp"] = cp_
    shared["ple_w_gate"] = np.ascontiguousarray(inp["ple_w_gate"])
    shared["ple_w_proj"] = np.ascontiguousarray(inp["ple_w_proj"])
    pw = np.zeros((2, 128, 2, 128), f)
    for l in range(2):
        for gi in range(4):
            t_, h_ = gi // 2, gi % 2
            pw[l, h_ * 64:(h_ + 1) * 64, t_, h_ * 64:(h_ + 1) * 64] = inp["pool_w"][l, gi]
    shared["pool_wbd"] = pw
    fix = np.zeros((128, 2, 16), f)
    for gi, win in enumerate((2, 4, 8, 16)):
        t_, h_ = gi // 2, gi % 2
        fix[h_ * 64:(h_ + 1) * 64, t_, :] = 1.0 / np.minimum(np.arange(16) + 1, win)
    shared["pool_fix"] = fix
    shared.update(nsa_consts(T))
    shared.update(rwkv_consts())
    rwp = np.zeros((2, 128, 64), f)
    for l in range(2):
        mu = inp["rw_mu"][l]
        rwp[l, 0:64, 0:12] = mu[0:768].reshape(12, 64).T
        rwp[l, 0:64, 12:14] = mu[768:896].reshape(2, 64).T
        rwp[l, :, 14] = mu[896:1024]
        for j, nm in enumerate(("rw_w0", "rw_a0", "rw_k_k", "rw_k_a")):
            rwp[l, 0:64, 16 + 4 * j:20 + 4 * j] = inp[nm][l].reshape(4, 64).T
        rwp[l, 0:64, 32:36] = inp["rw_r_k"][l].T
        rwp[l, 0:64, 36:40] = inp["rw_gn_g"][l].reshape(4, 64).T
        rwp[l, 0:64, 40:44] = inp["rw_gn_b"][l].reshape(4, 64).T
    shared["rwp"] = rwp
    shared["rw_w_up"] = np.ascontiguousarray(inp["rw_w_up"])
    shared["rw_a_up"] = np.ascontiguousarray(inp["rw_a_up"])
    shared["rw_g_up"] = np.ascontiguousarray(inp["rw_g_up"])
    shared["nsa_w_ck"] = np.ascontiguousarray(inp["nsa_w_ck"])
    shared["nsa_w_cv"] = np.ascontiguousarray(inp["nsa_w_cv"])
    shared["nsa_peT"] = np.ascontiguousarray(np.stack([np.transpose(inp["nsa_pe_k"], (0, 2, 1)), np.transpose(inp["nsa_pe_v"], (0, 2, 1))], axis=1))
    inv = (10000.0 ** (-np.arange(32, dtype=f) / 32)).astype(f)
    shared["invf"] = np.tile(inv, 4).reshape(128, 1).astype(f)
    return shared


def per_core(inp, b, T=8192):
    return {"xT": np.ascontiguousarray(inp["x"][b, :T].T),
            "pT": np.ascontiguousarray(np.transpose(inp["p"][:, b, :T], (0, 2, 1))),
            "pos": np.ascontiguousarray(inp["positions"][b:b + 1, :T]).astype(np.int32)}


def kernel(**inputs):
    T = 8192
    inp = {k: np.asarray(v) for k, v in inputs.items()}
    nc, g = build(T)
    shared = host_prep(inp, T)
    in_maps = []
    for b in range(8):
        m = dict(shared)
        m.update(per_core(inp, b, T))
        in_maps.append(m)
    res = run_bass_kernel_spmd(nc, in_maps, core_ids=list(range(8)))
    out = np.stack([np.ascontiguousarray(res.results[b]["outT"].T) for b in range(8)], axis=0)
    return out.astype(np.float32)
```

```python
import os
import numpy as np
from contextlib import ExitStack
import concourse.bass as bass
import concourse.mybir as mybir
from concourse.bass_utils import run_bass_kernel_spmd

F32 = mybir.dt.float32
BF16 = mybir.dt.bfloat16
I32 = mybir.dt.int32
AF = mybir.ActivationFunctionType
ALU = mybir.AluOpType
AX = mybir.AxisListType

DM = 1024
PI = float(np.pi)
BIG = 30000.0
NFEAT = 3200
NTOKC = 280
NEXT = NFEAT + NTOKC
DFF = 2816


class Buf:
    __slots__ = ("w", "rs", "rd", "excl")

    def __init__(self, excl=False):
        self.w = None
        self.rs = {}
        self.rd = []
        self.excl = excl


class Sched:
    EPOCH = 30000
    NDMA = 6

    def __init__(self, nc, stack):
        self.nc = nc
        self.stack = stack
        self.engs = {"pe": nc.tensor, "act": nc.scalar, "dve": nc.vector, "pool": nc.gpsimd, "sp": nc.sync}
        self.nsem = 0
        self.sem = {}
        self.cnt = {}
        self.seen = {k: {} for k in self.engs}
        for k in self.engs:
            self.sem[k] = self._newsem(k)
            self.cnt[k] = 0
        self.dsem = {}
        self.dpos = {}
        for k in ("sp", "act", "pool"):
            self.dsem[k] = [[self._newsem("d" + k), 0] for _ in range(self.NDMA)]
            self.dpos[k] = 0
        self.last = {}
        self.ninstr = 0
        self.store_q = "pool"

    def _newsem(self, name):
        self.nsem += 1
        return self.stack.enter_context(self.nc.semaphore(f"s_{name}_{self.nsem}"))

    def _wait(self, ek, tok):
        sem, val, src = tok
        d = self.seen[ek]
        key = id(sem)
        if d.get(key, 0) >= val:
            return
        self.engs[ek].wait_ge(sem, val)
        d[key] = val

    def _deps(self, ek, reads, writes):
        toks = []
        same_ok = (ek == "pe")
        for b in reads:
            if b.w is not None and not (b.w[2] == ek and same_ok):
                toks.append(b.w)
            if b.excl:
                for e, t in b.rs.items():
                    if e != ek:
                        toks.append(t)
        for b in writes:
            if b.w is not None and not (b.w[2] == ek and same_ok):
                toks.append(b.w)
            for e, t in b.rs.items():
                if not (e == ek and same_ok):
                    toks.append(t)
            toks.extend(b.rd)
        for t in toks:
            self._wait(ek, t)

    def _record(self, tok, reads, writes, is_dma):
        for b in reads:
            if is_dma:
                b.rd.append(tok)
                if len(b.rd) > 24:
                    del b.rd[0]
            else:
                b.rs[tok[2]] = tok
        for b in writes:
            b.w = tok
            b.rs = {}
            b.rd = []

    def op(self, ek, fn, reads=(), writes=()):
        self._deps(ek, reads, writes)
        ins = fn(self.engs[ek])
        self.cnt[ek] += 1
        ins.then_inc(self.sem[ek], 1)
        tok = (self.sem[ek], self.cnt[ek], ek)
        self.last[ek] = tok
        self._record(tok, reads, writes, False)
        self.ninstr += 1
        if self.cnt[ek] >= self.EPOCH:
            self.sem[ek] = self._newsem(ek)
            self.cnt[ek] = 0
        return tok

    def dma(self, qk, out, in_, reads=(), writes=(), **kw):
        if qk == "sp" and len(writes) == 0 and self.store_q is not None:
            qk = self.store_q
        self._deps(qk, reads, writes)
        slots = self.dsem[qk]
        i = self.dpos[qk]
        self.dpos[qk] = (i + 1) % len(slots)
        sem, val = slots[i]
        if val > 0:
            self._wait(qk, (sem, val, "dma"))
        if val + 16 > 60000:
            sem = self._newsem("d" + qk)
            val = 0
            slots[i][0] = sem
        ins = self.engs[qk].dma_start(out=out, in_=in_, **kw)
        val += 16
        ins.then_inc(sem, 16)
        slots[i][1] = val
        tok = (sem, val, "dma")
        self._record(tok, reads, writes, True)
        self.ninstr += 1
        return tok

    def barrier(self):
        toks = list(self.last.values())
        for qk in self.dsem:
            for sem, val in self.dsem[qk]:
                if val > 0:
                    toks.append((sem, val, "dma"))
        for ek in self.engs:
            for t in toks:
                self._wait(ek, t)


def inproj_cols():
    qb = 1280
    rot = lambda base, nh: [base + h * 64 + ((d + 32) % 64) for h in range(nh) for d in range(64)]
    cols = list(range(0, 1280))
    cols += list(range(qb, qb + 512)) + rot(qb, 8)
    for off in (512, 768, 1024):
        cols += list(range(qb + off, qb + off + 128))
    for off in (512, 768, 1024):
        cols += rot(qb + off, 2)
    cols += list(range(qb + 640, qb + 768))
    assert len(cols) == NFEAT
    cols += list(range(qb + 896, qb + 1024)) + list(range(qb + 1152, qb + 1280))
    cols += list(range(qb + 1280, qb + 1304))
    assert len(cols) == NEXT
    return np.array(cols)


class G:
    uid = 0
    debug = False
    use_f32r = True

    def dbg(self, name, ap, buf, shape):
        if not self.debug or name in self.D:
            return
        self.D[name] = self.nc.dram_tensor(name, list(shape), F32, kind="ExternalOutput").ap()
        self.S.dma("sp", self.D[name], ap, reads=[buf])

    def nm(self, name):
        self.uid += 1
        return f"{name}_{self.uid}"


def cp(S, ek, out, in_, reads, writes):
    if ek == "act":
        return S.op("act", lambda e: e.copy(out=out, in_=in_), reads=reads, writes=writes)
    return S.op(ek, lambda e: e.tensor_copy(out=out, in_=in_), reads=reads, writes=writes)


def load_w_bf16(g, dst, b_dst, src, KC, N, stage, b_stage, rows=128):
    S = g.S
    CH = stage[0].shape[-1]
    srcv = src.rearrange("(c p) n -> p c n", p=rows)
    for c in range(KC):
        for n0 in range(0, N, CH):
            n1 = min(N, n0 + CH)
            i = g.wctr % len(stage)
            g.wctr += 1
            S.dma("sp", stage[i][0:rows, 0:n1 - n0], srcv[:, c, n0:n1], writes=[b_stage[i]])
            ek = ("act", "dve", "pool")[g.wctr % 3]
            cp(S, ek, dst[0:rows, c, n0:n1], stage[i][0:rows, 0:n1 - n0], [b_stage[i]], [b_dst])


def rmsnorm_tile(g, xt, b_x, hT, b_h, gcol, b_g, N, R):
    S = g.S
    S.op("act", lambda e: e.activation(out=R["sq"][:, :, 0:N], in_=xt[:, :, 0:N], func=AF.Square), reads=[b_x], writes=[R["b_sq"]])
    for c in range(8):
        S.op("pe", lambda e: e.matmul(R["p_rms"][:, 0:N], lhsT=R["ones"][:], rhs=R["sq"][:, c, 0:N], start=(c == 0), stop=(c == 7)),
             reads=[R["b_ones"], R["b_sq"]], writes=[R["b_prms"]])
    S.op("act", lambda e: e.activation(out=R["rstd"][:, 0:N], in_=R["p_rms"][:, 0:N], func=AF.Sqrt, bias=R["eps"][:, 0:1], scale=1.0 / DM),
         reads=[R["b_prms"], R["b_eps"]], writes=[R["b_rstd"]])
    S.op("dve", lambda e: e.reciprocal(out=R["rstd"][:, 0:N], in_=R["rstd"][:, 0:N]), reads=[R["b_rstd"]], writes=[R["b_rstd"]])
    for c in range(8):
        S.op("dve", lambda e: e.scalar_tensor_tensor(out=hT[:, c, 0:N], in0=xt[:, c, 0:N], scalar=gcol[:, c:c + 1], in1=R["rstd"][:, 0:N],
                                                       op0=ALU.mult, op1=ALU.mult),
             reads=[b_x, b_g, R["b_rstd"]], writes=[b_h])


def rms_shared(g, sbf, psf, N):
    S = g.S
    R = {}
    R["sq"] = sbf("rsq", [128, 8, N], BF16); R["b_sq"] = Buf()
    R["rstd"] = sbf("rstd", [128, N]); R["b_rstd"] = Buf()
    R["p_rms"] = psf("p_rms", [128, 512]); R["b_prms"] = Buf(excl=True)
    R["ones"] = sbf("ones", [128, 128], BF16); R["b_ones"] = Buf()
    R["eps"] = sbf("epsb", [128, 1]); R["b_eps"] = Buf()
    S.op("pool", lambda e: e.memset(R["ones"][:], 1.0), writes=[R["b_ones"]])
    S.op("pool", lambda e: e.memset(R["eps"][:], 1e-6), writes=[R["b_eps"]])
    return R


def phase_rope(g):
    nc, S, D, T = g.nc, g.S, g.D, g.T
    with ExitStack() as st:
        sbf = lambda name, shape, dt=F32: st.enter_context(nc.sbuf_tensor(g.nm(name), list(shape), dt))
        CH = min(T, 2048)
        posi = sbf("posi", [128, CH], I32); b_posi = Buf()
        posf = sbf("posf", [128, CH]); b_posf = Buf()
        ang = sbf("ang", [128, CH]); b_ang = Buf()
        tab = sbf("tab", [128, CH]); b_tab = Buf()
        ki = sbf("ki", [128, CH], I32); b_ki = Buf()
        kf = sbf("kf", [128, CH]); b_kf = Buf()
        inv_sb = sbf("inv_sb", [128, 1]); b_inv = Buf()
        S.dma("sp", inv_sb[:], D["invf"], writes=[b_inv])
        C1 = 6.28125
        C2 = 2 * np.pi - 6.28125
        for c0 in range(0, T, CH):
            S.dma("sp", posi[:], D["pos"][:, c0:c0 + CH].partition_broadcast(128), writes=[b_posi])
            S.op("dve", lambda e: e.tensor_copy(out=posf[:], in_=posi[:]), reads=[b_posi], writes=[b_posf])
            S.op("dve", lambda e: e.tensor_scalar(out=posf[:], in0=posf[:], scalar1=inv_sb[:, 0:1], scalar2=None, op0=ALU.mult),
                 reads=[b_posf, b_inv], writes=[b_posf])
            S.op("dve", lambda e: e.tensor_scalar(out=kf[:], in0=posf[:], scalar1=float(1.0 / (2 * np.pi)), scalar2=None, op0=ALU.mult),
                 reads=[b_posf], writes=[b_kf])
            S.op("dve", lambda e: e.tensor_copy(out=ki[:], in_=kf[:]), reads=[b_kf], writes=[b_ki])
            S.op("dve", lambda e: e.tensor_copy(out=kf[:], in_=ki[:]), reads=[b_ki], writes=[b_kf])
            S.op("dve", lambda e: e.scalar_tensor_tensor(out=posf[:], in0=kf[:], scalar=-C1, in1=posf[:], op0=ALU.mult, op1=ALU.add),
                 reads=[b_kf, b_posf], writes=[b_posf])
            S.op("dve", lambda e: e.scalar_tensor_tensor(out=posf[:], in0=kf[:], scalar=-C2, in1=posf[:], op0=ALU.mult, op1=ALU.add),
                 reads=[b_kf, b_posf], writes=[b_posf])
            for which, shift, dst in (("sin", 0.0, D["sinT"]), ("cos", PI / 2, D["cosT"])):
                S.op("dve", lambda e: e.tensor_scalar(out=ang[:], in0=posf[:], scalar1=shift, scalar2=None, op0=ALU.add),
                     reads=[b_posf], writes=[b_ang])
                S.op("dve", lambda e: e.tensor_scalar(out=kf[:], in0=ang[:], scalar1=PI, scalar2=-2 * PI, op0=ALU.is_gt, op1=ALU.mult),
                     reads=[b_ang], writes=[b_kf])
                S.op("dve", lambda e: e.tensor_tensor(out=ang[:], in0=ang[:], in1=kf[:], op=ALU.add), reads=[b_ang, b_kf], writes=[b_ang])
                S.op("dve", lambda e: e.tensor_scalar(out=ang[:], in0=ang[:], scalar1=3.141592, scalar2=-3.141592, op0=ALU.min, op1=ALU.max),
                     reads=[b_ang], writes=[b_ang])
                S.op("act", lambda e: e.activation(out=tab[:], in_=ang[:], func=AF.Sin), reads=[b_ang], writes=[b_tab])
                if which == "sin":
                    for base in (0, 64):
                        S.op("dve", lambda e: e.tensor_scalar(out=tab[base:base + 32, :], in0=tab[base:base + 32, :], scalar1=-1.0, scalar2=None, op0=ALU.mult),
                             reads=[b_tab], writes=[b_tab])
                S.dma("sp", dst[:, c0:c0 + CH], tab[:], reads=[b_tab])
        S.barrier()


def phase_inproj(g, l, xin):
    nc, S, D, T = g.nc, g.S, g.D, g.T
    with ExitStack() as st:
        sbf = lambda name, shape, dt=F32: st.enter_context(nc.sbuf_tensor(g.nm(name), list(shape), dt))
        psf = lambda name, shape, dt=F32: st.enter_context(nc.psum_tensor(g.nm(name), list(shape), dt))
        Wb = sbf("Wb", [128, 8, NEXT], BF16); b_W = Buf()
        stage = [sbf(f"wst{i}", [128, 1740]) for i in range(2)]; b_stage = [Buf() for _ in range(2)]
        load_w_bf16(g, Wb, b_W, D["w_in"][l], 8, NEXT, stage, b_stage)
        sm = sbf("sm", [128, 8]); b_sm = Buf()
        S.dma("sp", sm[:], D["smalls"][l][:, 0:8], writes=[b_sm])
        R = rms_shared(g, sbf, psf, 512)
        xt = [sbf(f"xt{i}", [128, 8, 512]) for i in range(2)]; b_xt = [Buf() for _ in range(2)]
        hT = sbf("hT", [128, 8, 512], BF16); b_h = Buf()
        cs = [sbf(f"cs{i}", [128, 512]) for i in range(2)]; b_cs = [Buf() for _ in range(2)]
        sn = [sbf(f"sn{i}", [128, 512]) for i in range(2)]; b_sn = [Buf() for _ in range(2)]
        NEV = 4
        ev = [sbf(f"ev{i}", [128, 512]) for i in range(NEV)]; b_ev = [Buf() for _ in range(NEV)]
        evb = [sbf(f"evb{i}", [128, 512], BF16) for i in range(NEV)]; b_evb = [Buf() for _ in range(NEV)]
        t1 = [sbf(f"t1_{i}", [128, 512]) for i in range(2)]; b_t1 = [Buf() for _ in range(2)]
        t2 = [sbf(f"t2_{i}", [128, 512]) for i in range(2)]; b_t2 = [Buf() for _ in range(2)]
        vt = [sbf(f"vt{i}", [128, 256], BF16) for i in range(2)]; b_vt = [Buf() for _ in range(2)]
        gt = [sbf(f"gt{i}", [128, 24]) for i in range(2)]; b_gt = [Buf() for _ in range(2)]
        NPF = 5
        p_f = [psf(f"p_f{i}", [128, 512]) for i in range(NPF)]; b_pf = [Buf(excl=True) for _ in range(NPF)]
        p_t = [psf(f"p_t{i}", [128, 512]) for i in range(2)]; b_pt = [Buf(excl=True) for _ in range(2)]
        xv = xin.rearrange("(c p) t -> p c t", p=128)
        evc = 0
        pfc = 0

        def mm_feat(f, pf, bpf):
            for c in range(8):
                S.op("pe", lambda e: e.matmul(pf[:], lhsT=Wb[:, c, f * 128:(f + 1) * 128], rhs=hT[:, c, :], start=(c == 0), stop=(c == 7)),
                     reads=[b_W, b_h], writes=[bpf])
        for tt in range(T // 512):
            t0 = tt * 512
            xi = tt % 2
            S.dma("sp", xt[xi][:], xv[:, :, t0:t0 + 512], writes=[b_xt[xi]])
            S.dma("sp", cs[xi][:], D["cosT"][:, t0:t0 + 512], writes=[b_cs[xi]])
            S.dma("sp", sn[xi][:], D["sinT"][:, t0:t0 + 512], writes=[b_sn[xi]])
            rmsnorm_tile(g, xt[xi], b_xt[xi], hT, b_h, sm, b_sm, 512, R)
            plain = [(f, D["zaT"][f * 128:(f + 1) * 128, t0:t0 + 512], False) for f in range(8)]
            plain += [(8 + f, D["zbT"][f * 128:(f + 1) * 128, t0:t0 + 512], False) for f in range(2)]
            plain += [(24, D["vcT"][:, t0:t0 + 512], True)]
            for k_, (f, dst, isb) in enumerate(plain):
                pi = pfc % NPF; pfc += 1
                mm_feat(f, p_f[pi], b_pf[pi])
                ei = evc % NEV; evc += 1
                ek = "act" if k_ % 2 == 0 else "dve"
                if isb:
                    cp(S, ek, evb[ei][:], p_f[pi][:], [b_pf[pi]], [b_evb[ei]])
                    S.dma("sp", dst, evb[ei][:], reads=[b_evb[ei]])
                else:
                    cp(S, ek, ev[ei][:], p_f[pi][:], [b_pf[pi]], [b_ev[ei]])
                    S.dma("sp", dst, ev[ei][:], reads=[b_ev[ei]])
            for j in range(7):
                if j < 4:
                    f_a, f_b = 10 + j, 14 + j
                    dst = D["qT"][j * 128:(j + 1) * 128, t0:t0 + 512]
                else:
                    f_a, f_b = 18 + (j - 4), 21 + (j - 4)
                    dst = D["kT"][(j - 4) * 128:(j - 3) * 128, t0:t0 + 512]
                pa = pfc % NPF; pfc += 1
                mm_feat(f_a, p_f[pa], b_pf[pa])
                pb = pfc % NPF; pfc += 1
                mm_feat(f_b, p_f[pb], b_pf[pb])
                ti = j % 2
                S.op("dve", lambda e: e.tensor_tensor(out=t1[ti][:], in0=p_f[pa][:], in1=cs[xi][:], op=ALU.mult),
                     reads=[b_pf[pa], b_cs[xi]], writes=[b_t1[ti]])
                S.op("dve", lambda e: e.tensor_tensor(out=t2[ti][:], in0=p_f[pb][:], in1=sn[xi][:], op=ALU.mult),
                     reads=[b_pf[pb], b_sn[xi]], writes=[b_t2[ti]])
                ei = evc % NEV; evc += 1
                S.op("pool", lambda e: e.tensor_tensor(out=evb[ei][:], in0=t1[ti][:], in1=t2[ti][:], op=ALU.add),
                     reads=[b_t1[ti], b_t2[ti]], writes=[b_evb[ei]])
                S.dma("sp", dst, evb[ei][:], reads=[b_evb[ei]])
            for s4 in range(4):
                pi = s4 % 2
                for c in range(8):
                    S.op("pe", lambda e: e.matmul(p_t[pi][:, 0:NTOKC], lhsT=hT[:, c, s4 * 128:(s4 + 1) * 128], rhs=Wb[:, c, NFEAT:NEXT],
                                                  start=(c == 0), stop=(c == 7)),
                         reads=[b_W, b_h], writes=[b_pt[pi]])
                S.op("dve", lambda e: e.tensor_copy(out=vt[pi][:], in_=p_t[pi][:, 0:256]), reads=[b_pt[pi]], writes=[b_vt[pi]])
                S.op("act", lambda e: e.activation(out=gt[pi][:], in_=p_t[pi][:, 256:280], func=AF.Sigmoid), reads=[b_pt[pi]], writes=[b_gt[pi]])
                S.dma("sp", D["vtok"][t0 + s4 * 128:t0 + (s4 + 1) * 128, :], vt[pi][:], reads=[b_vt[pi]])
                S.dma("sp", D["gates"][t0 + s4 * 128:t0 + (s4 + 1) * 128, :], gt[pi][:], reads=[b_gt[pi]])
        S.barrier()


def phase_pool(g, l):
    nc, S, D, T = g.nc, g.S, g.D, g.T
    with ExitStack() as st:
        sbf = lambda name, shape, dt=F32: st.enter_context(nc.sbuf_tensor(g.nm(name), list(shape), dt))
        psf = lambda name, shape, dt=F32: st.enter_context(nc.psum_tensor(g.nm(name), list(shape), dt))
        CH = 512
        PAD = 16
        wp = sbf("wp", [128, 2, 128]); b_wp = Buf()
        S.dma("sp", wp[:], D["pool_wbd"][l], writes=[b_wp])
        sm = sbf("smp", [128, 2]); b_sm = Buf()
        S.dma("sp", sm[:], D["smalls"][l][:, 8:10], writes=[b_sm])
        fix = sbf("fix", [128, 2, 16]); b_fix = Buf()
        S.dma("sp", fix[:], D["pool_fix"], writes=[b_fix])
        z = [[sbf(f"pz{i}_{f}", [128, PAD + CH]) for f in range(2)] for i in range(2)]
        b_z = [[Buf() for f in range(2)] for i in range(2)]
        s_a = sbf("ps_a", [128, PAD + CH]); b_sa = Buf()
        s_b = sbf("ps_b", [128, PAD + CH]); b_sb = Buf()
        pl = sbf("ppl", [128, CH]); b_pl = Buf()
        ob = [sbf(f"pob{i}", [128, CH], BF16) for i in range(2)]; b_ob = [Buf() for _ in range(2)]
        pp = [psf(f"ppp{i}", [128, 512]) for i in range(2)]; b_pp = [Buf(excl=True) for _ in range(2)]
        k = 0
        for tt in range(T // CH):
            t0 = tt * CH
            zi = tt % 2
            for f in range(2):
                zt, bz = z[zi][f], b_z[zi][f]
                if tt == 0:
                    S.op("pool", lambda e: e.memset(zt[:, 0:PAD], 0.0), writes=[bz])
                    S.dma("sp", zt[:, PAD:PAD + CH], D["zbT"][f * 128:(f + 1) * 128, 0:CH], writes=[bz])
                else:
                    S.dma("sp", zt[:], D["zbT"][f * 128:(f + 1) * 128, t0 - PAD:t0 + CH], writes=[bz])
                W = PAD + CH
                S.op("dve", lambda e: e.tensor_tensor(out=s_a[:, 1:W], in0=zt[:, 1:W], in1=zt[:, 0:W - 1], op=ALU.add), reads=[bz], writes=[b_sa])
                S.op("dve", lambda e: e.tensor_tensor(out=s_b[:, 3:W], in0=s_a[:, 3:W], in1=s_a[:, 1:W - 2], op=ALU.add), reads=[b_sa], writes=[b_sb])
                if f == 0:
                    lo, hi = s_a, s_b
                    blo, bhi = b_sa, b_sb
                    wl, wh = 2, 4
                else:
                    S.op("dve", lambda e: e.tensor_tensor(out=s_a[:, 7:W], in0=s_b[:, 7:W], in1=s_b[:, 3:W - 4], op=ALU.add), reads=[b_sb], writes=[b_sa])
                    S.op("dve", lambda e: e.tensor_tensor(out=s_b[:, 15:W], in0=s_a[:, 15:W], in1=s_a[:, 7:W - 8], op=ALU.add), reads=[b_sa], writes=[b_sb])
                    lo, hi = s_a, s_b
                    blo, bhi = b_sa, b_sb
                    wl, wh = 8, 16
                S.op("dve", lambda e: e.scalar_tensor_tensor(out=pl[0:64, :], in0=lo[0:64, PAD:W], scalar=1.0 / wl, in1=zt[0:64, PAD:W], op0=ALU.mult, op1=ALU.subtract),
                     reads=[blo, bz], writes=[b_pl])
                S.op("dve", lambda e: e.scalar_tensor_tensor(out=pl[64:128, :], in0=hi[64:128, PAD:W], scalar=1.0 / wh, in1=zt[64:128, PAD:W], op0=ALU.mult, op1=ALU.subtract),
                     reads=[bhi, bz], writes=[b_pl])
                if tt == 0:
                    S.op("dve", lambda e: e.tensor_tensor(out=pl[0:64, 0:16], in0=lo[0:64, PAD:PAD + 16], in1=fix[0:64, f, :], op=ALU.mult), reads=[blo, b_fix], writes=[b_pl])
                    S.op("dve", lambda e: e.tensor_tensor(out=pl[64:128, 0:16], in0=hi[64:128, PAD:PAD + 16], in1=fix[64:128, f, :], op=ALU.mult), reads=[bhi, b_fix], writes=[b_pl])
                    S.op("dve", lambda e: e.tensor_tensor(out=pl[:, 0:16], in0=pl[:, 0:16], in1=zt[:, PAD:PAD + 16], op=ALU.subtract), reads=[b_pl, bz], writes=[b_pl])
                pi = k % 2; k += 1
                S.op("pe", lambda e: e.matmul(pp[pi][:, 0:CH], lhsT=wp[:, f, :], rhs=pl[:], start=True, stop=True), reads=[b_wp, b_pl], writes=[b_pp[pi]])
                S.op("act", lambda e: e.activation(out=ob[pi][:], in_=pp[pi][:, 0:CH], func=AF.Copy, scale=sm[:, f:f + 1]), reads=[b_pp[pi], b_sm], writes=[b_ob[pi]])
                S.dma("sp", D["ymixT"][256 + f * 128:256 + (f + 1) * 128, t0:t0 + CH], ob[pi][:], reads=[b_ob[pi]])
        S.barrier()


def phase_outproj(g, l, xin, xout):
    nc, S, D, T = g.nc, g.S, g.D, g.T
    with ExitStack() as st:
        sbf = lambda name, shape, dt=F32: st.enter_context(nc.sbuf_tensor(g.nm(name), list(shape), dt))
        psf = lambda name, shape, dt=F32: st.enter_context(nc.psum_tensor(g.nm(name), list(shape), dt))
        Wo = sbf("Wo", [128, 8, DM], BF16); b_W = Buf()
        stage = [sbf(f"wst{i}", [128, 1024]) for i in range(2)]; b_stage = [Buf() for _ in range(2)]
        load_w_bf16(g, Wo, b_W, D["w_out"][l], 8, DM, stage, b_stage)
        xt = [sbf(f"oxt{i}", [128, 8, 512]) for i in range(2)]; b_xt = [Buf() for _ in range(2)]
        ym = [sbf(f"oym{i}", [128, 8, 512], BF16) for i in range(2)]; b_ym = [Buf() for _ in range(2)]
        xo = [sbf(f"oxo{i}", [128, 8, 512]) for i in range(2)]; b_xo = [Buf() for _ in range(2)]
        pq = [psf(f"opq{i}", [128, 512]) for i in range(4)]; b_pq = [Buf(excl=True) for _ in range(4)]
        xv = xin.rearrange("(c p) t -> p c t", p=128)
        xov = xout.rearrange("(c p) t -> p c t", p=128)
        yv = D["ymixT"].rearrange("(c p) t -> p c t", p=128)
        k = 0
        for tt in range(T // 512):
            t0 = tt * 512
            xi = tt % 2
            S.dma("sp", xt[xi][:], xv[:, :, t0:t0 + 512], writes=[b_xt[xi]])
            S.dma("sp", ym[xi][:], yv[:, :, t0:t0 + 512], writes=[b_ym[xi]])
            for j in range(8):
                pi = k % 4; k += 1
                for c in range(8):
                    S.op("pe", lambda e: e.matmul(pq[pi][:], lhsT=Wo[:, c, j * 128:(j + 1) * 128], rhs=ym[xi][:, c, :], start=(c == 0), stop=(c == 7)),
                         reads=[b_W, b_ym[xi]], writes=[b_pq[pi]])
                S.op("dve", lambda e: e.tensor_tensor(out=xo[xi][:, j, :], in0=pq[pi][:], in1=xt[xi][:, j, :], op=ALU.add),
                     reads=[b_pq[pi], b_xt[xi]], writes=[b_xo[xi]])
            S.dma("sp", xov[:, :, t0:t0 + 512], xo[xi][:], reads=[b_xo[xi]])
        S.barrier()


def phase_ffn(g, l, xin, xout):
    nc, S, D, T = g.nc, g.S, g.D, g.T
    N = 512
    with ExitStack() as st:
        sbf = lambda name, shape, dt=F32: st.enter_context(nc.sbuf_tensor(g.nm(name), list(shape), dt))
        psf = lambda name, shape, dt=F32: st.enter_context(nc.psum_tensor(g.nm(name), list(shape), dt))
        Wu = sbf("Wu", [128, 8, 2 * DFF], BF16)
        Wd = sbf("Wd", [128, 22, DM], BF16)
        CHW = 512
        NCW = (2 * DFF) // CHW
        b_Wu = [Buf() for _ in range(NCW)]
        b_Wd = [Buf() for _ in range(22)]
        stage = [sbf(f"wst{i}", [128, CHW]) for i in range(2)]; b_stage = [Buf() for _ in range(2)]
        xt = sbf("fxt", [128, 8, N]); b_xt = Buf()
        xv = xin.rearrange("(c p) t -> p c t", p=128)
        xov = xout.rearrange("(c p) t -> p c t", p=128)
        S.dma("sp", xt[:], xv[:, :, 0:N], writes=[b_xt])
        wuv = D["ffn_w_up"][l].rearrange("(c p) n -> p c n", p=128)
        wdv = D["ffn_w_down"][l].rearrange("(c p) n -> p c n", p=128)
        wu_done = set()
        wd_done = set()

        def emit_wu(nchunk):
            if nchunk in wu_done:
                return
            wu_done.add(nchunk)
            for c in range(8):
                i = g.wctr % 2; g.wctr += 1
                S.dma("sp", stage[i][:, :], wuv[:, c, nchunk * CHW:(nchunk + 1) * CHW], writes=[b_stage[i]])
                cp(S, ("act", "dve", "pool")[g.wctr % 3], Wu[:, c, nchunk * CHW:(nchunk + 1) * CHW], stage[i][:, :], [b_stage[i]], [b_Wu[nchunk]])

        def emit_wd(c):
            if c in wd_done:
                return
            wd_done.add(c)
            for n0 in range(0, DM, CHW):
                n1 = min(DM, n0 + CHW)
                i = g.wctr % 2; g.wctr += 1
                S.dma("sp", stage[i][:, 0:n1 - n0], wdv[:, c, n0:n1], writes=[b_stage[i]])
                cp(S, ("act", "dve", "pool")[g.wctr % 3], Wd[:, c, n0:n1], stage[i][:, 0:n1 - n0], [b_stage[i]], [b_Wd[c]])
        sm = sbf("smf", [128, 8]); b_sm = Buf()
        S.dma("sp", sm[:], D["smalls"][l][:, 16:24], writes=[b_sm])
        cw = sbf("cw", [128, 44, 4]); b_cw = Buf()
        S.dma("sp", cw[:], D["convp"][l], writes=[b_cw])
        gated = sbf("gated", [128, 22, N], BF16); b_gt = Buf()
        R = {}
        R["sq"] = gated[:, 0:8, :]; R["b_sq"] = b_gt
        R["rstd"] = sbf("rstd", [128, N]); R["b_rstd"] = Buf()
        R["p_rms"] = psf("p_rms", [128, 512]); R["b_prms"] = Buf(excl=True)
        R["ones"] = sbf("ones", [128, 128], BF16); R["b_ones"] = Buf()
        R["eps"] = sbf("epsb", [128, 1]); R["b_eps"] = Buf()
        S.op("pool", lambda e: e.memset(R["ones"][:], 1.0), writes=[R["b_ones"]])
        S.op("pool", lambda e: e.memset(R["eps"][:], 1e-6), writes=[R["b_eps"]])
        hT = sbf("fhT", [128, 8, N], BF16); b_h = Buf()
        carry = sbf("carry", [128, 44, 2]); b_carry = Buf()
        S.op("pool", lambda e: e.memset(carry[:], 0.0), writes=[b_carry])
        NU = 3
        U = [sbf(f"U{i}", [128, N + 2]) for i in range(NU)]; b_U = [Buf() for _ in range(NU)]
        cg = [sbf(f"cg{i}", [128, N]) for i in range(2)]; b_cg = [Buf() for _ in range(2)]
        cv = [sbf(f"cv{i}", [128, N]) for i in range(2)]; b_cv = [Buf() for _ in range(2)]
        gi = [sbf(f"gi{i}", [128, N]) for i in range(2)]; b_gi = [Buf() for _ in range(2)]
        pu = [psf(f"fpu{i}", [128, 512]) for i in range(4)]; b_pu = [Buf(excl=True) for _ in range(4)]
        pd = [psf(f"fpd{i}", [128, 512]) for i in range(2)]; b_pd = [Buf(excl=True) for _ in range(2)]
        uc = 0
        pc_ = 0
        for tt in range(T // N):
            t0 = tt * N
            if tt > 0:
                S.dma("sp", xt[:], xv[:, :, t0:t0 + N], writes=[b_xt])
            rmsnorm_tile(g, xt, b_xt, hT, b_h, sm, b_sm, N, R)
            for i in range(22):
                k2 = i % 2
                for which, ch in ((0, i), (1, 22 + i)):
                    ui = uc % NU; uc += 1
                    pi = pc_ % 4; pc_ += 1
                    emit_wu((ch * 128) // CHW)
                    for c in range(8):
                        S.op("pe", lambda e: e.matmul(pu[pi][:, 0:N], lhsT=Wu[:, c, ch * 128:(ch + 1) * 128], rhs=hT[:, c, :], start=(c == 0), stop=(c == 7)),
                             reads=[b_Wu[(ch * 128) // CHW], b_h], writes=[b_pu[pi]])
                    S.op("act", lambda e: e.copy(out=U[ui][:, 2:N + 2], in_=pu[pi][:, 0:N]), reads=[b_pu[pi]], writes=[b_U[ui]])
                    S.op("act", lambda e: e.copy(out=U[ui][:, 0:2], in_=carry[:, ch, :]), reads=[b_carry], writes=[b_U[ui]])
                    dst, bd_ = (cg[k2], b_cg[k2]) if which == 0 else (cv[k2], b_cv[k2])
                    S.op("dve", lambda e: e.tensor_scalar(out=dst[:], in0=U[ui][:, 0:N], scalar1=cw[:, ch, 0:1], scalar2=cw[:, ch, 3:4], op0=ALU.mult, op1=ALU.add),
                         reads=[b_U[ui], b_cw], writes=[bd_])
                    S.op("dve", lambda e: e.scalar_tensor_tensor(out=dst[:], in0=U[ui][:, 1:N + 1], scalar=cw[:, ch, 1:2], in1=dst[:], op0=ALU.mult, op1=ALU.add),
                         reads=[b_U[ui], b_cw, bd_], writes=[bd_])
                    S.op("dve", lambda e: e.scalar_tensor_tensor(out=dst[:], in0=U[ui][:, 2:N + 2], scalar=cw[:, ch, 2:3], in1=dst[:], op0=ALU.mult, op1=ALU.add),
                         reads=[b_U[ui], b_cw, bd_], writes=[bd_])
                    S.op("act", lambda e: e.copy(out=carry[:, ch, :], in_=U[ui][:, N:N + 2]), reads=[b_U[ui]], writes=[b_carry])
                S.op("pool", lambda e: e.tensor_tensor(out=gi[k2][:], in0=cg[k2][:], in1=cg[k2][:], op=ALU.mult), reads=[b_cg[k2]], writes=[b_gi[k2]])
                S.op("pool", lambda e: e.tensor_scalar(out=gi[k2][:], in0=gi[k2][:], scalar1=0.044715, scalar2=1.0, op0=ALU.mult, op1=ALU.add), reads=[b_gi[k2]], writes=[b_gi[k2]])
                S.op("pool", lambda e: e.tensor_tensor(out=gi[k2][:], in0=gi[k2][:], in1=cg[k2][:], op=ALU.mult), reads=[b_gi[k2], b_cg[k2]], writes=[b_gi[k2]])
                S.op("act", lambda e: e.activation(out=gi[k2][:], in_=gi[k2][:], func=AF.Sigmoid, scale=1.5957691216057308), reads=[b_gi[k2]], writes=[b_gi[k2]])
                S.op("pool", lambda e: e.tensor_tensor(out=gi[k2][:], in0=gi[k2][:], in1=cg[k2][:], op=ALU.mult), reads=[b_gi[k2], b_cg[k2]], writes=[b_gi[k2]])
                S.op("dve", lambda e: e.tensor_tensor(out=gated[:, i, :], in0=gi[k2][:], in1=cv[k2][:], op=ALU.mult), reads=[b_gi[k2], b_cv[k2]], writes=[b_gt])
                emit_wd(i)
            for j in range(8):
                pi = j % 2
                for i in range(22):
                    S.op("pe", lambda e: e.matmul(pd[pi][:, 0:N], lhsT=Wd[:, i, j * 128:(j + 1) * 128], rhs=gated[:, i, :], start=(i == 0), stop=(i == 21)),
                         reads=[b_Wd[i], b_gt], writes=[b_pd[pi]])
                S.op("dve", lambda e: e.tensor_tensor(out=xt[:, j, :], in0=pd[pi][:, 0:N], in1=xt[:, j, :], op=ALU.add),
                     reads=[b_pd[pi], b_xt], writes=[b_xt])
            S.dma("sp", xov[:, :, t0:t0 + N], xt[:], reads=[b_xt])
        S.barrier()


def phase_ple(g, l, xin, xout, final):
    nc, S, D, T = g.nc, g.S, g.D, g.T
    N = 512
    with ExitStack() as st:
        sbf = lambda name, shape, dt=F32: st.enter_context(nc.sbuf_tensor(g.nm(name), list(shape), dt))
        psf = lambda name, shape, dt=F32: st.enter_context(nc.psum_tensor(g.nm(name), list(shape), dt))
        Wg = sbf("Wg", [128, 8, DM], BF16); b_Wg = Buf()
        Wp = sbf("Wp", [128, 2, DM], BF16); b_Wp = Buf()
        stage = [sbf(f"wst{i}", [128, 1024]) for i in range(2)]; b_stage = [Buf() for _ in range(2)]
        load_w_bf16(g, Wg, b_Wg, D["ple_w_gate"][l], 8, DM, stage, b_stage)
        load_w_bf16(g, Wp, b_Wp, D["ple_w_proj"][l], 2, DM, stage, b_stage)
        sm = sbf("smq", [128, 16]); b_sm = Buf()
        S.dma("sp", sm[:, 0:8], D["smalls"][l][:, 24:32], writes=[b_sm])
        S.dma("sp", sm[:, 8:16], D["smalls"][l][:, 32:40], writes=[b_sm])
        R = rms_shared(g, sbf, psf, N)
        xt = [sbf(f"pxt{i}", [128, 8, N]) for i in range(2)]; b_xt = [Buf() for _ in range(2)]
        pt = [sbf(f"ppt{i}", [128, 2, N]) for i in range(2)]; b_pt = [Buf() for _ in range(2)]
        ptb = sbf("pptb", [128, 2, N], BF16); b_ptb = Buf()
        hT = sbf("phT", [128, 8, N], BF16); b_h = Buf()
        xo = sbf("pxo", [128, 8, N]); b_xo = Buf()
        xf = sbf("pxf", [128, 8, N]); b_xf = Buf()
        gs = [sbf(f"pgs{i}", [128, N]) for i in range(2)]; b_gs = [Buf() for _ in range(2)]
        pg = [psf(f"ppg{i}", [128, 512]) for i in range(2)]; b_pg = [Buf(excl=True) for _ in range(2)]
        pq = [psf(f"ppq{i}", [128, 512]) for i in range(2)]; b_pq = [Buf(excl=True) for _ in range(2)]
        xv = xin.rearrange("(c p) t -> p c t", p=128)
        xov = xout.rearrange("(c p) t -> p c t", p=128)
        pv = D["pT"][l].rearrange("(c p) t -> p c t", p=128)
        for tt in range(T // N):
            t0 = tt * N
            xi = tt % 2
            S.dma("sp", xt[xi][:], xv[:, :, t0:t0 + N], writes=[b_xt[xi]])
            S.dma("sp", pt[xi][:], pv[:, :, t0:t0 + N], writes=[b_pt[xi]])
            S.op("pool", lambda e: e.tensor_copy(out=ptb[:], in_=pt[xi][:]), reads=[b_pt[xi]], writes=[b_ptb])
            rmsnorm_tile(g, xt[xi], b_xt[xi], hT, b_h, sm, b_sm, N, R)
            for j in range(8):
                pi = j % 2
                for c in range(8):
                    S.op("pe", lambda e: e.matmul(pg[pi][:], lhsT=Wg[:, c, j * 128:(j + 1) * 128], rhs=hT[:, c, :], start=(c == 0), stop=(c == 7)),
                         reads=[b_Wg, b_h], writes=[b_pg[pi]])
                for c in range(2):
                    S.op("pe", lambda e: e.matmul(pq[pi][:], lhsT=Wp[:, c, j * 128:(j + 1) * 128], rhs=ptb[:, c, :], start=(c == 0), stop=(c == 1)),
                         reads=[b_Wp, b_ptb], writes=[b_pq[pi]])
                S.op("act", lambda e: e.activation(out=gs[pi][:], in_=pg[pi][:], func=AF.Sigmoid), reads=[b_pg[pi]], writes=[b_gs[pi]])
                S.op("dve", lambda e: e.tensor_tensor(out=gs[pi][:], in0=pq[pi][:], in1=gs[pi][:], op=ALU.mult), reads=[b_pq[pi], b_gs[pi]], writes=[b_gs[pi]])
                S.op("pool", lambda e: e.tensor_tensor(out=xo[:, j, :], in0=gs[pi][:], in1=xt[xi][:, j, :], op=ALU.add), reads=[b_gs[pi], b_xt[xi]], writes=[b_xo])
            if not final:
                S.dma("sp", xov[:, :, t0:t0 + N], xo[:], reads=[b_xo])
            else:
                S.op("act", lambda e: e.activation(out=R["sq"][:], in_=xo[:], func=AF.Square), reads=[b_xo], writes=[R["b_sq"]])
                for c in range(8):
                    S.op("pe", lambda e: e.matmul(R["p_rms"][:], lhsT=R["ones"][:], rhs=R["sq"][:, c, :], start=(c == 0), stop=(c == 7)),
                         reads=[R["b_ones"], R["b_sq"]], writes=[R["b_prms"]])
                S.op("act", lambda e: e.activation(out=R["rstd"][:], in_=R["p_rms"][:], func=AF.Sqrt, bias=R["eps"][:, 0:1], scale=1.0 / DM),
                     reads=[R["b_prms"], R["b_eps"]], writes=[R["b_rstd"]])
                S.op("dve", lambda e: e.reciprocal(out=R["rstd"][:], in_=R["rstd"][:]), reads=[R["b_rstd"]], writes=[R["b_rstd"]])
                for c in range(8):
                    S.op("dve", lambda e: e.scalar_tensor_tensor(out=xf[:, c, :], in0=xo[:, c, :], scalar=sm[:, 8 + c:9 + c], in1=R["rstd"][:],
                                                                   op0=ALU.mult, op1=ALU.mult),
                         reads=[b_xo, b_sm, R["b_rstd"]], writes=[b_xf])
                S.dma("sp", xov[:, :, t0:t0 + N], xf[:], reads=[b_xf])
        S.barrier()


def nsa_consts(T):
    import ml_dtypes
    bf = ml_dtypes.bfloat16
    f = np.float32
    c = {}
    nl = np.arange(128)[:, None]
    ql = np.arange(128)[None, :]
    masks = np.zeros((19, 128, 128), f)
    for i in range(17):
        masks[i] = np.where(16 * nl + 31 - ql <= 128 * i, 0.0, -BIG)
    masks[17] = np.where(nl <= ql, 0.0, -BIG)
    masks[18] = np.where(nl > ql, 0.0, -BIG)
    m4 = np.tile(masks, (1, 1, 4))
    c["nsa_masks"] = np.ascontiguousarray(np.transpose(m4, (1, 0, 2))).astype(bf)
    c["identb"] = np.eye(128, dtype=f).astype(bf)
    c["identf"] = np.eye(128, dtype=f)
    key = np.arange(T)[None, :]
    c["E_all"] = (key // 64 == np.arange(128)[:, None]).astype(f).astype(bf)
    n_cmp = (T - 32) // 16 + 1
    ntc = (n_cmp + 127) // 128
    cs = 16 * np.arange(n_cmp)
    ce = cs + 31
    ss = 64 * np.arange(128)
    ov = np.minimum(ce[:, None], ss[None] + 63) - np.maximum(cs[:, None], ss[None]) + 1
    mcs = np.zeros((ntc * 128, 128), f)
    mcs[:n_cmp] = np.clip(ov, 0, 32).astype(f) / 32
    c["mcs"] = np.ascontiguousarray(mcs.reshape(ntc, 128, 128).transpose(1, 0, 2)).astype(bf)
    keep = np.zeros((128, 256), f)
    add = np.zeros((128, 256), f)
    for q in range(128):
        jc = 126 if q < 64 else 127
        cc = np.arange(256)
        keep[q] = (cc < jc - 1)
        add[q] = np.where(cc == jc - 1, 1.1e9, np.where(cc == jc, 1.2e9, np.where(cc > jc, -1e30, 0.0)))
    c["keepw"] = keep
    c["addw"] = add
    return c


def phase_nsa(g, l):
    nc, S, D, T = g.nc, g.S, g.D, g.T
    NQB = T // 128
    NCMP = (T - 32) // 16 + 1
    NTC = (NCMP + 127) // 128
    SK = dict(skip_group_check=True)
    with ExitStack() as st:
        sbf = lambda name, shape, dt=F32: st.enter_context(nc.sbuf_tensor(g.nm(name), list(shape), dt))
        psf = lambda name, shape, dt=F32: st.enter_context(nc.psum_tensor(g.nm(name), list(shape), dt))
        stp = [psf(f"nst{i}", [128, 512]) for i in range(3)]; b_stp = [Buf(excl=True) for _ in range(3)]
        acc = [psf(f"nacc{i}", [128, 512]) for i in range(3)]; b_acc = [Buf(excl=True) for _ in range(3)]
        imp = psf("nimp", [128, 512]); b_imp = Buf(excl=True)
        msc = psf("nmsc", [128, 512]); b_msc = Buf(excl=True)
        mscb = msc[:, 384:448].bitcast(BF16); b_mscb = b_msc
        KcT = sbf("KcT", [64, 2, NTC * 128], BF16); b_Kc = Buf()
        Vc = sbf("Vc", [128, NTC, 2, 128], BF16); b_Vc = Buf()
        S.op("pool", lambda e: e.memset(KcT[:], 0.0), writes=[b_Kc])
        S.op("pool", lambda e: e.memset(Vc[:], 0.0), writes=[b_Vc])
        with ExitStack() as st2:
            sb2 = lambda name, shape, dt=F32: st2.enter_context(nc.sbuf_tensor(g.nm(name), list(shape), dt))
            kc = sb2("kc", [64, 2, T], BF16); b_kc = Buf()
            vc = sb2("vc", [64, 2, T], BF16); b_vc = Buf()
            S.dma("sp", kc[:], D["kT"][0:128, :].rearrange("(h d) t -> d h t", d=64), writes=[b_kc])
            S.dma("sp", vc[:], D["vcT"].rearrange("(h d) t -> d h t", d=64), writes=[b_vc])
            wst = sb2("wckst", [64, 32, 64]); b_wst = Buf()
            wck = sb2("wck", [64, 32, 64], BF16); b_wck = Buf()
            wcv = sb2("wcv", [64, 32, 64], BF16); b_wcv = Buf()
            S.dma("sp", wst[:], D["nsa_w_ck"][l].rearrange("l d e -> d l e"), writes=[b_wst])
            cp(S, "dve", wck[:], wst[:], [b_wst], [b_wck])
            S.dma("sp", wst[:], D["nsa_w_cv"][l].rearrange("l d e -> d l e"), writes=[b_wst])
            cp(S, "dve", wcv[:], wst[:], [b_wst], [b_wcv])
            pest = sb2("pest", [64, 2, 32]); b_pest = Buf()
            peb = sb2("peb", [64, 2, 32], BF16); b_peb = Buf()
            S.dma("sp", pest[:], D["nsa_peT"][l].rearrange("w d l -> d w l"), writes=[b_pest])
            cp(S, "dve", peb[:], pest[:], [b_pest], [b_peb])
            biask = sb2("biask", [64, 1]); b_bk = Buf()
            biasv = sb2("biasv", [1, 64], BF16); b_bv = Buf()
            onesr = sb2("onesr", [1, 128], BF16); b_or = Buf()
            S.op("pool", lambda e: e.memset(onesr[:], 1.0), writes=[b_or])
            for i_ in range(32):
                S.op("pe", lambda e: e.matmul(msc[0:64, 0:1], lhsT=wck[:, i_, :], rhs=peb[:, 0, i_:i_ + 1], start=(i_ == 0), stop=(i_ == 31)),
                     reads=[b_wck, b_peb], writes=[b_msc])
            cp(S, "dve", biask[:], msc[0:64, 0:1], [b_msc], [b_bk])
            for i_ in range(32):
                S.op("pe", lambda e: e.matmul(msc[0:1, 0:64], lhsT=peb[:, 1, i_:i_ + 1], rhs=wcv[:, i_, :], start=(i_ == 0), stop=(i_ == 31)),
                     reads=[b_wcv, b_peb], writes=[b_msc])
            cp(S, "dve", biasv[:], msc[0:1, 0:64], [b_msc], [b_bv])
            span = 16 * (NCMP - 1) + 1
            for h in range(2):
                pk = stp[h]
                for i_ in range(32):
                    S.op("pe", lambda e: e.matmul(pk[0:64, 0:NCMP], lhsT=wck[:, i_, :], rhs=kc[:, h, i_:i_ + span:16], start=(i_ == 0), stop=(i_ == 31)),
                         reads=[b_wck, b_kc], writes=[b_stp[h]])
                S.op("act", lambda e: e.activation(out=KcT[:, h, 0:NCMP], in_=pk[0:64, 0:NCMP], func=AF.Identity, bias=biask[:, 0:1], scale=1.0),
                     reads=[b_stp[h], b_bk], writes=[b_Kc])
            k_ = 0
            for h in range(2):
                for nt in range(NTC):
                    nn = min(128, NCMP - nt * 128)
                    pv_ = acc[k_ % 3]; bpv = b_acc[k_ % 3]; k_ += 1
                    base = 16 * 128 * nt
                    sp_ = 16 * (nn - 1) + 1
                    for i_ in range(32):
                        S.op("pe", lambda e: e.matmul(pv_[0:nn, 0:64], lhsT=vc[:, h, base + i_:base + i_ + sp_:16], rhs=wcv[:, i_, :], start=(i_ == 0), stop=False),
                             reads=[b_wcv, b_vc], writes=[bpv])
                    S.op("pe", lambda e: e.matmul(pv_[0:nn, 0:64], lhsT=onesr[0:1, 0:nn], rhs=biasv[0:1, :], start=False, stop=True),
                         reads=[b_or, b_bv], writes=[bpv])
                    cp(S, "dve", Vc[0:nn, nt, h, 0:64], pv_[0:nn, 0:64], [bpv], [b_Vc])
                    S.op("dve", lambda e: e.memset(Vc[0:nn, nt, h, 64:65], 1.0), writes=[b_Vc])
            S.barrier()
        masks = sbf("masks", [128, 19, 512], BF16); b_masks = Buf()
        S.dma("sp", masks[:], D["nsa_masks"], writes=[b_masks])
        identb = sbf("identb", [128, 128], BF16); b_idb = Buf()
        S.dma("sp", identb[:], D["identb"], writes=[b_idb])
        identf = sbf("identf", [128, 128]); b_idf = Buf()
        S.dma("sp", identf[:], D["identf"], writes=[b_idf])
        MCS = sbf("MCS", [128, NTC, 128], BF16); b_mcs = Buf()
        S.dma("sp", MCS[:], D["mcs"], writes=[b_mcs])
        keepw = sbf("keepw", [128, 256]); b_kw_ = Buf()
        S.dma("sp", keepw[:], D["keepw"], writes=[b_kw_])
        addw = sbf("addw", [128, 256]); b_aw = Buf()
        S.dma("sp", addw[:], D["addw"], writes=[b_aw])
        LH = sbf("LH", [128, 2, T], BF16); b_LH = Buf()
        KwT = sbf("KwT", [64, 2, T], BF16); b_Kw = Buf()
        TH = min(T, 4096)
        ksv = D["kT"][128:256, :].rearrange("(h d) t -> d h t", d=64)
        S.dma("sp", LH[64:128, :, 0:TH], ksv[:, :, 0:TH], writes=[b_LH])
        for h_ in range(2):
            S.dma("sp", LH[0:64, h_, 0:TH], D["E_all"][0:64, 0:TH], writes=[b_LH])
        if T > TH:
            S.dma("sp", LH[0:64, :, TH:T], ksv[:, :, TH:T], writes=[b_LH])
            for h_ in range(2):
                S.dma("sp", LH[64:128, h_, TH:T], D["E_all"][64:128, TH:T], writes=[b_LH])
        S.dma("sp", KwT[:], D["kT"][256:384, :].rearrange("(h d) t -> d h t", d=64), writes=[b_Kw])
        VW = 128
        Vs = sbf("Vs", [128, NQB, 2, VW], BF16); b_Vs = Buf()
        Vw = sbf("Vw", [128, NQB, 2, VW], BF16); b_Vw = Buf()
        S.op("pool", lambda e: e.memset(Vs[:], 0.0), writes=[b_Vs])
        S.op("pool", lambda e: e.memset(Vw[:], 0.0), writes=[b_Vw])
        S.op("pool", lambda e: e.memset(Vs[:, :, :, 64:65], 1.0), writes=[b_Vs])
        S.op("pool", lambda e: e.memset(Vw[:, :, :, 64:65], 1.0), writes=[b_Vw])
        vtv = D["vtok"].rearrange("(kt p) (w h d) -> p kt w h d", p=128, w=2, h=2)
        for h in range(2):
            S.dma("sp", Vs[:, :, h, 0:64], vtv[:, :, 0, h, :], writes=[b_Vs])
            S.dma("sp", Vw[:, :, h, 0:64], vtv[:, :, 1, h, :], writes=[b_Vw])
        Qg = [sbf(f"Qg{i}", [64, 4, 128], BF16) for i in range(2)]; b_Qg = [Buf() for _ in range(2)]
        gt = [sbf(f"ngt{i}", [128, 24]) for i in range(2)]; b_gt = [Buf() for _ in range(2)]
        Pc = [sbf(f"Pc{i}", [128, 512], BF16) for i in range(max(NTC, 1))]; b_Pc = [Buf() for _ in range(max(NTC, 1))]
        NP = 3
        Pb = [sbf(f"Pb{i}", [128, 512], BF16) for i in range(NP)]; b_Pb = [Buf() for _ in range(NP)]
        zz = sbf("zz", [128, 3, 4]); b_zz = Buf()
        coef = sbf("coef", [128, 3, 4]); b_coef = Buf()
        impS = sbf("impS", [128, 128]); b_impS = Buf()
        imp2 = sbf("imp2", [128, 128]); b_imp2 = Buf()
        mx = sbf("mx", [128, 16]); b_mx = Buf()
        thr = sbf("thr", [128, 1]); b_thr = Buf()
        MBf = sbf("MBf", [128, 128]); b_MBf = Buf()
        MBb = sbf("MBb", [128, 128], BF16); b_MBb = Buf()
        MBT4 = sbf("MBT4", [128, 4, 128], BF16); b_MBT4 = Buf()
        yc = [sbf(f"yc{i}", [128, 512]) for i in range(2)]; b_yc = [Buf() for _ in range(2)]
        ycT = [sbf(f"ycT{i}", [128, 4, 128], BF16) for i in range(2)]; b_ycT = [Buf() for _ in range(2)]
        stc = 0
        pbc = 0
        qc = 0
        qv = D["qT"].rearrange("(hq d) t -> d hq t", d=64)
        ymv = D["ymixT"][512:1024, :].rearrange("(c p) t -> p c t", p=128)
        aS = sbf("aS", [65, 512]); b_aS = Buf()
        R0 = [sbf(f"R0_{i}", [128, 512], BF16) for i in range(2)]; b_R0 = [Buf() for _ in range(2)]
        R1 = [sbf(f"R1_{i}", [128, 512], BF16) for i in range(2)]; b_R1 = [Buf() for _ in range(2)]

        def score_tile(KT, bK, h, kt, Q, bQ, extra):
            nonlocal stc
            si = stc % 3; stc += 1
            n_mm = 1 + len(extra)
            rhs_ap = Q[:].rearrange("d g q -> d (g q)") if len(Q.shape) == 3 else Q[:]
            S.op("pe", lambda e: e.matmul(stp[si][:], lhsT=KT[:, h, kt * 128:(kt + 1) * 128], rhs=rhs_ap, start=True, stop=(n_mm == 1)),
                 reads=[bK, bQ], writes=[b_stp[si]])
            for i_, (la, ra, bufs) in enumerate(extra):
                S.op("pe", lambda e: e.matmul(stp[si][:], lhsT=la, rhs=ra, start=False, stop=(i_ == len(extra) - 1)),
                     reads=bufs, writes=[b_stp[si]])
            return si

        def run_branch(br, tiles, h, Q, bQ, V, bV, Pbufs=None, after_exp=None):
            nonlocal pbc
            n = len(tiles)
            tiles = [tl if len(tl) == 6 else tl + (Q, bQ) for tl in tiles]
            DEPTH = 2
            issued = []
            nxt = 0
            for i_ in range(n):
                while nxt < n and nxt <= i_ + DEPTH - 1 + (0 if i_ else 0):
                    KT2, bK2, kt2, extra2, Qx2, bQx2 = tiles[nxt]
                    issued.append(score_tile(KT2, bK2, h, kt2, Qx2, bQx2, extra2))
                    nxt += 1
                si = issued[i_]
                kt = tiles[i_][2]
                if Pbufs is None:
                    pi = pbc % NP; pbc += 1
                    P, bP = Pb[pi], b_Pb[pi]
                else:
                    P, bP = Pbufs[i_]
                S.op("act", lambda e: e.activation(out=P[:], in_=stp[si][:], func=AF.Exp, scale=0.125), reads=[b_stp[si]], writes=[bP])
                if nxt < n:
                    KT2, bK2, kt2, extra2, Qx2, bQx2 = tiles[nxt]
                    issued.append(score_tile(KT2, bK2, h, kt2, Qx2, bQx2, extra2))
                    nxt += 1
                S.op("pe", lambda e: e.matmul(acc[br][:, :], lhsT=V[:, kt, h, :], rhs=P[:], start=(i_ == 0), stop=(i_ == n - 1)),
                     reads=[bP, bV], writes=[b_acc[br]])
                if after_exp is not None:
                    after_exp(i_, P, bP)

        def combine(br, h, yi):
            cp(S, "act", aS[:], acc[br][0:65, :], [b_acc[br]], [b_aS])
            for gq in range(4):
                S.op("pe", lambda e: e.transpose(out=msc[:, gq * 65:(gq + 1) * 65], in_=aS[0:65, gq * 128:(gq + 1) * 128], identity=identf[0:65, 0:65]),
                     reads=[b_aS, b_idf], writes=[b_msc])
            a3 = msc[:, 0:260].rearrange("p (g c) -> p g c", c=65)
            g3 = gt[yi][:].rearrange("p (hg b) -> p hg b", b=3)
            S.op("dve", lambda e: e.tensor_scalar(out=zz[:, br, :], in0=a3[:, :, 64], scalar1=1e-30, scalar2=None, op0=ALU.max), reads=[b_msc], writes=[b_zz])
            S.op("dve", lambda e: e.reciprocal(out=zz[:, br, :], in_=zz[:, br, :]), reads=[b_zz], writes=[b_zz])
            S.op("dve", lambda e: e.tensor_tensor(out=coef[:, br, :], in0=zz[:, br, :], in1=g3[:, h * 4:(h + 1) * 4, br], op=ALU.mult), reads=[b_zz, b_gt[yi]], writes=[b_coef])
            for gq in range(4):
                o_ = yc[yi][:, (h * 4 + gq) * 64:(h * 4 + gq + 1) * 64]
                if br == 0:
                    S.op("dve", lambda e: e.tensor_scalar(out=o_, in0=a3[:, gq, 0:64], scalar1=coef[:, br, gq:gq + 1], scalar2=None, op0=ALU.mult),
                         reads=[b_msc, b_coef], writes=[b_yc[yi]])
                else:
                    S.op("dve", lambda e: e.scalar_tensor_tensor(out=o_, in0=a3[:, gq, 0:64], scalar=coef[:, br, gq:gq + 1], in1=o_, op0=ALU.mult, op1=ALU.add),
                         reads=[b_msc, b_coef, b_yc[yi]], writes=[b_yc[yi]])

        items = [(qb, h) for qb in range(NQB) for h in range(2)]
        qis = {}

        def stage1(qb, h):
            nonlocal qc
            q0 = qb * 128
            yi = qb % 2
            if h == 0:
                S.dma("sp", gt[yi][:], D["gates"][q0:q0 + 128, :], writes=[b_gt[yi]])
            qi = qc % 2; qc += 1
            qis[(qb, h)] = qi
            Q, bQ = Qg[qi], b_Qg[qi]
            S.dma("sp", Q[:], qv[:, h * 4:(h + 1) * 4, q0:q0 + 128], writes=[bQ])
            S.dma("sp", R0[qi][64:128, :].rearrange("d (g q) -> d g q", g=4), qv[:, h * 4:(h + 1) * 4, q0:q0 + 128], writes=[b_R0[qi]])
            if qb >= 32:
                S.dma("sp", R1[qi][0:64, :].rearrange("d (g q) -> d g q", g=4), qv[:, h * 4:(h + 1) * 4, q0:q0 + 128], writes=[b_R1[qi]])
            ntc = min(NTC, (8 * qb + 6) // 128 + 1)
            S.op("dve", lambda e: e.memset(imp[:], 0.0), writes=[b_imp])
            tiles = []
            for nt in range(ntc):
                delta = 128 * qb - 2048 * nt
                extra = []
                if delta < 2064:
                    extra.append((identb[:], masks[:, delta // 128, :], [b_idb, b_masks]))
                tiles.append((KcT, b_Kc, nt, extra))

            def imp_mm(i_, P, bP):
                for gq in range(4):
                    S.op("pe", lambda e: e.matmul(imp[:, gq * 128:(gq + 1) * 128], lhsT=P[:, gq * 128:(gq + 1) * 128], rhs=MCS[:, i_, :], start=False, stop=(i_ == ntc - 1), **SK),
                         reads=[bP, b_mcs], writes=[b_imp])
            run_branch(0, tiles, h, Q, bQ, Vc, b_Vc, Pbufs=[(Pc[i_], b_Pc[i_]) for i_ in range(ntc)], after_exp=imp_mm)
            combine(0, h, yi)
            S.op("dve", lambda e: e.tensor_scalar(out=impS[:], in0=imp[:, 0:128], scalar1=zz[:, 0, 0:1], scalar2=None, op0=ALU.mult), reads=[b_imp, b_zz], writes=[b_impS])
            for gq in range(1, 4):
                S.op("dve", lambda e: e.scalar_tensor_tensor(out=impS[:], in0=imp[:, gq * 128:(gq + 1) * 128], scalar=zz[:, 0, gq:gq + 1], in1=impS[:], op0=ALU.mult, op1=ALU.add),
                     reads=[b_imp, b_zz, b_impS], writes=[b_impS])
            c0 = 126 - 2 * qb
            S.op("dve", lambda e: e.tensor_tensor(out=impS[:], in0=impS[:], in1=keepw[:, c0:c0 + 128], op=ALU.mult), reads=[b_impS, b_kw_], writes=[b_impS])
            S.op("dve", lambda e: e.tensor_tensor(out=impS[:], in0=impS[:], in1=addw[:, c0:c0 + 128], op=ALU.add), reads=[b_impS, b_aw], writes=[b_impS])
            S.op("dve", lambda e: e.memset(impS[:, 0:1], 1.0e9), writes=[b_impS])
            S.op("dve", lambda e: e.max(out=mx[:, 0:8], in_=impS[:]), reads=[b_impS], writes=[b_mx])
            S.op("dve", lambda e: e.match_replace(out=imp2[:], in_to_replace=mx[:, 0:8], in_values=impS[:], imm_value=-3.0e38), reads=[b_mx, b_impS], writes=[b_imp2])
            S.op("dve", lambda e: e.max(out=mx[:, 8:16], in_=imp2[:]), reads=[b_imp2], writes=[b_mx])
            S.op("dve", lambda e: e.tensor_reduce(out=thr[:], in_=mx[:, 8:16], axis=AX.X, op=ALU.min), reads=[b_mx], writes=[b_thr])
            S.op("dve", lambda e: e.tensor_scalar(out=MBf[:], in0=impS[:], scalar1=thr[:, 0:1], scalar2=None, op0=ALU.is_ge), reads=[b_impS, b_thr], writes=[b_MBf])
            S.op("dve", lambda e: e.tensor_scalar(out=MBb[:], in0=MBf[:], scalar1=1.0, scalar2=BIG, op0=ALU.subtract, op1=ALU.mult), reads=[b_MBf], writes=[b_MBb])
            S.op("pe", lambda e: e.transpose(out=mscb[:], in_=MBb[:], identity=identb[:]), reads=[b_MBb, b_idb], writes=[b_mscb])
            for gq in range(4):
                cp(S, "act" if gq % 2 else "dve", R0[qi][0:64, gq * 128:(gq + 1) * 128], mscb[0:64, :], [b_mscb], [b_R0[qi]])
                if qb >= 32:
                    cp(S, "dve" if gq % 2 else "act", R1[qi][64:128, gq * 128:(gq + 1) * 128], mscb[64:128, :], [b_mscb], [b_R1[qi]])

        def stage2(qb, h):
            q0 = qb * 128
            yi = qb % 2
            qi = qis[(qb, h)]
            Q, bQ = Qg[qi], b_Qg[qi]
            tiles = []
            for kt in range(max(0, qb - 4), qb + 1):
                extra = []
                if kt == qb:
                    extra.append((identb[:], masks[:, 17, :], [b_idb, b_masks]))
                elif kt == qb - 4:
                    extra.append((identb[:], masks[:, 18, :], [b_idb, b_masks]))
                tiles.append((KwT, b_Kw, kt, extra))
            run_branch(2, tiles, h, Q, bQ, Vw, b_Vw)
            combine(2, h, yi)
            tiles = []
            for kt in range(qb + 1):
                extra = []
                if kt == qb:
                    extra.append((identb[:], masks[:, 17, :], [b_idb, b_masks]))
                if kt < 32:
                    tiles.append((LH, b_LH, kt, extra, R0[qi], b_R0[qi]))
                else:
                    tiles.append((LH, b_LH, kt, extra, R1[qi], b_R1[qi]))
            run_branch(1, tiles, h, Q, bQ, Vs, b_Vs)
            combine(1, h, yi)
            if h == 1:
                for c in range(4):
                    S.op("pe", lambda e: e.transpose(out=msc[:, c * 128:(c + 1) * 128], in_=yc[yi][:, c * 128:(c + 1) * 128], identity=identf[:]),
                         reads=[b_yc[yi], b_idf], writes=[b_msc])
                cp(S, "act", ycT[yi][:].rearrange("p c q -> p (c q)"), msc[:], [b_msc], [b_ycT[yi]])
                S.dma("sp", ymv[:, :, q0:q0 + 128], ycT[yi][:], reads=[b_ycT[yi]])

        stage1(*items[0])
        for n_ in range(len(items)):
            if n_ + 1 < len(items):
                stage1(*items[n_ + 1])
            stage2(*items[n_])
        S.barrier()


def rwkv_consts():
    f = np.float32
    c = {}
    hs = np.arange(128) // 64
    tt = np.arange(128) % 64
    same = hs[:, None] == hs[None, :]
    c["rw_msu"] = (same & (tt[:, None] < tt[None, :])).astype(f)
    c["rw_mu"] = (same & (tt[:, None] <= tt[None, :])).astype(f)
    c["rw_msl"] = (same & (tt[:, None] > tt[None, :])).astype(f)
    il = np.zeros((64, 128), f); il[np.arange(64), np.arange(64)] = 1
    ir = np.zeros((64, 128), f); ir[np.arange(64), 64 + np.arange(64)] = 1
    c["rw_il"] = il
    c["rw_ir"] = ir
    return c


def phase_rwkv(g, l):
    nc, S, D, T = g.nc, g.S, g.D, g.T
    TB = 256
    NCH = TB // 64
    SK = dict(skip_group_check=True)
    with ExitStack() as st:
        sbf = lambda name, shape, dt=F32: st.enter_context(nc.sbuf_tensor(g.nm(name), list(shape), dt))
        psf = lambda name, shape, dt=F32: st.enter_context(nc.psum_tensor(g.nm(name), list(shape), dt))
        NB = 8
        bank = [psf(f"rb{i}", [128, 512]) for i in range(NB)]; b_bank = [Buf(excl=True) for _ in range(NB)]
        bctr = [0]

        busy = [False] * NB
        F32R = mybir.dt.float32r
        MT = F32R if g.use_f32r else F32
        RR = lambda ap: ap
        AS32 = (lambda ap: ap.bitcast(F32)) if g.use_f32r else (lambda ap: ap)

        def nb():
            for k_ in range(NB):
                i = (bctr[0] + k_) % NB
                if not busy[i]:
                    bctr[0] = i + 1
                    busy[i] = True
                    return bank[i], b_bank[i]
            raise AssertionError("rwkv: no free PSUM bank (too many live tiles across a yield)")

        def rel(bbuf):
            busy[b_bank.index(bbuf)] = False

        def nbx():
            i = bctr[0] % NB
            bctr[0] += 1
            return bank[i], b_bank[i]
        def const(name, shape, src):
            t_ = sbf(name, shape); b_ = Buf()
            S.dma("sp", t_[:], src, writes=[b_])
            return t_, b_
        msu, b_msu = const("msu", [128, 128], D["rw_msu"])
        mu_, b_mu = const("mu", [128, 128], D["rw_mu"])
        msl, b_msl = const("msl", [128, 128], D["rw_msl"])
        il32, b_il32 = const("il32", [64, 128], D["rw_il"])
        ir32, b_ir32 = const("ir32", [64, 128], D["rw_ir"])
        idf, b_idf = const("idf", [128, 128], D["identf"])
        il = sbf("il", [64, 128], MT); b_il = Buf()
        ir = sbf("ir", [64, 128], MT); b_ir = Buf()
        cp(S, "dve", il[:], il32[:], [b_il32], [b_il])
        cp(S, "dve", ir[:], ir32[:], [b_ir32], [b_ir])
        rp, b_rp = const("rp", [128, 64], D["rwp"][l])
        wup32, b_wup32 = const("wup32", [64, 256], D["rw_w_up"][l])
        aup32, b_aup32 = const("aup32", [64, 256], D["rw_a_up"][l])
        gup32, b_gup32 = const("gup32", [128, 256], D["rw_g_up"][l])
        wup = sbf("wup", [64, 256], MT); b_wup = Buf()
        aup = sbf("aup", [64, 256], MT); b_aup = Buf()
        gup = sbf("gup", [128, 256], MT); b_gup = Buf()
        cp(S, "dve", wup[:], wup32[:], [b_wup32], [b_wup])
        cp(S, "dve", aup[:], aup32[:], [b_aup32], [b_aup])
        cp(S, "dve", gup[:], gup32[:], [b_gup32], [b_gup])
        ones32 = sbf("ones32", [64, 64]); b_o32 = Buf()
        S.op("pool", lambda e: e.memset(ones32[:], 1.0), writes=[b_o32])
        ones64 = sbf("ones64", [64, 64], MT); b_o64 = Buf()
        cp(S, "dve", ones64[:], ones32[:], [b_o32], [b_o64])
        omka = sbf("omka", [64, 4]); b_omka = Buf()
        S.op("dve", lambda e: e.tensor_scalar(out=omka[:], in0=rp[0:64, 28:32], scalar1=-1.0, scalar2=1.0, op0=ALU.mult, op1=ALU.add), reads=[b_rp], writes=[b_omka])
        cst = sbf("rcst", [128, 2]); b_cst = Buf()
        S.op("pool", lambda e: e.memset(cst[:, 0:1], 64e-5), writes=[b_cst])
        ST = [[sbf(f"ST{p}_{i}", [128, 64], MT) for i in range(2)] for p in range(2)]
        b_ST = [[Buf() for i in range(2)] for p in range(2)]
        for p in range(2):
            S.op("pool", lambda e: e.memset(AS32(ST[p][0][:]), 0.0), writes=[b_ST[p][0]])
        sidx = [0, 0]
        def arr(name, shape=None, dt=F32):
            return sbf(name, shape or [64, 4, TB], dt), Buf()
        Z3, b_Z3 = arr("Z3", [64, 12, TB + 1])
        ZL, b_ZL = arr("ZL", [64, 2, TB + 1])
        ZG, b_ZG = arr("ZG", [128, TB + 1])
        X3, b_X3 = arr("X3", [64, 12, TB])
        XL, b_XL = arr("XL", [64, 2, TB], MT)
        XG, b_XG = arr("XG", [128, TB], MT)
        Dt, b_Dt = arr("Dt", [128, 12, TB])
        lw, b_lw = arr("lw")
        cl2, b_cl2 = arr("cl2")
        aa, b_aa = arr("aa")
        kkn, b_kkn = arr("kkn")
        tmp, b_tmp = arr("tmp", None, MT)
        kfin, b_kfin = arr("kfin")
        epos, b_epos = arr("epos")
        eneg, b_eneg = arr("eneg")
        eprev, b_eprev = arr("eprev")
        eC, b_eC = arr("eC")
        AR, b_AR = arr("AR", [64, NCH, 2, 2, 2, 64], MT)
        Bt, b_Bt = arr("Bt", [64, NCH, 4, 64], MT)
        Kt, b_Kt = arr("Kt", [64, NCH, 4, 64], MT)
        Bh, b_Bh = arr("Bh", [64, NCH, 4, 64])
        Kh, b_Kh = arr("Kh", [64, NCH, 4, 64])
        Vc_, b_Vc_ = arr("Vcm", [64, NCH, 4, 64])
        hm = lambda a: a[:].rearrange("k h (c t) -> k h c t", t=64)
        cm = lambda a: a[:].rearrange("k c h t -> k h c t")
        arv = lambda ty: AR[:, :, :, ty, :, :].rearrange("k c p hh t -> k p hh c t")
        hm5 = lambda a: a[:].rearrange("k (p hh) (c t) -> k p hh c t", hh=2, t=64)
        bv, b_bv = arr("bv")
        gT, b_gT = arr("gT")
        YN, b_YN = arr("YN")
        PCf, b_PCf = arr("PCf", [64, 4, NCH], MT)
        PCc = sbf("PCc", [128, 2, NCH]); b_PCc = Buf()
        yo = sbf("yo", [64, 4, TB], BF16); b_yo = Buf()
        NTMP = 100
        tm = [sbf(f"tm{i}", [128, 128], MT) for i in range(NTMP)]; b_tm = [Buf() for _ in range(NTMP)]
        NTF = 16
        tf = [sbf(f"tf{i}", [128, 128]) for i in range(NTF)]; b_tf = [Buf() for _ in range(NTF)]
        fctr = [0]

        def ntf_():
            i = fctr[0] % NTF
            fctr[0] += 1
            return tf[i], b_tf[i]
        tctr = [0]

        def nt_():
            i = tctr[0] % NTMP
            tctr[0] += 1
            return tm[i], b_tm[i]
        NBD = 5
        bd = [[sbf(f"bd{k_}_{i}", [128, 128], MT) for i in range(NBD)] for k_ in range(3)]
        b_bd = [[Buf() for i in range(NBD)] for k_ in range(3)]
        for k_ in range(3):
            for i in range(NBD):
                S.op("pool", lambda e: e.memset(AS32(bd[k_][i][:]), 0.0), writes=[b_bd[k_][i]])
        bdc = [0]
        zav = D["zaT"]
        ev_ctr = [0]

        def evac(out, in_, reads, writes):
            ek = "act" if ev_ctr[0] % 2 == 0 else "dve"
            ev_ctr[0] += 1
            cp(S, ek, out, in_, reads, writes)

        for tt in range(T // TB):
            t0 = tt * TB
            if tt == 0:
                S.op("pool", lambda e: e.memset(Z3[:, :, 0:1], 0.0), writes=[b_Z3])
                S.op("pool", lambda e: e.memset(ZL[:, :, 0:1], 0.0), writes=[b_ZL])
                S.op("pool", lambda e: e.memset(ZG[:, 0:1], 0.0), writes=[b_ZG])
                S.dma("sp", Z3[:, :, 1:TB + 1], zav[0:768, 0:TB].rearrange("(gh k) t -> k gh t", k=64), writes=[b_Z3])
                S.dma("sp", ZL[:, :, 1:TB + 1], zav[768:896, 0:TB].rearrange("(g j) t -> j g t", j=64), writes=[b_ZL])
                S.dma("sp", ZG[:, 1:TB + 1], zav[896:1024, 0:TB], writes=[b_ZG])
            else:
                S.dma("sp", Z3[:], zav[0:768, t0 - 1:t0 + TB].rearrange("(gh k) t -> k gh t", k=64), writes=[b_Z3])
                S.dma("sp", ZL[:], zav[768:896, t0 - 1:t0 + TB].rearrange("(g j) t -> j g t", j=64), writes=[b_ZL])
                S.dma("sp", ZG[:], zav[896:1024, t0 - 1:t0 + TB], writes=[b_ZG])
            S.op("dve", lambda e: e.tensor_tensor(out=Dt[0:64, :, :], in0=Z3[:, :, 0:TB], in1=Z3[:, :, 1:TB + 1], op=ALU.subtract), reads=[b_Z3], writes=[b_Dt])
            for j in range(12):
                if j % 3 == 2:
                    S.op("act", lambda e: e.activation(out=Dt[0:64, j, :], in_=Dt[0:64, j, :], func=AF.Copy, scale=rp[0:64, j:j + 1]), reads=[b_Dt, b_rp], writes=[b_Dt])
                else:
                    S.op("dve", lambda e: e.tensor_scalar(out=Dt[0:64, j, :], in0=Dt[0:64, j, :], scalar1=rp[0:64, j:j + 1], scalar2=None, op0=ALU.mult), reads=[b_Dt, b_rp], writes=[b_Dt])
            S.op("dve", lambda e: e.tensor_tensor(out=X3[:], in0=Dt[0:64, :, :], in1=Z3[:, :, 1:TB + 1], op=ALU.add), reads=[b_Dt, b_Z3], writes=[b_X3])
            S.op("dve", lambda e: e.tensor_tensor(out=Dt[0:64, 0:2, :], in0=ZL[:, :, 0:TB], in1=ZL[:, :, 1:TB + 1], op=ALU.subtract), reads=[b_ZL], writes=[b_Dt])
            for j in range(2):
                S.op("dve", lambda e: e.scalar_tensor_tensor(out=XL[:, j, :], in0=Dt[0:64, j, :], scalar=rp[0:64, 12 + j:13 + j], in1=ZL[:, j, 1:TB + 1], op0=ALU.mult, op1=ALU.add),
                     reads=[b_Dt, b_rp, b_ZL], writes=[b_XL])
            S.op("dve", lambda e: e.tensor_tensor(out=Dt[:, 2, :], in0=ZG[:, 0:TB], in1=ZG[:, 1:TB + 1], op=ALU.subtract), reads=[b_ZG], writes=[b_Dt])
            S.op("dve", lambda e: e.scalar_tensor_tensor(out=XG[:], in0=Dt[:, 2, :], scalar=rp[:, 14:15], in1=ZG[:, 1:TB + 1], op0=ALU.mult, op1=ALU.add),
                 reads=[b_Dt, b_rp, b_ZG], writes=[b_XG])
            r_ = lambda h: X3[:, h, :]
            k_ = lambda h: X3[:, 4 + h, :]
            v_ = lambda h: X3[:, 8 + h, :]
            S.op("act", lambda e: e.activation(out=XL[:, 0, :], in_=XL[:, 0, :], func=AF.Tanh), reads=[b_XL], writes=[b_XL])
            S.op("act", lambda e: e.activation(out=XG[:], in_=XG[:], func=AF.Sigmoid), reads=[b_XG], writes=[b_XG])
            for h in range(4):
                pb, bpb = nbx()
                S.op("pe", lambda e: e.matmul(pb[0:64, 0:TB], lhsT=wup[:, h * 64:(h + 1) * 64], rhs=XL[:, 0, :], start=True, stop=True), reads=[b_wup, b_XL], writes=[bpb])
                S.op("act", lambda e: e.activation(out=lw[:, h, :], in_=pb[0:64, 0:TB], func=AF.Sigmoid, bias=rp[0:64, 16 + h:17 + h], scale=1.0), reads=[bpb, b_rp], writes=[b_lw])
                pb, bpb = nbx()
                S.op("pe", lambda e: e.matmul(pb[0:64, 0:TB], lhsT=aup[:, h * 64:(h + 1) * 64], rhs=XL[:, 1, :], start=True, stop=True), reads=[b_aup, b_XL], writes=[bpb])
                S.op("act", lambda e: e.activation(out=aa[:, h, :], in_=pb[0:64, 0:TB], func=AF.Sigmoid, bias=rp[0:64, 20 + h:21 + h], scale=1.0), reads=[bpb, b_rp], writes=[b_aa])
                pb, bpb = nbx()
                S.op("pe", lambda e: e.matmul(pb[0:64, 0:TB], lhsT=gup[:, h * 64:(h + 1) * 64], rhs=XG[:], start=True, stop=True), reads=[b_gup, b_XG], writes=[bpb])
                evac(gT[:, h, :], pb[0:64, 0:TB], [bpb], [b_gT])
            S.op("dve", lambda e: e.tensor_scalar(out=lw[:], in0=lw[:], scalar1=-0.6065306597126334, scalar2=None, op0=ALU.mult), reads=[b_lw], writes=[b_lw])
            for h in range(4):
                S.op("dve", lambda e: e.tensor_scalar(out=kkn[:, h, :], in0=k_(h), scalar1=rp[0:64, 24 + h:25 + h], scalar2=None, op0=ALU.mult), reads=[b_X3, b_rp], writes=[b_kkn])
            S.op("act", lambda e: e.activation(out=tmp[:], in_=kkn[:], func=AF.Square), reads=[b_kkn], writes=[b_tmp])
            for h in range(4):
                pb, bpb = nbx()
                S.op("pe", lambda e: e.matmul(pb[0:64, 0:TB], lhsT=ones64[:], rhs=tmp[:, h, :], start=True, stop=True), reads=[b_o64, b_tmp], writes=[bpb])
                S.op("act", lambda e: e.activation(out=eC[:, h, :], in_=pb[0:64, 0:TB], func=AF.Sqrt), reads=[bpb], writes=[b_eC])
            S.op("dve", lambda e: e.tensor_scalar(out=eC[:], in0=eC[:], scalar1=1e-12, scalar2=None, op0=ALU.max), reads=[b_eC], writes=[b_eC])
            S.op("dve", lambda e: e.reciprocal(out=eC[:], in_=eC[:]), reads=[b_eC], writes=[b_eC])
            S.op("dve", lambda e: e.tensor_tensor(out=kkn[:], in0=kkn[:], in1=eC[:], op=ALU.mult), reads=[b_kkn, b_eC], writes=[b_kkn])
            for h in range(4):
                S.op("dve", lambda e: e.tensor_scalar(out=tmp[:, h, :], in0=aa[:, h, :], scalar1=rp[0:64, 28 + h:29 + h], scalar2=omka[:, h:h + 1], op0=ALU.mult, op1=ALU.add),
                     reads=[b_aa, b_rp, b_omka], writes=[b_tmp])
            S.op("dve", lambda e: e.tensor_tensor(out=kfin[:], in0=X3[:, 4:8, :], in1=tmp[:], op=ALU.mult), reads=[b_X3, b_tmp], writes=[b_kfin])
            for h in range(4):
                S.op("dve", lambda e: e.scalar_tensor_tensor(out=tmp[:, h, :], in0=r_(h), scalar=rp[0:64, 32 + h:33 + h], in1=kfin[:, h, :], op0=ALU.mult, op1=ALU.mult),
                     reads=[b_X3, b_kfin, b_rp], writes=[b_tmp])
                pb, bpb = nbx()
                S.op("pe", lambda e: e.matmul(pb[0:64, 0:TB], lhsT=ones64[:], rhs=tmp[:, h, :], start=True, stop=True), reads=[b_o64, b_tmp], writes=[bpb])
                S.op("dve", lambda e: e.tensor_tensor(out=bv[:, h, :], in0=pb[0:64, 0:TB], in1=v_(h), op=ALU.mult), reads=[bpb, b_X3], writes=[b_bv])
            src, bsrc, dst, bdst = lw, b_lw, cl2, b_cl2
            cp(S, "act", eprev[:], lw[:], [b_lw], [b_eprev])
            for sft in (1, 2, 4, 8, 16, 32):
                s5 = src[:].rearrange("k h (c t) -> k h c t", t=64)
                d5 = dst[:].rearrange("k h (c t) -> k h c t", t=64)
                S.op("dve", lambda e: e.tensor_tensor(out=d5[:, :, :, sft:64], in0=s5[:, :, :, sft:64], in1=s5[:, :, :, 0:64 - sft], op=ALU.add), reads=[bsrc], writes=[bdst])
                cp(S, "act", d5[:, :, :, 0:sft], s5[:, :, :, 0:sft], [bsrc], [bdst])
                src, bsrc, dst, bdst = dst, bdst, src, bsrc
            cl, b_cl = src, bsrc
            S.op("act", lambda e: e.activation(out=epos[:], in_=cl[:], func=AF.Exp), reads=[b_cl], writes=[b_epos])
            S.op("act", lambda e: e.activation(out=eneg[:], in_=cl[:], func=AF.Exp, scale=-1.0), reads=[b_cl], writes=[b_eneg])
            S.op("dve", lambda e: e.tensor_tensor(out=eprev[:], in0=cl[:], in1=eprev[:], op=ALU.subtract), reads=[b_cl, b_eprev], writes=[b_eprev])
            S.op("act", lambda e: e.activation(out=eprev[:], in_=eprev[:], func=AF.Exp), reads=[b_eprev], writes=[b_eprev])
            ep5 = epos[:].rearrange("k h (c t) -> k h c t", t=64)
            S.op("dve", lambda e: e.tensor_copy(out=PCf[:], in_=ep5[:, :, :, 63]), reads=[b_epos], writes=[b_PCf])
            en5 = eneg[:].rearrange("k h (c t) -> k h c t", t=64)
            ec5 = eC[:].rearrange("k h (c t) -> k h c t", t=64)
            for h in range(4):
                for c in range(NCH):
                    if (h + c) % 2:
                        S.op("dve", lambda e: e.tensor_scalar(out=ec5[:, h, c, :], in0=en5[:, h, c, :], scalar1=PCf[:, h, c:c + 1], scalar2=None, op0=ALU.mult),
                             reads=[b_eneg, b_PCf], writes=[b_eC])
                    else:
                        S.op("act", lambda e: e.activation(out=ec5[:, h, c, :], in_=en5[:, h, c, :], func=AF.Copy, scale=AS32(PCf[:, h, c:c + 1])),
                             reads=[b_eneg, b_PCf], writes=[b_eC])
            for h in range(4):
                S.op("dve", lambda e: e.scalar_tensor_tensor(out=AR[:, :, h // 2, 0, h % 2, :], in0=hm(kkn)[:, h], scalar=-1.0, in1=hm(eprev)[:, h], op0=ALU.mult, op1=ALU.mult), reads=[b_kkn, b_eprev], writes=[b_AR])
                S.op("dve", lambda e: e.tensor_tensor(out=AR[:, :, h // 2, 1, h % 2, :], in0=X3[:, h, :].rearrange("k (c t) -> k c t", t=64), in1=hm(epos)[:, h], op=ALU.mult), reads=[b_X3, b_epos], writes=[b_AR])
            S.op("dve", lambda e: e.tensor_tensor(out=tmp[:], in0=kkn[:], in1=aa[:], op=ALU.mult), reads=[b_kkn, b_aa], writes=[b_tmp])
            for h in range(4):
                S.op("dve", lambda e: e.tensor_tensor(out=cm(Bt)[:, h], in0=hm(tmp)[:, h], in1=hm(eneg)[:, h], op=ALU.mult), reads=[b_tmp, b_eneg], writes=[b_Bt])
                S.op("pool", lambda e: e.tensor_tensor(out=cm(Bh)[:, h], in0=hm(tmp)[:, h], in1=hm(eC)[:, h], op=ALU.mult), reads=[b_tmp, b_eC], writes=[b_Bh])
                S.op("dve", lambda e: e.tensor_tensor(out=cm(Kt)[:, h], in0=hm(kfin)[:, h], in1=hm(eneg)[:, h], op=ALU.mult), reads=[b_kfin, b_eneg], writes=[b_Kt])
                S.op("pool", lambda e: e.tensor_tensor(out=cm(Kh)[:, h], in0=hm(kfin)[:, h], in1=hm(eC)[:, h], op=ALU.mult), reads=[b_kfin, b_eC], writes=[b_Kh])
                cp(S, "act", cm(Vc_)[:, h], X3[:, 8 + h, :].rearrange("k (c t) -> k c t", t=64), [b_X3], [b_Vc_])
            for p in range(2):
                pb, bpb = nbx()
                S.op("pe", lambda e: e.matmul(pb[:, 0:NCH], lhsT=il[:], rhs=PCf[:, 2 * p, :], start=True, stop=False), reads=[b_il, b_PCf], writes=[bpb])
                S.op("pe", lambda e: e.matmul(pb[:, 0:NCH], lhsT=ir[:], rhs=PCf[:, 2 * p + 1, :], start=False, stop=True), reads=[b_ir, b_PCf], writes=[bpb])
                evac(PCc[:, p, :], pb[:, 0:NCH], [bpb], [b_PCc])
            if tt == 0:
                g.dbg("d_X3", X3[:], b_X3, [64, 12, TB]); g.dbg("d_cl", cl[:], b_cl, [64, 4, TB]); g.dbg("d_aa", aa[:], b_aa, [64, 4, TB])
                g.dbg("d_kkn", kkn[:], b_kkn, [64, 4, TB]); g.dbg("d_kfin", kfin[:], b_kfin, [64, 4, TB]); g.dbg("d_bv", bv[:], b_bv, [64, 4, TB])
                g.dbg("d_gT", gT[:], b_gT, [64, 4, TB]); g.dbg("d_eC", eC[:], b_eC, [64, 4, TB])
                g.dbg("d_PCc", PCc[:], b_PCc, [128, 2, NCH])
            def unit(c, p):
                tc = slice(c * 64, (c + 1) * 64)
                hp = slice(2 * p, 2 * p + 2)
                fl = lambda ap: ap.rearrange("k h t -> k (h t)")
                At_ = fl(AR[:, c, p, 0, :, :]); Bt_ = fl(Bt[:, c, hp, :]); Kt_ = fl(Kt[:, c, hp, :])
                ARp = AR[:, c, p, :, :, :].rearrange("k a h t -> k (a h t)")
                p1, bp1 = nb(); p2, bp2 = nb()
                S.op("pe", lambda e: e.matmul(p1[:, 0:256], lhsT=RR(Bt_), rhs=RR(ARp), start=True, stop=True), reads=[b_Bt, b_AR], writes=[bp1])
                S.op("pe", lambda e: e.matmul(p2[:, 0:256], lhsT=RR(Kt_), rhs=RR(ARp), start=True, stop=True), reads=[b_Kt, b_AR], writes=[bp2])
                yield
                N0, bN0 = nt_(); ArbT, bArbT = nt_(); AakT, bAakT = nt_(); ArkT, bArkT = nt_()
                S.op("dve", lambda e: e.tensor_tensor(out=N0[:], in0=p1[:, 0:128], in1=msu[:], op=ALU.mult), reads=[bp1, b_msu], writes=[bN0])
                S.op("dve", lambda e: e.tensor_tensor(out=ArbT[:], in0=p1[:, 128:256], in1=mu_[:], op=ALU.mult), reads=[bp1, b_mu], writes=[bArbT])
                rel(bp1)
                S.op("dve", lambda e: e.tensor_tensor(out=AakT[:], in0=p2[:, 0:128], in1=msu[:], op=ALU.mult), reads=[bp2, b_msu], writes=[bAakT])
                S.op("dve", lambda e: e.tensor_tensor(out=ArkT[:], in0=p2[:, 128:256], in1=mu_[:], op=ALU.mult), reads=[bp2, b_mu], writes=[bArkT])
                rel(bp2)
                p3, bp3 = nb()
                S.op("pe", lambda e: e.matmul(p3[:, 0:128], lhsT=RR(At_), rhs=RR(Bt_), start=True, stop=True), reads=[b_AR, b_Bt], writes=[bp3])
                ptr, bptr = nb()
                srcs = [(At_, b_AR), (fl(Vc_[:, c, hp, :]), b_Vc_), (fl(Bh[:, c, hp, :]), b_Bh), (fl(Kh[:, c, hp, :]), b_Kh)]
                for i_, (sap, sb_) in enumerate(srcs):
                    S.op("pe", lambda e: e.transpose(out=ptr[:, i_ * 64:(i_ + 1) * 64], in_=AS32(sap) if i_ == 0 else sap, identity=idf[0:64, 0:64]), reads=[sb_, b_idf], writes=[bptr])
                yield
                NT0, bNT0 = nt_()
                S.op("dve", lambda e: e.tensor_tensor(out=NT0[:], in0=p3[:, 0:128], in1=msl[:], op=ALU.mult), reads=[bp3, b_msl], writes=[bNT0])
                rel(bp3)
                Z, bZ = nt_()
                S.op("pool", lambda e: e.tensor_tensor(out=Z[:], in0=N0[:], in1=idf[:], op=ALU.add), reads=[bN0, b_idf], writes=[bZ])
                TA, bTA = nt_()
                Vt, bVt = nt_()
                cp(S, "act", TA[:, 0:64], ptr[:, 0:64], [bptr], [bTA])
                cp(S, "act", Vt[:, 0:64], ptr[:, 64:128], [bptr], [bVt])
                bi = bdc[0] % NBD; bdc[0] += 1
                Bbd, bBbd = bd[0][bi], b_bd[0][bi]
                Kbd, bKbd = bd[1][bi], b_bd[1][bi]
                Apb, bApb = bd[2][bi], b_bd[2][bi]
                for hh in range(2):
                    rs = slice(hh * 64, (hh + 1) * 64)
                    cp(S, "act", Bbd[rs, rs], ptr[rs, 128:192], [bptr], [bBbd])
                    cp(S, "act", Kbd[rs, rs], ptr[rs, 192:256], [bptr], [bKbd])
                rel(bptr)
                yield
                X, bX, XT, bXT = N0, bN0, NT0, bNT0
                pw, bpw = nb()
                S.op("pe", lambda e: e.matmul(pw[:, 0:64], lhsT=RR(AakT[:]), rhs=RR(Vt[:, 0:64]), start=True, stop=True), reads=[bAakT, bVt], writes=[bpw])
                yield
                cp(S, "act", TA[:, 64:128], pw[:, 0:64], [bpw], [bTA])
                rel(bpw)
                for j in range(1, 6):
                    if j <= 4:
                        px, bpx = nb()
                        S.op("pe", lambda e: e.matmul(px[:, 0:128], lhsT=RR(XT[:]), rhs=RR(X[:]), start=True, stop=True), reads=[bXT, bX], writes=[bpx])
                    pxt, bpxt = nb()
                    S.op("pe", lambda e: e.matmul(pxt[:, 0:128], lhsT=RR(X[:]), rhs=RR(XT[:]), start=True, stop=True), reads=[bXT, bX], writes=[bpxt])
                    yield
                    if j <= 4:
                        Xn, bXn = nt_()
                        cp(S, "act", Xn[:], px[:, 0:128], [bpx], [bXn])
                        rel(bpx)
                    XTn, bXTn = nt_()
                    cp(S, "act" if j > 4 else "dve", XTn[:], pxt[:, 0:128], [bpxt], [bXTn])
                    rel(bpxt)
                    pz, bpz = nb()
                    S.op("pe", lambda e: e.matmul(pz[:, 0:128], lhsT=RR(XTn[:]), rhs=RR(Z[:]), start=True, stop=True), reads=[bXTn, bZ], writes=[bpz])
                    yield
                    Zn, bZn = nt_()
                    S.op("dve", lambda e: e.tensor_tensor(out=Zn[:], in0=pz[:, 0:128], in1=Z[:], op=ALU.add), reads=[bpz, bZ], writes=[bZn])
                    rel(bpz)
                    Z, bZ = Zn, bZn
                    if j <= 4:
                        X, bX = Xn, bXn
                    XT, bXT = XTn, bXTn
                pu, bpu = nb()
                S.op("pe", lambda e: e.matmul(pu[:, 0:128], lhsT=RR(Z[:]), rhs=RR(TA[:]), start=True, stop=True), reads=[bZ, bTA], writes=[bpu])
                yield
                U0, bU0 = nt_()
                cp(S, "dve", U0[:, 0:64], pu[:, 64:128], [bpu], [bU0])
                for hh in range(2):
                    rs = slice(hh * 64, (hh + 1) * 64)
                    cp(S, "act", Apb[rs, rs], pu[rs, 0:64], [bpu], [bApb])
                rel(bpu)
                pg, bpg = nb()
                S.op("pe", lambda e: e.matmul(pg[:, 0:64], lhsT=RR(Bbd[:]), rhs=RR(U0[:, 0:64]), start=True, stop=False), reads=[bBbd, bU0], writes=[bpg])
                S.op("pe", lambda e: e.matmul(pg[:, 0:64], lhsT=RR(Kbd[:]), rhs=RR(Vt[:, 0:64]), start=False, stop=True), reads=[bKbd, bVt], writes=[bpg])
                pf, bpf = nb()
                S.op("pe", lambda e: e.matmul(pf[:, 0:128], lhsT=RR(Apb[:]), rhs=RR(Bbd[:]), start=True, stop=True), reads=[bApb, bBbd], writes=[bpf])
                yield
                Gs, bGs = ntf_()
                cp(S, "act", Gs[:, 0:64], pg[:, 0:64], [bpg], [bGs])
                rel(bpg)
                PhiT, bPhiT = nt_()
                S.op("dve", lambda e: e.scalar_tensor_tensor(out=PhiT[:], in0=idf[:], scalar=PCc[:, p, c:c + 1], in1=pf[:, 0:128], op0=ALU.mult, op1=ALU.add),
                     reads=[b_idf, b_PCc, bpf], writes=[bPhiT])
                rel(bpf)
                pr, bpr = nb()
                S.op("pe", lambda e: e.matmul(pr[:, 0:64], lhsT=RR(il[:]), rhs=RR(AR[:, c, p, 1, 0, :]), start=True, stop=False, **SK), reads=[b_il, b_AR], writes=[bpr])
                S.op("pe", lambda e: e.matmul(pr[:, 64:128], lhsT=RR(ir[:]), rhs=RR(AR[:, c, p, 1, 1, :]), start=False, stop=False, **SK), reads=[b_ir, b_AR], writes=[bpr])
                S.op("pe", lambda e: e.matmul(pr[:, 0:128], lhsT=RR(Apb[:]), rhs=RR(ArbT[:]), start=False, stop=True, **SK), reads=[bApb, bArbT], writes=[bpr])
                yield
                RpT, bRpT = nt_()
                cp(S, "act", RpT[:], pr[:, 0:128], [bpr], [bRpT])
                rel(bpr)
                Scur, bScur = ST[p][sidx[p]], b_ST[p][sidx[p]]
                py, bpy = nb()
                S.op("pe", lambda e: e.matmul(py[:, 0:64], lhsT=RR(ArbT[:]), rhs=RR(U0[:, 0:64]), start=True, stop=False), reads=[bArbT, bU0], writes=[bpy])
                S.op("pe", lambda e: e.matmul(py[:, 0:64], lhsT=RR(ArkT[:]), rhs=RR(Vt[:, 0:64]), start=False, stop=False), reads=[bArkT, bVt], writes=[bpy])
                S.op("pe", lambda e: e.matmul(py[:, 0:64], lhsT=RR(RpT[:]), rhs=RR(Scur[:]), start=False, stop=True), reads=[bRpT, bScur], writes=[bpy])
                ps_, bps = nb()
                S.op("pe", lambda e: e.matmul(ps_[:, 0:64], lhsT=RR(PhiT[:]), rhs=RR(Scur[:]), start=True, stop=True), reads=[bPhiT, bScur], writes=[bps])
                sidx[p] ^= 1
                Snew, bSnew = ST[p][sidx[p]], b_ST[p][sidx[p]]
                S.op("dve", lambda e: e.tensor_tensor(out=Snew[:], in0=ps_[:, 0:64], in1=Gs[:, 0:64], op=ALU.add), reads=[bps, bGs], writes=[bSnew])
                rel(bps)
                Yt, bYt = ntf_()
                st_, bst = ntf_()
                cp(S, "act", Yt[:, 0:64], py[:, 0:64], [bpy], [bYt])
                rel(bpy)
                yield
                S.op("dve", lambda e: e.bn_stats(out=st_[:, 0:6], in_=Yt[:, 0:64]), reads=[bYt], writes=[bst])
                yield
                S.op("dve", lambda e: e.bn_aggr(out=st_[:, 8:10], in_=st_[:, 0:6]), reads=[bst], writes=[bst])
                yield
                S.op("act", lambda e: e.activation(out=st_[:, 10:11], in_=st_[:, 9:10], func=AF.Sqrt, bias=cst[:, 0:1], scale=1.0), reads=[bst, b_cst], writes=[bst])
                yield
                S.op("dve", lambda e: e.reciprocal(out=st_[:, 11:12], in_=st_[:, 10:11]), reads=[bst], writes=[bst])
                yield
                S.op("dve", lambda e: e.tensor_scalar(out=Yt[:, 0:64], in0=Yt[:, 0:64], scalar1=st_[:, 8:9], scalar2=st_[:, 11:12], op0=ALU.subtract, op1=ALU.mult),
                     reads=[bYt, bst], writes=[bYt])
                pyt, bpyt = nb()
                S.op("pe", lambda e: e.transpose(out=pyt[0:64, 0:128], in_=Yt[:, 0:64], identity=idf[:]), reads=[bYt, b_idf], writes=[bpyt])
                yield
                cp(S, "act", YN[:, hp, tc], pyt[0:64, 0:128].rearrange("v (h t) -> v h t", h=2), [bpyt], [b_YN])
                rel(bpyt)

            GRP = 2
            for c0_ in range(0, NCH, GRP):
                for k_ in range(NB):
                    busy[k_] = False
                gens = [unit(c_, p_) for c_ in range(c0_, min(NCH, c0_ + GRP)) for p_ in range(2)]
                alive = list(gens)
                while alive:
                    nxt = []
                    for gn_ in alive:
                        try:
                            next(gn_)
                            nxt.append(gn_)
                        except StopIteration:
                            pass
                    alive = nxt
            if tt == 0:
                g.dbg("d_YN", YN[:], b_YN, [64, 4, TB])
            for h in range(4):
                S.op("dve", lambda e: e.tensor_scalar(out=YN[:, h, :], in0=YN[:, h, :], scalar1=rp[0:64, 36 + h:37 + h], scalar2=rp[0:64, 40 + h:41 + h], op0=ALU.mult, op1=ALU.add),
                     reads=[b_YN, b_rp], writes=[b_YN])
            S.op("dve", lambda e: e.tensor_tensor(out=YN[:], in0=YN[:], in1=bv[:], op=ALU.add), reads=[b_YN, b_bv], writes=[b_YN])
            S.op("dve", lambda e: e.tensor_tensor(out=yo[:], in0=YN[:], in1=gT[:], op=ALU.mult), reads=[b_YN, b_gT], writes=[b_yo])
            S.dma("sp", D["ymixT"][0:256, t0:t0 + TB].rearrange("(h v) t -> v h t", v=64), yo[:], reads=[b_yo])
        S.barrier()


def build(T=8192, debug=False, phases=None, nlayers=2):
    nc = bass.Bass("TRN2", target_bir_lowering=False)
    g = G()
    g.nc, g.T, g.wctr = nc, T, 0
    g.debug = debug
    D = {}
    g.D = D

    def din(name, shape, dt=F32):
        D[name] = nc.dram_tensor(name, list(shape), dt, kind="ExternalInput").ap()

    def dscr(name, shape, dt=F32, out=False):
        D[name] = nc.dram_tensor(name, list(shape), dt, kind=("ExternalOutput" if (out or debug) else "Internal")).ap()

    din("xT", [DM, T]); din("pT", [2, 256, T]); din("pos", [1, T], I32); din("invf", [128, 1])
    din("w_in", [2, DM, NEXT]); din("smalls", [2, 128, 64]); din("w_out", [2, DM, DM])
    din("ffn_w_up", [2, DM, 2 * DFF]); din("ffn_w_down", [2, DFF, DM]); din("convp", [2, 128, 44, 4])
    din("ple_w_gate", [2, DM, DM]); din("ple_w_proj", [2, 256, DM])
    din("pool_wbd", [2, 128, 2, 128]); din("pool_fix", [128, 2, 16])
    for nm, shp, dt in g_extra_inputs(T):
        din(nm, shp, dt)
    dscr("cosT", [128, T]); dscr("sinT", [128, T])
    dscr("zaT", [1024, T]); dscr("zbT", [256, T]); dscr("qT", [512, T], BF16); dscr("kT", [384, T], BF16)
    dscr("vcT", [128, T], BF16); dscr("vtok", [T, 256], BF16); dscr("gates", [T, 24])
    dscr("ymixT", [1024, T], BF16)
    for nm, shp, dt in g_extra_scratch(T):
        dscr(nm, shp, dt)
    dscr("xs0", [DM, T]); dscr("xs1", [DM, T]); dscr("xs2", [DM, T])
    dscr("outT", [DM, T], out=True)
    with ExitStack() as stack:
        g.S = Sched(nc, stack)
        if phases is None:
            phases = ("rope", "inproj", "rwkv", "pool", "nsa", "outproj", "ffn", "ple")
        if "rope" in phases:
            phase_rope(g)
        xcur = D["xT"]
        for l in range(nlayers):
            if "inproj" in phases:
                phase_inproj(g, l, xcur)
            if "rwkv" in phases:
                phase_rwkv(g, l)
            if "pool" in phases:
                phase_pool(g, l)
            if "nsa" in phases:
                phase_nsa(g, l)
            if "outproj" in phases:
                phase_outproj(g, l, xcur, D["xs0"])
            if "ffn" in phases:
                phase_ffn(g, l, D["xs0"], D["xs1"])
            if "ple" in phases:
                last = (l == nlayers - 1)
                phase_ple(g, l, D["xs1"], D["outT"] if last else D["xs2"], last)
            xcur = D["xs2"]
        g.S.barrier()
        g.ninstr = g.S.ninstr
    return nc, g


def g_extra_inputs(T):
    NCMP = (T - 32) // 16 + 1
    NTC = (NCMP + 127) // 128
    return [("nsa_masks", [128, 19, 512], BF16), ("identb", [128, 128], BF16), ("identf", [128, 128], F32),
            ("E_all", [128, T], BF16), ("mcs", [128, NTC, 128], BF16), ("keepw", [128, 256], F32), ("addw", [128, 256], F32),
            ("rw_msu", [128, 128], F32), ("rw_mu", [128, 128], F32), ("rw_msl", [128, 128], F32), ("rw_il", [64, 128], F32), ("rw_ir", [64, 128], F32),
            ("rwp", [2, 128, 64], F32), ("rw_w_up", [2, 64, 256], F32), ("rw_a_up", [2, 64, 256], F32), ("rw_g_up", [2, 128, 256], F32),
            ("nsa_w_ck", [2, 32, 64, 64], F32), ("nsa_w_cv", [2, 32, 64, 64], F32), ("nsa_peT", [2, 2, 64, 32], F32)]


def g_extra_scratch(T):
    return []


def host_prep(inp, T=8192):
    f = np.float32
    cols = inproj_cols()
    shared = {}
    shared["w_in"] = np.ascontiguousarray(inp["w_in"][:, :, cols])
    sm = np.zeros((2, 128, 64), f)
    for l in range(2):
        sm[l, :, 0:8] = inp["g_mix"][l].reshape(8, 128).T
        sm[l, :, 8:10] = inp["pool_scale"][l].reshape(2, 128).T
        sm[l, :, 16:24] = inp["g_ffn"][l].reshape(8, 128).T
        sm[l, :, 24:32] = inp["g_ple"][l].reshape(8, 128).T
        sm[l, :, 32:40] = inp["g_final"].reshape(8, 128).T
    shared["smalls"] = sm
    shared["w_out"] = np.ascontiguousarray(inp["w_out"])
    shared["ffn_w_up"] = np.ascontiguousarray(inp["ffn_w_up"])
    shared["ffn_w_down"] = np.ascontiguousarray(inp["ffn_w_down"])
    cp_ = np.zeros((2, 128, 44, 4), f)
    for l in range(2):
        cw = inp["ffn_conv_w"][l][:, 0, :]
        for i in range(3):
            cp_[l, :, :, i] = cw[i].reshape(44, 128).T
        cp_[l, :, :, 3] = inp["ffn_conv_b"][l].reshape(44, 128).T
    shared["convp"] = cp_
    shared["ple_w_gate"] = np.ascontiguousarray(inp["ple_w_gate"])
    shared["ple_w_proj"] = np.ascontiguousarray(inp["ple_w_proj"])
    pw = np.zeros((2, 128, 2, 128), f)
    for l in range(2):
        for gi in range(4):
            t_, h_ = gi // 2, gi % 2
            pw[l, h_ * 64:(h_ + 1) * 64, t_, h_ * 64:(h_ + 1) * 64] = inp["pool_w"][l, gi]
    shared["pool_wbd"] = pw
    fix = np.zeros((128, 2, 16), f)
    for gi, win in enumerate((2, 4, 8, 16)):
        t_, h_ = gi // 2, gi % 2
        fix[h_ * 64:(h_ + 1) * 64, t_, :] = 1.0 / np.minimum(np.arange(16) + 1, win)
    shared["pool_fix"] = fix
    shared.update(nsa_consts(T))
    shared.update(rwkv_consts())
    rwp = np.zeros((2, 128, 64), f)
    for l in range(2):
        mu = inp["rw_mu"][l]
        rwp[l, 0:64, 0:12] = mu[0:768].reshape(12, 64).T
        rwp[l, 0:64, 12:14] = mu[768:896].reshape(2, 64).T
        rwp[l, :, 14] = mu[896:1024]
        for j, nm in enumerate(("rw_w0", "rw_a0", "rw_k_k", "rw_k_a")):
            rwp[l, 0:64, 16 + 4 * j:20 + 4 * j] = inp[nm][l].reshape(4, 64).T
        rwp[l, 0:64, 32:36] = inp["rw_r_k"][l].T
        rwp[l, 0:64, 36:40] = inp["rw_gn_g"][l].reshape(4, 64).T
        rwp[l, 0:64, 40:44] = inp["rw_gn_b"][l].reshape(4, 64).T
    shared["rwp"] = rwp
    shared["rw_w_up"] = np.ascontiguousarray(inp["rw_w_up"])
    shared["rw_a_up"] = np.ascontiguousarray(inp["rw_a_up"])
    shared["rw_g_up"] = np.ascontiguousarray(inp["rw_g_up"])
    shared["nsa_w_ck"] = np.ascontiguousarray(inp["nsa_w_ck"])
    shared["nsa_w_cv"] = np.ascontiguousarray(inp["nsa_w_cv"])
    shared["nsa_peT"] = np.ascontiguousarray(np.stack([np.transpose(inp["nsa_pe_k"], (0, 2, 1)), np.transpose(inp["nsa_pe_v"], (0, 2, 1))], axis=1))
    inv = (10000.0 ** (-np.arange(32, dtype=f) / 32)).astype(f)
    shared["invf"] = np.tile(inv, 4).reshape(128, 1).astype(f)
    return shared


def per_core(inp, b, T=8192):
    return {"xT": np.ascontiguousarray(inp["x"][b, :T].T),
            "pT": np.ascontiguousarray(np.transpose(inp["p"][:, b, :T], (0, 2, 1))),
            "pos": np.ascontiguousarray(inp["positions"][b:b + 1, :T]).astype(np.int32)}


def kernel(**inputs):
    T = 8192
    inp = {k: np.asarray(v) for k, v in inputs.items()}
    nc, g = build(T)
    shared = host_prep(inp, T)
    in_maps = []
    for b in range(8):
        m = dict(shared)
        m.update(per_core(inp, b, T))
        in_maps.append(m)
    res = run_bass_kernel_spmd(nc, in_maps, core_ids=list(range(8)))
    out = np.stack([np.ascontiguousarray(res.results[b]["outT"].T) for b in range(8)], axis=0)
    return out.astype(np.float32)
```

```python
import os
import numpy as np
from contextlib import ExitStack
import concourse.bass as bass
import concourse.mybir as mybir
from concourse.bass_utils import run_bass_kernel_spmd

F32 = mybir.dt.float32
BF16 = mybir.dt.bfloat16
I32 = mybir.dt.int32
AF = mybir.ActivationFunctionType
ALU = mybir.AluOpType
AX = mybir.AxisListType

DM = 1024
PI = float(np.pi)
BIG = 30000.0
NFEAT = 3200
NTOKC = 280
NEXT = NFEAT + NTOKC
DFF = 2816


class Buf:
    __slots__ = ("w", "rs", "rd", "excl")

    def __init__(self, excl=False):
        self.w = None
        self.rs = {}
        self.rd = []
        self.excl = excl


class Sched:
    EPOCH = 30000
    NDMA = 6

    def __init__(self, nc, stack):
        self.nc = nc
        self.stack = stack
        self.engs = {"pe": nc.tensor, "act": nc.scalar, "dve": nc.vector, "pool": nc.gpsimd, "sp": nc.sync}
        self.nsem = 0
        self.sem = {}
        self.cnt = {}
        self.seen = {k: {} for k in self.engs}
        for k in self.engs:
            self.sem[k] = self._newsem(k)
            self.cnt[k] = 0
        self.dsem = {}
        self.dpos = {}
        for k in ("sp", "act", "pool"):
            self.dsem[k] = [[self._newsem("d" + k), 0] for _ in range(self.NDMA)]
            self.dpos[k] = 0
        self.last = {}
        self.ninstr = 0
        self.store_q = "pool"

    def _newsem(self, name):
        self.nsem += 1
        return self.stack.enter_context(self.nc.semaphore(f"s_{name}_{self.nsem}"))

    def _wait(self, ek, tok):
        sem, val, src = tok
        d = self.seen[ek]
        key = id(sem)
        if d.get(key, 0) >= val:
            return
        self.engs[ek].wait_ge(sem, val)
        d[key] = val

    def _deps(self, ek, reads, writes):
        toks = []
        same_ok = (ek == "pe")
        for b in reads:
            if b.w is not None and not (b.w[2] == ek and same_ok):
                toks.append(b.w)
            if b.excl:
                for e, t in b.rs.items():
                    if e != ek:
                        toks.append(t)
        for b in writes:
            if b.w is not None and not (b.w[2] == ek and same_ok):
                toks.append(b.w)
            for e, t in b.rs.items():
                if not (e == ek and same_ok):
                    toks.append(t)
            toks.extend(b.rd)
        for t in toks:
            self._wait(ek, t)

    def _record(self, tok, reads, writes, is_dma):
        for b in reads:
            if is_dma:
                b.rd.append(tok)
                if len(b.rd) > 24:
                    del b.rd[0]
            else:
                b.rs[tok[2]] = tok
        for b in writes:
            b.w = tok
            b.rs = {}
            b.rd = []

    def op(self, ek, fn, reads=(), writes=()):
        self._deps(ek, reads, writes)
        ins = fn(self.engs[ek])
        self.cnt[ek] += 1
        ins.then_inc(self.sem[ek], 1)
        tok = (self.sem[ek], self.cnt[ek], ek)
        self.last[ek] = tok
        self._record(tok, reads, writes, False)
        self.ninstr += 1
        if self.cnt[ek] >= self.EPOCH:
            self.sem[ek] = self._newsem(ek)
            self.cnt[ek] = 0
        return tok

    def dma(self, qk, out, in_, reads=(), writes=(), **kw):
        if qk == "sp" and len(writes) == 0 and self.store_q is not None:
            qk = self.store_q
        self._deps(qk, reads, writes)
        slots = self.dsem[qk]
        i = self.dpos[qk]
        self.dpos[qk] = (i + 1) % len(slots)
        sem, val = slots[i]
        if val > 0:
            self._wait(qk, (sem, val, "dma"))
        if val + 16 > 60000:
            sem = self._newsem("d" + qk)
            val = 0
            slots[i][0] = sem
        ins = self.engs[qk].dma_start(out=out, in_=in_, **kw)
        val += 16
        ins.then_inc(sem, 16)
        slots[i][1] = val
        tok = (sem, val, "dma")
        self._record(tok, reads, writes, True)
        self.ninstr += 1
        return tok

    def barrier(self):
        toks = list(self.last.values())
        for qk in self.dsem:
            for sem, val in self.dsem[qk]:
                if val > 0:
                    toks.append((sem, val, "dma"))
        for ek in self.engs:
            for t in toks:
                self._wait(ek, t)


def inproj_cols():
    qb = 1280
    rot = lambda base, nh: [base + h * 64 + ((d + 32) % 64) for h in range(nh) for d in range(64)]
    cols = list(range(0, 1280))
    cols += list(range(qb, qb + 512)) + rot(qb, 8)
    for off in (512, 768, 1024):
        cols += list(range(qb + off, qb + off + 128))
    for off in (512, 768, 1024):
        cols += rot(qb + off, 2)
    cols += list(range(qb + 640, qb + 768))
    assert len(cols) == NFEAT
    cols += list(range(qb + 896, qb + 1024)) + list(range(qb + 1152, qb + 1280))
    cols += list(range(qb + 1280, qb + 1304))
    assert len(cols) == NEXT
    return np.array(cols)


class G:
    uid = 0
    debug = False
    use_f32r = True

    def dbg(self, name, ap, buf, shape):
        if not self.debug or name in self.D:
            return
        self.D[name] = self.nc.dram_tensor(name, list(shape), F32, kind="ExternalOutput").ap()
        self.S.dma("sp", self.D[name], ap, reads=[buf])

    def nm(self, name):
        self.uid += 1
        return f"{name}_{self.uid}"


def cp(S, ek, out, in_, reads, writes):
    if ek == "act":
        return S.op("act", lambda e: e.copy(out=out, in_=in_), reads=reads, writes=writes)
    return S.op(ek, lambda e: e.tensor_copy(out=out, in_=in_), reads=reads, writes=writes)


def load_w_bf16(g, dst, b_dst, src, KC, N, stage, b_stage, rows=128):
    S = g.S
    CH = stage[0].shape[-1]
    srcv = src.rearrange("(c p) n -> p c n", p=rows)
    for c in range(KC):
        for n0 in range(0, N, CH):
            n1 = min(N, n0 + CH)
            i = g.wctr % len(stage)
            g.wctr += 1
            S.dma("sp", stage[i][0:rows, 0:n1 - n0], srcv[:, c, n0:n1], writes=[b_stage[i]])
            ek = ("act", "dve", "pool")[g.wctr % 3]
            cp(S, ek, dst[0:rows, c, n0:n1], stage[i][0:rows, 0:n1 - n0], [b_stage[i]], [b_dst])


def rmsnorm_tile(g, xt, b_x, hT, b_h, gcol, b_g, N, R):
    S = g.S
    S.op("act", lambda e: e.activation(out=R["sq"][:, :, 0:N], in_=xt[:, :, 0:N], func=AF.Square), reads=[b_x], writes=[R["b_sq"]])
    for c in range(8):
        S.op("pe", lambda e: e.matmul(R["p_rms"][:, 0:N], lhsT=R["ones"][:], rhs=R["sq"][:, c, 0:N], start=(c == 0), stop=(c == 7)),
             reads=[R["b_ones"], R["b_sq"]], writes=[R["b_prms"]])
    S.op("act", lambda e: e.activation(out=R["rstd"][:, 0:N], in_=R["p_rms"][:, 0:N], func=AF.Sqrt, bias=R["eps"][:, 0:1], scale=1.0 / DM),
         reads=[R["b_prms"], R["b_eps"]], writes=[R["b_rstd"]])
    S.op("dve", lambda e: e.reciprocal(out=R["rstd"][:, 0:N], in_=R["rstd"][:, 0:N]), reads=[R["b_rstd"]], writes=[R["b_rstd"]])
    for c in range(8):
        S.op("dve", lambda e: e.scalar_tensor_tensor(out=hT[:, c, 0:N], in0=xt[:, c, 0:N], scalar=gcol[:, c:c + 1], in1=R["rstd"][:, 0:N],
                                                       op0=ALU.mult, op1=ALU.mult),
             reads=[b_x, b_g, R["b_rstd"]], writes=[b_h])


def rms_shared(g, sbf, psf, N):
    S = g.S
    R = {}
    R["sq"] = sbf("rsq", [128, 8, N], BF16); R["b_sq"] = Buf()
    R["rstd"] = sbf("rstd", [128, N]); R["b_rstd"] = Buf()
    R["p_rms"] = psf("p_rms", [128, 512]); R["b_prms"] = Buf(excl=True)
    R["ones"] = sbf("ones", [128, 128], BF16); R["b_ones"] = Buf()
    R["eps"] = sbf("epsb", [128, 1]); R["b_eps"] = Buf()
    S.op("pool", lambda e: e.memset(R["ones"][:], 1.0), writes=[R["b_ones"]])
    S.op("pool", lambda e: e.memset(R["eps"][:], 1e-6), writes=[R["b_eps"]])
    return R


def phase_rope(g):
    nc, S, D, T = g.nc, g.S, g.D, g.T
    with ExitStack() as st:
        sbf = lambda name, shape, dt=F32: st.enter_context(nc.sbuf_tensor(g.nm(name), list(shape), dt))
        CH = min(T, 2048)
        posi = sbf("posi", [128, CH], I32); b_posi = Buf()
        posf = sbf("posf", [128, CH]); b_posf = Buf()
        ang = sbf("ang", [128, CH]); b_ang = Buf()
        tab = sbf("tab", [128, CH]); b_tab = Buf()
        ki = sbf("ki", [128, CH], I32); b_ki = Buf()
        kf = sbf("kf", [128, CH]); b_kf = Buf()
        inv_sb = sbf("inv_sb", [128, 1]); b_inv = Buf()
        S.dma("sp", inv_sb[:], D["invf"], writes=[b_inv])
        C1 = 6.28125
        C2 = 2 * np.pi - 6.28125
        for c0 in range(0, T, CH):
            S.dma("sp", posi[:], D["pos"][:, c0:c0 + CH].partition_broadcast(128), writes=[b_posi])
            S.op("dve", lambda e: e.tensor_copy(out=posf[:], in_=posi[:]), reads=[b_posi], writes=[b_posf])
            S.op("dve", lambda e: e.tensor_scalar(out=posf[:], in0=posf[:], scalar1=inv_sb[:, 0:1], scalar2=None, op0=ALU.mult),
                 reads=[b_posf, b_inv], writes=[b_posf])
            S.op("dve", lambda e: e.tensor_scalar(out=kf[:], in0=posf[:], scalar1=float(1.0 / (2 * np.pi)), scalar2=None, op0=ALU.mult),
                 reads=[b_posf], writes=[b_kf])
            S.op("dve", lambda e: e.tensor_copy(out=ki[:], in_=kf[:]), reads=[b_kf], writes=[b_ki])
            S.op("dve", lambda e: e.tensor_copy(out=kf[:], in_=ki[:]), reads=[b_ki], writes=[b_kf])
            S.op("dve", lambda e: e.scalar_tensor_tensor(out=posf[:], in0=kf[:], scalar=-C1, in1=posf[:], op0=ALU.mult, op1=ALU.add),
                 reads=[b_kf, b_posf], writes=[b_posf])
            S.op("dve", lambda e: e.scalar_tensor_tensor(out=posf[:], in0=kf[:], scalar=-C2, in1=posf[:], op0=ALU.mult, op1=ALU.add),
                 reads=[b_kf, b_posf], writes=[b_posf])
            for which, shift, dst in (("sin", 0.0, D["sinT"]), ("cos", PI / 2, D["cosT"])):
                S.op("dve", lambda e: e.tensor_scalar(out=ang[:], in0=posf[:], scalar1=shift, scalar2=None, op0=ALU.add),
                     reads=[b_posf], writes=[b_ang])
                S.op("dve", lambda e: e.tensor_scalar(out=kf[:], in0=ang[:], scalar1=PI, scalar2=-2 * PI, op0=ALU.is_gt, op1=ALU.mult),
                     reads=[b_ang], writes=[b_kf])
                S.op("dve", lambda e: e.tensor_tensor(out=ang[:], in0=ang[:], in1=kf[:], op=ALU.add), reads=[b_ang, b_kf], writes=[b_ang])
                S.op("dve", lambda e: e.tensor_scalar(out=ang[:], in0=ang[:], scalar1=3.141592, scalar2=-3.141592, op0=ALU.min, op1=ALU.max),
                     reads=[b_ang], writes=[b_ang])
                S.op("act", lambda e: e.activation(out=tab[:], in_=ang[:], func=AF.Sin), reads=[b_ang], writes=[b_tab])
                if which == "sin":
                    for base in (0, 64):
                        S.op("dve", lambda e: e.tensor_scalar(out=tab[base:base + 32, :], in0=tab[base:base + 32, :], scalar1=-1.0, scalar2=None, op0=ALU.mult),
                             reads=[b_tab], writes=[b_tab])
                S.dma("sp", dst[:, c0:c0 + CH], tab[:], reads=[b_tab])
        S.barrier()


def phase_inproj(g, l, xin):
    nc, S, D, T = g.nc, g.S, g.D, g.T
    with ExitStack() as st:
        sbf = lambda name, shape, dt=F32: st.enter_context(nc.sbuf_tensor(g.nm(name), list(shape), dt))
        psf = lambda name, shape, dt=F32: st.enter_context(nc.psum_tensor(g.nm(name), list(shape), dt))
        Wb = sbf("Wb", [128, 8, NEXT], BF16); b_W = Buf()
        stage = [sbf(f"wst{i}", [128, 1740]) for i in range(2)]; b_stage = [Buf() for _ in range(2)]
        load_w_bf16(g, Wb, b_W, D["w_in"][l], 8, NEXT, stage, b_stage)
        sm = sbf("sm", [128, 8]); b_sm = Buf()
        S.dma("sp", sm[:], D["smalls"][l][:, 0:8], writes=[b_sm])
        R = rms_shared(g, sbf, psf, 512)
        xt = [sbf(f"xt{i}", [128, 8, 512]) for i in range(2)]; b_xt = [Buf() for _ in range(2)]
        hTs = [sbf(f"hT{i}", [128, 8, 512], BF16) for i in range(2)]; b_hs = [Buf() for _ in range(2)]
        cs = [sbf(f"cs{i}", [128, 512]) for i in range(2)]; b_cs = [Buf() for _ in range(2)]
        sn = [sbf(f"sn{i}", [128, 512]) for i in range(2)]; b_sn = [Buf() for _ in range(2)]
        NEV = 4
        ev = [sbf(f"ev{i}", [128, 512]) for i in range(NEV)]; b_ev = [Buf() for _ in range(NEV)]
        evb = [sbf(f"evb{i}", [128, 512], BF16) for i in range(NEV)]; b_evb = [Buf() for _ in range(NEV)]
        t1 = [sbf(f"t1_{i}", [128, 512]) for i in range(2)]; b_t1 = [Buf() for _ in range(2)]
        t2 = [sbf(f"t2_{i}", [128, 512]) for i in range(2)]; b_t2 = [Buf() for _ in range(2)]
        vt = [sbf(f"vt{i}", [128, 256], BF16) for i in range(2)]; b_vt = [Buf() for _ in range(2)]
        gt = [sbf(f"gt{i}", [128, 24]) for i in range(2)]; b_gt = [Buf() for _ in range(2)]
        NPF = 5
        p_f = [psf(f"p_f{i}", [128, 512]) for i in range(NPF)]; b_pf = [Buf(excl=True) for _ in range(NPF)]
        p_t = [psf(f"p_t{i}", [128, 512]) for i in range(2)]; b_pt = [Buf(excl=True) for _ in range(2)]
        xv = xin.rearrange("(c p) t -> p c t", p=128)
        evc = 0
        pfc = 0

        NTL = T // 512

        def prep(tt):
            xi = tt % 2
            t0 = tt * 512
            S.dma("sp", xt[xi][:], xv[:, :, t0:t0 + 512], writes=[b_xt[xi]])
            S.dma("sp", cs[xi][:], D["cosT"][:, t0:t0 + 512], writes=[b_cs[xi]])
            S.dma("sp", sn[xi][:], D["sinT"][:, t0:t0 + 512], writes=[b_sn[xi]])
            rmsnorm_tile(g, xt[xi], b_xt[xi], hTs[xi], b_hs[xi], sm, b_sm, 512, R)

        prep(0)
        for tt in range(NTL):
            t0 = tt * 512
            xi = tt % 2
            if tt + 1 < NTL:
                prep(tt + 1)
            hT, b_h = hTs[xi], b_hs[xi]

            def mm_feat(f, pf, bpf):
                for c in range(8):
                    S.op("pe", lambda e: e.matmul(pf[:], lhsT=Wb[:, c, f * 128:(f + 1) * 128], rhs=hT[:, c, :], start=(c == 0), stop=(c == 7)),
                         reads=[b_W, b_h], writes=[bpf])
            plain = [(f, D["zaT"][f * 128:(f + 1) * 128, t0:t0 + 512], False) for f in range(8)]
            plain += [(8 + f, D["zbT"][f * 128:(f + 1) * 128, t0:t0 + 512], False) for f in range(2)]
            plain += [(24, D["vcT"][:, t0:t0 + 512], True)]
            for k_, (f, dst, isb) in enumerate(plain):
                pi = pfc % NPF; pfc += 1
                mm_feat(f, p_f[pi], b_pf[pi])
                ei = evc % NEV; evc += 1
                ek = "act" if k_ % 2 == 0 else "dve"
                if isb:
                    cp(S, ek, evb[ei][:], p_f[pi][:], [b_pf[pi]], [b_evb[ei]])
                    S.dma("sp", dst, evb[ei][:], reads=[b_evb[ei]])
                else:
                    cp(S, ek, ev[ei][:], p_f[pi][:], [b_pf[pi]], [b_ev[ei]])
                    S.dma("sp", dst, ev[ei][:], reads=[b_ev[ei]])
            for j in range(7):
                if j < 4:
                    f_a, f_b = 10 + j, 14 + j
                    dst = D["qT"][j * 128:(j + 1) * 128, t0:t0 + 512]
                else:
                    f_a, f_b = 18 + (j - 4), 21 + (j - 4)
                    dst = D["kT"][(j - 4) * 128:(j - 3) * 128, t0:t0 + 512]
                pa = pfc % NPF; pfc += 1
                mm_feat(f_a, p_f[pa], b_pf[pa])
                pb = pfc % NPF; pfc += 1
                mm_feat(f_b, p_f[pb], b_pf[pb])
                ti = j % 2
                S.op("dve", lambda e: e.tensor_tensor(out=t1[ti][:], in0=p_f[pa][:], in1=cs[xi][:], op=ALU.mult),
                     reads=[b_pf[pa], b_cs[xi]], writes=[b_t1[ti]])
                S.op("dve", lambda e: e.tensor_tensor(out=t2[ti][:], in0=p_f[pb][:], in1=sn[xi][:], op=ALU.mult),
                     reads=[b_pf[pb], b_sn[xi]], writes=[b_t2[ti]])
                ei = evc % NEV; evc += 1
                S.op("pool", lambda e: e.tensor_tensor(out=evb[ei][:], in0=t1[ti][:], in1=t2[ti][:], op=ALU.add),
                     reads=[b_t1[ti], b_t2[ti]], writes=[b_evb[ei]])
                S.dma("sp", dst, evb[ei][:], reads=[b_evb[ei]])
            for s4 in range(4):
                pi = s4 % 2
                for c in range(8):
                    S.op("pe", lambda e: e.matmul(p_t[pi][:, 0:NTOKC], lhsT=hT[:, c, s4 * 128:(s4 + 1) * 128], rhs=Wb[:, c, NFEAT:NEXT],
                                                  start=(c == 0), stop=(c == 7)),
                         reads=[b_W, b_h], writes=[b_pt[pi]])
                S.op("dve", lambda e: e.tensor_copy(out=vt[pi][:], in_=p_t[pi][:, 0:256]), reads=[b_pt[pi]], writes=[b_vt[pi]])
                S.op("act", lambda e: e.activation(out=gt[pi][:], in_=p_t[pi][:, 256:280], func=AF.Sigmoid), reads=[b_pt[pi]], writes=[b_gt[pi]])
                S.dma("sp", D["vtok"][t0 + s4 * 128:t0 + (s4 + 1) * 128, :], vt[pi][:], reads=[b_vt[pi]])
                S.dma("sp", D["gates"][t0 + s4 * 128:t0 + (s4 + 1) * 128, :], gt[pi][:], reads=[b_gt[pi]])
        S.barrier()


def phase_pool(g, l):
    nc, S, D, T = g.nc, g.S, g.D, g.T
    with ExitStack() as st:
        sbf = lambda name, shape, dt=F32: st.enter_context(nc.sbuf_tensor(g.nm(name), list(shape), dt))
        psf = lambda name, shape, dt=F32: st.enter_context(nc.psum_tensor(g.nm(name), list(shape), dt))
        CH = 512
        PAD = 16
        wp = sbf("wp", [128, 2, 128]); b_wp = Buf()
        S.dma("sp", wp[:], D["pool_wbd"][l], writes=[b_wp])
        sm = sbf("smp", [128, 2]); b_sm = Buf()
        S.dma("sp", sm[:], D["smalls"][l][:, 8:10], writes=[b_sm])
        fix = sbf("fix", [128, 2, 16]); b_fix = Buf()
        S.dma("sp", fix[:], D["pool_fix"], writes=[b_fix])
        z = [[sbf(f"pz{i}_{f}", [128, PAD + CH]) for f in range(2)] for i in range(2)]
        b_z = [[Buf() for f in range(2)] for i in range(2)]
        s_a = sbf("ps_a", [128, PAD + CH]); b_sa = Buf()
        s_b = sbf("ps_b", [128, PAD + CH]); b_sb = Buf()
        pl = sbf("ppl", [128, CH]); b_pl = Buf()
        ob = [sbf(f"pob{i}", [128, CH], BF16) for i in range(2)]; b_ob = [Buf() for _ in range(2)]
        pp = [psf(f"ppp{i}", [128, 512]) for i in range(2)]; b_pp = [Buf(excl=True) for _ in range(2)]
        k = 0
        for tt in range(T // CH):
            t0 = tt * CH
            zi = tt % 2
            for f in range(2):
                zt, bz = z[zi][f], b_z[zi][f]
                if tt == 0:
                    S.op("pool", lambda e: e.memset(zt[:, 0:PAD], 0.0), writes=[bz])
                    S.dma("sp", zt[:, PAD:PAD + CH], D["zbT"][f * 128:(f + 1) * 128, 0:CH], writes=[bz])
                else:
                    S.dma("sp", zt[:], D["zbT"][f * 128:(f + 1) * 128, t0 - PAD:t0 + CH], writes=[bz])
                W = PAD + CH
                S.op("dve", lambda e: e.tensor_tensor(out=s_a[:, 1:W], in0=zt[:, 1:W], in1=zt[:, 0:W - 1], op=ALU.add), reads=[bz], writes=[b_sa])
                S.op("dve", lambda e: e.tensor_tensor(out=s_b[:, 3:W], in0=s_a[:, 3:W], in1=s_a[:, 1:W - 2], op=ALU.add), reads=[b_sa], writes=[b_sb])
                if f == 0:
                    lo, hi = s_a, s_b
                    blo, bhi = b_sa, b_sb
                    wl, wh = 2, 4
                else:
                    S.op("dve", lambda e: e.tensor_tensor(out=s_a[:, 7:W], in0=s_b[:, 7:W], in1=s_b[:, 3:W - 4], op=ALU.add), reads=[b_sb], writes=[b_sa])
                    S.op("dve", lambda e: e.tensor_tensor(out=s_b[:, 15:W], in0=s_a[:, 15:W], in1=s_a[:, 7:W - 8], op=ALU.add), reads=[b_sa], writes=[b_sb])
                    lo, hi = s_a, s_b
                    blo, bhi = b_sa, b_sb
                    wl, wh = 8, 16
                S.op("dve", lambda e: e.scalar_tensor_tensor(out=pl[0:64, :], in0=lo[0:64, PAD:W], scalar=1.0 / wl, in1=zt[0:64, PAD:W], op0=ALU.mult, op1=ALU.subtract),
                     reads=[blo, bz], writes=[b_pl])
                S.op("dve", lambda e: e.scalar_tensor_tensor(out=pl[64:128, :], in0=hi[64:128, PAD:W], scalar=1.0 / wh, in1=zt[64:128, PAD:W], op0=ALU.mult, op1=ALU.subtract),
                     reads=[bhi, bz], writes=[b_pl])
                if tt == 0:
                    S.op("dve", lambda e: e.tensor_tensor(out=pl[0:64, 0:16], in0=lo[0:64, PAD:PAD + 16], in1=fix[0:64, f, :], op=ALU.mult), reads=[blo, b_fix], writes=[b_pl])
                    S.op("dve", lambda e: e.tensor_tensor(out=pl[64:128, 0:16], in0=hi[64:128, PAD:PAD + 16], in1=fix[64:128, f, :], op=ALU.mult), reads=[bhi, b_fix], writes=[b_pl])
                    S.op("dve", lambda e: e.tensor_tensor(out=pl[:, 0:16], in0=pl[:, 0:16], in1=zt[:, PAD:PAD + 16], op=ALU.subtract), reads=[b_pl, bz], writes=[b_pl])
                pi = k % 2; k += 1
                S.op("pe", lambda e: e.matmul(pp[pi][:, 0:CH], lhsT=wp[:, f, :], rhs=pl[:], start=True, stop=True), reads=[b_wp, b_pl], writes=[b_pp[pi]])
                S.op("act", lambda e: e.activation(out=ob[pi][:], in_=pp[pi][:, 0:CH], func=AF.Copy, scale=sm[:, f:f + 1]), reads=[b_pp[pi], b_sm], writes=[b_ob[pi]])
                S.dma("sp", D["ymixT"][256 + f * 128:256 + (f + 1) * 128, t0:t0 + CH], ob[pi][:], reads=[b_ob[pi]])
        S.barrier()


def phase_outproj(g, l, xin, xout):
    nc, S, D, T = g.nc, g.S, g.D, g.T
    with ExitStack() as st:
        sbf = lambda name, shape, dt=F32: st.enter_context(nc.sbuf_tensor(g.nm(name), list(shape), dt))
        psf = lambda name, shape, dt=F32: st.enter_context(nc.psum_tensor(g.nm(name), list(shape), dt))
        Wo = sbf("Wo", [128, 8, DM], BF16); b_W = Buf()
        stage = [sbf(f"wst{i}", [128, 1024]) for i in range(2)]; b_stage = [Buf() for _ in range(2)]
        load_w_bf16(g, Wo, b_W, D["w_out"][l], 8, DM, stage, b_stage)
        xt = [sbf(f"oxt{i}", [128, 8, 512]) for i in range(2)]; b_xt = [Buf() for _ in range(2)]
        ym = [sbf(f"oym{i}", [128, 8, 512], BF16) for i in range(2)]; b_ym = [Buf() for _ in range(2)]
        xo = [sbf(f"oxo{i}", [128, 8, 512]) for i in range(2)]; b_xo = [Buf() for _ in range(2)]
        pq = [psf(f"opq{i}", [128, 512]) for i in range(4)]; b_pq = [Buf(excl=True) for _ in range(4)]
        xv = xin.rearrange("(c p) t -> p c t", p=128)
        xov = xout.rearrange("(c p) t -> p c t", p=128)
        yv = D["ymixT"].rearrange("(c p) t -> p c t", p=128)
        k = 0
        for tt in range(T // 512):
            t0 = tt * 512
            xi = tt % 2
            S.dma("sp", xt[xi][:], xv[:, :, t0:t0 + 512], writes=[b_xt[xi]])
            S.dma("sp", ym[xi][:], yv[:, :, t0:t0 + 512], writes=[b_ym[xi]])
            for j in range(8):
                pi = k % 4; k += 1
                for c in range(8):
                    S.op("pe", lambda e: e.matmul(pq[pi][:], lhsT=Wo[:, c, j * 128:(j + 1) * 128], rhs=ym[xi][:, c, :], start=(c == 0), stop=(c == 7)),
                         reads=[b_W, b_ym[xi]], writes=[b_pq[pi]])
                S.op("dve", lambda e: e.tensor_tensor(out=xo[xi][:, j, :], in0=pq[pi][:], in1=xt[xi][:, j, :], op=ALU.add),
                     reads=[b_pq[pi], b_xt[xi]], writes=[b_xo[xi]])
            S.dma("sp", xov[:, :, t0:t0 + 512], xo[xi][:], reads=[b_xo[xi]])
        S.barrier()


def phase_ffn(g, l, xin, xout):
    nc, S, D, T = g.nc, g.S, g.D, g.T
    N = 512
    with ExitStack() as st:
        sbf = lambda name, shape, dt=F32: st.enter_context(nc.sbuf_tensor(g.nm(name), list(shape), dt))
        psf = lambda name, shape, dt=F32: st.enter_context(nc.psum_tensor(g.nm(name), list(shape), dt))
        Wu = sbf("Wu", [128, 8, 2 * DFF], BF16)
        Wd = sbf("Wd", [128, 22, DM], BF16)
        CHW = 512
        NCW = (2 * DFF) // CHW
        b_Wu = [Buf() for _ in range(NCW)]
        b_Wd = [Buf() for _ in range(22)]
        stage = [sbf(f"wst{i}", [128, CHW]) for i in range(2)]; b_stage = [Buf() for _ in range(2)]
        xt = sbf("fxt", [128, 8, N]); b_xt = Buf()
        xv = xin.rearrange("(c p) t -> p c t", p=128)
        xov = xout.rearrange("(c p) t -> p c t", p=128)
        S.dma("sp", xt[:], xv[:, :, 0:N], writes=[b_xt])
        wuv = D["ffn_w_up"][l].rearrange("(c p) n -> p c n", p=128)
        wdv = D["ffn_w_down"][l].rearrange("(c p) n -> p c n", p=128)
        wu_done = set()
        wd_done = set()

        def emit_wu(nchunk):
            if nchunk in wu_done:
                return
            wu_done.add(nchunk)
            for c in range(8):
                i = g.wctr % 2; g.wctr += 1
                S.dma("sp", stage[i][:, :], wuv[:, c, nchunk * CHW:(nchunk + 1) * CHW], writes=[b_stage[i]])
                cp(S, ("act", "dve", "pool")[g.wctr % 3], Wu[:, c, nchunk * CHW:(nchunk + 1) * CHW], stage[i][:, :], [b_stage[i]], [b_Wu[nchunk]])

        def emit_wd(c):
            if c in wd_done:
                return
            wd_done.add(c)
            for n0 in range(0, DM, CHW):
                n1 = min(DM, n0 + CHW)
                i = g.wctr % 2; g.wctr += 1
                S.dma("sp", stage[i][:, 0:n1 - n0], wdv[:, c, n0:n1], writes=[b_stage[i]])
                cp(S, ("act", "dve", "pool")[g.wctr % 3], Wd[:, c, n0:n1], stage[i][:, 0:n1 - n0], [b_stage[i]], [b_Wd[c]])
        sm = sbf("smf", [128, 8]); b_sm = Buf()
        S.dma("sp", sm[:], D["smalls"][l][:, 16:24], writes=[b_sm])
        cw = sbf("cw", [128, 44, 4]); b_cw = Buf()
        S.dma("sp", cw[:], D["convp"][l], writes=[b_cw])
        gated = sbf("gated", [128, 22, N], BF16); b_gt = Buf()
        R = {}
        R["sq"] = gated[:, 0:8, :]; R["b_sq"] = b_gt
        R["rstd"] = sbf("rstd", [128, N]); R["b_rstd"] = Buf()
        R["p_rms"] = psf("p_rms", [128, 512]); R["b_prms"] = Buf(excl=True)
        R["ones"] = sbf("ones", [128, 128], BF16); R["b_ones"] = Buf()
        R["eps"] = sbf("epsb", [128, 1]); R["b_eps"] = Buf()
        S.op("pool", lambda e: e.memset(R["ones"][:], 1.0), writes=[R["b_ones"]])
        S.op("pool", lambda e: e.memset(R["eps"][:], 1e-6), writes=[R["b_eps"]])
        hT = sbf("fhT", [128, 8, N], BF16); b_h = Buf()
        carry = sbf("carry", [128, 44, 2]); b_carry = Buf()
        S.op("pool", lambda e: e.memset(carry[:], 0.0), writes=[b_carry])
        NU = 3
        U = [sbf(f"U{i}", [128, N + 2]) for i in range(NU)]; b_U = [Buf() for _ in range(NU)]
        cg = [sbf(f"cg{i}", [128, N]) for i in range(2)]; b_cg = [Buf() for _ in range(2)]
        cv = [sbf(f"cv{i}", [128, N]) for i in range(2)]; b_cv = [Buf() for _ in range(2)]
        gi = [sbf(f"gi{i}", [128, N]) for i in range(2)]; b_gi = [Buf() for _ in range(2)]
        pu = [psf(f"fpu{i}", [128, 512]) for i in range(4)]; b_pu = [Buf(excl=True) for _ in range(4)]
        pd = [psf(f"fpd{i}", [128, 512]) for i in range(2)]; b_pd = [Buf(excl=True) for _ in range(2)]
        uc = 0
        pc_ = 0
        for tt in range(T // N):
            t0 = tt * N
            if tt > 0:
                S.dma("sp", xt[:], xv[:, :, t0:t0 + N], writes=[b_xt])
            rmsnorm_tile(g, xt, b_xt, hT, b_h, sm, b_sm, N, R)
            for i in range(22):
                k2 = i % 2
                for which, ch in ((0, i), (1, 22 + i)):
                    ui = uc % NU; uc += 1
                    pi = pc_ % 4; pc_ += 1
                    emit_wu((ch * 128) // CHW)
                    for c in range(8):
                        S.op("pe", lambda e: e.matmul(pu[pi][:, 0:N], lhsT=Wu[:, c, ch * 128:(ch + 1) * 128], rhs=hT[:, c, :], start=(c == 0), stop=(c == 7)),
                             reads=[b_Wu[(ch * 128) // CHW], b_h], writes=[b_pu[pi]])
                    S.op("act", lambda e: e.copy(out=U[ui][:, 2:N + 2], in_=pu[pi][:, 0:N]), reads=[b_pu[pi]], writes=[b_U[ui]])
                    S.op("act", lambda e: e.copy(out=U[ui][:, 0:2], in_=carry[:, ch, :]), reads=[b_carry], writes=[b_U[ui]])
                    dst, bd_ = (cg[k2], b_cg[k2]) if which == 0 else (cv[k2], b_cv[k2])
                    S.op("dve", lambda e: e.tensor_scalar(out=dst[:], in0=U[ui][:, 0:N], scalar1=cw[:, ch, 0:1], scalar2=cw[:, ch, 3:4], op0=ALU.mult, op1=ALU.add),
                         reads=[b_U[ui], b_cw], writes=[bd_])
                    S.op("dve", lambda e: e.scalar_tensor_tensor(out=dst[:], in0=U[ui][:, 1:N + 1], scalar=cw[:, ch, 1:2], in1=dst[:], op0=ALU.mult, op1=ALU.add),
                         reads=[b_U[ui], b_cw, bd_], writes=[bd_])
                    S.op("dve", lambda e: e.scalar_tensor_tensor(out=dst[:], in0=U[ui][:, 2:N + 2], scalar=cw[:, ch, 2:3], in1=dst[:], op0=ALU.mult, op1=ALU.add),
                         reads=[b_U[ui], b_cw, bd_], writes=[bd_])
                    S.op("act", lambda e: e.copy(out=carry[:, ch, :], in_=U[ui][:, N:N + 2]), reads=[b_U[ui]], writes=[b_carry])
                S.op("pool", lambda e: e.tensor_tensor(out=gi[k2][:], in0=cg[k2][:], in1=cg[k2][:], op=ALU.mult), reads=[b_cg[k2]], writes=[b_gi[k2]])
                S.op("pool", lambda e: e.tensor_scalar(out=gi[k2][:], in0=gi[k2][:], scalar1=0.044715, scalar2=1.0, op0=ALU.mult, op1=ALU.add), reads=[b_gi[k2]], writes=[b_gi[k2]])
                S.op("pool", lambda e: e.tensor_tensor(out=gi[k2][:], in0=gi[k2][:], in1=cg[k2][:], op=ALU.mult), reads=[b_gi[k2], b_cg[k2]], writes=[b_gi[k2]])
                S.op("act", lambda e: e.activation(out=gi[k2][:], in_=gi[k2][:], func=AF.Sigmoid, scale=1.5957691216057308), reads=[b_gi[k2]], writes=[b_gi[k2]])
                S.op("pool", lambda e: e.tensor_tensor(out=gi[k2][:], in0=gi[k2][:], in1=cg[k2][:], op=ALU.mult), reads=[b_gi[k2], b_cg[k2]], writes=[b_gi[k2]])
                S.op("dve", lambda e: e.tensor_tensor(out=gated[:, i, :], in0=gi[k2][:], in1=cv[k2][:], op=ALU.mult), reads=[b_gi[k2], b_cv[k2]], writes=[b_gt])
                emit_wd(i)
            for j in range(8):
                pi = j % 2
                for i in range(22):
                    S.op("pe", lambda e: e.matmul(pd[pi][:, 0:N], lhsT=Wd[:, i, j * 128:(j + 1) * 128], rhs=gated[:, i, :], start=(i == 0), stop=(i == 21)),
                         reads=[b_Wd[i], b_gt], writes=[b_pd[pi]])
                S.op("dve", lambda e: e.tensor_tensor(out=xt[:, j, :], in0=pd[pi][:, 0:N], in1=xt[:, j, :], op=ALU.add),
                     reads=[b_pd[pi], b_xt], writes=[b_xt])
            S.dma("sp", xov[:, :, t0:t0 + N], xt[:], reads=[b_xt])
        S.barrier()


def phase_ple(g, l, xin, xout, final):
    nc, S, D, T = g.nc, g.S, g.D, g.T
    N = 512
    with ExitStack() as st:
        sbf = lambda name, shape, dt=F32: st.enter_context(nc.sbuf_tensor(g.nm(name), list(shape), dt))
        psf = lambda name, shape, dt=F32: st.enter_context(nc.psum_tensor(g.nm(name), list(shape), dt))
        Wg = sbf("Wg", [128, 8, DM], BF16); b_Wg = Buf()
        Wp = sbf("Wp", [128, 2, DM], BF16); b_Wp = Buf()
        stage = [sbf(f"wst{i}", [128, 1024]) for i in range(2)]; b_stage = [Buf() for _ in range(2)]
        load_w_bf16(g, Wg, b_Wg, D["ple_w_gate"][l], 8, DM, stage, b_stage)
        load_w_bf16(g, Wp, b_Wp, D["ple_w_proj"][l], 2, DM, stage, b_stage)
        sm = sbf("smq", [128, 16]); b_sm = Buf()
        S.dma("sp", sm[:, 0:8], D["smalls"][l][:, 24:32], writes=[b_sm])
        S.dma("sp", sm[:, 8:16], D["smalls"][l][:, 32:40], writes=[b_sm])
        R = rms_shared(g, sbf, psf, N)
        xt = [sbf(f"pxt{i}", [128, 8, N]) for i in range(2)]; b_xt = [Buf() for _ in range(2)]
        pt = [sbf(f"ppt{i}", [128, 2, N]) for i in range(2)]; b_pt = [Buf() for _ in range(2)]
        ptbs = [sbf(f"pptb{i}", [128, 2, N], BF16) for i in range(2)]; b_ptbs = [Buf() for _ in range(2)]
        hTs = [sbf(f"phT{i}", [128, 8, N], BF16) for i in range(2)]; b_hs = [Buf() for _ in range(2)]
        xos = [sbf(f"pxo{i}", [128, 8, N]) for i in range(2)]; b_xos = [Buf() for _ in range(2)]
        xf = sbf("pxf", [128, 8, N]); b_xf = Buf()
        gs = [sbf(f"pgs{i}", [128, N]) for i in range(2)]; b_gs = [Buf() for _ in range(2)]
        pg = [psf(f"ppg{i}", [128, 512]) for i in range(2)]; b_pg = [Buf(excl=True) for _ in range(2)]
        pq = [psf(f"ppq{i}", [128, 512]) for i in range(2)]; b_pq = [Buf(excl=True) for _ in range(2)]
        xv = xin.rearrange("(c p) t -> p c t", p=128)
        xov = xout.rearrange("(c p) t -> p c t", p=128)
        pv = D["pT"][l].rearrange("(c p) t -> p c t", p=128)
        NTL = T // N

        def prep(tt):
            xi = tt % 2
            t0 = tt * N
            S.dma("sp", xt[xi][:], xv[:, :, t0:t0 + N], writes=[b_xt[xi]])
            S.dma("sp", pt[xi][:], pv[:, :, t0:t0 + N], writes=[b_pt[xi]])
            S.op("pool", lambda e: e.tensor_copy(out=ptbs[xi][:], in_=pt[xi][:]), reads=[b_pt[xi]], writes=[b_ptbs[xi]])
            rmsnorm_tile(g, xt[xi], b_xt[xi], hTs[xi], b_hs[xi], sm, b_sm, N, R)

        prep(0)
        for tt in range(NTL):
            t0 = tt * N
            xi = tt % 2
            if tt + 1 < NTL:
                prep(tt + 1)
            hT, b_h = hTs[xi], b_hs[xi]
            ptb, b_ptb = ptbs[xi], b_ptbs[xi]
            xo, b_xo = xos[xi], b_xos[xi]
            for j in range(8):
                pi = j % 2
                for c in range(8):
                    S.op("pe", lambda e: e.matmul(pg[pi][:], lhsT=Wg[:, c, j * 128:(j + 1) * 128], rhs=hT[:, c, :], start=(c == 0), stop=(c == 7)),
                         reads=[b_Wg, b_h], writes=[b_pg[pi]])
                for c in range(2):
                    S.op("pe", lambda e: e.matmul(pq[pi][:], lhsT=Wp[:, c, j * 128:(j + 1) * 128], rhs=ptb[:, c, :], start=(c == 0), stop=(c == 1)),
                         reads=[b_Wp, b_ptb], writes=[b_pq[pi]])
                S.op("act", lambda e: e.activation(out=gs[pi][:], in_=pg[pi][:], func=AF.Sigmoid), reads=[b_pg[pi]], writes=[b_gs[pi]])
                S.op("dve", lambda e: e.tensor_tensor(out=gs[pi][:], in0=pq[pi][:], in1=gs[pi][:], op=ALU.mult), reads=[b_pq[pi], b_gs[pi]], writes=[b_gs[pi]])
                S.op("pool", lambda e: e.tensor_tensor(out=xo[:, j, :], in0=gs[pi][:], in1=xt[xi][:, j, :], op=ALU.add), reads=[b_gs[pi], b_xt[xi]], writes=[b_xo])
            if not final:
                S.dma("sp", xov[:, :, t0:t0 + N], xo[:], reads=[b_xo])
            else:
                S.op("act", lambda e: e.activation(out=R["sq"][:], in_=xo[:], func=AF.Square), reads=[b_xo], writes=[R["b_sq"]])
                for c in range(8):
                    S.op("pe", lambda e: e.matmul(R["p_rms"][:], lhsT=R["ones"][:], rhs=R["sq"][:, c, :], start=(c == 0), stop=(c == 7)),
                         reads=[R["b_ones"], R["b_sq"]], writes=[R["b_prms"]])
                S.op("act", lambda e: e.activation(out=R["rstd"][:], in_=R["p_rms"][:], func=AF.Sqrt, bias=R["eps"][:, 0:1], scale=1.0 / DM),
                     reads=[R["b_prms"], R["b_eps"]], writes=[R["b_rstd"]])
                S.op("dve", lambda e: e.reciprocal(out=R["rstd"][:], in_=R["rstd"][:]), reads=[R["b_rstd"]], writes=[R["b_rstd"]])
                for c in range(8):
                    S.op("dve", lambda e: e.scalar_tensor_tensor(out=xf[:, c, :], in0=xo[:, c, :], scalar=sm[:, 8 + c:9 + c], in1=R["rstd"][:],
                                                                   op0=ALU.mult, op1=ALU.mult),
                         reads=[b_xo, b_sm, R["b_rstd"]], writes=[b_xf])
                S.dma("sp", xov[:, :, t0:t0 + N], xf[:], reads=[b_xf])
        S.barrier()


def nsa_consts(T):
    import ml_dtypes
    bf = ml_dtypes.bfloat16
    f = np.float32
    c = {}
    nl = np.arange(128)[:, None]
    ql = np.arange(128)[None, :]
    masks = np.zeros((19, 128, 128), f)
    for i in range(17):
        masks[i] = np.where(16 * nl + 31 - ql <= 128 * i, 0.0, -BIG)
    masks[17] = np.where(nl <= ql, 0.0, -BIG)
    masks[18] = np.where(nl > ql, 0.0, -BIG)
    m4 = np.tile(masks, (1, 1, 4))
    c["nsa_masks"] = np.ascontiguousarray(np.transpose(m4, (1, 0, 2))).astype(bf)
    c["identb"] = np.eye(128, dtype=f).astype(bf)
    c["identf"] = np.eye(128, dtype=f)
    key = np.arange(T)[None, :]
    c["E_all"] = (key // 64 == np.arange(128)[:, None]).astype(f).astype(bf)
    n_cmp = (T - 32) // 16 + 1
    ntc = (n_cmp + 127) // 128
    cs = 16 * np.arange(n_cmp)
    ce = cs + 31
    ss = 64 * np.arange(128)
    ov = np.minimum(ce[:, None], ss[None] + 63) - np.maximum(cs[:, None], ss[None]) + 1
    mcs = np.zeros((ntc * 128, 128), f)
    mcs[:n_cmp] = np.clip(ov, 0, 32).astype(f) / 32
    c["mcs"] = np.ascontiguousarray(mcs.reshape(ntc, 128, 128).transpose(1, 0, 2)).astype(bf)
    keep = np.zeros((128, 256), f)
    add = np.zeros((128, 256), f)
    for q in range(128):
        jc = 126 if q < 64 else 127
        cc = np.arange(256)
        keep[q] = (cc < jc - 1)
        add[q] = np.where(cc == jc - 1, 1.1e9, np.where(cc == jc, 1.2e9, np.where(cc > jc, -1e30, 0.0)))
    c["keepw"] = keep
    c["addw"] = add
    return c


def phase_nsa(g, l):
    nc, S, D, T = g.nc, g.S, g.D, g.T
    NQB = T // 128
    NCMP = (T - 32) // 16 + 1
    NTC = (NCMP + 127) // 128
    SK = dict(skip_group_check=True)
    with ExitStack() as st:
        sbf = lambda name, shape, dt=F32: st.enter_context(nc.sbuf_tensor(g.nm(name), list(shape), dt))
        psf = lambda name, shape, dt=F32: st.enter_context(nc.psum_tensor(g.nm(name), list(shape), dt))
        stp = [psf(f"nst{i}", [128, 512]) for i in range(3)]; b_stp = [Buf(excl=True) for _ in range(3)]
        acc = [psf(f"nacc{i}", [128, 512]) for i in range(3)]; b_acc = [Buf(excl=True) for _ in range(3)]
        imp = psf("nimp", [128, 512]); b_imp = Buf(excl=True)
        msc = psf("nmsc", [128, 512]); b_msc = Buf(excl=True)
        mscb = msc[:, 384:448].bitcast(BF16); b_mscb = b_msc
        KcT = sbf("KcT", [64, 2, NTC * 128], BF16); b_Kc = Buf()
        Vc = sbf("Vc", [128, NTC, 2, 128], BF16); b_Vc = Buf()
        S.op("pool", lambda e: e.memset(KcT[:], 0.0), writes=[b_Kc])
        S.op("pool", lambda e: e.memset(Vc[:], 0.0), writes=[b_Vc])
        with ExitStack() as st2:
            sb2 = lambda name, shape, dt=F32: st2.enter_context(nc.sbuf_tensor(g.nm(name), list(shape), dt))
            kc = sb2("kc", [64, 2, T], BF16); b_kc = Buf()
            vc = sb2("vc", [64, 2, T], BF16); b_vc = Buf()
            S.dma("sp", kc[:], D["kT"][0:128, :].rearrange("(h d) t -> d h t", d=64), writes=[b_kc])
            S.dma("sp", vc[:], D["vcT"].rearrange("(h d) t -> d h t", d=64), writes=[b_vc])
            wst = sb2("wckst", [64, 32, 64]); b_wst = Buf()
            wck = sb2("wck", [64, 32, 64], BF16); b_wck = Buf()
            wcv = sb2("wcv", [64, 32, 64], BF16); b_wcv = Buf()
            S.dma("sp", wst[:], D["nsa_w_ck"][l].rearrange("l d e -> d l e"), writes=[b_wst])
            cp(S, "dve", wck[:], wst[:], [b_wst], [b_wck])
            S.dma("sp", wst[:], D["nsa_w_cv"][l].rearrange("l d e -> d l e"), writes=[b_wst])
            cp(S, "dve", wcv[:], wst[:], [b_wst], [b_wcv])
            pest = sb2("pest", [64, 2, 32]); b_pest = Buf()
            peb = sb2("peb", [64, 2, 32], BF16); b_peb = Buf()
            S.dma("sp", pest[:], D["nsa_peT"][l].rearrange("w d l -> d w l"), writes=[b_pest])
            cp(S, "dve", peb[:], pest[:], [b_pest], [b_peb])
            biask = sb2("biask", [64, 1]); b_bk = Buf()
            biasv = sb2("biasv", [1, 64], BF16); b_bv = Buf()
            onesr = sb2("onesr", [1, 128], BF16); b_or = Buf()
            S.op("pool", lambda e: e.memset(onesr[:], 1.0), writes=[b_or])
            for i_ in range(32):
                S.op("pe", lambda e: e.matmul(msc[0:64, 0:1], lhsT=wck[:, i_, :], rhs=peb[:, 0, i_:i_ + 1], start=(i_ == 0), stop=(i_ == 31)),
                     reads=[b_wck, b_peb], writes=[b_msc])
            cp(S, "dve", biask[:], msc[0:64, 0:1], [b_msc], [b_bk])
            for i_ in range(32):
                S.op("pe", lambda e: e.matmul(msc[0:1, 0:64], lhsT=peb[:, 1, i_:i_ + 1], rhs=wcv[:, i_, :], start=(i_ == 0), stop=(i_ == 31)),
                     reads=[b_wcv, b_peb], writes=[b_msc])
            cp(S, "dve", biasv[:], msc[0:1, 0:64], [b_msc], [b_bv])
            span = 16 * (NCMP - 1) + 1
            for h in range(2):
                pk = stp[h]
                for i_ in range(32):
                    S.op("pe", lambda e: e.matmul(pk[0:64, 0:NCMP], lhsT=wck[:, i_, :], rhs=kc[:, h, i_:i_ + span:16], start=(i_ == 0), stop=(i_ == 31)),
                         reads=[b_wck, b_kc], writes=[b_stp[h]])
                S.op("act", lambda e: e.activation(out=KcT[:, h, 0:NCMP], in_=pk[0:64, 0:NCMP], func=AF.Identity, bias=biask[:, 0:1], scale=1.0),
                     reads=[b_stp[h], b_bk], writes=[b_Kc])
            k_ = 0
            for h in range(2):
                for nt in range(NTC):
                    nn = min(128, NCMP - nt * 128)
                    pv_ = acc[k_ % 3]; bpv = b_acc[k_ % 3]; k_ += 1
                    base = 16 * 128 * nt
                    sp_ = 16 * (nn - 1) + 1
                    for i_ in range(32):
                        S.op("pe", lambda e: e.matmul(pv_[0:nn, 0:64], lhsT=vc[:, h, base + i_:base + i_ + sp_:16], rhs=wcv[:, i_, :], start=(i_ == 0), stop=False),
                             reads=[b_wcv, b_vc], writes=[bpv])
                    S.op("pe", lambda e: e.matmul(pv_[0:nn, 0:64], lhsT=onesr[0:1, 0:nn], rhs=biasv[0:1, :], start=False, stop=True),
                         reads=[b_or, b_bv], writes=[bpv])
                    cp(S, "dve", Vc[0:nn, nt, h, 0:64], pv_[0:nn, 0:64], [bpv], [b_Vc])
                    S.op("dve", lambda e: e.memset(Vc[0:nn, nt, h, 64:65], 1.0), writes=[b_Vc])
            S.barrier()
        masks = sbf("masks", [128, 19, 512], BF16); b_masks = Buf()
        S.dma("sp", masks[:], D["nsa_masks"], writes=[b_masks])
        identb = sbf("identb", [128, 128], BF16); b_idb = Buf()
        S.dma("sp", identb[:], D["identb"], writes=[b_idb])
        identf = sbf("identf", [128, 128]); b_idf = Buf()
        S.dma("sp", identf[:], D["identf"], writes=[b_idf])
        MCS = sbf("MCS", [128, NTC, 128], BF16); b_mcs = Buf()
        S.dma("sp", MCS[:], D["mcs"], writes=[b_mcs])
        keepw = sbf("keepw", [128, 256]); b_kw_ = Buf()
        S.dma("sp", keepw[:], D["keepw"], writes=[b_kw_])
        addw = sbf("addw", [128, 256]); b_aw = Buf()
        S.dma("sp", addw[:], D["addw"], writes=[b_aw])
        LH = sbf("LH", [128, 2, T], BF16); b_LH = Buf()
        KwT = sbf("KwT", [64, 2, T], BF16); b_Kw = Buf()
        TH = min(T, 4096)
        ksv = D["kT"][128:256, :].rearrange("(h d) t -> d h t", d=64)
        S.dma("sp", LH[64:128, :, 0:TH], ksv[:, :, 0:TH], writes=[b_LH])
        for h_ in range(2):
            S.dma("sp", LH[0:64, h_, 0:TH], D["E_all"][0:64, 0:TH], writes=[b_LH])
        if T > TH:
            S.dma("sp", LH[0:64, :, TH:T], ksv[:, :, TH:T], writes=[b_LH])
            for h_ in range(2):
                S.dma("sp", LH[64:128, h_, TH:T], D["E_all"][64:128, TH:T], writes=[b_LH])
        S.dma("sp", KwT[:], D["kT"][256:384, :].rearrange("(h d) t -> d h t", d=64), writes=[b_Kw])
        VW = 128
        Vs = sbf("Vs", [128, NQB, 2, VW], BF16); b_Vs = Buf()
        Vw = sbf("Vw", [128, NQB, 2, VW], BF16); b_Vw = Buf()
        S.op("pool", lambda e: e.memset(Vs[:], 0.0), writes=[b_Vs])
        S.op("pool", lambda e: e.memset(Vw[:], 0.0), writes=[b_Vw])
        S.op("pool", lambda e: e.memset(Vs[:, :, :, 64:65], 1.0), writes=[b_Vs])
        S.op("pool", lambda e: e.memset(Vw[:, :, :, 64:65], 1.0), writes=[b_Vw])
        vtv = D["vtok"].rearrange("(kt p) (w h d) -> p kt w h d", p=128, w=2, h=2)
        for h in range(2):
            S.dma("sp", Vs[:, :, h, 0:64], vtv[:, :, 0, h, :], writes=[b_Vs])
            S.dma("sp", Vw[:, :, h, 0:64], vtv[:, :, 1, h, :], writes=[b_Vw])
        Qg = [sbf(f"Qg{i}", [64, 4, 128], BF16) for i in range(2)]; b_Qg = [Buf() for _ in range(2)]
        gt = [sbf(f"ngt{i}", [128, 24]) for i in range(2)]; b_gt = [Buf() for _ in range(2)]
        Pc = [sbf(f"Pc{i}", [128, 512], BF16) for i in range(max(NTC, 1))]; b_Pc = [Buf() for _ in range(max(NTC, 1))]
        NP = 3
        Pb = [sbf(f"Pb{i}", [128, 512], BF16) for i in range(NP)]; b_Pb = [Buf() for _ in range(NP)]
        zz = sbf("zz", [128, 3, 4]); b_zz = Buf()
        coef = sbf("coef", [128, 3, 4]); b_coef = Buf()
        impS = sbf("impS", [128, 128]); b_impS = Buf()
        imp2 = sbf("imp2", [128, 128]); b_imp2 = Buf()
        mx = sbf("mx", [128, 16]); b_mx = Buf()
        thr = sbf("thr", [128, 1]); b_thr = Buf()
        MBf = sbf("MBf", [128, 128]); b_MBf = Buf()
        MBb = sbf("MBb", [128, 128], BF16); b_MBb = Buf()
        MBT4 = sbf("MBT4", [128, 4, 128], BF16); b_MBT4 = Buf()
        yc = [sbf(f"yc{i}", [128, 512]) for i in range(2)]; b_yc = [Buf() for _ in range(2)]
        ycT = [sbf(f"ycT{i}", [128, 4, 128], BF16) for i in range(2)]; b_ycT = [Buf() for _ in range(2)]
        stc = 0
        pbc = 0
        qc = 0
        qv = D["qT"].rearrange("(hq d) t -> d hq t", d=64)
        ymv = D["ymixT"][512:1024, :].rearrange("(c p) t -> p c t", p=128)
        aS = sbf("aS", [65, 512]); b_aS = Buf()
        R0 = [sbf(f"R0_{i}", [128, 512], BF16) for i in range(2)]; b_R0 = [Buf() for _ in range(2)]
        R1 = [sbf(f"R1_{i}", [128, 512], BF16) for i in range(2)]; b_R1 = [Buf() for _ in range(2)]

        def score_tile(KT, bK, h, kt, Q, bQ, extra):
            nonlocal stc
            si = stc % 3; stc += 1
            n_mm = 1 + len(extra)
            rhs_ap = Q[:].rearrange("d g q -> d (g q)") if len(Q.shape) == 3 else Q[:]
            S.op("pe", lambda e: e.matmul(stp[si][:], lhsT=KT[:, h, kt * 128:(kt + 1) * 128], rhs=rhs_ap, start=True, stop=(n_mm == 1)),
                 reads=[bK, bQ], writes=[b_stp[si]])
            for i_, (la, ra, bufs) in enumerate(extra):
                S.op("pe", lambda e: e.matmul(stp[si][:], lhsT=la, rhs=ra, start=False, stop=(i_ == len(extra) - 1)),
                     reads=bufs, writes=[b_stp[si]])
            return si

        def run_branch(br, tiles, h, Q, bQ, V, bV, Pbufs=None, after_exp=None):
            nonlocal pbc
            n = len(tiles)
            tiles = [tl if len(tl) == 6 else tl + (Q, bQ) for tl in tiles]
            DEPTH = 2
            issued = []
            nxt = 0
            for i_ in range(n):
                while nxt < n and nxt <= i_ + DEPTH - 1 + (0 if i_ else 0):
                    KT2, bK2, kt2, extra2, Qx2, bQx2 = tiles[nxt]
                    issued.append(score_tile(KT2, bK2, h, kt2, Qx2, bQx2, extra2))
                    nxt += 1
                si = issued[i_]
                kt = tiles[i_][2]
                if Pbufs is None:
                    pi = pbc % NP; pbc += 1
                    P, bP = Pb[pi], b_Pb[pi]
                else:
                    P, bP = Pbufs[i_]
                S.op("act", lambda e: e.activation(out=P[:], in_=stp[si][:], func=AF.Exp, scale=0.125), reads=[b_stp[si]], writes=[bP])
                if nxt < n:
                    KT2, bK2, kt2, extra2, Qx2, bQx2 = tiles[nxt]
                    issued.append(score_tile(KT2, bK2, h, kt2, Qx2, bQx2, extra2))
                    nxt += 1
                S.op("pe", lambda e: e.matmul(acc[br][:, :], lhsT=V[:, kt, h, :], rhs=P[:], start=(i_ == 0), stop=(i_ == n - 1)),
                     reads=[bP, bV], writes=[b_acc[br]])
                if after_exp is not None:
                    after_exp(i_, P, bP)

        def combine(br, h, yi):
            cp(S, "act", aS[:], acc[br][0:65, :], [b_acc[br]], [b_aS])
            for gq in range(4):
                S.op("pe", lambda e: e.transpose(out=msc[:, gq * 65:(gq + 1) * 65], in_=aS[0:65, gq * 128:(gq + 1) * 128], identity=identf[0:65, 0:65]),
                     reads=[b_aS, b_idf], writes=[b_msc])
            a3 = msc[:, 0:260].rearrange("p (g c) -> p g c", c=65)
            g3 = gt[yi][:].rearrange("p (hg b) -> p hg b", b=3)
            S.op("dve", lambda e: e.tensor_scalar(out=zz[:, br, :], in0=a3[:, :, 64], scalar1=1e-30, scalar2=None, op0=ALU.max), reads=[b_msc], writes=[b_zz])
            S.op("dve", lambda e: e.reciprocal(out=zz[:, br, :], in_=zz[:, br, :]), reads=[b_zz], writes=[b_zz])
            S.op("dve", lambda e: e.tensor_tensor(out=coef[:, br, :], in0=zz[:, br, :], in1=g3[:, h * 4:(h + 1) * 4, br], op=ALU.mult), reads=[b_zz, b_gt[yi]], writes=[b_coef])
            for gq in range(4):
                o_ = yc[yi][:, (h * 4 + gq) * 64:(h * 4 + gq + 1) * 64]
                if br == 0:
                    S.op("dve", lambda e: e.tensor_scalar(out=o_, in0=a3[:, gq, 0:64], scalar1=coef[:, br, gq:gq + 1], scalar2=None, op0=ALU.mult),
                         reads=[b_msc, b_coef], writes=[b_yc[yi]])
                else:
                    S.op("dve", lambda e: e.scalar_tensor_tensor(out=o_, in0=a3[:, gq, 0:64], scalar=coef[:, br, gq:gq + 1], in1=o_, op0=ALU.mult, op1=ALU.add),
                         reads=[b_msc, b_coef, b_yc[yi]], writes=[b_yc[yi]])

        items = [(qb, h) for qb in range(NQB) for h in range(2)]
        qis = {}

        def stage1(qb, h):
            nonlocal qc
            q0 = qb * 128
            yi = qb % 2
            if h == 0:
                S.dma("sp", gt[yi][:], D["gates"][q0:q0 + 128, :], writes=[b_gt[yi]])
            qi = qc % 2; qc += 1
            qis[(qb, h)] = qi
            Q, bQ = Qg[qi], b_Qg[qi]
            S.dma("sp", Q[:], qv[:, h * 4:(h + 1) * 4, q0:q0 + 128], writes=[bQ])
            S.dma("sp", R0[qi][64:128, :].rearrange("d (g q) -> d g q", g=4), qv[:, h * 4:(h + 1) * 4, q0:q0 + 128], writes=[b_R0[qi]])
            if qb >= 32:
                S.dma("sp", R1[qi][0:64, :].rearrange("d (g q) -> d g q", g=4), qv[:, h * 4:(h + 1) * 4, q0:q0 + 128], writes=[b_R1[qi]])
            ntc = min(NTC, (8 * qb + 6) // 128 + 1)
            S.op("dve", lambda e: e.memset(imp[:], 0.0), writes=[b_imp])
            tiles = []
            for nt in range(ntc):
                delta = 128 * qb - 2048 * nt
                extra = []
                if delta < 2064:
                    extra.append((identb[:], masks[:, delta // 128, :], [b_idb, b_masks]))
                tiles.append((KcT, b_Kc, nt, extra))

            def imp_mm(i_, P, bP):
                for gq in range(4):
                    S.op("pe", lambda e: e.matmul(imp[:, gq * 128:(gq + 1) * 128], lhsT=P[:, gq * 128:(gq + 1) * 128], rhs=MCS[:, i_, :], start=False, stop=(i_ == ntc - 1), **SK),
                         reads=[bP, b_mcs], writes=[b_imp])
            run_branch(0, tiles, h, Q, bQ, Vc, b_Vc, Pbufs=[(Pc[i_], b_Pc[i_]) for i_ in range(ntc)], after_exp=imp_mm)
            combine(0, h, yi)
            S.op("dve", lambda e: e.tensor_scalar(out=impS[:], in0=imp[:, 0:128], scalar1=zz[:, 0, 0:1], scalar2=None, op0=ALU.mult), reads=[b_imp, b_zz], writes=[b_impS])
            for gq in range(1, 4):
                S.op("dve", lambda e: e.scalar_tensor_tensor(out=impS[:], in0=imp[:, gq * 128:(gq + 1) * 128], scalar=zz[:, 0, gq:gq + 1], in1=impS[:], op0=ALU.mult, op1=ALU.add),
                     reads=[b_imp, b_zz, b_impS], writes=[b_impS])
            c0 = 126 - 2 * qb
            S.op("dve", lambda e: e.tensor_tensor(out=impS[:], in0=impS[:], in1=keepw[:, c0:c0 + 128], op=ALU.mult), reads=[b_impS, b_kw_], writes=[b_impS])
            S.op("dve", lambda e: e.tensor_tensor(out=impS[:], in0=impS[:], in1=addw[:, c0:c0 + 128], op=ALU.add), reads=[b_impS, b_aw], writes=[b_impS])
            S.op("dve", lambda e: e.memset(impS[:, 0:1], 1.0e9), writes=[b_impS])
            S.op("dve", lambda e: e.max(out=mx[:, 0:8], in_=impS[:]), reads=[b_impS], writes=[b_mx])
            S.op("dve", lambda e: e.match_replace(out=imp2[:], in_to_replace=mx[:, 0:8], in_values=impS[:], imm_value=-3.0e38), reads=[b_mx, b_impS], writes=[b_imp2])
            S.op("dve", lambda e: e.max(out=mx[:, 8:16], in_=imp2[:]), reads=[b_imp2], writes=[b_mx])
            S.op("dve", lambda e: e.tensor_reduce(out=thr[:], in_=mx[:, 8:16], axis=AX.X, op=ALU.min), reads=[b_mx], writes=[b_thr])
            S.op("dve", lambda e: e.tensor_scalar(out=MBf[:], in0=impS[:], scalar1=thr[:, 0:1], scalar2=None, op0=ALU.is_ge), reads=[b_impS, b_thr], writes=[b_MBf])
            S.op("dve", lambda e: e.tensor_scalar(out=MBb[:], in0=MBf[:], scalar1=1.0, scalar2=BIG, op0=ALU.subtract, op1=ALU.mult), reads=[b_MBf], writes=[b_MBb])
            S.op("pe", lambda e: e.transpose(out=mscb[:], in_=MBb[:], identity=identb[:]), reads=[b_MBb, b_idb], writes=[b_mscb])
            for gq in range(4):
                cp(S, "act" if gq % 2 else "dve", R0[qi][0:64, gq * 128:(gq + 1) * 128], mscb[0:64, :], [b_mscb], [b_R0[qi]])
                if qb >= 32:
                    cp(S, "dve" if gq % 2 else "act", R1[qi][64:128, gq * 128:(gq + 1) * 128], mscb[64:128, :], [b_mscb], [b_R1[qi]])

        def stage2(qb, h):
            q0 = qb * 128
            yi = qb % 2
            qi = qis[(qb, h)]
            Q, bQ = Qg[qi], b_Qg[qi]
            tiles = []
            for kt in range(max(0, qb - 4), qb + 1):
                extra = []
                if kt == qb:
                    extra.append((identb[:], masks[:, 17, :], [b_idb, b_masks]))
                elif kt == qb - 4:
                    extra.append((identb[:], masks[:, 18, :], [b_idb, b_masks]))
                tiles.append((KwT, b_Kw, kt, extra))
            run_branch(2, tiles, h, Q, bQ, Vw, b_Vw)
            combine(2, h, yi)
            tiles = []
            for kt in range(qb + 1):
                extra = []
                if kt == qb:
                    extra.append((identb[:], masks[:, 17, :], [b_idb, b_masks]))
                if kt < 32:
                    tiles.append((LH, b_LH, kt, extra, R0[qi], b_R0[qi]))
                else:
                    tiles.append((LH, b_LH, kt, extra, R1[qi], b_R1[qi]))
            run_branch(1, tiles, h, Q, bQ, Vs, b_Vs)
            combine(1, h, yi)
            if h == 1:
                for c in range(4):
                    S.op("pe", lambda e: e.transpose(out=msc[:, c * 128:(c + 1) * 128], in_=yc[yi][:, c * 128:(c + 1) * 128], identity=identf[:]),
                         reads=[b_yc[yi], b_idf], writes=[b_msc])
                cp(S, "act", ycT[yi][:].rearrange("p c q -> p (c q)"), msc[:], [b_msc], [b_ycT[yi]])
                S.dma("sp", ymv[:, :, q0:q0 + 128], ycT[yi][:], reads=[b_ycT[yi]])

        stage1(*items[0])
        for n_ in range(len(items)):
            if n_ + 1 < len(items):
                stage1(*items[n_ + 1])
            stage2(*items[n_])
        S.barrier()


def rwkv_consts():
    f = np.float32
    c = {}
    hs = np.arange(128) // 64
    tt = np.arange(128) % 64
    same = hs[:, None] == hs[None, :]
    c["rw_msu"] = (same & (tt[:, None] < tt[None, :])).astype(f)
    c["rw_mu"] = (same & (tt[:, None] <= tt[None, :])).astype(f)
    c["rw_msl"] = (same & (tt[:, None] > tt[None, :])).astype(f)
    il = np.zeros((64, 128), f); il[np.arange(64), np.arange(64)] = 1
    ir = np.zeros((64, 128), f); ir[np.arange(64), 64 + np.arange(64)] = 1
    c["rw_il"] = il
    c["rw_ir"] = ir
    return c


def phase_rwkv(g, l):
    nc, S, D, T = g.nc, g.S, g.D, g.T
    TB = 256
    NCH = TB // 64
    SK = dict(skip_group_check=True)
    with ExitStack() as st:
        sbf = lambda name, shape, dt=F32: st.enter_context(nc.sbuf_tensor(g.nm(name), list(shape), dt))
        psf = lambda name, shape, dt=F32: st.enter_context(nc.psum_tensor(g.nm(name), list(shape), dt))
        NB = 8
        bank = [psf(f"rb{i}", [128, 512]) for i in range(NB)]; b_bank = [Buf(excl=True) for _ in range(NB)]
        bctr = [0]

        busy = [False] * NB
        F32R = mybir.dt.float32r
        MT = F32R if g.use_f32r else F32
        RR = lambda ap: ap
        AS32 = (lambda ap: ap.bitcast(F32)) if g.use_f32r else (lambda ap: ap)

        def nb():
            for k_ in range(NB):
                i = (bctr[0] + k_) % NB
                if not busy[i]:
                    bctr[0] = i + 1
                    busy[i] = True
                    return bank[i], b_bank[i]
            raise AssertionError("rwkv: no free PSUM bank (too many live tiles across a yield)")

        def rel(bbuf):
            busy[b_bank.index(bbuf)] = False

        def nbx():
            i = bctr[0] % NB
            bctr[0] += 1
            return bank[i], b_bank[i]
        def const(name, shape, src):
            t_ = sbf(name, shape); b_ = Buf()
            S.dma("sp", t_[:], src, writes=[b_])
            return t_, b_
        msu, b_msu = const("msu", [128, 128], D["rw_msu"])
        mu_, b_mu = const("mu", [128, 128], D["rw_mu"])
        msl, b_msl = const("msl", [128, 128], D["rw_msl"])
        il32, b_il32 = const("il32", [64, 128], D["rw_il"])
        ir32, b_ir32 = const("ir32", [64, 128], D["rw_ir"])
        idf, b_idf = const("idf", [128, 128], D["identf"])
        il = sbf("il", [64, 128], MT); b_il = Buf()
        ir = sbf("ir", [64, 128], MT); b_ir = Buf()
        cp(S, "dve", il[:], il32[:], [b_il32], [b_il])
        cp(S, "dve", ir[:], ir32[:], [b_ir32], [b_ir])
        rp, b_rp = const("rp", [128, 64], D["rwp"][l])
        wup32, b_wup32 = const("wup32", [64, 256], D["rw_w_up"][l])
        aup32, b_aup32 = const("aup32", [64, 256], D["rw_a_up"][l])
        gup32, b_gup32 = const("gup32", [128, 256], D["rw_g_up"][l])
        wup = sbf("wup", [64, 256], MT); b_wup = Buf()
        aup = sbf("aup", [64, 256], MT); b_aup = Buf()
        gup = sbf("gup", [128, 256], MT); b_gup = Buf()
        cp(S, "dve", wup[:], wup32[:], [b_wup32], [b_wup])
        cp(S, "dve", aup[:], aup32[:], [b_aup32], [b_aup])
        cp(S, "dve", gup[:], gup32[:], [b_gup32], [b_gup])
        ones32 = sbf("ones32", [64, 64]); b_o32 = Buf()
        S.op("pool", lambda e: e.memset(ones32[:], 1.0), writes=[b_o32])
        ones64 = sbf("ones64", [64, 64], MT); b_o64 = Buf()
        cp(S, "dve", ones64[:], ones32[:], [b_o32], [b_o64])
        omka = sbf("omka", [64, 4]); b_omka = Buf()
        S.op("dve", lambda e: e.tensor_scalar(out=omka[:], in0=rp[0:64, 28:32], scalar1=-1.0, scalar2=1.0, op0=ALU.mult, op1=ALU.add), reads=[b_rp], writes=[b_omka])
        cst = sbf("rcst", [128, 2]); b_cst = Buf()
        S.op("pool", lambda e: e.memset(cst[:, 0:1], 64e-5), writes=[b_cst])
        ST = [[sbf(f"ST{p}_{i}", [128, 64], MT) for i in range(2)] for p in range(2)]
        b_ST = [[Buf() for i in range(2)] for p in range(2)]
        for p in range(2):
            S.op("pool", lambda e: e.memset(AS32(ST[p][0][:]), 0.0), writes=[b_ST[p][0]])
        sidx = [0, 0]
        def arr(name, shape=None, dt=F32):
            return sbf(name, shape or [64, 4, TB], dt), Buf()
        Z3, b_Z3 = arr("Z3", [64, 12, TB + 1])
        ZL, b_ZL = arr("ZL", [64, 2, TB + 1])
        ZG, b_ZG = arr("ZG", [128, TB + 1])
        X3, b_X3 = arr("X3", [64, 12, TB])
        XL, b_XL = arr("XL", [64, 2, TB], MT)
        XG, b_XG = arr("XG", [128, TB], MT)
        Dt, b_Dt = arr("Dt", [128, 12, TB])
        lw, b_lw = arr("lw")
        cl2, b_cl2 = arr("cl2")
        aa, b_aa = arr("aa")
        kkn, b_kkn = arr("kkn")
        tmp, b_tmp = arr("tmp", None, MT)
        kfin, b_kfin = arr("kfin")
        epos, b_epos = arr("epos")
        eneg, b_eneg = arr("eneg")
        eprev, b_eprev = arr("eprev")
        eC, b_eC = arr("eC")
        AR, b_AR = arr("AR", [64, NCH, 2, 2, 2, 64], MT)
        Bt, b_Bt = arr("Bt", [64, NCH, 4, 64], MT)
        Kt, b_Kt = arr("Kt", [64, NCH, 4, 64], MT)
        Bh, b_Bh = arr("Bh", [64, NCH, 4, 64])
        Kh, b_Kh = arr("Kh", [64, NCH, 4, 64])
        Vc_, b_Vc_ = arr("Vcm", [64, NCH, 4, 64])
        hm = lambda a: a[:].rearrange("k h (c t) -> k h c t", t=64)
        cm = lambda a: a[:].rearrange("k c h t -> k h c t")
        arv = lambda ty: AR[:, :, :, ty, :, :].rearrange("k c p hh t -> k p hh c t")
        hm5 = lambda a: a[:].rearrange("k (p hh) (c t) -> k p hh c t", hh=2, t=64)
        bv, b_bv = arr("bv")
        gT, b_gT = arr("gT")
        YN, b_YN = arr("YN")
        PCf, b_PCf = arr("PCf", [64, 4, NCH], MT)
        PCc = sbf("PCc", [128, 2, NCH]); b_PCc = Buf()
        yo = sbf("yo", [64, 4, TB], BF16); b_yo = Buf()
        NTMP = 100
        tm = [sbf(f"tm{i}", [128, 128], MT) for i in range(NTMP)]; b_tm = [Buf() for _ in range(NTMP)]
        NTF = 16
        tf = [sbf(f"tf{i}", [128, 128]) for i in range(NTF)]; b_tf = [Buf() for _ in range(NTF)]
        fctr = [0]

        def ntf_():
            i = fctr[0] % NTF
            fctr[0] += 1
            return tf[i], b_tf[i]
        tctr = [0]

        def nt_():
            i = tctr[0] % NTMP
            tctr[0] += 1
            return tm[i], b_tm[i]
        NBD = 5
        bd = [[sbf(f"bd{k_}_{i}", [128, 128], MT) for i in range(NBD)] for k_ in range(3)]
        b_bd = [[Buf() for i in range(NBD)] for k_ in range(3)]
        for k_ in range(3):
            for i in range(NBD):
                S.op("pool", lambda e: e.memset(AS32(bd[k_][i][:]), 0.0), writes=[b_bd[k_][i]])
        bdc = [0]
        zav = D["zaT"]
        ev_ctr = [0]

        def evac(out, in_, reads, writes):
            ek = "act" if ev_ctr[0] % 2 == 0 else "dve"
            ev_ctr[0] += 1
            cp(S, ek, out, in_, reads, writes)

        for tt in range(T // TB):
            t0 = tt * TB
            if tt == 0:
                S.op("pool", lambda e: e.memset(Z3[:, :, 0:1], 0.0), writes=[b_Z3])
                S.op("pool", lambda e: e.memset(ZL[:, :, 0:1], 0.0), writes=[b_ZL])
                S.op("pool", lambda e: e.memset(ZG[:, 0:1], 0.0), writes=[b_ZG])
                S.dma("sp", Z3[:, :, 1:TB + 1], zav[0:768, 0:TB].rearrange("(gh k) t -> k gh t", k=64), writes=[b_Z3])
                S.dma("sp", ZL[:, :, 1:TB + 1], zav[768:896, 0:TB].rearrange("(g j) t -> j g t", j=64), writes=[b_ZL])
                S.dma("sp", ZG[:, 1:TB + 1], zav[896:1024, 0:TB], writes=[b_ZG])
            else:
                S.dma("sp", Z3[:], zav[0:768, t0 - 1:t0 + TB].rearrange("(gh k) t -> k gh t", k=64), writes=[b_Z3])
                S.dma("sp", ZL[:], zav[768:896, t0 - 1:t0 + TB].rearrange("(g j) t -> j g t", j=64), writes=[b_ZL])
                S.dma("sp", ZG[:], zav[896:1024, t0 - 1:t0 + TB], writes=[b_ZG])
            S.op("dve", lambda e: e.tensor_tensor(out=Dt[0:64, :, :], in0=Z3[:, :, 0:TB], in1=Z3[:, :, 1:TB + 1], op=ALU.subtract), reads=[b_Z3], writes=[b_Dt])
            for j in range(12):
                if j % 3 == 2:
                    S.op("act", lambda e: e.activation(out=Dt[0:64, j, :], in_=Dt[0:64, j, :], func=AF.Copy, scale=rp[0:64, j:j + 1]), reads=[b_Dt, b_rp], writes=[b_Dt])
                else:
                    S.op("dve", lambda e: e.tensor_scalar(out=Dt[0:64, j, :], in0=Dt[0:64, j, :], scalar1=rp[0:64, j:j + 1], scalar2=None, op0=ALU.mult), reads=[b_Dt, b_rp], writes=[b_Dt])
            S.op("dve", lambda e: e.tensor_tensor(out=X3[:], in0=Dt[0:64, :, :], in1=Z3[:, :, 1:TB + 1], op=ALU.add), reads=[b_Dt, b_Z3], writes=[b_X3])
            S.op("dve", lambda e: e.tensor_tensor(out=Dt[0:64, 0:2, :], in0=ZL[:, :, 0:TB], in1=ZL[:, :, 1:TB + 1], op=ALU.subtract), reads=[b_ZL], writes=[b_Dt])
            for j in range(2):
                S.op("dve", lambda e: e.scalar_tensor_tensor(out=XL[:, j, :], in0=Dt[0:64, j, :], scalar=rp[0:64, 12 + j:13 + j], in1=ZL[:, j, 1:TB + 1], op0=ALU.mult, op1=ALU.add),
                     reads=[b_Dt, b_rp, b_ZL], writes=[b_XL])
            S.op("dve", lambda e: e.tensor_tensor(out=Dt[:, 2, :], in0=ZG[:, 0:TB], in1=ZG[:, 1:TB + 1], op=ALU.subtract), reads=[b_ZG], writes=[b_Dt])
            S.op("dve", lambda e: e.scalar_tensor_tensor(out=XG[:], in0=Dt[:, 2, :], scalar=rp[:, 14:15], in1=ZG[:, 1:TB + 1], op0=ALU.mult, op1=ALU.add),
                 reads=[b_Dt, b_rp, b_ZG], writes=[b_XG])
            r_ = lambda h: X3[:, h, :]
            k_ = lambda h: X3[:, 4 + h, :]
            v_ = lambda h: X3[:, 8 + h, :]
            S.op("act", lambda e: e.activation(out=XL[:, 0, :], in_=XL[:, 0, :], func=AF.Tanh), reads=[b_XL], writes=[b_XL])
            S.op("act", lambda e: e.activation(out=XG[:], in_=XG[:], func=AF.Sigmoid), reads=[b_XG], writes=[b_XG])
            for h in range(4):
                pb, bpb = nbx()
                S.op("pe", lambda e: e.matmul(pb[0:64, 0:TB], lhsT=wup[:, h * 64:(h + 1) * 64], rhs=XL[:, 0, :], start=True, stop=True), reads=[b_wup, b_XL], writes=[bpb])
                S.op("act", lambda e: e.activation(out=lw[:, h, :], in_=pb[0:64, 0:TB], func=AF.Sigmoid, bias=rp[0:64, 16 + h:17 + h], scale=1.0), reads=[bpb, b_rp], writes=[b_lw])
                pb, bpb = nbx()
                S.op("pe", lambda e: e.matmul(pb[0:64, 0:TB], lhsT=aup[:, h * 64:(h + 1) * 64], rhs=XL[:, 1, :], start=True, stop=True), reads=[b_aup, b_XL], writes=[bpb])
                S.op("act", lambda e: e.activation(out=aa[:, h, :], in_=pb[0:64, 0:TB], func=AF.Sigmoid, bias=rp[0:64, 20 + h:21 + h], scale=1.0), reads=[bpb, b_rp], writes=[b_aa])
                pb, bpb = nbx()
                S.op("pe", lambda e: e.matmul(pb[0:64, 0:TB], lhsT=gup[:, h * 64:(h + 1) * 64], rhs=XG[:], start=True, stop=True), reads=[b_gup, b_XG], writes=[bpb])
                evac(gT[:, h, :], pb[0:64, 0:TB], [bpb], [b_gT])
            S.op("dve", lambda e: e.tensor_scalar(out=lw[:], in0=lw[:], scalar1=-0.6065306597126334, scalar2=None, op0=ALU.mult), reads=[b_lw], writes=[b_lw])
            for h in range(4):
                S.op("dve", lambda e: e.tensor_scalar(out=kkn[:, h, :], in0=k_(h), scalar1=rp[0:64, 24 + h:25 + h], scalar2=None, op0=ALU.mult), reads=[b_X3, b_rp], writes=[b_kkn])
            S.op("act", lambda e: e.activation(out=tmp[:], in_=kkn[:], func=AF.Square), reads=[b_kkn], writes=[b_tmp])
            for h in range(4):
                pb, bpb = nbx()
                S.op("pe", lambda e: e.matmul(pb[0:64, 0:TB], lhsT=ones64[:], rhs=tmp[:, h, :], start=True, stop=True), reads=[b_o64, b_tmp], writes=[bpb])
                S.op("act", lambda e: e.activation(out=eC[:, h, :], in_=pb[0:64, 0:TB], func=AF.Sqrt), reads=[bpb], writes=[b_eC])
            S.op("dve", lambda e: e.tensor_scalar(out=eC[:], in0=eC[:], scalar1=1e-12, scalar2=None, op0=ALU.max), reads=[b_eC], writes=[b_eC])
            S.op("dve", lambda e: e.reciprocal(out=eC[:], in_=eC[:]), reads=[b_eC], writes=[b_eC])
            S.op("dve", lambda e: e.tensor_tensor(out=kkn[:], in0=kkn[:], in1=eC[:], op=ALU.mult), reads=[b_kkn, b_eC], writes=[b_kkn])
            for h in range(4):
                S.op("dve", lambda e: e.tensor_scalar(out=tmp[:, h, :], in0=aa[:, h, :], scalar1=rp[0:64, 28 + h:29 + h], scalar2=omka[:, h:h + 1], op0=ALU.mult, op1=ALU.add),
                     reads=[b_aa, b_rp, b_omka], writes=[b_tmp])
            S.op("dve", lambda e: e.tensor_tensor(out=kfin[:], in0=X3[:, 4:8, :], in1=tmp[:], op=ALU.mult), reads=[b_X3, b_tmp], writes=[b_kfin])
            for h in range(4):
                S.op("dve", lambda e: e.scalar_tensor_tensor(out=tmp[:, h, :], in0=r_(h), scalar=rp[0:64, 32 + h:33 + h], in1=kfin[:, h, :], op0=ALU.mult, op1=ALU.mult),
                     reads=[b_X3, b_kfin, b_rp], writes=[b_tmp])
                pb, bpb = nbx()
                S.op("pe", lambda e: e.matmul(pb[0:64, 0:TB], lhsT=ones64[:], rhs=tmp[:, h, :], start=True, stop=True), reads=[b_o64, b_tmp], writes=[bpb])
                S.op("dve", lambda e: e.tensor_tensor(out=bv[:, h, :], in0=pb[0:64, 0:TB], in1=v_(h), op=ALU.mult), reads=[bpb, b_X3], writes=[b_bv])
            src, bsrc, dst, bdst = lw, b_lw, cl2, b_cl2
            cp(S, "act", eprev[:], lw[:], [b_lw], [b_eprev])
            for sft in (1, 2, 4, 8, 16, 32):
                s5 = src[:].rearrange("k h (c t) -> k h c t", t=64)
                d5 = dst[:].rearrange("k h (c t) -> k h c t", t=64)
                S.op("dve", lambda e: e.tensor_tensor(out=d5[:, :, :, sft:64], in0=s5[:, :, :, sft:64], in1=s5[:, :, :, 0:64 - sft], op=ALU.add), reads=[bsrc], writes=[bdst])
                cp(S, "act", d5[:, :, :, 0:sft], s5[:, :, :, 0:sft], [bsrc], [bdst])
                src, bsrc, dst, bdst = dst, bdst, src, bsrc
            cl, b_cl = src, bsrc
            S.op("act", lambda e: e.activation(out=epos[:], in_=cl[:], func=AF.Exp), reads=[b_cl], writes=[b_epos])
            S.op("act", lambda e: e.activation(out=eneg[:], in_=cl[:], func=AF.Exp, scale=-1.0), reads=[b_cl], writes=[b_eneg])
            S.op("dve", lambda e: e.tensor_tensor(out=eprev[:], in0=cl[:], in1=eprev[:], op=ALU.subtract), reads=[b_cl, b_eprev], writes=[b_eprev])
            S.op("act", lambda e: e.activation(out=eprev[:], in_=eprev[:], func=AF.Exp), reads=[b_eprev], writes=[b_eprev])
            ep5 = epos[:].rearrange("k h (c t) -> k h c t", t=64)
            S.op("dve", lambda e: e.tensor_copy(out=PCf[:], in_=ep5[:, :, :, 63]), reads=[b_epos], writes=[b_PCf])
            en5 = eneg[:].rearrange("k h (c t) -> k h c t", t=64)
            ec5 = eC[:].rearrange("k h (c t) -> k h c t", t=64)
            for h in range(4):
                for c in range(NCH):
                    if (h + c) % 2:
                        S.op("dve", lambda e: e.tensor_scalar(out=ec5[:, h, c, :], in0=en5[:, h, c, :], scalar1=PCf[:, h, c:c + 1], scalar2=None, op0=ALU.mult),
                             reads=[b_eneg, b_PCf], writes=[b_eC])
                    else:
                        S.op("act", lambda e: e.activation(out=ec5[:, h, c, :], in_=en5[:, h, c, :], func=AF.Copy, scale=AS32(PCf[:, h, c:c + 1])),
                             reads=[b_eneg, b_PCf], writes=[b_eC])
            for h in range(4):
                S.op("dve", lambda e: e.scalar_tensor_tensor(out=AR[:, :, h // 2, 0, h % 2, :], in0=hm(kkn)[:, h], scalar=-1.0, in1=hm(eprev)[:, h], op0=ALU.mult, op1=ALU.mult), reads=[b_kkn, b_eprev], writes=[b_AR])
                S.op("dve", lambda e: e.tensor_tensor(out=AR[:, :, h // 2, 1, h % 2, :], in0=X3[:, h, :].rearrange("k (c t) -> k c t", t=64), in1=hm(epos)[:, h], op=ALU.mult), reads=[b_X3, b_epos], writes=[b_AR])
            S.op("dve", lambda e: e.tensor_tensor(out=tmp[:], in0=kkn[:], in1=aa[:], op=ALU.mult), reads=[b_kkn, b_aa], writes=[b_tmp])
            for h in range(4):
                S.op("dve", lambda e: e.tensor_tensor(out=cm(Bt)[:, h], in0=hm(tmp)[:, h], in1=hm(eneg)[:, h], op=ALU.mult), reads=[b_tmp, b_eneg], writes=[b_Bt])
                S.op("pool", lambda e: e.tensor_tensor(out=cm(Bh)[:, h], in0=hm(tmp)[:, h], in1=hm(eC)[:, h], op=ALU.mult), reads=[b_tmp, b_eC], writes=[b_Bh])
                S.op("dve", lambda e: e.tensor_tensor(out=cm(Kt)[:, h], in0=hm(kfin)[:, h], in1=hm(eneg)[:, h], op=ALU.mult), reads=[b_kfin, b_eneg], writes=[b_Kt])
                S.op("pool", lambda e: e.tensor_tensor(out=cm(Kh)[:, h], in0=hm(kfin)[:, h], in1=hm(eC)[:, h], op=ALU.mult), reads=[b_kfin, b_eC], writes=[b_Kh])
                cp(S, "act", cm(Vc_)[:, h], X3[:, 8 + h, :].rearrange("k (c t) -> k c t", t=64), [b_X3], [b_Vc_])
            for p in range(2):
                pb, bpb = nbx()
                S.op("pe", lambda e: e.matmul(pb[:, 0:NCH], lhsT=il[:], rhs=PCf[:, 2 * p, :], start=True, stop=False), reads=[b_il, b_PCf], writes=[bpb])
                S.op("pe", lambda e: e.matmul(pb[:, 0:NCH], lhsT=ir[:], rhs=PCf[:, 2 * p + 1, :], start=False, stop=True), reads=[b_ir, b_PCf], writes=[bpb])
                evac(PCc[:, p, :], pb[:, 0:NCH], [bpb], [b_PCc])
            if tt == 0:
                g.dbg("d_X3", X3[:], b_X3, [64, 12, TB]); g.dbg("d_cl", cl[:], b_cl, [64, 4, TB]); g.dbg("d_aa", aa[:], b_aa, [64, 4, TB])
                g.dbg("d_kkn", kkn[:], b_kkn, [64, 4, TB]); g.dbg("d_kfin", kfin[:], b_kfin, [64, 4, TB]); g.dbg("d_bv", bv[:], b_bv, [64, 4, TB])
                g.dbg("d_gT", gT[:], b_gT, [64, 4, TB]); g.dbg("d_eC", eC[:], b_eC, [64, 4, TB])
                g.dbg("d_PCc", PCc[:], b_PCc, [128, 2, NCH])
            def unit(c, p):
                tc = slice(c * 64, (c + 1) * 64)
                hp = slice(2 * p, 2 * p + 2)
                fl = lambda ap: ap.rearrange("k h t -> k (h t)")
                At_ = fl(AR[:, c, p, 0, :, :]); Bt_ = fl(Bt[:, c, hp, :]); Kt_ = fl(Kt[:, c, hp, :])
                ARp = AR[:, c, p, :, :, :].rearrange("k a h t -> k (a h t)")
                p1, bp1 = nb(); p2, bp2 = nb()
                S.op("pe", lambda e: e.matmul(p1[:, 0:256], lhsT=RR(Bt_), rhs=RR(ARp), start=True, stop=True), reads=[b_Bt, b_AR], writes=[bp1])
                S.op("pe", lambda e: e.matmul(p2[:, 0:256], lhsT=RR(Kt_), rhs=RR(ARp), start=True, stop=True), reads=[b_Kt, b_AR], writes=[bp2])
                yield
                N0, bN0 = nt_(); ArbT, bArbT = nt_(); AakT, bAakT = nt_(); ArkT, bArkT = nt_()
                S.op("dve", lambda e: e.tensor_tensor(out=N0[:], in0=p1[:, 0:128], in1=msu[:], op=ALU.mult), reads=[bp1, b_msu], writes=[bN0])
                S.op("dve", lambda e: e.tensor_tensor(out=ArbT[:], in0=p1[:, 128:256], in1=mu_[:], op=ALU.mult), reads=[bp1, b_mu], writes=[bArbT])
                rel(bp1)
                S.op("dve", lambda e: e.tensor_tensor(out=AakT[:], in0=p2[:, 0:128], in1=msu[:], op=ALU.mult), reads=[bp2, b_msu], writes=[bAakT])
                S.op("dve", lambda e: e.tensor_tensor(out=ArkT[:], in0=p2[:, 128:256], in1=mu_[:], op=ALU.mult), reads=[bp2, b_mu], writes=[bArkT])
                rel(bp2)
                p3, bp3 = nb()
                S.op("pe", lambda e: e.matmul(p3[:, 0:128], lhsT=RR(At_), rhs=RR(Bt_), start=True, stop=True), reads=[b_AR, b_Bt], writes=[bp3])
                ptr, bptr = nb()
                srcs = [(At_, b_AR), (fl(Vc_[:, c, hp, :]), b_Vc_), (fl(Bh[:, c, hp, :]), b_Bh), (fl(Kh[:, c, hp, :]), b_Kh)]
                for i_, (sap, sb_) in enumerate(srcs):
                    S.op("pe", lambda e: e.transpose(out=ptr[:, i_ * 64:(i_ + 1) * 64], in_=AS32(sap) if i_ == 0 else sap, identity=idf[0:64, 0:64]), reads=[sb_, b_idf], writes=[bptr])
                yield
                NT0, bNT0 = nt_()
                S.op("dve", lambda e: e.tensor_tensor(out=NT0[:], in0=p3[:, 0:128], in1=msl[:], op=ALU.mult), reads=[bp3, b_msl], writes=[bNT0])
                rel(bp3)
                Z, bZ = nt_()
                S.op("pool", lambda e: e.tensor_tensor(out=Z[:], in0=N0[:], in1=idf[:], op=ALU.add), reads=[bN0, b_idf], writes=[bZ])
                TA, bTA = nt_()
                Vt, bVt = nt_()
                cp(S, "act", TA[:, 0:64], ptr[:, 0:64], [bptr], [bTA])
                cp(S, "act", Vt[:, 0:64], ptr[:, 64:128], [bptr], [bVt])
                bi = bdc[0] % NBD; bdc[0] += 1
                Bbd, bBbd = bd[0][bi], b_bd[0][bi]
                Kbd, bKbd = bd[1][bi], b_bd[1][bi]
                Apb, bApb = bd[2][bi], b_bd[2][bi]
                for hh in range(2):
                    rs = slice(hh * 64, (hh + 1) * 64)
                    cp(S, "act", Bbd[rs, rs], ptr[rs, 128:192], [bptr], [bBbd])
                    cp(S, "act", Kbd[rs, rs], ptr[rs, 192:256], [bptr], [bKbd])
                rel(bptr)
                yield
                X, bX, XT, bXT = N0, bN0, NT0, bNT0
                pw, bpw = nb()
                S.op("pe", lambda e: e.matmul(pw[:, 0:64], lhsT=RR(AakT[:]), rhs=RR(Vt[:, 0:64]), start=True, stop=True), reads=[bAakT, bVt], writes=[bpw])
                yield
                cp(S, "act", TA[:, 64:128], pw[:, 0:64], [bpw], [bTA])
                rel(bpw)
                for j in range(1, 6):
                    if j <= 4:
                        px, bpx = nb()
                        S.op("pe", lambda e: e.matmul(px[:, 0:128], lhsT=RR(XT[:]), rhs=RR(X[:]), start=True, stop=True), reads=[bXT, bX], writes=[bpx])
                    pxt, bpxt = nb()
                    S.op("pe", lambda e: e.matmul(pxt[:, 0:128], lhsT=RR(X[:]), rhs=RR(XT[:]), start=True, stop=True), reads=[bXT, bX], writes=[bpxt])
                    yield
                    if j <= 4:
                        Xn, bXn = nt_()
                        cp(S, "act", Xn[:], px[:, 0:128], [bpx], [bXn])
                        rel(bpx)
                    XTn, bXTn = nt_()
                    cp(S, "act" if j > 4 else "dve", XTn[:], pxt[:, 0:128], [bpxt], [bXTn])
                    rel(bpxt)
                    pz, bpz = nb()
                    S.op("pe", lambda e: e.matmul(pz[:, 0:128], lhsT=RR(XTn[:]), rhs=RR(Z[:]), start=True, stop=True), reads=[bXTn, bZ], writes=[bpz])
                    yield
                    Zn, bZn = nt_()
                    S.op("dve", lambda e: e.tensor_tensor(out=Zn[:], in0=pz[:, 0:128], in1=Z[:], op=ALU.add), reads=[bpz, bZ], writes=[bZn])
                    rel(bpz)
                    Z, bZ = Zn, bZn
                    if j <= 4:
                        X, bX = Xn, bXn
                    XT, bXT = XTn, bXTn
                pu, bpu = nb()
                S.op("pe", lambda e: e.matmul(pu[:, 0:128], lhsT=RR(Z[:]), rhs=RR(TA[:]), start=True, stop=True), reads=[bZ, bTA], writes=[bpu])
                yield
                U0, bU0 = nt_()
                cp(S, "dve", U0[:, 0:64], pu[:, 64:128], [bpu], [bU0])
                for hh in range(2):
                    rs = slice(hh * 64, (hh + 1) * 64)
                    cp(S, "act", Apb[rs, rs], pu[rs, 0:64], [bpu], [bApb])
                rel(bpu)
                pg, bpg = nb()
                S.op("pe", lambda e: e.matmul(pg[:, 0:64], lhsT=RR(Bbd[:]), rhs=RR(U0[:, 0:64]), start=True, stop=False), reads=[bBbd, bU0], writes=[bpg])
                S.op("pe", lambda e: e.matmul(pg[:, 0:64], lhsT=RR(Kbd[:]), rhs=RR(Vt[:, 0:64]), start=False, stop=True), reads=[bKbd, bVt], writes=[bpg])
                pf, bpf = nb()
                S.op("pe", lambda e: e.matmul(pf[:, 0:128], lhsT=RR(Apb[:]), rhs=RR(Bbd[:]), start=True, stop=True), reads=[bApb, bBbd], writes=[bpf])
                yield
                Gs, bGs = ntf_()
                cp(S, "act", Gs[:, 0:64], pg[:, 0:64], [bpg], [bGs])
                rel(bpg)
                PhiT, bPhiT = nt_()
                S.op("dve", lambda e: e.scalar_tensor_tensor(out=PhiT[:], in0=idf[:], scalar=PCc[:, p, c:c + 1], in1=pf[:, 0:128], op0=ALU.mult, op1=ALU.add),
                     reads=[b_idf, b_PCc, bpf], writes=[bPhiT])
                rel(bpf)
                pr, bpr = nb()
                S.op("pe", lambda e: e.matmul(pr[:, 0:64], lhsT=RR(il[:]), rhs=RR(AR[:, c, p, 1, 0, :]), start=True, stop=False, **SK), reads=[b_il, b_AR], writes=[bpr])
                S.op("pe", lambda e: e.matmul(pr[:, 64:128], lhsT=RR(ir[:]), rhs=RR(AR[:, c, p, 1, 1, :]), start=False, stop=False, **SK), reads=[b_ir, b_AR], writes=[bpr])
                S.op("pe", lambda e: e.matmul(pr[:, 0:128], lhsT=RR(Apb[:]), rhs=RR(ArbT[:]), start=False, stop=True, **SK), reads=[bApb, bArbT], writes=[bpr])
                yield
                RpT, bRpT = nt_()
                cp(S, "act", RpT[:], pr[:, 0:128], [bpr], [bRpT])
                rel(bpr)
                Scur, bScur = ST[p][sidx[p]], b_ST[p][sidx[p]]
                py, bpy = nb()
                S.op("pe", lambda e: e.matmul(py[:, 0:64], lhsT=RR(ArbT[:]), rhs=RR(U0[:, 0:64]), start=True, stop=False), reads=[bArbT, bU0], writes=[bpy])
                S.op("pe", lambda e: e.matmul(py[:, 0:64], lhsT=RR(ArkT[:]), rhs=RR(Vt[:, 0:64]), start=False, stop=False), reads=[bArkT, bVt], writes=[bpy])
                S.op("pe", lambda e: e.matmul(py[:, 0:64], lhsT=RR(RpT[:]), rhs=RR(Scur[:]), start=False, stop=True), reads=[bRpT, bScur], writes=[bpy])
                ps_, bps = nb()
                S.op("pe", lambda e: e.matmul(ps_[:, 0:64], lhsT=RR(PhiT[:]), rhs=RR(Scur[:]), start=True, stop=True), reads=[bPhiT, bScur], writes=[bps])
                sidx[p] ^= 1
                Snew, bSnew = ST[p][sidx[p]], b_ST[p][sidx[p]]
                S.op("dve", lambda e: e.tensor_tensor(out=Snew[:], in0=ps_[:, 0:64], in1=Gs[:, 0:64], op=ALU.add), reads=[bps, bGs], writes=[bSnew])
                rel(bps)
                Yt, bYt = ntf_()
                st_, bst = ntf_()
                cp(S, "act", Yt[:, 0:64], py[:, 0:64], [bpy], [bYt])
                rel(bpy)
                yield
                S.op("dve", lambda e: e.bn_stats(out=st_[:, 0:6], in_=Yt[:, 0:64]), reads=[bYt], writes=[bst])
                yield
                S.op("dve", lambda e: e.bn_aggr(out=st_[:, 8:10], in_=st_[:, 0:6]), reads=[bst], writes=[bst])
                yield
                S.op("act", lambda e: e.activation(out=st_[:, 10:11], in_=st_[:, 9:10], func=AF.Sqrt, bias=cst[:, 0:1], scale=1.0), reads=[bst, b_cst], writes=[bst])
                yield
                S.op("dve", lambda e: e.reciprocal(out=st_[:, 11:12], in_=st_[:, 10:11]), reads=[bst], writes=[bst])
                yield
                S.op("dve", lambda e: e.tensor_scalar(out=Yt[:, 0:64], in0=Yt[:, 0:64], scalar1=st_[:, 8:9], scalar2=st_[:, 11:12], op0=ALU.subtract, op1=ALU.mult),
                     reads=[bYt, bst], writes=[bYt])
                pyt, bpyt = nb()
                S.op("pe", lambda e: e.transpose(out=pyt[0:64, 0:128], in_=Yt[:, 0:64], identity=idf[:]), reads=[bYt, b_idf], writes=[bpyt])
                yield
                cp(S, "act", YN[:, hp, tc], pyt[0:64, 0:128].rearrange("v (h t) -> v h t", h=2), [bpyt], [b_YN])
                rel(bpyt)

            GRP = 2
            for c0_ in range(0, NCH, GRP):
                for k_ in range(NB):
                    busy[k_] = False
                gens = [unit(c_, p_) for c_ in range(c0_, min(NCH, c0_ + GRP)) for p_ in range(2)]
                alive = list(gens)
                while alive:
                    nxt = []
                    for gn_ in alive:
                        try:
                            next(gn_)
                            nxt.append(gn_)
                        except StopIteration:
                            pass
                    alive = nxt
            if tt == 0:
                g.dbg("d_YN", YN[:], b_YN, [64, 4, TB])
            for h in range(4):
                S.op("dve", lambda e: e.tensor_scalar(out=YN[:, h, :], in0=YN[:, h, :], scalar1=rp[0:64, 36 + h:37 + h], scalar2=rp[0:64, 40 + h:41 + h], op0=ALU.mult, op1=ALU.add),
                     reads=[b_YN, b_rp], writes=[b_YN])
            S.op("dve", lambda e: e.tensor_tensor(out=YN[:], in0=YN[:], in1=bv[:], op=ALU.add), reads=[b_YN, b_bv], writes=[b_YN])
            S.op("dve", lambda e: e.tensor_tensor(out=yo[:], in0=YN[:], in1=gT[:], op=ALU.mult), reads=[b_YN, b_gT], writes=[b_yo])
            S.dma("sp", D["ymixT"][0:256, t0:t0 + TB].rearrange("(h v) t -> v h t", v=64), yo[:], reads=[b_yo])
        S.barrier()


def build(T=8192, debug=False, phases=None, nlayers=2):
    nc = bass.Bass("TRN2", target_bir_lowering=False)
    g = G()
    g.nc, g.T, g.wctr = nc, T, 0
    g.debug = debug
    D = {}
    g.D = D

    def din(name, shape, dt=F32):
        D[name] = nc.dram_tensor(name, list(shape), dt, kind="ExternalInput").ap()

    def dscr(name, shape, dt=F32, out=False):
        D[name] = nc.dram_tensor(name, list(shape), dt, kind=("ExternalOutput" if (out or debug) else "Internal")).ap()

    din("xT", [DM, T]); din("pT", [2, 256, T]); din("pos", [1, T], I32); din("invf", [128, 1])
    din("w_in", [2, DM, NEXT]); din("smalls", [2, 128, 64]); din("w_out", [2, DM, DM])
    din("ffn_w_up", [2, DM, 2 * DFF]); din("ffn_w_down", [2, DFF, DM]); din("convp", [2, 128, 44, 4])
    din("ple_w_gate", [2, DM, DM]); din("ple_w_proj", [2, 256, DM])
    din("pool_wbd", [2, 128, 2, 128]); din("pool_fix", [128, 2, 16])
    for nm, shp, dt in g_extra_inputs(T):
        din(nm, shp, dt)
    dscr("cosT", [128, T]); dscr("sinT", [128, T])
    dscr("zaT", [1024, T]); dscr("zbT", [256, T]); dscr("qT", [512, T], BF16); dscr("kT", [384, T], BF16)
    dscr("vcT", [128, T], BF16); dscr("vtok", [T, 256], BF16); dscr("gates", [T, 24])
    dscr("ymixT", [1024, T], BF16)
    for nm, shp, dt in g_extra_scratch(T):
        dscr(nm, shp, dt)
    dscr("xs0", [DM, T]); dscr("xs1", [DM, T]); dscr("xs2", [DM, T])
    dscr("outT", [DM, T], out=True)
    with ExitStack() as stack:
        g.S = Sched(nc, stack)
        if phases is None:
            phases = ("rope", "inproj", "rwkv", "pool", "nsa", "outproj", "ffn", "ple")
        if "rope" in phases:
            phase_rope(g)
        xcur = D["xT"]
        for l in range(nlayers):
            if "inproj" in phases:
                phase_inproj(g, l, xcur)
            if "rwkv" in phases:
                phase_rwkv(g, l)
            if "pool" in phases:
                phase_pool(g, l)
            if "nsa" in phases:
                phase_nsa(g, l)
            if "outproj" in phases:
                phase_outproj(g, l, xcur, D["xs0"])
            if "ffn" in phases:
                phase_ffn(g, l, D["xs0"], D["xs1"])
            if "ple" in phases:
                last = (l == nlayers - 1)
                phase_ple(g, l, D["xs1"], D["outT"] if last else D["xs2"], last)
            xcur = D["xs2"]
        g.S.barrier()
        g.ninstr = g.S.ninstr
    return nc, g


def g_extra_inputs(T):
    NCMP = (T - 32) // 16 + 1
    NTC = (NCMP + 127) // 128
    return [("nsa_masks", [128, 19, 512], BF16), ("identb", [128, 128], BF16), ("identf", [128, 128], F32),
            ("E_all", [128, T], BF16), ("mcs", [128, NTC, 128], BF16), ("keepw", [128, 256], F32), ("addw", [128, 256], F32),
            ("rw_msu", [128, 128], F32), ("rw_mu", [128, 128], F32), ("rw_msl", [128, 128], F32), ("rw_il", [64, 128], F32), ("rw_ir", [64, 128], F32),
            ("rwp", [2, 128, 64], F32), ("rw_w_up", [2, 64, 256], F32), ("rw_a_up", [2, 64, 256], F32), ("rw_g_up", [2, 128, 256], F32),
            ("nsa_w_ck", [2, 32, 64, 64], F32), ("nsa_w_cv", [2, 32, 64, 64], F32), ("nsa_peT", [2, 2, 64, 32], F32)]


def g_extra_scratch(T):
    return []


def host_prep(inp, T=8192):
    f = np.float32
    cols = inproj_cols()
    shared = {}
    shared["w_in"] = np.ascontiguousarray(inp["w_in"][:, :, cols])
    sm = np.zeros((2, 128, 64), f)
    for l in range(2):
        sm[l, :, 0:8] = inp["g_mix"][l].reshape(8, 128).T
        sm[l, :, 8:10] = inp["pool_scale"][l].reshape(2, 128).T
        sm[l, :, 16:24] = inp["g_ffn"][l].reshape(8, 128).T
        sm[l, :, 24:32] = inp["g_ple"][l].reshape(8, 128).T
        sm[l, :, 32:40] = inp["g_final"].reshape(8, 128).T
    shared["smalls"] = sm
    shared["w_out"] = np.ascontiguousarray(inp["w_out"])
    shared["ffn_w_up"] = np.ascontiguousarray(inp["ffn_w_up"])
    shared["ffn_w_down"] = np.ascontiguousarray(inp["ffn_w_down"])
    cp_ = np.zeros((2, 128, 44, 4), f)
    for l in range(2):
        cw = inp["ffn_conv_w"][l][:, 0, :]
        for i in range(3):
            cp_[l, :, :, i] = cw[i].reshape(44, 128).T
        cp_[l, :, :, 3] = inp["ffn_conv_b"][l].reshape(44, 128).T
    shared["convp"] = cp_
    shared["ple_w_gate"] = np.ascontiguousarray(inp["ple_w_gate"])
    shared["ple_w_proj"] = np.ascontiguousarray(inp["ple_w_proj"])
    pw = np.zeros((2, 128, 2, 128), f)
    for l in range(2):
        for gi in range(4):
            t_, h_ = gi // 2, gi % 2
            pw[l, h_ * 64:(h_ + 1) * 64, t_, h_ * 64:(h_ + 1) * 64] = inp["pool_w"][l, gi]
    shared["pool_wbd"] = pw
    fix = np.zeros((128, 2, 16), f)
    for gi, win in enumerate((2, 4, 8, 16)):
        t_, h_ = gi // 2, gi % 2
        fix[h_ * 64:(h_ + 1) * 64, t_, :] = 1.0 / np.minimum(np.arange(16) + 1, win)
    shared["pool_fix"] = fix
    shared.update(nsa_consts(T))
    shared.update(rwkv_consts())
    rwp = np.zeros((2, 128, 64), f)
    for l in range(2):
        mu = inp["rw_mu"][l]
        rwp[l, 0:64, 0:12] = mu[0:768].reshape(12, 64).T
        rwp[l, 0:64, 12:14] = mu[768:896].reshape(2, 64).T
        rwp[l, :, 14] = mu[896:1024]
        for j, nm in enumerate(("rw_w0", "rw_a0", "rw_k_k", "rw_k_a")):
            rwp[l, 0:64, 16 + 4 * j:20 + 4 * j] = inp[nm][l].reshape(4, 64).T
        rwp[l, 0:64, 32:36] = inp["rw_r_k"][l].T
        rwp[l, 0:64, 36:40] = inp["rw_gn_g"][l].reshape(4, 64).T
        rwp[l, 0:64, 40:44] = inp["rw_gn_b"][l].reshape(4, 64).T
    shared["rwp"] = rwp
    shared["rw_w_up"] = np.ascontiguousarray(inp["rw_w_up"])
    shared["rw_a_up"] = np.ascontiguousarray(inp["rw_a_up"])
    shared["rw_g_up"] = np.ascontiguousarray(inp["rw_g_up"])
    shared["nsa_w_ck"] = np.ascontiguousarray(inp["nsa_w_ck"])
    shared["nsa_w_cv"] = np.ascontiguousarray(inp["nsa_w_cv"])
    shared["nsa_peT"] = np.ascontiguousarray(np.stack([np.transpose(inp["nsa_pe_k"], (0, 2, 1)), np.transpose(inp["nsa_pe_v"], (0, 2, 1))], axis=1))
    inv = (10000.0 ** (-np.arange(32, dtype=f) / 32)).astype(f)
    shared["invf"] = np.tile(inv, 4).reshape(128, 1).astype(f)
    return shared


def per_core(inp, b, T=8192):
    return {"xT": np.ascontiguousarray(inp["x"][b, :T].T),
            "pT": np.ascontiguousarray(np.transpose(inp["p"][:, b, :T], (0, 2, 1))),
            "pos": np.ascontiguousarray(inp["positions"][b:b + 1, :T]).astype(np.int32)}


def kernel(**inputs):
    T = 8192
    inp = {k: np.asarray(v) for k, v in inputs.items()}
    nc, g = build(T)
    shared = host_prep(inp, T)
    in_maps = []
    for b in range(8):
        m = dict(shared)
        m.update(per_core(inp, b, T))
        in_maps.append(m)
    res = run_bass_kernel_spmd(nc, in_maps, core_ids=list(range(8)))
    out = np.stack([np.ascontiguousarray(res.results[b]["outT"].T) for b in range(8)], axis=0)
    return out.astype(np.float32)
```

```python
import os
import numpy as np
from contextlib import ExitStack
import concourse.bass as bass
import concourse.mybir as mybir
from concourse.bass_utils import run_bass_kernel_spmd

F32 = mybir.dt.float32
BF16 = mybir.dt.bfloat16
I32 = mybir.dt.int32
AF = mybir.ActivationFunctionType
ALU = mybir.AluOpType
AX = mybir.AxisListType

DM = 1024
PI = float(np.pi)
BIG = 30000.0
NFEAT = 3200
NTOKC = 280
NEXT = NFEAT + NTOKC
DFF = 2816


class Buf:
    __slots__ = ("w", "rs", "rd", "excl")

    def __init__(self, excl=False):
        self.w = None
        self.rs = {}
        self.rd = []
        self.excl = excl


class Sched:
    EPOCH = 30000
    NDMA = 12

    def __init__(self, nc, stack):
        self.nc = nc
        self.stack = stack
        self.engs = {"pe": nc.tensor, "act": nc.scalar, "dve": nc.vector, "pool": nc.gpsimd, "sp": nc.sync}
        self.nsem = 0
        self.sem = {}
        self.cnt = {}
        self.seen = {k: {} for k in self.engs}
        for k in self.engs:
            self.sem[k] = self._newsem(k)
            self.cnt[k] = 0
        self.dsem = {}
        self.dpos = {}
        for k in ("sp", "act", "pool"):
            self.dsem[k] = [[self._newsem("d" + k), 0] for _ in range(self.NDMA)]
            self.dpos[k] = 0
        self.last = {}
        self.ninstr = 0
        self.store_q = "pool"

    def _newsem(self, name):
        self.nsem += 1
        return self.stack.enter_context(self.nc.semaphore(f"s_{name}_{self.nsem}"))

    def _wait(self, ek, tok):
        sem, val, src = tok
        d = self.seen[ek]
        key = id(sem)
        if d.get(key, 0) >= val:
            return
        self.engs[ek].wait_ge(sem, val)
        d[key] = val

    def _deps(self, ek, reads, writes):
        toks = []
        same_ok = (ek == "pe")
        for b in reads:
            if b.w is not None and not (b.w[2] == ek and same_ok):
                toks.append(b.w)
            if b.excl:
                for e, t in b.rs.items():
                    if e != ek:
                        toks.append(t)
        for b in writes:
            if b.w is not None and not (b.w[2] == ek and same_ok):
                toks.append(b.w)
            for e, t in b.rs.items():
                if not (e == ek and same_ok):
                    toks.append(t)
            toks.extend(b.rd)
        for t in toks:
            self._wait(ek, t)

    def _record(self, tok, reads, writes, is_dma):
        for b in reads:
            if is_dma:
                b.rd.append(tok)
                if len(b.rd) > 24:
                    del b.rd[0]
            else:
                b.rs[tok[2]] = tok
        for b in writes:
            b.w = tok
            b.rs = {}
            b.rd = []

    def op(self, ek, fn, reads=(), writes=()):
        self._deps(ek, reads, writes)
        ins = fn(self.engs[ek])
        self.cnt[ek] += 1
        ins.then_inc(self.sem[ek], 1)
        tok = (self.sem[ek], self.cnt[ek], ek)
        self.last[ek] = tok
        self._record(tok, reads, writes, False)
        self.ninstr += 1
        if self.cnt[ek] >= self.EPOCH:
            self.sem[ek] = self._newsem(ek)
            self.cnt[ek] = 0
        return tok

    def dma(self, qk, out, in_, reads=(), writes=(), **kw):
        if qk == "sp" and len(writes) == 0 and self.store_q is not None:
            qk = self.store_q
        self._deps(qk, reads, writes)
        slots = self.dsem[qk]
        i = self.dpos[qk]
        self.dpos[qk] = (i + 1) % len(slots)
        sem, val = slots[i]
        if val > 0:
            self._wait(qk, (sem, val, "dma"))
        if val + 16 > 60000:
            sem = self._newsem("d" + qk)
            val = 0
            slots[i][0] = sem
        ins = self.engs[qk].dma_start(out=out, in_=in_, **kw)
        val += 16
        ins.then_inc(sem, 16)
        slots[i][1] = val
        tok = (sem, val, "dma")
        self._record(tok, reads, writes, True)
        self.ninstr += 1
        return tok

    def barrier(self):
        toks = list(self.last.values())
        for qk in self.dsem:
            for sem, val in self.dsem[qk]:
                if val > 0:
                    toks.append((sem, val, "dma"))
        for ek in self.engs:
            for t in toks:
                self._wait(ek, t)


def inproj_cols():
    qb = 1280
    rot = lambda base, nh: [base + h * 64 + ((d + 32) % 64) for h in range(nh) for d in range(64)]
    cols = list(range(0, 1280))
    cols += list(range(qb, qb + 512)) + rot(qb, 8)
    for off in (512, 768, 1024):
        cols += list(range(qb + off, qb + off + 128))
    for off in (512, 768, 1024):
        cols += rot(qb + off, 2)
    cols += list(range(qb + 640, qb + 768))
    assert len(cols) == NFEAT
    cols += list(range(qb + 896, qb + 1024)) + list(range(qb + 1152, qb + 1280))
    cols += list(range(qb + 1280, qb + 1304))
    assert len(cols) == NEXT
    return np.array(cols)


class G:
    uid = 0
    debug = False
    use_f32r = True

    def dbg(self, name, ap, buf, shape):
        if not self.debug or name in self.D:
            return
        self.D[name] = self.nc.dram_tensor(name, list(shape), F32, kind="ExternalOutput").ap()
        self.S.dma("sp", self.D[name], ap, reads=[buf])

    def nm(self, name):
        self.uid += 1
        return f"{name}_{self.uid}"


def cp(S, ek, out, in_, reads, writes):
    if ek == "act":
        return S.op("act", lambda e: e.copy(out=out, in_=in_), reads=reads, writes=writes)
    return S.op(ek, lambda e: e.tensor_copy(out=out, in_=in_), reads=reads, writes=writes)


def load_w_bf16(g, dst, b_dst, src, KC, N, stage, b_stage, rows=128):
    S = g.S
    CH = stage[0].shape[-1]
    srcv = src.rearrange("(c p) n -> p c n", p=rows)
    for c in range(KC):
        for n0 in range(0, N, CH):
            n1 = min(N, n0 + CH)
            i = g.wctr % len(stage)
            g.wctr += 1
            S.dma("sp", stage[i][0:rows, 0:n1 - n0], srcv[:, c, n0:n1], writes=[b_stage[i]])
            ek = ("act", "dve", "pool")[g.wctr % 3]
            cp(S, ek, dst[0:rows, c, n0:n1], stage[i][0:rows, 0:n1 - n0], [b_stage[i]], [b_dst])


def rmsnorm_tile(g, xt, b_x, hT, b_h, gcol, b_g, N, R):
    S = g.S
    S.op("act", lambda e: e.activation(out=R["sq"][:, :, 0:N], in_=xt[:, :, 0:N], func=AF.Square), reads=[b_x], writes=[R["b_sq"]])
    for c in range(8):
        S.op("pe", lambda e: e.matmul(R["p_rms"][:, 0:N], lhsT=R["ones"][:], rhs=R["sq"][:, c, 0:N], start=(c == 0), stop=(c == 7)),
             reads=[R["b_ones"], R["b_sq"]], writes=[R["b_prms"]])
    S.op("act", lambda e: e.activation(out=R["rstd"][:, 0:N], in_=R["p_rms"][:, 0:N], func=AF.Sqrt, bias=R["eps"][:, 0:1], scale=1.0 / DM),
         reads=[R["b_prms"], R["b_eps"]], writes=[R["b_rstd"]])
    S.op("dve", lambda e: e.reciprocal(out=R["rstd"][:, 0:N], in_=R["rstd"][:, 0:N]), reads=[R["b_rstd"]], writes=[R["b_rstd"]])
    for c in range(8):
        S.op("dve", lambda e: e.scalar_tensor_tensor(out=hT[:, c, 0:N], in0=xt[:, c, 0:N], scalar=gcol[:, c:c + 1], in1=R["rstd"][:, 0:N],
                                                       op0=ALU.mult, op1=ALU.mult),
             reads=[b_x, b_g, R["b_rstd"]], writes=[b_h])


def rms_shared(g, sbf, psf, N):
    S = g.S
    R = {}
    R["sq"] = sbf("rsq", [128, 8, N], BF16); R["b_sq"] = Buf()
    R["rstd"] = sbf("rstd", [128, N]); R["b_rstd"] = Buf()
    R["p_rms"] = psf("p_rms", [128, 512]); R["b_prms"] = Buf(excl=True)
    R["ones"] = sbf("ones", [128, 128], BF16); R["b_ones"] = Buf()
    R["eps"] = sbf("epsb", [128, 1]); R["b_eps"] = Buf()
    S.op("pool", lambda e: e.memset(R["ones"][:], 1.0), writes=[R["b_ones"]])
    S.op("pool", lambda e: e.memset(R["eps"][:], 1e-6), writes=[R["b_eps"]])
    return R


def phase_rope(g):
    nc, S, D, T = g.nc, g.S, g.D, g.T
    with ExitStack() as st:
        sbf = lambda name, shape, dt=F32: st.enter_context(nc.sbuf_tensor(g.nm(name), list(shape), dt))
        CH = min(T, 2048)
        posi = sbf("posi", [128, CH], I32); b_posi = Buf()
        posf = sbf("posf", [128, CH]); b_posf = Buf()
        ang = sbf("ang", [128, CH]); b_ang = Buf()
        tab = sbf("tab", [128, CH]); b_tab = Buf()
        ki = sbf("ki", [128, CH], I32); b_ki = Buf()
        kf = sbf("kf", [128, CH]); b_kf = Buf()
        inv_sb = sbf("inv_sb", [128, 1]); b_inv = Buf()
        S.dma("sp", inv_sb[:], D["invf"], writes=[b_inv])
        C1 = 6.28125
        C2 = 2 * np.pi - 6.28125
        for c0 in range(0, T, CH):
            S.dma("sp", posi[:], D["pos"][:, c0:c0 + CH].partition_broadcast(128), writes=[b_posi])
            S.op("dve", lambda e: e.tensor_copy(out=posf[:], in_=posi[:]), reads=[b_posi], writes=[b_posf])
            S.op("dve", lambda e: e.tensor_scalar(out=posf[:], in0=posf[:], scalar1=inv_sb[:, 0:1], scalar2=None, op0=ALU.mult),
                 reads=[b_posf, b_inv], writes=[b_posf])
            S.op("dve", lambda e: e.tensor_scalar(out=kf[:], in0=posf[:], scalar1=float(1.0 / (2 * np.pi)), scalar2=None, op0=ALU.mult),
                 reads=[b_posf], writes=[b_kf])
            S.op("dve", lambda e: e.tensor_copy(out=ki[:], in_=kf[:]), reads=[b_kf], writes=[b_ki])
            S.op("dve", lambda e: e.tensor_copy(out=kf[:], in_=ki[:]), reads=[b_ki], writes=[b_kf])
            S.op("dve", lambda e: e.scalar_tensor_tensor(out=posf[:], in0=kf[:], scalar=-C1, in1=posf[:], op0=ALU.mult, op1=ALU.add),
                 reads=[b_kf, b_posf], writes=[b_posf])
            S.op("dve", lambda e: e.scalar_tensor_tensor(out=posf[:], in0=kf[:], scalar=-C2, in1=posf[:], op0=ALU.mult, op1=ALU.add),
                 reads=[b_kf, b_posf], writes=[b_posf])
            for which, shift, dst in (("sin", 0.0, D["sinT"]), ("cos", PI / 2, D["cosT"])):
                S.op("dve", lambda e: e.tensor_scalar(out=ang[:], in0=posf[:], scalar1=shift, scalar2=None, op0=ALU.add),
                     reads=[b_posf], writes=[b_ang])
                S.op("dve", lambda e: e.tensor_scalar(out=kf[:], in0=ang[:], scalar1=PI, scalar2=-2 * PI, op0=ALU.is_gt, op1=ALU.mult),
                     reads=[b_ang], writes=[b_kf])
                S.op("dve", lambda e: e.tensor_tensor(out=ang[:], in0=ang[:], in1=kf[:], op=ALU.add), reads=[b_ang, b_kf], writes=[b_ang])
                S.op("dve", lambda e: e.tensor_scalar(out=ang[:], in0=ang[:], scalar1=3.141592, scalar2=-3.141592, op0=ALU.min, op1=ALU.max),
                     reads=[b_ang], writes=[b_ang])
                S.op("act", lambda e: e.activation(out=tab[:], in_=ang[:], func=AF.Sin), reads=[b_ang], writes=[b_tab])
                if which == "sin":
                    for base in (0, 64):
                        S.op("dve", lambda e: e.tensor_scalar(out=tab[base:base + 32, :], in0=tab[base:base + 32, :], scalar1=-1.0, scalar2=None, op0=ALU.mult),
                             reads=[b_tab], writes=[b_tab])
                S.dma("sp", dst[:, c0:c0 + CH], tab[:], reads=[b_tab])
        S.barrier()


def phase_inproj(g, l, xin):
    nc, S, D, T = g.nc, g.S, g.D, g.T
    with ExitStack() as st:
        sbf = lambda name, shape, dt=F32: st.enter_context(nc.sbuf_tensor(g.nm(name), list(shape), dt))
        psf = lambda name, shape, dt=F32: st.enter_context(nc.psum_tensor(g.nm(name), list(shape), dt))
        Wb = sbf("Wb", [128, 8, NEXT], BF16); b_W = Buf()
        stage = [sbf(f"wst{i}", [128, 1740]) for i in range(2)]; b_stage = [Buf() for _ in range(2)]
        load_w_bf16(g, Wb, b_W, D["w_in"][l], 8, NEXT, stage, b_stage)
        sm = sbf("sm", [128, 8]); b_sm = Buf()
        S.dma("sp", sm[:], D["smalls"][l][:, 0:8], writes=[b_sm])
        R = rms_shared(g, sbf, psf, 512)
        xt = [sbf(f"xt{i}", [128, 8, 512]) for i in range(2)]; b_xt = [Buf() for _ in range(2)]
        hTs = [sbf(f"hT{i}", [128, 8, 512], BF16) for i in range(2)]; b_hs = [Buf() for _ in range(2)]
        cs = [sbf(f"cs{i}", [128, 512]) for i in range(2)]; b_cs = [Buf() for _ in range(2)]
        sn = [sbf(f"sn{i}", [128, 512]) for i in range(2)]; b_sn = [Buf() for _ in range(2)]
        NEV = 4
        ev = [sbf(f"ev{i}", [128, 512]) for i in range(NEV)]; b_ev = [Buf() for _ in range(NEV)]
        evb = [sbf(f"evb{i}", [128, 512], BF16) for i in range(NEV)]; b_evb = [Buf() for _ in range(NEV)]
        t1 = [sbf(f"t1_{i}", [128, 512]) for i in range(2)]; b_t1 = [Buf() for _ in range(2)]
        t2 = [sbf(f"t2_{i}", [128, 512]) for i in range(2)]; b_t2 = [Buf() for _ in range(2)]
        vt = [sbf(f"vt{i}", [128, 256], BF16) for i in range(2)]; b_vt = [Buf() for _ in range(2)]
        gt = [sbf(f"gt{i}", [128, 24]) for i in range(2)]; b_gt = [Buf() for _ in range(2)]
        NPF = 5
        p_f = [psf(f"p_f{i}", [128, 512]) for i in range(NPF)]; b_pf = [Buf(excl=True) for _ in range(NPF)]
        p_t = [psf(f"p_t{i}", [128, 512]) for i in range(2)]; b_pt = [Buf(excl=True) for _ in range(2)]
        xv = xin.rearrange("(c p) t -> p c t", p=128)
        evc = 0
        pfc = 0

        NTL = T // 512

        def prep(tt):
            xi = tt % 2
            t0 = tt * 512
            S.dma("sp", xt[xi][:], xv[:, :, t0:t0 + 512], writes=[b_xt[xi]])
            S.dma("sp", cs[xi][:], D["cosT"][:, t0:t0 + 512], writes=[b_cs[xi]])
            S.dma("sp", sn[xi][:], D["sinT"][:, t0:t0 + 512], writes=[b_sn[xi]])
            rmsnorm_tile(g, xt[xi], b_xt[xi], hTs[xi], b_hs[xi], sm, b_sm, 512, R)

        prep(0)
        for tt in range(NTL):
            t0 = tt * 512
            xi = tt % 2
            if tt + 1 < NTL:
                prep(tt + 1)
            hT, b_h = hTs[xi], b_hs[xi]

            def mm_feat(f, pf, bpf):
                for c in range(8):
                    S.op("pe", lambda e: e.matmul(pf[:], lhsT=Wb[:, c, f * 128:(f + 1) * 128], rhs=hT[:, c, :], start=(c == 0), stop=(c == 7)),
                         reads=[b_W, b_h], writes=[bpf])
            plain = [(f, D["zaT"][f * 128:(f + 1) * 128, t0:t0 + 512], False) for f in range(8)]
            plain += [(8 + f, D["zbT"][f * 128:(f + 1) * 128, t0:t0 + 512], False) for f in range(2)]
            plain += [(24, D["vcT"][:, t0:t0 + 512], True)]
            for k_, (f, dst, isb) in enumerate(plain):
                pi = pfc % NPF; pfc += 1
                mm_feat(f, p_f[pi], b_pf[pi])
                ei = evc % NEV; evc += 1
                ek = "act" if k_ % 2 == 0 else "dve"
                if isb:
                    cp(S, ek, evb[ei][:], p_f[pi][:], [b_pf[pi]], [b_evb[ei]])
                    S.dma("sp", dst, evb[ei][:], reads=[b_evb[ei]])
                else:
                    cp(S, ek, ev[ei][:], p_f[pi][:], [b_pf[pi]], [b_ev[ei]])
                    S.dma("sp", dst, ev[ei][:], reads=[b_ev[ei]])
            for j in range(7):
                if j < 4:
                    f_a, f_b = 10 + j, 14 + j
                    dst = D["qT"][j * 128:(j + 1) * 128, t0:t0 + 512]
                else:
                    f_a, f_b = 18 + (j - 4), 21 + (j - 4)
                    dst = D["kT"][(j - 4) * 128:(j - 3) * 128, t0:t0 + 512]
                pa = pfc % NPF; pfc += 1
                mm_feat(f_a, p_f[pa], b_pf[pa])
                pb = pfc % NPF; pfc += 1
                mm_feat(f_b, p_f[pb], b_pf[pb])
                ti = j % 2
                S.op("dve", lambda e: e.tensor_tensor(out=t1[ti][:], in0=p_f[pa][:], in1=cs[xi][:], op=ALU.mult),
                     reads=[b_pf[pa], b_cs[xi]], writes=[b_t1[ti]])
                S.op("dve", lambda e: e.tensor_tensor(out=t2[ti][:], in0=p_f[pb][:], in1=sn[xi][:], op=ALU.mult),
                     reads=[b_pf[pb], b_sn[xi]], writes=[b_t2[ti]])
                ei = evc % NEV; evc += 1
                S.op("pool", lambda e: e.tensor_tensor(out=evb[ei][:], in0=t1[ti][:], in1=t2[ti][:], op=ALU.add),
                     reads=[b_t1[ti], b_t2[ti]], writes=[b_evb[ei]])
                S.dma("sp", dst, evb[ei][:], reads=[b_evb[ei]])
            for s4 in range(4):
                pi = s4 % 2
                for c in range(8):
                    S.op("pe", lambda e: e.matmul(p_t[pi][:, 0:NTOKC], lhsT=hT[:, c, s4 * 128:(s4 + 1) * 128], rhs=Wb[:, c, NFEAT:NEXT],
                                                  start=(c == 0), stop=(c == 7)),
                         reads=[b_W, b_h], writes=[b_pt[pi]])
                S.op("dve", lambda e: e.tensor_copy(out=vt[pi][:], in_=p_t[pi][:, 0:256]), reads=[b_pt[pi]], writes=[b_vt[pi]])
                S.op("act", lambda e: e.activation(out=gt[pi][:], in_=p_t[pi][:, 256:280], func=AF.Sigmoid), reads=[b_pt[pi]], writes=[b_gt[pi]])
                S.dma("sp", D["vtok"][t0 + s4 * 128:t0 + (s4 + 1) * 128, :], vt[pi][:], reads=[b_vt[pi]])
                S.dma("sp", D["gates"][t0 + s4 * 128:t0 + (s4 + 1) * 128, :], gt[pi][:], reads=[b_gt[pi]])
        S.barrier()


def phase_pool(g, l):
    nc, S, D, T = g.nc, g.S, g.D, g.T
    with ExitStack() as st:
        sbf = lambda name, shape, dt=F32: st.enter_context(nc.sbuf_tensor(g.nm(name), list(shape), dt))
        psf = lambda name, shape, dt=F32: st.enter_context(nc.psum_tensor(g.nm(name), list(shape), dt))
        CH = 512
        PAD = 16
        wp = sbf("wp", [128, 2, 128]); b_wp = Buf()
        S.dma("sp", wp[:], D["pool_wbd"][l], writes=[b_wp])
        sm = sbf("smp", [128, 2]); b_sm = Buf()
        S.dma("sp", sm[:], D["smalls"][l][:, 8:10], writes=[b_sm])
        fix = sbf("fix", [128, 2, 16]); b_fix = Buf()
        S.dma("sp", fix[:], D["pool_fix"], writes=[b_fix])
        z = [[sbf(f"pz{i}_{f}", [128, PAD + CH]) for f in range(2)] for i in range(2)]
        b_z = [[Buf() for f in range(2)] for i in range(2)]
        s_a = sbf("ps_a", [128, PAD + CH]); b_sa = Buf()
        s_b = sbf("ps_b", [128, PAD + CH]); b_sb = Buf()
        pl = sbf("ppl", [128, CH]); b_pl = Buf()
        ob = [sbf(f"pob{i}", [128, CH], BF16) for i in range(2)]; b_ob = [Buf() for _ in range(2)]
        pp = [psf(f"ppp{i}", [128, 512]) for i in range(2)]; b_pp = [Buf(excl=True) for _ in range(2)]
        k = 0
        for tt in range(T // CH):
            t0 = tt * CH
            zi = tt % 2
            for f in range(2):
                zt, bz = z[zi][f], b_z[zi][f]
                if tt == 0:
                    S.op("pool", lambda e: e.memset(zt[:, 0:PAD], 0.0), writes=[bz])
                    S.dma("sp", zt[:, PAD:PAD + CH], D["zbT"][f * 128:(f + 1) * 128, 0:CH], writes=[bz])
                else:
                    S.dma("sp", zt[:], D["zbT"][f * 128:(f + 1) * 128, t0 - PAD:t0 + CH], writes=[bz])
                W = PAD + CH
                S.op("dve", lambda e: e.tensor_tensor(out=s_a[:, 1:W], in0=zt[:, 1:W], in1=zt[:, 0:W - 1], op=ALU.add), reads=[bz], writes=[b_sa])
                S.op("dve", lambda e: e.tensor_tensor(out=s_b[:, 3:W], in0=s_a[:, 3:W], in1=s_a[:, 1:W - 2], op=ALU.add), reads=[b_sa], writes=[b_sb])
                if f == 0:
                    lo, hi = s_a, s_b
                    blo, bhi = b_sa, b_sb
                    wl, wh = 2, 4
                else:
                    S.op("dve", lambda e: e.tensor_tensor(out=s_a[:, 7:W], in0=s_b[:, 7:W], in1=s_b[:, 3:W - 4], op=ALU.add), reads=[b_sb], writes=[b_sa])
                    S.op("dve", lambda e: e.tensor_tensor(out=s_b[:, 15:W], in0=s_a[:, 15:W], in1=s_a[:, 7:W - 8], op=ALU.add), reads=[b_sa], writes=[b_sb])
                    lo, hi = s_a, s_b
                    blo, bhi = b_sa, b_sb
                    wl, wh = 8, 16
                S.op("dve", lambda e: e.scalar_tensor_tensor(out=pl[0:64, :], in0=lo[0:64, PAD:W], scalar=1.0 / wl, in1=zt[0:64, PAD:W], op0=ALU.mult, op1=ALU.subtract),
                     reads=[blo, bz], writes=[b_pl])
                S.op("dve", lambda e: e.scalar_tensor_tensor(out=pl[64:128, :], in0=hi[64:128, PAD:W], scalar=1.0 / wh, in1=zt[64:128, PAD:W], op0=ALU.mult, op1=ALU.subtract),
                     reads=[bhi, bz], writes=[b_pl])
                if tt == 0:
                    S.op("dve", lambda e: e.tensor_tensor(out=pl[0:64, 0:16], in0=lo[0:64, PAD:PAD + 16], in1=fix[0:64, f, :], op=ALU.mult), reads=[blo, b_fix], writes=[b_pl])
                    S.op("dve", lambda e: e.tensor_tensor(out=pl[64:128, 0:16], in0=hi[64:128, PAD:PAD + 16], in1=fix[64:128, f, :], op=ALU.mult), reads=[bhi, b_fix], writes=[b_pl])
                    S.op("dve", lambda e: e.tensor_tensor(out=pl[:, 0:16], in0=pl[:, 0:16], in1=zt[:, PAD:PAD + 16], op=ALU.subtract), reads=[b_pl, bz], writes=[b_pl])
                pi = k % 2; k += 1
                S.op("pe", lambda e: e.matmul(pp[pi][:, 0:CH], lhsT=wp[:, f, :], rhs=pl[:], start=True, stop=True), reads=[b_wp, b_pl], writes=[b_pp[pi]])
                S.op("act", lambda e: e.activation(out=ob[pi][:], in_=pp[pi][:, 0:CH], func=AF.Copy, scale=sm[:, f:f + 1]), reads=[b_pp[pi], b_sm], writes=[b_ob[pi]])
                S.dma("sp", D["ymixT"][256 + f * 128:256 + (f + 1) * 128, t0:t0 + CH], ob[pi][:], reads=[b_ob[pi]])
        S.barrier()


def phase_outproj(g, l, xin, xout):
    nc, S, D, T = g.nc, g.S, g.D, g.T
    with ExitStack() as st:
        sbf = lambda name, shape, dt=F32: st.enter_context(nc.sbuf_tensor(g.nm(name), list(shape), dt))
        psf = lambda name, shape, dt=F32: st.enter_context(nc.psum_tensor(g.nm(name), list(shape), dt))
        Wo = sbf("Wo", [128, 8, DM], BF16); b_W = Buf()
        stage = [sbf(f"wst{i}", [128, 1024]) for i in range(2)]; b_stage = [Buf() for _ in range(2)]
        load_w_bf16(g, Wo, b_W, D["w_out"][l], 8, DM, stage, b_stage)
        xt = [sbf(f"oxt{i}", [128, 8, 512]) for i in range(2)]; b_xt = [Buf() for _ in range(2)]
        ym = [sbf(f"oym{i}", [128, 8, 512], BF16) for i in range(2)]; b_ym = [Buf() for _ in range(2)]
        xo = [sbf(f"oxo{i}", [128, 8, 512]) for i in range(2)]; b_xo = [Buf() for _ in range(2)]
        pq = [psf(f"opq{i}", [128, 512]) for i in range(4)]; b_pq = [Buf(excl=True) for _ in range(4)]
        xv = xin.rearrange("(c p) t -> p c t", p=128)
        xov = xout.rearrange("(c p) t -> p c t", p=128)
        yv = D["ymixT"].rearrange("(c p) t -> p c t", p=128)
        k = 0
        for tt in range(T // 512):
            t0 = tt * 512
            xi = tt % 2
            S.dma("sp", xt[xi][:], xv[:, :, t0:t0 + 512], writes=[b_xt[xi]])
            S.dma("sp", ym[xi][:], yv[:, :, t0:t0 + 512], writes=[b_ym[xi]])
            for j in range(8):
                pi = k % 4; k += 1
                for c in range(8):
                    S.op("pe", lambda e: e.matmul(pq[pi][:], lhsT=Wo[:, c, j * 128:(j + 1) * 128], rhs=ym[xi][:, c, :], start=(c == 0), stop=(c == 7)),
                         reads=[b_W, b_ym[xi]], writes=[b_pq[pi]])
                S.op("dve", lambda e: e.tensor_tensor(out=xo[xi][:, j, :], in0=pq[pi][:], in1=xt[xi][:, j, :], op=ALU.add),
                     reads=[b_pq[pi], b_xt[xi]], writes=[b_xo[xi]])
            S.dma("sp", xov[:, :, t0:t0 + 512], xo[xi][:], reads=[b_xo[xi]])
        S.barrier()


def phase_ffn(g, l, xin, xout):
    nc, S, D, T = g.nc, g.S, g.D, g.T
    N = 512
    with ExitStack() as st:
        sbf = lambda name, shape, dt=F32: st.enter_context(nc.sbuf_tensor(g.nm(name), list(shape), dt))
        psf = lambda name, shape, dt=F32: st.enter_context(nc.psum_tensor(g.nm(name), list(shape), dt))
        Wu = sbf("Wu", [128, 8, 2 * DFF], BF16)
        Wd = sbf("Wd", [128, 22, DM], BF16)
        CHW = 512
        NCW = (2 * DFF) // CHW
        b_Wu = [Buf() for _ in range(NCW)]
        b_Wd = [Buf() for _ in range(22)]
        stage = [sbf(f"wst{i}", [128, CHW]) for i in range(2)]; b_stage = [Buf() for _ in range(2)]
        xt = sbf("fxt", [128, 8, N]); b_xt = Buf()
        xv = xin.rearrange("(c p) t -> p c t", p=128)
        xov = xout.rearrange("(c p) t -> p c t", p=128)
        S.dma("sp", xt[:], xv[:, :, 0:N], writes=[b_xt])
        wuv = D["ffn_w_up"][l].rearrange("(c p) n -> p c n", p=128)
        wdv = D["ffn_w_down"][l].rearrange("(c p) n -> p c n", p=128)
        wu_done = set()
        wd_done = set()

        def emit_wu(nchunk):
            if nchunk in wu_done:
                return
            wu_done.add(nchunk)
            for c in range(8):
                i = g.wctr % 2; g.wctr += 1
                S.dma("sp", stage[i][:, :], wuv[:, c, nchunk * CHW:(nchunk + 1) * CHW], writes=[b_stage[i]])
                cp(S, ("act", "dve", "pool")[g.wctr % 3], Wu[:, c, nchunk * CHW:(nchunk + 1) * CHW], stage[i][:, :], [b_stage[i]], [b_Wu[nchunk]])

        def emit_wd(c):
            if c in wd_done:
                return
            wd_done.add(c)
            for n0 in range(0, DM, CHW):
                n1 = min(DM, n0 + CHW)
                i = g.wctr % 2; g.wctr += 1
                S.dma("sp", stage[i][:, 0:n1 - n0], wdv[:, c, n0:n1], writes=[b_stage[i]])
                cp(S, ("act", "dve", "pool")[g.wctr % 3], Wd[:, c, n0:n1], stage[i][:, 0:n1 - n0], [b_stage[i]], [b_Wd[c]])
        sm = sbf("smf", [128, 8]); b_sm = Buf()
        S.dma("sp", sm[:], D["smalls"][l][:, 16:24], writes=[b_sm])
        cw = sbf("cw", [128, 44, 4]); b_cw = Buf()
        S.dma("sp", cw[:], D["convp"][l], writes=[b_cw])
        gated = sbf("gated", [128, 22, N], BF16); b_gt = Buf()
        R = {}
        R["sq"] = gated[:, 0:8, :]; R["b_sq"] = b_gt
        R["rstd"] = sbf("rstd", [128, N]); R["b_rstd"] = Buf()
        R["p_rms"] = psf("p_rms", [128, 512]); R["b_prms"] = Buf(excl=True)
        R["ones"] = sbf("ones", [128, 128], BF16); R["b_ones"] = Buf()
        R["eps"] = sbf("epsb", [128, 1]); R["b_eps"] = Buf()
        S.op("pool", lambda e: e.memset(R["ones"][:], 1.0), writes=[R["b_ones"]])
        S.op("pool", lambda e: e.memset(R["eps"][:], 1e-6), writes=[R["b_eps"]])
        hT = sbf("fhT", [128, 8, N], BF16); b_h = Buf()
        carry = sbf("carry", [128, 44, 2]); b_carry = Buf()
        S.op("pool", lambda e: e.memset(carry[:], 0.0), writes=[b_carry])
        NU = 3
        U = [sbf(f"U{i}", [128, N + 2]) for i in range(NU)]; b_U = [Buf() for _ in range(NU)]
        cg = [sbf(f"cg{i}", [128, N]) for i in range(2)]; b_cg = [Buf() for _ in range(2)]
        cv = [sbf(f"cv{i}", [128, N]) for i in range(2)]; b_cv = [Buf() for _ in range(2)]
        gi = [sbf(f"gi{i}", [128, N]) for i in range(2)]; b_gi = [Buf() for _ in range(2)]
        pu = [psf(f"fpu{i}", [128, 512]) for i in range(4)]; b_pu = [Buf(excl=True) for _ in range(4)]
        pd = [psf(f"fpd{i}", [128, 512]) for i in range(2)]; b_pd = [Buf(excl=True) for _ in range(2)]
        uc = 0
        pc_ = 0
        for tt in range(T // N):
            t0 = tt * N
            if tt > 0:
                S.dma("sp", xt[:], xv[:, :, t0:t0 + N], writes=[b_xt])
            rmsnorm_tile(g, xt, b_xt, hT, b_h, sm, b_sm, N, R)
            for i in range(22):
                k2 = i % 2
                for which, ch in ((0, i), (1, 22 + i)):
                    ui = uc % NU; uc += 1
                    pi = pc_ % 4; pc_ += 1
                    emit_wu((ch * 128) // CHW)
                    for c in range(8):
                        S.op("pe", lambda e: e.matmul(pu[pi][:, 0:N], lhsT=Wu[:, c, ch * 128:(ch + 1) * 128], rhs=hT[:, c, :], start=(c == 0), stop=(c == 7)),
                             reads=[b_Wu[(ch * 128) // CHW], b_h], writes=[b_pu[pi]])
                    S.op("act", lambda e: e.copy(out=U[ui][:, 2:N + 2], in_=pu[pi][:, 0:N]), reads=[b_pu[pi]], writes=[b_U[ui]])
                    S.op("act", lambda e: e.copy(out=U[ui][:, 0:2], in_=carry[:, ch, :]), reads=[b_carry], writes=[b_U[ui]])
                    dst, bd_ = (cg[k2], b_cg[k2]) if which == 0 else (cv[k2], b_cv[k2])
                    S.op("dve", lambda e: e.tensor_scalar(out=dst[:], in0=U[ui][:, 0:N], scalar1=cw[:, ch, 0:1], scalar2=cw[:, ch, 3:4], op0=ALU.mult, op1=ALU.add),
                         reads=[b_U[ui], b_cw], writes=[bd_])
                    S.op("dve", lambda e: e.scalar_tensor_tensor(out=dst[:], in0=U[ui][:, 1:N + 1], scalar=cw[:, ch, 1:2], in1=dst[:], op0=ALU.mult, op1=ALU.add),
                         reads=[b_U[ui], b_cw, bd_], writes=[bd_])
                    S.op("dve", lambda e: e.scalar_tensor_tensor(out=dst[:], in0=U[ui][:, 2:N + 2], scalar=cw[:, ch, 2:3], in1=dst[:], op0=ALU.mult, op1=ALU.add),
                         reads=[b_U[ui], b_cw, bd_], writes=[bd_])
                    S.op("act", lambda e: e.copy(out=carry[:, ch, :], in_=U[ui][:, N:N + 2]), reads=[b_U[ui]], writes=[b_carry])
                S.op("pool", lambda e: e.tensor_tensor(out=gi[k2][:], in0=cg[k2][:], in1=cg[k2][:], op=ALU.mult), reads=[b_cg[k2]], writes=[b_gi[k2]])
                S.op("pool", lambda e: e.tensor_scalar(out=gi[k2][:], in0=gi[k2][:], scalar1=0.044715, scalar2=1.0, op0=ALU.mult, op1=ALU.add), reads=[b_gi[k2]], writes=[b_gi[k2]])
                S.op("pool", lambda e: e.tensor_tensor(out=gi[k2][:], in0=gi[k2][:], in1=cg[k2][:], op=ALU.mult), reads=[b_gi[k2], b_cg[k2]], writes=[b_gi[k2]])
                S.op("act", lambda e: e.activation(out=gi[k2][:], in_=gi[k2][:], func=AF.Sigmoid, scale=1.5957691216057308), reads=[b_gi[k2]], writes=[b_gi[k2]])
                S.op("pool", lambda e: e.tensor_tensor(out=gi[k2][:], in0=gi[k2][:], in1=cg[k2][:], op=ALU.mult), reads=[b_gi[k2], b_cg[k2]], writes=[b_gi[k2]])
                S.op("dve", lambda e: e.tensor_tensor(out=gated[:, i, :], in0=gi[k2][:], in1=cv[k2][:], op=ALU.mult), reads=[b_gi[k2], b_cv[k2]], writes=[b_gt])
                emit_wd(i)
            for j in range(8):
                pi = j % 2
                for i in range(22):
                    S.op("pe", lambda e: e.matmul(pd[pi][:, 0:N], lhsT=Wd[:, i, j * 128:(j + 1) * 128], rhs=gated[:, i, :], start=(i == 0), stop=(i == 21)),
                         reads=[b_Wd[i], b_gt], writes=[b_pd[pi]])
                S.op("dve", lambda e: e.tensor_tensor(out=xt[:, j, :], in0=pd[pi][:, 0:N], in1=xt[:, j, :], op=ALU.add),
                     reads=[b_pd[pi], b_xt], writes=[b_xt])
            S.dma("sp", xov[:, :, t0:t0 + N], xt[:], reads=[b_xt])
        S.barrier()


def phase_ple(g, l, xin, xout, final):
    nc, S, D, T = g.nc, g.S, g.D, g.T
    N = 512
    with ExitStack() as st:
        sbf = lambda name, shape, dt=F32: st.enter_context(nc.sbuf_tensor(g.nm(name), list(shape), dt))
        psf = lambda name, shape, dt=F32: st.enter_context(nc.psum_tensor(g.nm(name), list(shape), dt))
        Wg = sbf("Wg", [128, 8, DM], BF16); b_Wg = Buf()
        Wp = sbf("Wp", [128, 2, DM], BF16); b_Wp = Buf()
        stage = [sbf(f"wst{i}", [128, 1024]) for i in range(2)]; b_stage = [Buf() for _ in range(2)]
        load_w_bf16(g, Wg, b_Wg, D["ple_w_gate"][l], 8, DM, stage, b_stage)
        load_w_bf16(g, Wp, b_Wp, D["ple_w_proj"][l], 2, DM, stage, b_stage)
        sm = sbf("smq", [128, 16]); b_sm = Buf()
        S.dma("sp", sm[:, 0:8], D["smalls"][l][:, 24:32], writes=[b_sm])
        S.dma("sp", sm[:, 8:16], D["smalls"][l][:, 32:40], writes=[b_sm])
        R = rms_shared(g, sbf, psf, N)
        xt = [sbf(f"pxt{i}", [128, 8, N]) for i in range(2)]; b_xt = [Buf() for _ in range(2)]
        pt = [sbf(f"ppt{i}", [128, 2, N]) for i in range(2)]; b_pt = [Buf() for _ in range(2)]
        ptbs = [sbf(f"pptb{i}", [128, 2, N], BF16) for i in range(2)]; b_ptbs = [Buf() for _ in range(2)]
        hTs = [sbf(f"phT{i}", [128, 8, N], BF16) for i in range(2)]; b_hs = [Buf() for _ in range(2)]
        xos = [sbf(f"pxo{i}", [128, 8, N]) for i in range(2)]; b_xos = [Buf() for _ in range(2)]
        xf = sbf("pxf", [128, 8, N]); b_xf = Buf()
        gs = [sbf(f"pgs{i}", [128, N]) for i in range(2)]; b_gs = [Buf() for _ in range(2)]
        pg = [psf(f"ppg{i}", [128, 512]) for i in range(2)]; b_pg = [Buf(excl=True) for _ in range(2)]
        pq = [psf(f"ppq{i}", [128, 512]) for i in range(2)]; b_pq = [Buf(excl=True) for _ in range(2)]
        xv = xin.rearrange("(c p) t -> p c t", p=128)
        xov = xout.rearrange("(c p) t -> p c t", p=128)
        pv = D["pT"][l].rearrange("(c p) t -> p c t", p=128)
        NTL = T // N

        def prep(tt):
            xi = tt % 2
            t0 = tt * N
            S.dma("sp", xt[xi][:], xv[:, :, t0:t0 + N], writes=[b_xt[xi]])
            S.dma("sp", pt[xi][:], pv[:, :, t0:t0 + N], writes=[b_pt[xi]])
            S.op("pool", lambda e: e.tensor_copy(out=ptbs[xi][:], in_=pt[xi][:]), reads=[b_pt[xi]], writes=[b_ptbs[xi]])
            rmsnorm_tile(g, xt[xi], b_xt[xi], hTs[xi], b_hs[xi], sm, b_sm, N, R)

        prep(0)
        for tt in range(NTL):
            t0 = tt * N
            xi = tt % 2
            if tt + 1 < NTL:
                prep(tt + 1)
            hT, b_h = hTs[xi], b_hs[xi]
            ptb, b_ptb = ptbs[xi], b_ptbs[xi]
            xo, b_xo = xos[xi], b_xos[xi]
            for j in range(8):
                pi = j % 2
                for c in range(8):
                    S.op("pe", lambda e: e.matmul(pg[pi][:], lhsT=Wg[:, c, j * 128:(j + 1) * 128], rhs=hT[:, c, :], start=(c == 0), stop=(c == 7)),
                         reads=[b_Wg, b_h], writes=[b_pg[pi]])
                for c in range(2):
                    S.op("pe", lambda e: e.matmul(pq[pi][:], lhsT=Wp[:, c, j * 128:(j + 1) * 128], rhs=ptb[:, c, :], start=(c == 0), stop=(c == 1)),
                         reads=[b_Wp, b_ptb], writes=[b_pq[pi]])
                S.op("act", lambda e: e.activation(out=gs[pi][:], in_=pg[pi][:], func=AF.Sigmoid), reads=[b_pg[pi]], writes=[b_gs[pi]])
                S.op("dve", lambda e: e.tensor_tensor(out=gs[pi][:], in0=pq[pi][:], in1=gs[pi][:], op=ALU.mult), reads=[b_pq[pi], b_gs[pi]], writes=[b_gs[pi]])
                S.op("pool", lambda e: e.tensor_tensor(out=xo[:, j, :], in0=gs[pi][:], in1=xt[xi][:, j, :], op=ALU.add), reads=[b_gs[pi], b_xt[xi]], writes=[b_xo])
            if not final:
                S.dma("sp", xov[:, :, t0:t0 + N], xo[:], reads=[b_xo])
            else:
                S.op("act", lambda e: e.activation(out=R["sq"][:], in_=xo[:], func=AF.Square), reads=[b_xo], writes=[R["b_sq"]])
                for c in range(8):
                    S.op("pe", lambda e: e.matmul(R["p_rms"][:], lhsT=R["ones"][:], rhs=R["sq"][:, c, :], start=(c == 0), stop=(c == 7)),
                         reads=[R["b_ones"], R["b_sq"]], writes=[R["b_prms"]])
                S.op("act", lambda e: e.activation(out=R["rstd"][:], in_=R["p_rms"][:], func=AF.Sqrt, bias=R["eps"][:, 0:1], scale=1.0 / DM),
                     reads=[R["b_prms"], R["b_eps"]], writes=[R["b_rstd"]])
                S.op("dve", lambda e: e.reciprocal(out=R["rstd"][:], in_=R["rstd"][:]), reads=[R["b_rstd"]], writes=[R["b_rstd"]])
                for c in range(8):
                    S.op("dve", lambda e: e.scalar_tensor_tensor(out=xf[:, c, :], in0=xo[:, c, :], scalar=sm[:, 8 + c:9 + c], in1=R["rstd"][:],
                                                                   op0=ALU.mult, op1=ALU.mult),
                         reads=[b_xo, b_sm, R["b_rstd"]], writes=[b_xf])
                S.dma("sp", xov[:, :, t0:t0 + N], xf[:], reads=[b_xf])
        S.barrier()


def nsa_consts(T):
    import ml_dtypes
    bf = ml_dtypes.bfloat16
    f = np.float32
    c = {}
    nl = np.arange(128)[:, None]
    ql = np.arange(128)[None, :]
    masks = np.zeros((19, 128, 128), f)
    for i in range(17):
        masks[i] = np.where(16 * nl + 31 - ql <= 128 * i, 0.0, -BIG)
    masks[17] = np.where(nl <= ql, 0.0, -BIG)
    masks[18] = np.where(nl > ql, 0.0, -BIG)
    m4 = np.tile(masks, (1, 1, 4))
    c["nsa_masks"] = np.ascontiguousarray(np.transpose(m4, (1, 0, 2))).astype(bf)
    c["identb"] = np.eye(128, dtype=f).astype(bf)
    c["identf"] = np.eye(128, dtype=f)
    key = np.arange(T)[None, :]
    c["E_all"] = (key // 64 == np.arange(128)[:, None]).astype(f).astype(bf)
    n_cmp = (T - 32) // 16 + 1
    ntc = (n_cmp + 127) // 128
    cs = 16 * np.arange(n_cmp)
    ce = cs + 31
    ss = 64 * np.arange(128)
    ov = np.minimum(ce[:, None], ss[None] + 63) - np.maximum(cs[:, None], ss[None]) + 1
    mcs = np.zeros((ntc * 128, 128), f)
    mcs[:n_cmp] = np.clip(ov, 0, 32).astype(f) / 32
    c["mcs"] = np.ascontiguousarray(mcs.reshape(ntc, 128, 128).transpose(1, 0, 2)).astype(bf)
    keep = np.zeros((128, 256), f)
    add = np.zeros((128, 256), f)
    for q in range(128):
        jc = 126 if q < 64 else 127
        cc = np.arange(256)
        keep[q] = (cc < jc - 1)
        add[q] = np.where(cc == jc - 1, 1.1e9, np.where(cc == jc, 1.2e9, np.where(cc > jc, -1e30, 0.0)))
    c["keepw"] = keep
    c["addw"] = add
    return c


def phase_nsa(g, l):
    nc, S, D, T = g.nc, g.S, g.D, g.T
    NQB = T // 128
    NCMP = (T - 32) // 16 + 1
    NTC = (NCMP + 127) // 128
    SK = dict(skip_group_check=True)
    with ExitStack() as st:
        sbf = lambda name, shape, dt=F32: st.enter_context(nc.sbuf_tensor(g.nm(name), list(shape), dt))
        psf = lambda name, shape, dt=F32: st.enter_context(nc.psum_tensor(g.nm(name), list(shape), dt))
        stp = [psf(f"nst{i}", [128, 512]) for i in range(3)]; b_stp = [Buf(excl=True) for _ in range(3)]
        acc = [psf(f"nacc{i}", [128, 512]) for i in range(3)]; b_acc = [Buf(excl=True) for _ in range(3)]
        imp = psf("nimp", [128, 512]); b_imp = Buf(excl=True)
        msc = psf("nmsc", [128, 512]); b_msc = Buf(excl=True)
        mscb = msc[:, 384:448].bitcast(BF16); b_mscb = b_msc
        KcT = sbf("KcT", [64, 2, NTC * 128], BF16); b_Kc = Buf()
        Vc = sbf("Vc", [128, NTC, 2, 128], BF16); b_Vc = Buf()
        S.op("pool", lambda e: e.memset(KcT[:], 0.0), writes=[b_Kc])
        S.op("pool", lambda e: e.memset(Vc[:], 0.0), writes=[b_Vc])
        with ExitStack() as st2:
            sb2 = lambda name, shape, dt=F32: st2.enter_context(nc.sbuf_tensor(g.nm(name), list(shape), dt))
            kc = sb2("kc", [64, 2, T], BF16); b_kc = Buf()
            vc = sb2("vc", [64, 2, T], BF16); b_vc = Buf()
            S.dma("sp", kc[:], D["kT"][0:128, :].rearrange("(h d) t -> d h t", d=64), writes=[b_kc])
            S.dma("sp", vc[:], D["vcT"].rearrange("(h d) t -> d h t", d=64), writes=[b_vc])
            wst = sb2("wckst", [64, 32, 64]); b_wst = Buf()
            wck = sb2("wck", [64, 32, 64], BF16); b_wck = Buf()
            wcv = sb2("wcv", [64, 32, 64], BF16); b_wcv = Buf()
            S.dma("sp", wst[:], D["nsa_w_ck"][l].rearrange("l d e -> d l e"), writes=[b_wst])
            cp(S, "dve", wck[:], wst[:], [b_wst], [b_wck])
            S.dma("sp", wst[:], D["nsa_w_cv"][l].rearrange("l d e -> d l e"), writes=[b_wst])
            cp(S, "dve", wcv[:], wst[:], [b_wst], [b_wcv])
            pest = sb2("pest", [64, 2, 32]); b_pest = Buf()
            peb = sb2("peb", [64, 2, 32], BF16); b_peb = Buf()
            S.dma("sp", pest[:], D["nsa_peT"][l].rearrange("w d l -> d w l"), writes=[b_pest])
            cp(S, "dve", peb[:], pest[:], [b_pest], [b_peb])
            biask = sb2("biask", [64, 1]); b_bk = Buf()
            biasv = sb2("biasv", [1, 64], BF16); b_bv = Buf()
            onesr = sb2("onesr", [1, 128], BF16); b_or = Buf()
            S.op("pool", lambda e: e.memset(onesr[:], 1.0), writes=[b_or])
            for i_ in range(32):
                S.op("pe", lambda e: e.matmul(msc[0:64, 0:1], lhsT=wck[:, i_, :], rhs=peb[:, 0, i_:i_ + 1], start=(i_ == 0), stop=(i_ == 31)),
                     reads=[b_wck, b_peb], writes=[b_msc])
            cp(S, "dve", biask[:], msc[0:64, 0:1], [b_msc], [b_bk])
            for i_ in range(32):
                S.op("pe", lambda e: e.matmul(msc[0:1, 0:64], lhsT=peb[:, 1, i_:i_ + 1], rhs=wcv[:, i_, :], start=(i_ == 0), stop=(i_ == 31)),
                     reads=[b_wcv, b_peb], writes=[b_msc])
            cp(S, "dve", biasv[:], msc[0:1, 0:64], [b_msc], [b_bv])
            span = 16 * (NCMP - 1) + 1
            for h in range(2):
                pk = stp[h]
                for i_ in range(32):
                    S.op("pe", lambda e: e.matmul(pk[0:64, 0:NCMP], lhsT=wck[:, i_, :], rhs=kc[:, h, i_:i_ + span:16], start=(i_ == 0), stop=(i_ == 31)),
                         reads=[b_wck, b_kc], writes=[b_stp[h]])
                S.op("act", lambda e: e.activation(out=KcT[:, h, 0:NCMP], in_=pk[0:64, 0:NCMP], func=AF.Identity, bias=biask[:, 0:1], scale=1.0),
                     reads=[b_stp[h], b_bk], writes=[b_Kc])
            k_ = 0
            for h in range(2):
                for nt in range(NTC):
                    nn = min(128, NCMP - nt * 128)
                    pv_ = acc[k_ % 3]; bpv = b_acc[k_ % 3]; k_ += 1
                    base = 16 * 128 * nt
                    sp_ = 16 * (nn - 1) + 1
                    for i_ in range(32):
                        S.op("pe", lambda e: e.matmul(pv_[0:nn, 0:64], lhsT=vc[:, h, base + i_:base + i_ + sp_:16], rhs=wcv[:, i_, :], start=(i_ == 0), stop=False),
                             reads=[b_wcv, b_vc], writes=[bpv])
                    S.op("pe", lambda e: e.matmul(pv_[0:nn, 0:64], lhsT=onesr[0:1, 0:nn], rhs=biasv[0:1, :], start=False, stop=True),
                         reads=[b_or, b_bv], writes=[bpv])
                    cp(S, "dve", Vc[0:nn, nt, h, 0:64], pv_[0:nn, 0:64], [bpv], [b_Vc])
                    S.op("dve", lambda e: e.memset(Vc[0:nn, nt, h, 64:65], 1.0), writes=[b_Vc])
            S.barrier()
        masks = sbf("masks", [128, 19, 512], BF16); b_masks = Buf()
        S.dma("sp", masks[:], D["nsa_masks"], writes=[b_masks])
        identb = sbf("identb", [128, 128], BF16); b_idb = Buf()
        S.dma("sp", identb[:], D["identb"], writes=[b_idb])
        identf = sbf("identf", [128, 128]); b_idf = Buf()
        S.dma("sp", identf[:], D["identf"], writes=[b_idf])
        MCS = sbf("MCS", [128, NTC, 128], BF16); b_mcs = Buf()
        S.dma("sp", MCS[:], D["mcs"], writes=[b_mcs])
        keepw = sbf("keepw", [128, 256]); b_kw_ = Buf()
        S.dma("sp", keepw[:], D["keepw"], writes=[b_kw_])
        addw = sbf("addw", [128, 256]); b_aw = Buf()
        S.dma("sp", addw[:], D["addw"], writes=[b_aw])
        LH = sbf("LH", [128, 2, T], BF16); b_LH = Buf()
        KwT = sbf("KwT", [64, 2, T], BF16); b_Kw = Buf()
        TH = min(T, 4096)
        ksv = D["kT"][128:256, :].rearrange("(h d) t -> d h t", d=64)
        S.dma("sp", LH[64:128, :, 0:TH], ksv[:, :, 0:TH], writes=[b_LH])
        for h_ in range(2):
            S.dma("sp", LH[0:64, h_, 0:TH], D["E_all"][0:64, 0:TH], writes=[b_LH])
        if T > TH:
            S.dma("sp", LH[0:64, :, TH:T], ksv[:, :, TH:T], writes=[b_LH])
            for h_ in range(2):
                S.dma("sp", LH[64:128, h_, TH:T], D["E_all"][64:128, TH:T], writes=[b_LH])
        S.dma("sp", KwT[:], D["kT"][256:384, :].rearrange("(h d) t -> d h t", d=64), writes=[b_Kw])
        VW = 128
        Vs = sbf("Vs", [128, NQB, 2, VW], BF16); b_Vs = Buf()
        Vw = sbf("Vw", [128, NQB, 2, VW], BF16); b_Vw = Buf()
        S.op("pool", lambda e: e.memset(Vs[:], 0.0), writes=[b_Vs])
        S.op("pool", lambda e: e.memset(Vw[:], 0.0), writes=[b_Vw])
        S.op("pool", lambda e: e.memset(Vs[:, :, :, 64:65], 1.0), writes=[b_Vs])
        S.op("pool", lambda e: e.memset(Vw[:, :, :, 64:65], 1.0), writes=[b_Vw])
        vtv = D["vtok"].rearrange("(kt p) (w h d) -> p kt w h d", p=128, w=2, h=2)
        for h in range(2):
            S.dma("sp", Vs[:, :, h, 0:64], vtv[:, :, 0, h, :], writes=[b_Vs])
            S.dma("sp", Vw[:, :, h, 0:64], vtv[:, :, 1, h, :], writes=[b_Vw])
        Qg = [sbf(f"Qg{i}", [64, 4, 128], BF16) for i in range(2)]; b_Qg = [Buf() for _ in range(2)]
        gt = [sbf(f"ngt{i}", [128, 24]) for i in range(2)]; b_gt = [Buf() for _ in range(2)]
        Pc = [sbf(f"Pc{i}", [128, 512], BF16) for i in range(max(NTC, 1))]; b_Pc = [Buf() for _ in range(max(NTC, 1))]
        NP = 3
        Pb = [sbf(f"Pb{i}", [128, 512], BF16) for i in range(NP)]; b_Pb = [Buf() for _ in range(NP)]
        zz = sbf("zz", [128, 3, 4]); b_zz = Buf()
        coef = sbf("coef", [128, 3, 4]); b_coef = Buf()
        impS = sbf("impS", [128, 128]); b_impS = Buf()
        imp2 = sbf("imp2", [128, 128]); b_imp2 = Buf()
        mx = sbf("mx", [128, 16]); b_mx = Buf()
        thr = sbf("thr", [128, 1]); b_thr = Buf()
        MBf = sbf("MBf", [128, 128]); b_MBf = Buf()
        MBb = sbf("MBb", [128, 128], BF16); b_MBb = Buf()
        MBT4 = sbf("MBT4", [128, 4, 128], BF16); b_MBT4 = Buf()
        yc = [sbf(f"yc{i}", [128, 512]) for i in range(2)]; b_yc = [Buf() for _ in range(2)]
        ycT = [sbf(f"ycT{i}", [128, 4, 128], BF16) for i in range(2)]; b_ycT = [Buf() for _ in range(2)]
        stc = 0
        pbc = 0
        qc = 0
        qv = D["qT"].rearrange("(hq d) t -> d hq t", d=64)
        ymv = D["ymixT"][512:1024, :].rearrange("(c p) t -> p c t", p=128)
        aS = sbf("aS", [65, 512]); b_aS = Buf()
        R0 = [sbf(f"R0_{i}", [128, 512], BF16) for i in range(2)]; b_R0 = [Buf() for _ in range(2)]
        R1 = [sbf(f"R1_{i}", [128, 512], BF16) for i in range(2)]; b_R1 = [Buf() for _ in range(2)]

        def score_tile(KT, bK, h, kt, Q, bQ, extra):
            nonlocal stc
            si = stc % 3; stc += 1
            n_mm = 1 + len(extra)
            rhs_ap = Q[:].rearrange("d g q -> d (g q)") if len(Q.shape) == 3 else Q[:]
            S.op("pe", lambda e: e.matmul(stp[si][:], lhsT=KT[:, h, kt * 128:(kt + 1) * 128], rhs=rhs_ap, start=True, stop=(n_mm == 1)),
                 reads=[bK, bQ], writes=[b_stp[si]])
            for i_, (la, ra, bufs) in enumerate(extra):
                S.op("pe", lambda e: e.matmul(stp[si][:], lhsT=la, rhs=ra, start=False, stop=(i_ == len(extra) - 1)),
                     reads=bufs, writes=[b_stp[si]])
            return si

        def run_branch(br, tiles, h, Q, bQ, V, bV, Pbufs=None, after_exp=None):
            nonlocal pbc
            n = len(tiles)
            tiles = [tl if len(tl) == 6 else tl + (Q, bQ) for tl in tiles]
            DEPTH = 2
            issued = []
            nxt = 0
            for i_ in range(n):
                while nxt < n and nxt <= i_ + DEPTH - 1 + (0 if i_ else 0):
                    KT2, bK2, kt2, extra2, Qx2, bQx2 = tiles[nxt]
                    issued.append(score_tile(KT2, bK2, h, kt2, Qx2, bQx2, extra2))
                    nxt += 1
                si = issued[i_]
                kt = tiles[i_][2]
                if Pbufs is None:
                    pi = pbc % NP; pbc += 1
                    P, bP = Pb[pi], b_Pb[pi]
                else:
                    P, bP = Pbufs[i_]
                S.op("act", lambda e: e.activation(out=P[:], in_=stp[si][:], func=AF.Exp, scale=0.125), reads=[b_stp[si]], writes=[bP])
                if nxt < n:
                    KT2, bK2, kt2, extra2, Qx2, bQx2 = tiles[nxt]
                    issued.append(score_tile(KT2, bK2, h, kt2, Qx2, bQx2, extra2))
                    nxt += 1
                S.op("pe", lambda e: e.matmul(acc[br][:, :], lhsT=V[:, kt, h, :], rhs=P[:], start=(i_ == 0), stop=(i_ == n - 1)),
                     reads=[bP, bV], writes=[b_acc[br]])
                if after_exp is not None:
                    after_exp(i_, P, bP)

        def combine(br, h, yi):
            cp(S, "act", aS[:], acc[br][0:65, :], [b_acc[br]], [b_aS])
            for gq in range(4):
                S.op("pe", lambda e: e.transpose(out=msc[:, gq * 65:(gq + 1) * 65], in_=aS[0:65, gq * 128:(gq + 1) * 128], identity=identf[0:65, 0:65]),
                     reads=[b_aS, b_idf], writes=[b_msc])
            a3 = msc[:, 0:260].rearrange("p (g c) -> p g c", c=65)
            g3 = gt[yi][:].rearrange("p (hg b) -> p hg b", b=3)
            S.op("dve", lambda e: e.tensor_scalar(out=zz[:, br, :], in0=a3[:, :, 64], scalar1=1e-30, scalar2=None, op0=ALU.max), reads=[b_msc], writes=[b_zz])
            S.op("dve", lambda e: e.reciprocal(out=zz[:, br, :], in_=zz[:, br, :]), reads=[b_zz], writes=[b_zz])
            S.op("dve", lambda e: e.tensor_tensor(out=coef[:, br, :], in0=zz[:, br, :], in1=g3[:, h * 4:(h + 1) * 4, br], op=ALU.mult), reads=[b_zz, b_gt[yi]], writes=[b_coef])
            for gq in range(4):
                o_ = yc[yi][:, (h * 4 + gq) * 64:(h * 4 + gq + 1) * 64]
                if br == 0:
                    S.op("dve", lambda e: e.tensor_scalar(out=o_, in0=a3[:, gq, 0:64], scalar1=coef[:, br, gq:gq + 1], scalar2=None, op0=ALU.mult),
                         reads=[b_msc, b_coef], writes=[b_yc[yi]])
                else:
                    S.op("dve", lambda e: e.scalar_tensor_tensor(out=o_, in0=a3[:, gq, 0:64], scalar=coef[:, br, gq:gq + 1], in1=o_, op0=ALU.mult, op1=ALU.add),
                         reads=[b_msc, b_coef, b_yc[yi]], writes=[b_yc[yi]])

        items = [(qb, h) for qb in range(NQB) for h in range(2)]
        qis = {}

        def stage1(qb, h):
            nonlocal qc
            q0 = qb * 128
            yi = qb % 2
            if h == 0:
                S.dma("sp", gt[yi][:], D["gates"][q0:q0 + 128, :], writes=[b_gt[yi]])
            qi = qc % 2; qc += 1
            qis[(qb, h)] = qi
            Q, bQ = Qg[qi], b_Qg[qi]
            S.dma("sp", Q[:], qv[:, h * 4:(h + 1) * 4, q0:q0 + 128], writes=[bQ])
            S.dma("sp", R0[qi][64:128, :].rearrange("d (g q) -> d g q", g=4), qv[:, h * 4:(h + 1) * 4, q0:q0 + 128], writes=[b_R0[qi]])
            if qb >= 32:
                S.dma("sp", R1[qi][0:64, :].rearrange("d (g q) -> d g q", g=4), qv[:, h * 4:(h + 1) * 4, q0:q0 + 128], writes=[b_R1[qi]])
            ntc = min(NTC, (8 * qb + 6) // 128 + 1)
            S.op("dve", lambda e: e.memset(imp[:], 0.0), writes=[b_imp])
            tiles = []
            for nt in range(ntc):
                delta = 128 * qb - 2048 * nt
                extra = []
                if delta < 2064:
                    extra.append((identb[:], masks[:, delta // 128, :], [b_idb, b_masks]))
                tiles.append((KcT, b_Kc, nt, extra))

            def imp_mm(i_, P, bP):
                for gq in range(4):
                    S.op("pe", lambda e: e.matmul(imp[:, gq * 128:(gq + 1) * 128], lhsT=P[:, gq * 128:(gq + 1) * 128], rhs=MCS[:, i_, :], start=False, stop=(i_ == ntc - 1), **SK),
                         reads=[bP, b_mcs], writes=[b_imp])
            run_branch(0, tiles, h, Q, bQ, Vc, b_Vc, Pbufs=[(Pc[i_], b_Pc[i_]) for i_ in range(ntc)], after_exp=imp_mm)
            combine(0, h, yi)
            S.op("dve", lambda e: e.tensor_scalar(out=impS[:], in0=imp[:, 0:128], scalar1=zz[:, 0, 0:1], scalar2=None, op0=ALU.mult), reads=[b_imp, b_zz], writes=[b_impS])
            for gq in range(1, 4):
                S.op("dve", lambda e: e.scalar_tensor_tensor(out=impS[:], in0=imp[:, gq * 128:(gq + 1) * 128], scalar=zz[:, 0, gq:gq + 1], in1=impS[:], op0=ALU.mult, op1=ALU.add),
                     reads=[b_imp, b_zz, b_impS], writes=[b_impS])
            c0 = 126 - 2 * qb
            S.op("dve", lambda e: e.tensor_tensor(out=impS[:], in0=impS[:], in1=keepw[:, c0:c0 + 128], op=ALU.mult), reads=[b_impS, b_kw_], writes=[b_impS])
            S.op("dve", lambda e: e.tensor_tensor(out=impS[:], in0=impS[:], in1=addw[:, c0:c0 + 128], op=ALU.add), reads=[b_impS, b_aw], writes=[b_impS])
            S.op("dve", lambda e: e.memset(impS[:, 0:1], 1.0e9), writes=[b_impS])
            S.op("dve", lambda e: e.max(out=mx[:, 0:8], in_=impS[:]), reads=[b_impS], writes=[b_mx])
            S.op("dve", lambda e: e.match_replace(out=imp2[:], in_to_replace=mx[:, 0:8], in_values=impS[:], imm_value=-3.0e38), reads=[b_mx, b_impS], writes=[b_imp2])
            S.op("dve", lambda e: e.max(out=mx[:, 8:16], in_=imp2[:]), reads=[b_imp2], writes=[b_mx])
            S.op("dve", lambda e: e.tensor_reduce(out=thr[:], in_=mx[:, 8:16], axis=AX.X, op=ALU.min), reads=[b_mx], writes=[b_thr])
            S.op("dve", lambda e: e.tensor_scalar(out=MBf[:], in0=impS[:], scalar1=thr[:, 0:1], scalar2=None, op0=ALU.is_ge), reads=[b_impS, b_thr], writes=[b_MBf])
            S.op("dve", lambda e: e.tensor_scalar(out=MBb[:], in0=MBf[:], scalar1=1.0, scalar2=BIG, op0=ALU.subtract, op1=ALU.mult), reads=[b_MBf], writes=[b_MBb])
            S.op("pe", lambda e: e.transpose(out=mscb[:], in_=MBb[:], identity=identb[:]), reads=[b_MBb, b_idb], writes=[b_mscb])
            for gq in range(4):
                cp(S, "act" if gq % 2 else "dve", R0[qi][0:64, gq * 128:(gq + 1) * 128], mscb[0:64, :], [b_mscb], [b_R0[qi]])
                if qb >= 32:
                    cp(S, "dve" if gq % 2 else "act", R1[qi][64:128, gq * 128:(gq + 1) * 128], mscb[64:128, :], [b_mscb], [b_R1[qi]])

        def stage2(qb, h):
            q0 = qb * 128
            yi = qb % 2
            qi = qis[(qb, h)]
            Q, bQ = Qg[qi], b_Qg[qi]
            tiles = []
            for kt in range(max(0, qb - 4), qb + 1):
                extra = []
                if kt == qb:
                    extra.append((identb[:], masks[:, 17, :], [b_idb, b_masks]))
                elif kt == qb - 4:
                    extra.append((identb[:], masks[:, 18, :], [b_idb, b_masks]))
                tiles.append((KwT, b_Kw, kt, extra))
            run_branch(2, tiles, h, Q, bQ, Vw, b_Vw)
            combine(2, h, yi)
            tiles = []
            for kt in range(qb + 1):
                extra = []
                if kt == qb:
                    extra.append((identb[:], masks[:, 17, :], [b_idb, b_masks]))
                if kt < 32:
                    tiles.append((LH, b_LH, kt, extra, R0[qi], b_R0[qi]))
                else:
                    tiles.append((LH, b_LH, kt, extra, R1[qi], b_R1[qi]))
            run_branch(1, tiles, h, Q, bQ, Vs, b_Vs)
            combine(1, h, yi)
            if h == 1:
                for c in range(4):
                    S.op("pe", lambda e: e.transpose(out=msc[:, c * 128:(c + 1) * 128], in_=yc[yi][:, c * 128:(c + 1) * 128], identity=identf[:]),
                         reads=[b_yc[yi], b_idf], writes=[b_msc])
                cp(S, "act", ycT[yi][:].rearrange("p c q -> p (c q)"), msc[:], [b_msc], [b_ycT[yi]])
                S.dma("sp", ymv[:, :, q0:q0 + 128], ycT[yi][:], reads=[b_ycT[yi]])

        stage1(*items[0])
        for n_ in range(len(items)):
            if n_ + 1 < len(items):
                stage1(*items[n_ + 1])
            stage2(*items[n_])
        S.barrier()


def rwkv_consts():
    f = np.float32
    c = {}
    hs = np.arange(128) // 64
    tt = np.arange(128) % 64
    same = hs[:, None] == hs[None, :]
    c["rw_msu"] = (same & (tt[:, None] < tt[None, :])).astype(f)
    c["rw_mu"] = (same & (tt[:, None] <= tt[None, :])).astype(f)
    c["rw_msl"] = (same & (tt[:, None] > tt[None, :])).astype(f)
    il = np.zeros((64, 128), f); il[np.arange(64), np.arange(64)] = 1
    ir = np.zeros((64, 128), f); ir[np.arange(64), 64 + np.arange(64)] = 1
    c["rw_il"] = il
    c["rw_ir"] = ir
    return c


def phase_rwkv(g, l):
    nc, S, D, T = g.nc, g.S, g.D, g.T
    TB = 256
    NCH = TB // 64
    SK = dict(skip_group_check=True)
    with ExitStack() as st:
        sbf = lambda name, shape, dt=F32: st.enter_context(nc.sbuf_tensor(g.nm(name), list(shape), dt))
        psf = lambda name, shape, dt=F32: st.enter_context(nc.psum_tensor(g.nm(name), list(shape), dt))
        NB = 8
        bank = [psf(f"rb{i}", [128, 512]) for i in range(NB)]; b_bank = [Buf(excl=True) for _ in range(NB)]
        bctr = [0]

        busy = [False] * NB
        F32R = mybir.dt.float32r
        MT = F32R if g.use_f32r else F32
        RR = lambda ap: ap
        AS32 = (lambda ap: ap.bitcast(F32)) if g.use_f32r else (lambda ap: ap)

        def nb():
            for k_ in range(NB):
                i = (bctr[0] + k_) % NB
                if not busy[i]:
                    bctr[0] = i + 1
                    busy[i] = True
                    return bank[i], b_bank[i]
            raise AssertionError("rwkv: no free PSUM bank (too many live tiles across a yield)")

        def rel(bbuf):
            busy[b_bank.index(bbuf)] = False

        def nbx():
            i = bctr[0] % NB
            bctr[0] += 1
            return bank[i], b_bank[i]
        def const(name, shape, src):
            t_ = sbf(name, shape); b_ = Buf()
            S.dma("sp", t_[:], src, writes=[b_])
            return t_, b_
        msu, b_msu = const("msu", [128, 128], D["rw_msu"])
        mu_, b_mu = const("mu", [128, 128], D["rw_mu"])
        msl, b_msl = const("msl", [128, 128], D["rw_msl"])
        il32, b_il32 = const("il32", [64, 128], D["rw_il"])
        ir32, b_ir32 = const("ir32", [64, 128], D["rw_ir"])
        idf, b_idf = const("idf", [128, 128], D["identf"])
        il = sbf("il", [64, 128], MT); b_il = Buf()
        ir = sbf("ir", [64, 128], MT); b_ir = Buf()
        cp(S, "dve", il[:], il32[:], [b_il32], [b_il])
        cp(S, "dve", ir[:], ir32[:], [b_ir32], [b_ir])
        rp, b_rp = const("rp", [128, 64], D["rwp"][l])
        wup32, b_wup32 = const("wup32", [64, 256], D["rw_w_up"][l])
        aup32, b_aup32 = const("aup32", [64, 256], D["rw_a_up"][l])
        gup32, b_gup32 = const("gup32", [128, 256], D["rw_g_up"][l])
        wup = sbf("wup", [64, 256], MT); b_wup = Buf()
        aup = sbf("aup", [64, 256], MT); b_aup = Buf()
        gup = sbf("gup", [128, 256], MT); b_gup = Buf()
        cp(S, "dve", wup[:], wup32[:], [b_wup32], [b_wup])
        cp(S, "dve", aup[:], aup32[:], [b_aup32], [b_aup])
        cp(S, "dve", gup[:], gup32[:], [b_gup32], [b_gup])
        ones32 = sbf("ones32", [64, 64]); b_o32 = Buf()
        S.op("pool", lambda e: e.memset(ones32[:], 1.0), writes=[b_o32])
        ones64 = sbf("ones64", [64, 64], MT); b_o64 = Buf()
        cp(S, "dve", ones64[:], ones32[:], [b_o32], [b_o64])
        omka = sbf("omka", [64, 4]); b_omka = Buf()
        S.op("dve", lambda e: e.tensor_scalar(out=omka[:], in0=rp[0:64, 28:32], scalar1=-1.0, scalar2=1.0, op0=ALU.mult, op1=ALU.add), reads=[b_rp], writes=[b_omka])
        cst = sbf("rcst", [128, 2]); b_cst = Buf()
        S.op("pool", lambda e: e.memset(cst[:, 0:1], 64e-5), writes=[b_cst])
        ST = [[sbf(f"ST{p}_{i}", [128, 64], MT) for i in range(2)] for p in range(2)]
        b_ST = [[Buf() for i in range(2)] for p in range(2)]
        for p in range(2):
            S.op("pool", lambda e: e.memset(AS32(ST[p][0][:]), 0.0), writes=[b_ST[p][0]])
        sidx = [0, 0]
        def arr(name, shape=None, dt=F32):
            return sbf(name, shape or [64, 4, TB], dt), Buf()
        Z3, b_Z3 = arr("Z3", [64, 12, TB + 1])
        ZL, b_ZL = arr("ZL", [64, 2, TB + 1])
        ZG, b_ZG = arr("ZG", [128, TB + 1])
        X3, b_X3 = arr("X3", [64, 12, TB])
        XL, b_XL = arr("XL", [64, 2, TB], MT)
        XG, b_XG = arr("XG", [128, TB], MT)
        Dt, b_Dt = arr("Dt", [128, 12, TB])
        lw, b_lw = arr("lw")
        cl2, b_cl2 = arr("cl2")
        aa, b_aa = arr("aa")
        kkn, b_kkn = arr("kkn")
        tmp, b_tmp = arr("tmp", None, MT)
        kfin, b_kfin = arr("kfin")
        epos, b_epos = arr("epos")
        eneg, b_eneg = arr("eneg")
        eprev, b_eprev = arr("eprev")
        eC, b_eC = arr("eC")
        AR, b_AR = arr("AR", [64, NCH, 2, 2, 2, 64], MT)
        Bt, b_Bt = arr("Bt", [64, NCH, 4, 64], MT)
        Kt, b_Kt = arr("Kt", [64, NCH, 4, 64], MT)
        Bh, b_Bh = arr("Bh", [64, NCH, 4, 64])
        Kh, b_Kh = arr("Kh", [64, NCH, 4, 64])
        Vc_, b_Vc_ = arr("Vcm", [64, NCH, 4, 64])
        hm = lambda a: a[:].rearrange("k h (c t) -> k h c t", t=64)
        cm = lambda a: a[:].rearrange("k c h t -> k h c t")
        arv = lambda ty: AR[:, :, :, ty, :, :].rearrange("k c p hh t -> k p hh c t")
        hm5 = lambda a: a[:].rearrange("k (p hh) (c t) -> k p hh c t", hh=2, t=64)
        bv, b_bv = arr("bv")
        gT, b_gT = arr("gT")
        YN, b_YN = arr("YN")
        PCf, b_PCf = arr("PCf", [64, 4, NCH], MT)
        PCc = sbf("PCc", [128, 2, NCH]); b_PCc = Buf()
        yo = sbf("yo", [64, 4, TB], BF16); b_yo = Buf()
        NTMP = 100
        tm = [sbf(f"tm{i}", [128, 128], MT) for i in range(NTMP)]; b_tm = [Buf() for _ in range(NTMP)]
        NTF = 16
        tf = [sbf(f"tf{i}", [128, 128]) for i in range(NTF)]; b_tf = [Buf() for _ in range(NTF)]
        fctr = [0]

        def ntf_():
            i = fctr[0] % NTF
            fctr[0] += 1
            return tf[i], b_tf[i]
        tctr = [0]

        def nt_():
            i = tctr[0] % NTMP
            tctr[0] += 1
            return tm[i], b_tm[i]
        NBD = 5
        bd = [[sbf(f"bd{k_}_{i}", [128, 128], MT) for i in range(NBD)] for k_ in range(3)]
        b_bd = [[Buf() for i in range(NBD)] for k_ in range(3)]
        for k_ in range(3):
            for i in range(NBD):
                S.op("pool", lambda e: e.memset(AS32(bd[k_][i][:]), 0.0), writes=[b_bd[k_][i]])
        bdc = [0]
        zav = D["zaT"]
        ev_ctr = [0]

        def evac(out, in_, reads, writes):
            ek = "act" if ev_ctr[0] % 2 == 0 else "dve"
            ev_ctr[0] += 1
            cp(S, ek, out, in_, reads, writes)

        for tt in range(T // TB):
            t0 = tt * TB
            if tt == 0:
                S.op("pool", lambda e: e.memset(Z3[:, :, 0:1], 0.0), writes=[b_Z3])
                S.op("pool", lambda e: e.memset(ZL[:, :, 0:1], 0.0), writes=[b_ZL])
                S.op("pool", lambda e: e.memset(ZG[:, 0:1], 0.0), writes=[b_ZG])
                S.dma("sp", Z3[:, :, 1:TB + 1], zav[0:768, 0:TB].rearrange("(gh k) t -> k gh t", k=64), writes=[b_Z3])
                S.dma("sp", ZL[:, :, 1:TB + 1], zav[768:896, 0:TB].rearrange("(g j) t -> j g t", j=64), writes=[b_ZL])
                S.dma("sp", ZG[:, 1:TB + 1], zav[896:1024, 0:TB], writes=[b_ZG])
            else:
                S.dma("sp", Z3[:], zav[0:768, t0 - 1:t0 + TB].rearrange("(gh k) t -> k gh t", k=64), writes=[b_Z3])
                S.dma("sp", ZL[:], zav[768:896, t0 - 1:t0 + TB].rearrange("(g j) t -> j g t", j=64), writes=[b_ZL])
                S.dma("sp", ZG[:], zav[896:1024, t0 - 1:t0 + TB], writes=[b_ZG])
            S.op("dve", lambda e: e.tensor_tensor(out=Dt[0:64, :, :], in0=Z3[:, :, 0:TB], in1=Z3[:, :, 1:TB + 1], op=ALU.subtract), reads=[b_Z3], writes=[b_Dt])
            for j in range(12):
                if j % 3 == 2:
                    S.op("act", lambda e: e.activation(out=Dt[0:64, j, :], in_=Dt[0:64, j, :], func=AF.Copy, scale=rp[0:64, j:j + 1]), reads=[b_Dt, b_rp], writes=[b_Dt])
                else:
                    S.op("dve", lambda e: e.tensor_scalar(out=Dt[0:64, j, :], in0=Dt[0:64, j, :], scalar1=rp[0:64, j:j + 1], scalar2=None, op0=ALU.mult), reads=[b_Dt, b_rp], writes=[b_Dt])
            S.op("dve", lambda e: e.tensor_tensor(out=X3[:], in0=Dt[0:64, :, :], in1=Z3[:, :, 1:TB + 1], op=ALU.add), reads=[b_Dt, b_Z3], writes=[b_X3])
            S.op("dve", lambda e: e.tensor_tensor(out=Dt[0:64, 0:2, :], in0=ZL[:, :, 0:TB], in1=ZL[:, :, 1:TB + 1], op=ALU.subtract), reads=[b_ZL], writes=[b_Dt])
            for j in range(2):
                S.op("dve", lambda e: e.scalar_tensor_tensor(out=XL[:, j, :], in0=Dt[0:64, j, :], scalar=rp[0:64, 12 + j:13 + j], in1=ZL[:, j, 1:TB + 1], op0=ALU.mult, op1=ALU.add),
                     reads=[b_Dt, b_rp, b_ZL], writes=[b_XL])
            S.op("dve", lambda e: e.tensor_tensor(out=Dt[:, 2, :], in0=ZG[:, 0:TB], in1=ZG[:, 1:TB + 1], op=ALU.subtract), reads=[b_ZG], writes=[b_Dt])
            S.op("dve", lambda e: e.scalar_tensor_tensor(out=XG[:], in0=Dt[:, 2, :], scalar=rp[:, 14:15], in1=ZG[:, 1:TB + 1], op0=ALU.mult, op1=ALU.add),
                 reads=[b_Dt, b_rp, b_ZG], writes=[b_XG])
            r_ = lambda h: X3[:, h, :]
            k_ = lambda h: X3[:, 4 + h, :]
            v_ = lambda h: X3[:, 8 + h, :]
            S.op("act", lambda e: e.activation(out=XL[:, 0, :], in_=XL[:, 0, :], func=AF.Tanh), reads=[b_XL], writes=[b_XL])
            S.op("act", lambda e: e.activation(out=XG[:], in_=XG[:], func=AF.Sigmoid), reads=[b_XG], writes=[b_XG])
            for h in range(4):
                pb, bpb = nbx()
                S.op("pe", lambda e: e.matmul(pb[0:64, 0:TB], lhsT=wup[:, h * 64:(h + 1) * 64], rhs=XL[:, 0, :], start=True, stop=True), reads=[b_wup, b_XL], writes=[bpb])
                S.op("act", lambda e: e.activation(out=lw[:, h, :], in_=pb[0:64, 0:TB], func=AF.Sigmoid, bias=rp[0:64, 16 + h:17 + h], scale=1.0), reads=[bpb, b_rp], writes=[b_lw])
                pb, bpb = nbx()
                S.op("pe", lambda e: e.matmul(pb[0:64, 0:TB], lhsT=aup[:, h * 64:(h + 1) * 64], rhs=XL[:, 1, :], start=True, stop=True), reads=[b_aup, b_XL], writes=[bpb])
                S.op("act", lambda e: e.activation(out=aa[:, h, :], in_=pb[0:64, 0:TB], func=AF.Sigmoid, bias=rp[0:64, 20 + h:21 + h], scale=1.0), reads=[bpb, b_rp], writes=[b_aa])
                pb, bpb = nbx()
                S.op("pe", lambda e: e.matmul(pb[0:64, 0:TB], lhsT=gup[:, h * 64:(h + 1) * 64], rhs=XG[:], start=True, stop=True), reads=[b_gup, b_XG], writes=[bpb])
                evac(gT[:, h, :], pb[0:64, 0:TB], [bpb], [b_gT])
            S.op("dve", lambda e: e.tensor_scalar(out=lw[:], in0=lw[:], scalar1=-0.6065306597126334, scalar2=None, op0=ALU.mult), reads=[b_lw], writes=[b_lw])
            for h in range(4):
                S.op("dve", lambda e: e.tensor_scalar(out=kkn[:, h, :], in0=k_(h), scalar1=rp[0:64, 24 + h:25 + h], scalar2=None, op0=ALU.mult), reads=[b_X3, b_rp], writes=[b_kkn])
            S.op("act", lambda e: e.activation(out=tmp[:], in_=kkn[:], func=AF.Square), reads=[b_kkn], writes=[b_tmp])
            for h in range(4):
                pb, bpb = nbx()
                S.op("pe", lambda e: e.matmul(pb[0:64, 0:TB], lhsT=ones64[:], rhs=tmp[:, h, :], start=True, stop=True), reads=[b_o64, b_tmp], writes=[bpb])
                S.op("act", lambda e: e.activation(out=eC[:, h, :], in_=pb[0:64, 0:TB], func=AF.Sqrt), reads=[bpb], writes=[b_eC])
            S.op("dve", lambda e: e.tensor_scalar(out=eC[:], in0=eC[:], scalar1=1e-12, scalar2=None, op0=ALU.max), reads=[b_eC], writes=[b_eC])
            S.op("dve", lambda e: e.reciprocal(out=eC[:], in_=eC[:]), reads=[b_eC], writes=[b_eC])
            S.op("dve", lambda e: e.tensor_tensor(out=kkn[:], in0=kkn[:], in1=eC[:], op=ALU.mult), reads=[b_kkn, b_eC], writes=[b_kkn])
            for h in range(4):
                S.op("dve", lambda e: e.tensor_scalar(out=tmp[:, h, :], in0=aa[:, h, :], scalar1=rp[0:64, 28 + h:29 + h], scalar2=omka[:, h:h + 1], op0=ALU.mult, op1=ALU.add),
                     reads=[b_aa, b_rp, b_omka], writes=[b_tmp])
            S.op("dve", lambda e: e.tensor_tensor(out=kfin[:], in0=X3[:, 4:8, :], in1=tmp[:], op=ALU.mult), reads=[b_X3, b_tmp], writes=[b_kfin])
            for h in range(4):
                S.op("dve", lambda e: e.scalar_tensor_tensor(out=tmp[:, h, :], in0=r_(h), scalar=rp[0:64, 32 + h:33 + h], in1=kfin[:, h, :], op0=ALU.mult, op1=ALU.mult),
                     reads=[b_X3, b_kfin, b_rp], writes=[b_tmp])
                pb, bpb = nbx()
                S.op("pe", lambda e: e.matmul(pb[0:64, 0:TB], lhsT=ones64[:], rhs=tmp[:, h, :], start=True, stop=True), reads=[b_o64, b_tmp], writes=[bpb])
                S.op("dve", lambda e: e.tensor_tensor(out=bv[:, h, :], in0=pb[0:64, 0:TB], in1=v_(h), op=ALU.mult), reads=[bpb, b_X3], writes=[b_bv])
            src, bsrc, dst, bdst = lw, b_lw, cl2, b_cl2
            cp(S, "act", eprev[:], lw[:], [b_lw], [b_eprev])
            for sft in (1, 2, 4, 8, 16, 32):
                s5 = src[:].rearrange("k h (c t) -> k h c t", t=64)
                d5 = dst[:].rearrange("k h (c t) -> k h c t", t=64)
                S.op("dve", lambda e: e.tensor_tensor(out=d5[:, :, :, sft:64], in0=s5[:, :, :, sft:64], in1=s5[:, :, :, 0:64 - sft], op=ALU.add), reads=[bsrc], writes=[bdst])
                cp(S, "act", d5[:, :, :, 0:sft], s5[:, :, :, 0:sft], [bsrc], [bdst])
                src, bsrc, dst, bdst = dst, bdst, src, bsrc
            cl, b_cl = src, bsrc
            S.op("act", lambda e: e.activation(out=epos[:], in_=cl[:], func=AF.Exp), reads=[b_cl], writes=[b_epos])
            S.op("act", lambda e: e.activation(out=eneg[:], in_=cl[:], func=AF.Exp, scale=-1.0), reads=[b_cl], writes=[b_eneg])
            S.op("dve", lambda e: e.tensor_tensor(out=eprev[:], in0=cl[:], in1=eprev[:], op=ALU.subtract), reads=[b_cl, b_eprev], writes=[b_eprev])
            S.op("act", lambda e: e.activation(out=eprev[:], in_=eprev[:], func=AF.Exp), reads=[b_eprev], writes=[b_eprev])
            ep5 = epos[:].rearrange("k h (c t) -> k h c t", t=64)
            S.op("dve", lambda e: e.tensor_copy(out=PCf[:], in_=ep5[:, :, :, 63]), reads=[b_epos], writes=[b_PCf])
            en5 = eneg[:].rearrange("k h (c t) -> k h c t", t=64)
            ec5 = eC[:].rearrange("k h (c t) -> k h c t", t=64)
            for h in range(4):
                for c in range(NCH):
                    if (h + c) % 2:
                        S.op("dve", lambda e: e.tensor_scalar(out=ec5[:, h, c, :], in0=en5[:, h, c, :], scalar1=PCf[:, h, c:c + 1], scalar2=None, op0=ALU.mult),
                             reads=[b_eneg, b_PCf], writes=[b_eC])
                    else:
                        S.op("act", lambda e: e.activation(out=ec5[:, h, c, :], in_=en5[:, h, c, :], func=AF.Copy, scale=AS32(PCf[:, h, c:c + 1])),
                             reads=[b_eneg, b_PCf], writes=[b_eC])
            for h in range(4):
                S.op("dve", lambda e: e.scalar_tensor_tensor(out=AR[:, :, h // 2, 0, h % 2, :], in0=hm(kkn)[:, h], scalar=-1.0, in1=hm(eprev)[:, h], op0=ALU.mult, op1=ALU.mult), reads=[b_kkn, b_eprev], writes=[b_AR])
                S.op("dve", lambda e: e.tensor_tensor(out=AR[:, :, h // 2, 1, h % 2, :], in0=X3[:, h, :].rearrange("k (c t) -> k c t", t=64), in1=hm(epos)[:, h], op=ALU.mult), reads=[b_X3, b_epos], writes=[b_AR])
            S.op("dve", lambda e: e.tensor_tensor(out=tmp[:], in0=kkn[:], in1=aa[:], op=ALU.mult), reads=[b_kkn, b_aa], writes=[b_tmp])
            for h in range(4):
                S.op("dve", lambda e: e.tensor_tensor(out=cm(Bt)[:, h], in0=hm(tmp)[:, h], in1=hm(eneg)[:, h], op=ALU.mult), reads=[b_tmp, b_eneg], writes=[b_Bt])
                S.op("pool", lambda e: e.tensor_tensor(out=cm(Bh)[:, h], in0=hm(tmp)[:, h], in1=hm(eC)[:, h], op=ALU.mult), reads=[b_tmp, b_eC], writes=[b_Bh])
                S.op("dve", lambda e: e.tensor_tensor(out=cm(Kt)[:, h], in0=hm(kfin)[:, h], in1=hm(eneg)[:, h], op=ALU.mult), reads=[b_kfin, b_eneg], writes=[b_Kt])
                S.op("pool", lambda e: e.tensor_tensor(out=cm(Kh)[:, h], in0=hm(kfin)[:, h], in1=hm(eC)[:, h], op=ALU.mult), reads=[b_kfin, b_eC], writes=[b_Kh])
                cp(S, "act", cm(Vc_)[:, h], X3[:, 8 + h, :].rearrange("k (c t) -> k c t", t=64), [b_X3], [b_Vc_])
            for p in range(2):
                pb, bpb = nbx()
                S.op("pe", lambda e: e.matmul(pb[:, 0:NCH], lhsT=il[:], rhs=PCf[:, 2 * p, :], start=True, stop=False), reads=[b_il, b_PCf], writes=[bpb])
                S.op("pe", lambda e: e.matmul(pb[:, 0:NCH], lhsT=ir[:], rhs=PCf[:, 2 * p + 1, :], start=False, stop=True), reads=[b_ir, b_PCf], writes=[bpb])
                evac(PCc[:, p, :], pb[:, 0:NCH], [bpb], [b_PCc])
            if tt == 0:
                g.dbg("d_X3", X3[:], b_X3, [64, 12, TB]); g.dbg("d_cl", cl[:], b_cl, [64, 4, TB]); g.dbg("d_aa", aa[:], b_aa, [64, 4, TB])
                g.dbg("d_kkn", kkn[:], b_kkn, [64, 4, TB]); g.dbg("d_kfin", kfin[:], b_kfin, [64, 4, TB]); g.dbg("d_bv", bv[:], b_bv, [64, 4, TB])
                g.dbg("d_gT", gT[:], b_gT, [64, 4, TB]); g.dbg("d_eC", eC[:], b_eC, [64, 4, TB])
                g.dbg("d_PCc", PCc[:], b_PCc, [128, 2, NCH])
            def unit(c, p):
                tc = slice(c * 64, (c + 1) * 64)
                hp = slice(2 * p, 2 * p + 2)
                fl = lambda ap: ap.rearrange("k h t -> k (h t)")
                At_ = fl(AR[:, c, p, 0, :, :]); Bt_ = fl(Bt[:, c, hp, :]); Kt_ = fl(Kt[:, c, hp, :])
                ARp = AR[:, c, p, :, :, :].rearrange("k a h t -> k (a h t)")
                p1, bp1 = nb(); p2, bp2 = nb()
                S.op("pe", lambda e: e.matmul(p1[:, 0:256], lhsT=RR(Bt_), rhs=RR(ARp), start=True, stop=True), reads=[b_Bt, b_AR], writes=[bp1])
                S.op("pe", lambda e: e.matmul(p2[:, 0:256], lhsT=RR(Kt_), rhs=RR(ARp), start=True, stop=True), reads=[b_Kt, b_AR], writes=[bp2])
                yield
                N0, bN0 = nt_(); ArbT, bArbT = nt_(); AakT, bAakT = nt_(); ArkT, bArkT = nt_()
                S.op("dve", lambda e: e.tensor_tensor(out=N0[:], in0=p1[:, 0:128], in1=msu[:], op=ALU.mult), reads=[bp1, b_msu], writes=[bN0])
                S.op("dve", lambda e: e.tensor_tensor(out=ArbT[:], in0=p1[:, 128:256], in1=mu_[:], op=ALU.mult), reads=[bp1, b_mu], writes=[bArbT])
                rel(bp1)
                S.op("dve", lambda e: e.tensor_tensor(out=AakT[:], in0=p2[:, 0:128], in1=msu[:], op=ALU.mult), reads=[bp2, b_msu], writes=[bAakT])
                S.op("dve", lambda e: e.tensor_tensor(out=ArkT[:], in0=p2[:, 128:256], in1=mu_[:], op=ALU.mult), reads=[bp2, b_mu], writes=[bArkT])
                rel(bp2)
                p3, bp3 = nb()
                S.op("pe", lambda e: e.matmul(p3[:, 0:128], lhsT=RR(At_), rhs=RR(Bt_), start=True, stop=True), reads=[b_AR, b_Bt], writes=[bp3])
                ptr, bptr = nb()
                srcs = [(At_, b_AR), (fl(Vc_[:, c, hp, :]), b_Vc_), (fl(Bh[:, c, hp, :]), b_Bh), (fl(Kh[:, c, hp, :]), b_Kh)]
                for i_, (sap, sb_) in enumerate(srcs):
                    S.op("pe", lambda e: e.transpose(out=ptr[:, i_ * 64:(i_ + 1) * 64], in_=AS32(sap) if i_ == 0 else sap, identity=idf[0:64, 0:64]), reads=[sb_, b_idf], writes=[bptr])
                yield
                NT0, bNT0 = nt_()
                S.op("dve", lambda e: e.tensor_tensor(out=NT0[:], in0=p3[:, 0:128], in1=msl[:], op=ALU.mult), reads=[bp3, b_msl], writes=[bNT0])
                rel(bp3)
                Z, bZ = nt_()
                S.op("pool", lambda e: e.tensor_tensor(out=Z[:], in0=N0[:], in1=idf[:], op=ALU.add), reads=[bN0, b_idf], writes=[bZ])
                TA, bTA = nt_()
                Vt, bVt = nt_()
                cp(S, "act", TA[:, 0:64], ptr[:, 0:64], [bptr], [bTA])
                cp(S, "act", Vt[:, 0:64], ptr[:, 64:128], [bptr], [bVt])
                bi = bdc[0] % NBD; bdc[0] += 1
                Bbd, bBbd = bd[0][bi], b_bd[0][bi]
                Kbd, bKbd = bd[1][bi], b_bd[1][bi]
                Apb, bApb = bd[2][bi], b_bd[2][bi]
                for hh in range(2):
                    rs = slice(hh * 64, (hh + 1) * 64)
                    cp(S, "act", Bbd[rs, rs], ptr[rs, 128:192], [bptr], [bBbd])
                    cp(S, "act", Kbd[rs, rs], ptr[rs, 192:256], [bptr], [bKbd])
                rel(bptr)
                yield
                X, bX, XT, bXT = N0, bN0, NT0, bNT0
                pw, bpw = nb()
                S.op("pe", lambda e: e.matmul(pw[:, 0:64], lhsT=RR(AakT[:]), rhs=RR(Vt[:, 0:64]), start=True, stop=True), reads=[bAakT, bVt], writes=[bpw])
                yield
                cp(S, "act", TA[:, 64:128], pw[:, 0:64], [bpw], [bTA])
                rel(bpw)
                for j in range(1, 6):
                    if j <= 4:
                        px, bpx = nb()
                        S.op("pe", lambda e: e.matmul(px[:, 0:128], lhsT=RR(XT[:]), rhs=RR(X[:]), start=True, stop=True), reads=[bXT, bX], writes=[bpx])
                    pxt, bpxt = nb()
                    S.op("pe", lambda e: e.matmul(pxt[:, 0:128], lhsT=RR(X[:]), rhs=RR(XT[:]), start=True, stop=True), reads=[bXT, bX], writes=[bpxt])
                    yield
                    if j <= 4:
                        Xn, bXn = nt_()
                        cp(S, "act", Xn[:], px[:, 0:128], [bpx], [bXn])
                        rel(bpx)
                    XTn, bXTn = nt_()
                    cp(S, "act" if (j > 4 or j % 2 == 0) else "dve", XTn[:], pxt[:, 0:128], [bpxt], [bXTn])
                    rel(bpxt)
                    pz, bpz = nb()
                    S.op("pe", lambda e: e.matmul(pz[:, 0:128], lhsT=RR(XTn[:]), rhs=RR(Z[:]), start=True, stop=True), reads=[bXTn, bZ], writes=[bpz])
                    yield
                    Zn, bZn = nt_()
                    S.op("dve", lambda e: e.tensor_tensor(out=Zn[:], in0=pz[:, 0:128], in1=Z[:], op=ALU.add), reads=[bpz, bZ], writes=[bZn])
                    rel(bpz)
                    Z, bZ = Zn, bZn
                    if j <= 4:
                        X, bX = Xn, bXn
                    XT, bXT = XTn, bXTn
                pu, bpu = nb()
                S.op("pe", lambda e: e.matmul(pu[:, 0:128], lhsT=RR(Z[:]), rhs=RR(TA[:]), start=True, stop=True), reads=[bZ, bTA], writes=[bpu])
                yield
                U0, bU0 = nt_()
                cp(S, "dve", U0[:, 0:64], pu[:, 64:128], [bpu], [bU0])
                for hh in range(2):
                    rs = slice(hh * 64, (hh + 1) * 64)
                    cp(S, "act", Apb[rs, rs], pu[rs, 0:64], [bpu], [bApb])
                rel(bpu)
                pg, bpg = nb()
                S.op("pe", lambda e: e.matmul(pg[:, 0:64], lhsT=RR(Bbd[:]), rhs=RR(U0[:, 0:64]), start=True, stop=False), reads=[bBbd, bU0], writes=[bpg])
                S.op("pe", lambda e: e.matmul(pg[:, 0:64], lhsT=RR(Kbd[:]), rhs=RR(Vt[:, 0:64]), start=False, stop=True), reads=[bKbd, bVt], writes=[bpg])
                pf, bpf = nb()
                S.op("pe", lambda e: e.matmul(pf[:, 0:128], lhsT=RR(Apb[:]), rhs=RR(Bbd[:]), start=True, stop=True), reads=[bApb, bBbd], writes=[bpf])
                yield
                Gs, bGs = ntf_()
                cp(S, "act", Gs[:, 0:64], pg[:, 0:64], [bpg], [bGs])
                rel(bpg)
                PhiT, bPhiT = nt_()
                S.op("dve", lambda e: e.scalar_tensor_tensor(out=PhiT[:], in0=idf[:], scalar=PCc[:, p, c:c + 1], in1=pf[:, 0:128], op0=ALU.mult, op1=ALU.add),
                     reads=[b_idf, b_PCc, bpf], writes=[bPhiT])
                rel(bpf)
                pr, bpr = nb()
                S.op("pe", lambda e: e.matmul(pr[:, 0:64], lhsT=RR(il[:]), rhs=RR(AR[:, c, p, 1, 0, :]), start=True, stop=False, **SK), reads=[b_il, b_AR], writes=[bpr])
                S.op("pe", lambda e: e.matmul(pr[:, 64:128], lhsT=RR(ir[:]), rhs=RR(AR[:, c, p, 1, 1, :]), start=False, stop=False, **SK), reads=[b_ir, b_AR], writes=[bpr])
                S.op("pe", lambda e: e.matmul(pr[:, 0:128], lhsT=RR(Apb[:]), rhs=RR(ArbT[:]), start=False, stop=True, **SK), reads=[bApb, bArbT], writes=[bpr])
                yield
                RpT, bRpT = nt_()
                cp(S, "act", RpT[:], pr[:, 0:128], [bpr], [bRpT])
                rel(bpr)
                Scur, bScur = ST[p][sidx[p]], b_ST[p][sidx[p]]
                py, bpy = nb()
                S.op("pe", lambda e: e.matmul(py[:, 0:64], lhsT=RR(ArbT[:]), rhs=RR(U0[:, 0:64]), start=True, stop=False), reads=[bArbT, bU0], writes=[bpy])
                S.op("pe", lambda e: e.matmul(py[:, 0:64], lhsT=RR(ArkT[:]), rhs=RR(Vt[:, 0:64]), start=False, stop=False), reads=[bArkT, bVt], writes=[bpy])
                S.op("pe", lambda e: e.matmul(py[:, 0:64], lhsT=RR(RpT[:]), rhs=RR(Scur[:]), start=False, stop=True), reads=[bRpT, bScur], writes=[bpy])
                ps_, bps = nb()
                S.op("pe", lambda e: e.matmul(ps_[:, 0:64], lhsT=RR(PhiT[:]), rhs=RR(Scur[:]), start=True, stop=True), reads=[bPhiT, bScur], writes=[bps])
                sidx[p] ^= 1
                Snew, bSnew = ST[p][sidx[p]], b_ST[p][sidx[p]]
                S.op("dve", lambda e: e.tensor_tensor(out=Snew[:], in0=ps_[:, 0:64], in1=Gs[:, 0:64], op=ALU.add), reads=[bps, bGs], writes=[bSnew])
                rel(bps)
                Yt, bYt = ntf_()
                st_, bst = ntf_()
                cp(S, "act", Yt[:, 0:64], py[:, 0:64], [bpy], [bYt])
                rel(bpy)
                yield
                S.op("dve", lambda e: e.bn_stats(out=st_[:, 0:6], in_=Yt[:, 0:64]), reads=[bYt], writes=[bst])
                yield
                S.op("dve", lambda e: e.bn_aggr(out=st_[:, 8:10], in_=st_[:, 0:6]), reads=[bst], writes=[bst])
                yield
                S.op("act", lambda e: e.activation(out=st_[:, 10:11], in_=st_[:, 9:10], func=AF.Sqrt, bias=cst[:, 0:1], scale=1.0), reads=[bst, b_cst], writes=[bst])
                yield
                S.op("dve", lambda e: e.reciprocal(out=st_[:, 11:12], in_=st_[:, 10:11]), reads=[bst], writes=[bst])
                yield
                S.op("dve", lambda e: e.tensor_scalar(out=Yt[:, 0:64], in0=Yt[:, 0:64], scalar1=st_[:, 8:9], scalar2=st_[:, 11:12], op0=ALU.subtract, op1=ALU.mult),
                     reads=[bYt, bst], writes=[bYt])
                pyt, bpyt = nb()
                S.op("pe", lambda e: e.transpose(out=pyt[0:64, 0:128], in_=Yt[:, 0:64], identity=idf[:]), reads=[bYt, b_idf], writes=[bpyt])
                yield
                cp(S, "act", YN[:, hp, tc], pyt[0:64, 0:128].rearrange("v (h t) -> v h t", h=2), [bpyt], [b_YN])
                rel(bpyt)

            GRP = 2
            for c0_ in range(0, NCH, GRP):
                for k_ in range(NB):
                    busy[k_] = False
                gens = [unit(c_, p_) for c_ in range(c0_, min(NCH, c0_ + GRP)) for p_ in range(2)]
                alive = list(gens)
                while alive:
                    nxt = []
                    for gn_ in alive:
                        try:
                            next(gn_)
                            nxt.append(gn_)
                        except StopIteration:
                            pass
                    alive = nxt
            if tt == 0:
                g.dbg("d_YN", YN[:], b_YN, [64, 4, TB])
            for h in range(4):
                S.op("dve", lambda e: e.tensor_scalar(out=YN[:, h, :], in0=YN[:, h, :], scalar1=rp[0:64, 36 + h:37 + h], scalar2=rp[0:64, 40 + h:41 + h], op0=ALU.mult, op1=ALU.add),
                     reads=[b_YN, b_rp], writes=[b_YN])
            S.op("dve", lambda e: e.tensor_tensor(out=YN[:], in0=YN[:], in1=bv[:], op=ALU.add), reads=[b_YN, b_bv], writes=[b_YN])
            S.op("dve", lambda e: e.tensor_tensor(out=yo[:], in0=YN[:], in1=gT[:], op=ALU.mult), reads=[b_YN, b_gT], writes=[b_yo])
            S.dma("sp", D["ymixT"][0:256, t0:t0 + TB].rearrange("(h v) t -> v h t", v=64), yo[:], reads=[b_yo])
        S.barrier()


def build(T=8192, debug=False, phases=None, nlayers=2):
    nc = bass.Bass("TRN2", target_bir_lowering=False)
    g = G()
    g.nc, g.T, g.wctr = nc, T, 0
    g.debug = debug
    D = {}
    g.D = D

    def din(name, shape, dt=F32):
        D[name] = nc.dram_tensor(name, list(shape), dt, kind="ExternalInput").ap()

    def dscr(name, shape, dt=F32, out=False):
        D[name] = nc.dram_tensor(name, list(shape), dt, kind=("ExternalOutput" if (out or debug) else "Internal")).ap()

    din("xT", [DM, T]); din("pT", [2, 256, T]); din("pos", [1, T], I32); din("invf", [128, 1])
    din("w_in", [2, DM, NEXT]); din("smalls", [2, 128, 64]); din("w_out", [2, DM, DM])
    din("ffn_w_up", [2, DM, 2 * DFF]); din("ffn_w_down", [2, DFF, DM]); din("convp", [2, 128, 44, 4])
    din("ple_w_gate", [2, DM, DM]); din("ple_w_proj", [2, 256, DM])
    din("pool_wbd", [2, 128, 2, 128]); din("pool_fix", [128, 2, 16])
    for nm, shp, dt in g_extra_inputs(T):
        din(nm, shp, dt)
    dscr("cosT", [128, T]); dscr("sinT", [128, T])
    dscr("zaT", [1024, T]); dscr("zbT", [256, T]); dscr("qT", [512, T], BF16); dscr("kT", [384, T], BF16)
    dscr("vcT", [128, T], BF16); dscr("vtok", [T, 256], BF16); dscr("gates", [T, 24])
    dscr("ymixT", [1024, T], BF16)
    for nm, shp, dt in g_extra_scratch(T):
        dscr(nm, shp, dt)
    dscr("xs0", [DM, T]); dscr("xs1", [DM, T]); dscr("xs2", [DM, T])
    dscr("outT", [DM, T], out=True)
    with ExitStack() as stack:
        g.S = Sched(nc, stack)
        if phases is None:
            phases = ("rope", "inproj", "rwkv", "pool", "nsa", "outproj", "ffn", "ple")
        if "rope" in phases:
            phase_rope(g)
        xcur = D["xT"]
        for l in range(nlayers):
            if "inproj" in phases:
                phase_inproj(g, l, xcur)
            if "rwkv" in phases:
                phase_rwkv(g, l)
            if "pool" in phases:
                phase_pool(g, l)
            if "nsa" in phases:
                phase_nsa(g, l)
            if "outproj" in phases:
                phase_outproj(g, l, xcur, D["xs0"])
            if "ffn" in phases:
                phase_ffn(g, l, D["xs0"], D["xs1"])
            if "ple" in phases:
                last = (l == nlayers - 1)
                phase_ple(g, l, D["xs1"], D["outT"] if last else D["xs2"], last)
            xcur = D["xs2"]
        g.S.barrier()
        g.ninstr = g.S.ninstr
    return nc, g


def g_extra_inputs(T):
    NCMP = (T - 32) // 16 + 1
    NTC = (NCMP + 127) // 128
    return [("nsa_masks", [128, 19, 512], BF16), ("identb", [128, 128], BF16), ("identf", [128, 128], F32),
            ("E_all", [128, T], BF16), ("mcs", [128, NTC, 128], BF16), ("keepw", [128, 256], F32), ("addw", [128, 256], F32),
            ("rw_msu", [128, 128], F32), ("rw_mu", [128, 128], F32), ("rw_msl", [128, 128], F32), ("rw_il", [64, 128], F32), ("rw_ir", [64, 128], F32),
            ("rwp", [2, 128, 64], F32), ("rw_w_up", [2, 64, 256], F32), ("rw_a_up", [2, 64, 256], F32), ("rw_g_up", [2, 128, 256], F32),
            ("nsa_w_ck", [2, 32, 64, 64], F32), ("nsa_w_cv", [2, 32, 64, 64], F32), ("nsa_peT", [2, 2, 64, 32], F32)]


def g_extra_scratch(T):
    return []


def host_prep(inp, T=8192):
    f = np.float32
    cols = inproj_cols()
    shared = {}
    shared["w_in"] = np.ascontiguousarray(inp["w_in"][:, :, cols])
    sm = np.zeros((2, 128, 64), f)
    for l in range(2):
        sm[l, :, 0:8] = inp["g_mix"][l].reshape(8, 128).T
        sm[l, :, 8:10] = inp["pool_scale"][l].reshape(2, 128).T
        sm[l, :, 16:24] = inp["g_ffn"][l].reshape(8, 128).T
        sm[l, :, 24:32] = inp["g_ple"][l].reshape(8, 128).T
        sm[l, :, 32:40] = inp["g_final"].reshape(8, 128).T
    shared["smalls"] = sm
    shared["w_out"] = np.ascontiguousarray(inp["w_out"])
    shared["ffn_w_up"] = np.ascontiguousarray(inp["ffn_w_up"])
    shared["ffn_w_down"] = np.ascontiguousarray(inp["ffn_w_down"])
    cp_ = np.zeros((2, 128, 44, 4), f)
    for l in range(2):
        cw = inp["ffn_conv_w"][l][:, 0, :]
        for i in range(3):
            cp_[l, :, :, i] = cw[i].reshape(44, 128).T
        cp_[l, :, :, 3] = inp["ffn_conv_b"][l].reshape(44, 128).T
    shared["convp"] = cp_
    shared["ple_w_gate"] = np.ascontiguousarray(inp["ple_w_gate"])
    shared["ple_w_proj"] = np.ascontiguousarray(inp["ple_w_proj"])
    pw = np.zeros((2, 128, 2, 128), f)
    for l in range(2):
        for gi in range(4):
            t_, h_ = gi // 2, gi % 2
            pw[l, h_ * 64:(h_ + 1) * 64, t_, h_ * 64:(h_ + 1) * 64] = inp["pool_w"][l, gi]
    shared["pool_wbd"] = pw
    fix = np.zeros((128, 2, 16), f)
    for gi, win in enumerate((2, 4, 8, 16)):
        t_, h_ = gi // 2, gi % 2
        fix[h_ * 64:(h_ + 1) * 64, t_, :] = 1.0 / np.minimum(np.arange(16) + 1, win)
    shared["pool_fix"] = fix
    shared.update(nsa_consts(T))
    shared.update(rwkv_consts())
    rwp = np.zeros((2, 128, 64), f)
    for l in range(2):
        mu = inp["rw_mu"][l]
        rwp[l, 0:64, 0:12] = mu[0:768].reshape(12, 64).T
        rwp[l, 0:64, 12:14] = mu[768:896].reshape(2, 64).T
        rwp[l, :, 14] = mu[896:1024]
        for j, nm in enumerate(("rw_w0", "rw_a0", "rw_k_k", "rw_k_a")):
            rwp[l, 0:64, 16 + 4 * j:20 + 4 * j] = inp[nm][l].reshape(4, 64).T
        rwp[l, 0:64, 32:36] = inp["rw_r_k"][l].T
        rwp[l, 0:64, 36:40] = inp["rw_gn_g"][l].reshape(4, 64).T
        rwp[l, 0:64, 40:44] = inp["rw_gn_b"][l].reshape(4, 64).T
    shared["rwp"] = rwp
    shared["rw_w_up"] = np.ascontiguousarray(inp["rw_w_up"])
    shared["rw_a_up"] = np.ascontiguousarray(inp["rw_a_up"])
    shared["rw_g_up"] = np.ascontiguousarray(inp["rw_g_up"])
    shared["nsa_w_ck"] = np.ascontiguousarray(inp["nsa_w_ck"])
    shared["nsa_w_cv"] = np.ascontiguousarray(inp["nsa_w_cv"])
    shared["nsa_peT"] = np.ascontiguousarray(np.stack([np.transpose(inp["nsa_pe_k"], (0, 2, 1)), np.transpose(inp["nsa_pe_v"], (0, 2, 1))], axis=1))
    inv = (10000.0 ** (-np.arange(32, dtype=f) / 32)).astype(f)
    shared["invf"] = np.tile(inv, 4).reshape(128, 1).astype(f)
    return shared


def per_core(inp, b, T=8192):
    return {"xT": np.ascontiguousarray(inp["x"][b, :T].T),
            "pT": np.ascontiguousarray(np.transpose(inp["p"][:, b, :T], (0, 2, 1))),
            "pos": np.ascontiguousarray(inp["positions"][b:b + 1, :T]).astype(np.int32)}


def kernel(**inputs):
    T = 8192
    inp = {k: np.asarray(v) for k, v in inputs.items()}
    nc, g = build(T)
    shared = host_prep(inp, T)
    in_maps = []
    for b in range(8):
        m = dict(shared)
        m.update(per_core(inp, b, T))
        in_maps.append(m)
    res = run_bass_kernel_spmd(nc, in_maps, core_ids=list(range(8)))
    out = np.stack([np.ascontiguousarray(res.results[b]["outT"].T) for b in range(8)], axis=0)
    return out.astype(np.float32)
```

```python
import os
import numpy as np
from contextlib import ExitStack
import concourse.bass as bass
import concourse.mybir as mybir
from concourse.bass_utils import run_bass_kernel_spmd

F32 = mybir.dt.float32
BF16 = mybir.dt.bfloat16
I32 = mybir.dt.int32
AF = mybir.ActivationFunctionType
ALU = mybir.AluOpType
AX = mybir.AxisListType

DM = 1024
PI = float(np.pi)
BIG = 30000.0
NFEAT = 2304
NTOKC = 280
NEXT = NFEAT + NTOKC
DFF = 2816


class Buf:
    __slots__ = ("w", "rs", "rd", "excl")

    def __init__(self, excl=False):
        self.w = None
        self.rs = {}
        self.rd = []
        self.excl = excl


class Sched:
    EPOCH = 30000
    NDMA = 6

    def __init__(self, nc, stack):
        self.nc = nc
        self.stack = stack
        self.engs = {"pe": nc.tensor, "act": nc.scalar, "dve": nc.vector, "pool": nc.gpsimd, "sp": nc.sync}
        self.nsem = 0
        self.sem = {}
        self.cnt = {}
        self.seen = {k: {} for k in self.engs}
        for k in self.engs:
            self.sem[k] = self._newsem(k)
            self.cnt[k] = 0
        self.dsem = {}
        self.dpos = {}
        for k in ("sp", "act", "pool"):
            self.dsem[k] = [[self._newsem("d" + k), 0] for _ in range(self.NDMA)]
            self.dpos[k] = 0
        self.last = {}
        self.ninstr = 0
        self.store_q = "pool"

    def _newsem(self, name):
        self.nsem += 1
        return self.stack.enter_context(self.nc.semaphore(f"s_{name}_{self.nsem}"))

    def _wait(self, ek, tok):
        sem, val, src = tok
        d = self.seen[ek]
        key = id(sem)
        if d.get(key, 0) >= val:
            return
        self.engs[ek].wait_ge(sem, val)
        d[key] = val

    def _deps(self, ek, reads, writes):
        toks = []
        same_ok = (ek == "pe")
        for b in reads:
            if b.w is not None and not (b.w[2] == ek and same_ok):
                toks.append(b.w)
            if b.excl:
                for e, t in b.rs.items():
                    if e != ek:
                        toks.append(t)
        for b in writes:
            if b.w is not None and not (b.w[2] == ek and same_ok):
                toks.append(b.w)
            for e, t in b.rs.items():
                if not (e == ek and same_ok):
                    toks.append(t)
            toks.extend(b.rd)
        for t in toks:
            self._wait(ek, t)

    def _record(self, tok, reads, writes, is_dma):
        for b in reads:
            if is_dma:
                b.rd.append(tok)
                if len(b.rd) > 24:
                    del b.rd[0]
            else:
                b.rs[tok[2]] = tok
        for b in writes:
            b.w = tok
            b.rs = {}
            b.rd = []

    def op(self, ek, fn, reads=(), writes=()):
        self._deps(ek, reads, writes)
        ins = fn(self.engs[ek])
        self.cnt[ek] += 1
        ins.then_inc(self.sem[ek], 1)
        tok = (self.sem[ek], self.cnt[ek], ek)
        self.last[ek] = tok
        self._record(tok, reads, writes, False)
        self.ninstr += 1
        if self.cnt[ek] >= self.EPOCH:
            self.sem[ek] = self._newsem(ek)
            self.cnt[ek] = 0
        return tok

    def dma(self, qk, out, in_, reads=(), writes=(), **kw):
        if qk == "sp" and len(writes) == 0 and self.store_q is not None:
            qk = self.store_q
        self._deps(qk, reads, writes)
        slots = self.dsem[qk]
        i = self.dpos[qk]
        self.dpos[qk] = (i + 1) % len(slots)
        sem, val = slots[i]
        if val > 0:
            self._wait(qk, (sem, val, "dma"))
        if val + 16 > 60000:
            sem = self._newsem("d" + qk)
            val = 0
            slots[i][0] = sem
        ins = self.engs[qk].dma_start(out=out, in_=in_, **kw)
        val += 16
        ins.then_inc(sem, 16)
        slots[i][1] = val
        tok = (sem, val, "dma")
        self._record(tok, reads, writes, True)
        self.ninstr += 1
        return tok

    def barrier(self):
        toks = list(self.last.values())
        for qk in self.dsem:
            for sem, val in self.dsem[qk]:
                if val > 0:
                    toks.append((sem, val, "dma"))
        for ek in self.engs:
            for t in toks:
                self._wait(ek, t)


def inproj_cols():
    qb = 1280
    cols = list(range(0, 1280))
    cols += list(range(qb, qb + 512))
    for off in (512, 768, 1024):
        cols += list(range(qb + off, qb + off + 128))
    cols += list(range(qb + 640, qb + 768))
    assert len(cols) == NFEAT
    cols += list(range(qb + 896, qb + 1024)) + list(range(qb + 1152, qb + 1280))
    cols += list(range(qb + 1280, qb + 1304))
    assert len(cols) == NEXT
    return np.array(cols)


class G:
    uid = 0
    debug = False
    use_f32r = True

    def dbg(self, name, ap, buf, shape):
        if not self.debug or name in self.D:
            return
        self.D[name] = self.nc.dram_tensor(name, list(shape), F32, kind="ExternalOutput").ap()
        self.S.dma("sp", self.D[name], ap, reads=[buf])

    def nm(self, name):
        self.uid += 1
        return f"{name}_{self.uid}"


def cp(S, ek, out, in_, reads, writes):
    if ek == "act":
        return S.op("act", lambda e: e.copy(out=out, in_=in_), reads=reads, writes=writes)
    return S.op(ek, lambda e: e.tensor_copy(out=out, in_=in_), reads=reads, writes=writes)


def load_w_bf16(g, dst, b_dst, src, KC, N, stage, b_stage, rows=128):
    S = g.S
    CH = stage[0].shape[-1]
    srcv = src.rearrange("(c p) n -> p c n", p=rows)
    for c in range(KC):
        for n0 in range(0, N, CH):
            n1 = min(N, n0 + CH)
            i = g.wctr % len(stage)
            g.wctr += 1
            S.dma("sp", stage[i][0:rows, 0:n1 - n0], srcv[:, c, n0:n1], writes=[b_stage[i]])
            ek = ("act", "dve", "pool")[g.wctr % 3]
            cp(S, ek, dst[0:rows, c, n0:n1], stage[i][0:rows, 0:n1 - n0], [b_stage[i]], [b_dst])


def rmsnorm_tile(g, xt, b_x, hT, b_h, gcol, b_g, N, R):
    S = g.S
    S.op("act", lambda e: e.activation(out=R["sq"][:, :, 0:N], in_=xt[:, :, 0:N], func=AF.Square), reads=[b_x], writes=[R["b_sq"]])
    for c in range(8):
        S.op("pe", lambda e: e.matmul(R["p_rms"][:, 0:N], lhsT=R["ones"][:], rhs=R["sq"][:, c, 0:N], start=(c == 0), stop=(c == 7)),
             reads=[R["b_ones"], R["b_sq"]], writes=[R["b_prms"]])
    S.op("act", lambda e: e.activation(out=R["rstd"][:, 0:N], in_=R["p_rms"][:, 0:N], func=AF.Sqrt, bias=R["eps"][:, 0:1], scale=1.0 / DM),
         reads=[R["b_prms"], R["b_eps"]], writes=[R["b_rstd"]])
    S.op("dve", lambda e: e.reciprocal(out=R["rstd"][:, 0:N], in_=R["rstd"][:, 0:N]), reads=[R["b_rstd"]], writes=[R["b_rstd"]])
    for c in range(8):
        S.op("dve", lambda e: e.scalar_tensor_tensor(out=hT[:, c, 0:N], in0=xt[:, c, 0:N], scalar=gcol[:, c:c + 1], in1=R["rstd"][:, 0:N],
                                                       op0=ALU.mult, op1=ALU.mult),
             reads=[b_x, b_g, R["b_rstd"]], writes=[b_h])


def rms_shared(g, sbf, psf, N):
    S = g.S
    R = {}
    R["sq"] = sbf("rsq", [128, 8, N], BF16); R["b_sq"] = Buf()
    R["rstd"] = sbf("rstd", [128, N]); R["b_rstd"] = Buf()
    R["p_rms"] = psf("p_rms", [128, 512]); R["b_prms"] = Buf(excl=True)
    R["ones"] = sbf("ones", [128, 128], BF16); R["b_ones"] = Buf()
    R["eps"] = sbf("epsb", [128, 1]); R["b_eps"] = Buf()
    S.op("pool", lambda e: e.memset(R["ones"][:], 1.0), writes=[R["b_ones"]])
    S.op("pool", lambda e: e.memset(R["eps"][:], 1e-6), writes=[R["b_eps"]])
    return R


def phase_rope(g):
    nc, S, D, T = g.nc, g.S, g.D, g.T
    with ExitStack() as st:
        sbf = lambda name, shape, dt=F32: st.enter_context(nc.sbuf_tensor(g.nm(name), list(shape), dt))
        CH = min(T, 2048)
        posi = sbf("posi", [128, CH], I32); b_posi = Buf()
        posf = sbf("posf", [128, CH]); b_posf = Buf()
        ang = sbf("ang", [128, CH]); b_ang = Buf()
        tab = sbf("tab", [128, CH]); b_tab = Buf()
        ki = sbf("ki", [128, CH], I32); b_ki = Buf()
        kf = sbf("kf", [128, CH]); b_kf = Buf()
        inv_sb = sbf("inv_sb", [128, 1]); b_inv = Buf()
        S.dma("sp", inv_sb[:], D["invf"], writes=[b_inv])
        C1 = 6.28125
        C2 = 2 * np.pi - 6.28125
        for c0 in range(0, T, CH):
            S.dma("sp", posi[:], D["pos"][:, c0:c0 + CH].partition_broadcast(128), writes=[b_posi])
            S.op("dve", lambda e: e.tensor_copy(out=posf[:], in_=posi[:]), reads=[b_posi], writes=[b_posf])
            S.op("dve", lambda e: e.tensor_scalar(out=posf[:], in0=posf[:], scalar1=inv_sb[:, 0:1], scalar2=None, op0=ALU.mult),
                 reads=[b_posf, b_inv], writes=[b_posf])
            S.op("dve", lambda e: e.tensor_scalar(out=kf[:], in0=posf[:], scalar1=float(1.0 / (2 * np.pi)), scalar2=None, op0=ALU.mult),
                 reads=[b_posf], writes=[b_kf])
            S.op("dve", lambda e: e.tensor_copy(out=ki[:], in_=kf[:]), reads=[b_kf], writes=[b_ki])
            S.op("dve", lambda e: e.tensor_copy(out=kf[:], in_=ki[:]), reads=[b_ki], writes=[b_kf])
            S.op("dve", lambda e: e.scalar_tensor_tensor(out=posf[:], in0=kf[:], scalar=-C1, in1=posf[:], op0=ALU.mult, op1=ALU.add),
                 reads=[b_kf, b_posf], writes=[b_posf])
            S.op("dve", lambda e: e.scalar_tensor_tensor(out=posf[:], in0=kf[:], scalar=-C2, in1=posf[:], op0=ALU.mult, op1=ALU.add),
                 reads=[b_kf, b_posf], writes=[b_posf])
            for which, shift, dst in (("sin", 0.0, D["sinT"]), ("cos", PI / 2, D["cosT"])):
                S.op("dve", lambda e: e.tensor_scalar(out=ang[:], in0=posf[:], scalar1=shift, scalar2=None, op0=ALU.add),
                     reads=[b_posf], writes=[b_ang])
                S.op("dve", lambda e: e.tensor_scalar(out=kf[:], in0=ang[:], scalar1=PI, scalar2=-2 * PI, op0=ALU.is_gt, op1=ALU.mult),
                     reads=[b_ang], writes=[b_kf])
                S.op("dve", lambda e: e.tensor_tensor(out=ang[:], in0=ang[:], in1=kf[:], op=ALU.add), reads=[b_ang, b_kf], writes=[b_ang])
                S.op("dve", lambda e: e.tensor_scalar(out=ang[:], in0=ang[:], scalar1=3.141592, scalar2=-3.141592, op0=ALU.min, op1=ALU.max),
                     reads=[b_ang], writes=[b_ang])
                S.op("act", lambda e: e.activation(out=tab[:], in_=ang[:], func=AF.Sin), reads=[b_ang], writes=[b_tab])
                if which == "sin":
                    for base in (0, 64):
                        S.op("dve", lambda e: e.tensor_scalar(out=tab[base:base + 32, :], in0=tab[base:base + 32, :], scalar1=-1.0, scalar2=None, op0=ALU.mult),
                             reads=[b_tab], writes=[b_tab])
                S.dma("sp", dst[:, c0:c0 + CH], tab[:], reads=[b_tab])
        S.barrier()


def phase_inproj(g, l, xin):
    nc, S, D, T = g.nc, g.S, g.D, g.T
    with ExitStack() as st:
        sbf = lambda name, shape, dt=F32: st.enter_context(nc.sbuf_tensor(g.nm(name), list(shape), dt))
        psf = lambda name, shape, dt=F32: st.enter_context(nc.psum_tensor(g.nm(name), list(shape), dt))
        Wb = sbf("Wb", [128, 8, NEXT], BF16); b_W = Buf()
        stage = [sbf(f"wst{i}", [128, 1740]) for i in range(2)]; b_stage = [Buf() for _ in range(2)]
        load_w_bf16(g, Wb, b_W, D["w_in"][l], 8, NEXT, stage, b_stage)
        sm = sbf("sm", [128, 8]); b_sm = Buf()
        S.dma("sp", sm[:], D["smalls"][l][:, 0:8], writes=[b_sm])
        R = rms_shared(g, sbf, psf, 512)
        permb = sbf("permb", [128, 128], BF16); b_perm = Buf()
        S.dma("sp", permb[:], D["permb"], writes=[b_perm])
        qbf = [sbf(f"qbf{i}", [128, 512], BF16) for i in range(2)]; b_qbf = [Buf() for _ in range(2)]
        xt = [sbf(f"xt{i}", [128, 8, 512]) for i in range(2)]; b_xt = [Buf() for _ in range(2)]
        hTs = [sbf(f"hT{i}", [128, 8, 512], BF16) for i in range(2)]; b_hs = [Buf() for _ in range(2)]
        cs = [sbf(f"cs{i}", [128, 512]) for i in range(2)]; b_cs = [Buf() for _ in range(2)]
        sn = [sbf(f"sn{i}", [128, 512]) for i in range(2)]; b_sn = [Buf() for _ in range(2)]
        NEV = 4
        ev = [sbf(f"ev{i}", [128, 512]) for i in range(NEV)]; b_ev = [Buf() for _ in range(NEV)]
        evb = [sbf(f"evb{i}", [128, 512], BF16) for i in range(NEV)]; b_evb = [Buf() for _ in range(NEV)]
        t1 = [sbf(f"t1_{i}", [128, 512]) for i in range(2)]; b_t1 = [Buf() for _ in range(2)]
        t2 = [sbf(f"t2_{i}", [128, 512]) for i in range(2)]; b_t2 = [Buf() for _ in range(2)]
        vt = [sbf(f"vt{i}", [128, 256], BF16) for i in range(2)]; b_vt = [Buf() for _ in range(2)]
        gt = [sbf(f"gt{i}", [128, 24]) for i in range(2)]; b_gt = [Buf() for _ in range(2)]
        NPF = 5
        p_f = [psf(f"p_f{i}", [128, 512]) for i in range(NPF)]; b_pf = [Buf(excl=True) for _ in range(NPF)]
        p_t = [psf(f"p_t{i}", [128, 512]) for i in range(2)]; b_pt = [Buf(excl=True) for _ in range(2)]
        xv = xin.rearrange("(c p) t -> p c t", p=128)
        evc = 0
        pfc = 0

        NTL = T // 512

        def prep(tt):
            xi = tt % 2
            t0 = tt * 512
            S.dma("sp", xt[xi][:], xv[:, :, t0:t0 + 512], writes=[b_xt[xi]])
            S.dma("sp", cs[xi][:], D["cosT"][:, t0:t0 + 512], writes=[b_cs[xi]])
            S.dma("sp", sn[xi][:], D["sinT"][:, t0:t0 + 512], writes=[b_sn[xi]])
            rmsnorm_tile(g, xt[xi], b_xt[xi], hTs[xi], b_hs[xi], sm, b_sm, 512, R)

        prep(0)
        for tt in range(NTL):
            t0 = tt * 512
            xi = tt % 2
            if tt + 1 < NTL:
                prep(tt + 1)
            hT, b_h = hTs[xi], b_hs[xi]

            def mm_feat(f, pf, bpf):
                for c in range(8):
                    S.op("pe", lambda e: e.matmul(pf[:], lhsT=Wb[:, c, f * 128:(f + 1) * 128], rhs=hT[:, c, :], start=(c == 0), stop=(c == 7)),
                         reads=[b_W, b_h], writes=[bpf])
            plain = [(f, D["zaT"][f * 128:(f + 1) * 128, t0:t0 + 512], False) for f in range(8)]
            plain += [(8 + f, D["zbT"][f * 128:(f + 1) * 128, t0:t0 + 512], False) for f in range(2)]
            plain += [(17, D["vcT"][:, t0:t0 + 512], True)]
            for k_, (f, dst, isb) in enumerate(plain):
                pi = pfc % NPF; pfc += 1
                mm_feat(f, p_f[pi], b_pf[pi])
                ei = evc % NEV; evc += 1
                ek = "act" if k_ % 2 == 0 else "dve"
                if isb:
                    cp(S, ek, evb[ei][:], p_f[pi][:], [b_pf[pi]], [b_evb[ei]])
                    S.dma("sp", dst, evb[ei][:], reads=[b_evb[ei]])
                else:
                    cp(S, ek, ev[ei][:], p_f[pi][:], [b_pf[pi]], [b_ev[ei]])
                    S.dma("sp", dst, ev[ei][:], reads=[b_ev[ei]])
            pend = None
            for j in range(8):
                cur = None
                if j < 7:
                    if j < 4:
                        f_a = 10 + j
                        dst = D["qT"][j * 128:(j + 1) * 128, t0:t0 + 512]
                    else:
                        f_a = 14 + (j - 4)
                        dst = D["kT"][(j - 4) * 128:(j - 3) * 128, t0:t0 + 512]
                    pa = pfc % NPF; pfc += 1
                    mm_feat(f_a, p_f[pa], b_pf[pa])
                    qi_ = j % 2
                    cp(S, "act", qbf[qi_][:], p_f[pa][:], [b_pf[pa]], [b_qbf[qi_]])
                    cur = (pa, qi_, dst, j)
                if pend is not None:
                    pa_, qj_, dst_, jj = pend
                    pb = pfc % NPF; pfc += 1
                    S.op("pe", lambda e: e.matmul(p_f[pb][:], lhsT=permb[:], rhs=qbf[qj_][:], start=True, stop=True),
                         reads=[b_perm, b_qbf[qj_]], writes=[b_pf[pb]])
                    ti = jj % 2
                    S.op("dve", lambda e: e.tensor_tensor(out=t1[ti][:], in0=p_f[pa_][:], in1=cs[xi][:], op=ALU.mult),
                         reads=[b_pf[pa_], b_cs[xi]], writes=[b_t1[ti]])
                    S.op("dve", lambda e: e.tensor_tensor(out=t2[ti][:], in0=p_f[pb][:], in1=sn[xi][:], op=ALU.mult),
                         reads=[b_pf[pb], b_sn[xi]], writes=[b_t2[ti]])
                    ei = evc % NEV; evc += 1
                    S.op("pool", lambda e: e.tensor_tensor(out=evb[ei][:], in0=t1[ti][:], in1=t2[ti][:], op=ALU.add),
                         reads=[b_t1[ti], b_t2[ti]], writes=[b_evb[ei]])
                    S.dma("sp", dst_, evb[ei][:], reads=[b_evb[ei]])
                pend = cur
            for s4 in range(4):
                pi = s4 % 2
                for c in range(8):
                    S.op("pe", lambda e: e.matmul(p_t[pi][:, 0:NTOKC], lhsT=hT[:, c, s4 * 128:(s4 + 1) * 128], rhs=Wb[:, c, NFEAT:NEXT],
                                                  start=(c == 0), stop=(c == 7)),
                         reads=[b_W, b_h], writes=[b_pt[pi]])
                S.op("dve", lambda e: e.tensor_copy(out=vt[pi][:], in_=p_t[pi][:, 0:256]), reads=[b_pt[pi]], writes=[b_vt[pi]])
                S.op("act", lambda e: e.activation(out=gt[pi][:], in_=p_t[pi][:, 256:280], func=AF.Sigmoid), reads=[b_pt[pi]], writes=[b_gt[pi]])
                S.dma("sp", D["vtok"][t0 + s4 * 128:t0 + (s4 + 1) * 128, :], vt[pi][:], reads=[b_vt[pi]])
                S.dma("sp", D["gates"][t0 + s4 * 128:t0 + (s4 + 1) * 128, :], gt[pi][:], reads=[b_gt[pi]])
        S.barrier()


def phase_pool(g, l):
    nc, S, D, T = g.nc, g.S, g.D, g.T
    with ExitStack() as st:
        sbf = lambda name, shape, dt=F32: st.enter_context(nc.sbuf_tensor(g.nm(name), list(shape), dt))
        psf = lambda name, shape, dt=F32: st.enter_context(nc.psum_tensor(g.nm(name), list(shape), dt))
        CH = 512
        PAD = 16
        wp = sbf("wp", [128, 2, 128]); b_wp = Buf()
        S.dma("sp", wp[:], D["pool_wbd"][l], writes=[b_wp])
        sm = sbf("smp", [128, 2]); b_sm = Buf()
        S.dma("sp", sm[:], D["smalls"][l][:, 8:10], writes=[b_sm])
        fix = sbf("fix", [128, 2, 16]); b_fix = Buf()
        S.dma("sp", fix[:], D["pool_fix"], writes=[b_fix])
        z = [[sbf(f"pz{i}_{f}", [128, PAD + CH]) for f in range(2)] for i in range(2)]
        b_z = [[Buf() for f in range(2)] for i in range(2)]
        s_a = sbf("ps_a", [128, PAD + CH]); b_sa = Buf()
        s_b = sbf("ps_b", [128, PAD + CH]); b_sb = Buf()
        pl = sbf("ppl", [128, CH]); b_pl = Buf()
        ob = [sbf(f"pob{i}", [128, CH], BF16) for i in range(2)]; b_ob = [Buf() for _ in range(2)]
        pp = [psf(f"ppp{i}", [128, 512]) for i in range(2)]; b_pp = [Buf(excl=True) for _ in range(2)]
        k = 0
        for tt in range(T // CH):
            t0 = tt * CH
            zi = tt % 2
            for f in range(2):
                zt, bz = z[zi][f], b_z[zi][f]
                if tt == 0:
                    S.op("pool", lambda e: e.memset(zt[:, 0:PAD], 0.0), writes=[bz])
                    S.dma("sp", zt[:, PAD:PAD + CH], D["zbT"][f * 128:(f + 1) * 128, 0:CH], writes=[bz])
                else:
                    S.dma("sp", zt[:], D["zbT"][f * 128:(f + 1) * 128, t0 - PAD:t0 + CH], writes=[bz])
                W = PAD + CH
                S.op("dve", lambda e: e.tensor_tensor(out=s_a[:, 1:W], in0=zt[:, 1:W], in1=zt[:, 0:W - 1], op=ALU.add), reads=[bz], writes=[b_sa])
                S.op("dve", lambda e: e.tensor_tensor(out=s_b[:, 3:W], in0=s_a[:, 3:W], in1=s_a[:, 1:W - 2], op=ALU.add), reads=[b_sa], writes=[b_sb])
                if f == 0:
                    lo, hi = s_a, s_b
                    blo, bhi = b_sa, b_sb
                    wl, wh = 2, 4
                else:
                    S.op("dve", lambda e: e.tensor_tensor(out=s_a[:, 7:W], in0=s_b[:, 7:W], in1=s_b[:, 3:W - 4], op=ALU.add), reads=[b_sb], writes=[b_sa])
                    S.op("dve", lambda e: e.tensor_tensor(out=s_b[:, 15:W], in0=s_a[:, 15:W], in1=s_a[:, 7:W - 8], op=ALU.add), reads=[b_sa], writes=[b_sb])
                    lo, hi = s_a, s_b
                    blo, bhi = b_sa, b_sb
                    wl, wh = 8, 16
                S.op("dve", lambda e: e.scalar_tensor_tensor(out=pl[0:64, :], in0=lo[0:64, PAD:W], scalar=1.0 / wl, in1=zt[0:64, PAD:W], op0=ALU.mult, op1=ALU.subtract),
                     reads=[blo, bz], writes=[b_pl])
                S.op("dve", lambda e: e.scalar_tensor_tensor(out=pl[64:128, :], in0=hi[64:128, PAD:W], scalar=1.0 / wh, in1=zt[64:128, PAD:W], op0=ALU.mult, op1=ALU.subtract),
                     reads=[bhi, bz], writes=[b_pl])
                if tt == 0:
                    S.op("dve", lambda e: e.tensor_tensor(out=pl[0:64, 0:16], in0=lo[0:64, PAD:PAD + 16], in1=fix[0:64, f, :], op=ALU.mult), reads=[blo, b_fix], writes=[b_pl])
                    S.op("dve", lambda e: e.tensor_tensor(out=pl[64:128, 0:16], in0=hi[64:128, PAD:PAD + 16], in1=fix[64:128, f, :], op=ALU.mult), reads=[bhi, b_fix], writes=[b_pl])
                    S.op("dve", lambda e: e.tensor_tensor(out=pl[:, 0:16], in0=pl[:, 0:16], in1=zt[:, PAD:PAD + 16], op=ALU.subtract), reads=[b_pl, bz], writes=[b_pl])
                pi = k % 2; k += 1
                S.op("pe", lambda e: e.matmul(pp[pi][:, 0:CH], lhsT=wp[:, f, :], rhs=pl[:], start=True, stop=True), reads=[b_wp, b_pl], writes=[b_pp[pi]])
                S.op("act", lambda e: e.activation(out=ob[pi][:], in_=pp[pi][:, 0:CH], func=AF.Copy, scale=sm[:, f:f + 1]), reads=[b_pp[pi], b_sm], writes=[b_ob[pi]])
                S.dma("sp", D["ymixT"][256 + f * 128:256 + (f + 1) * 128, t0:t0 + CH], ob[pi][:], reads=[b_ob[pi]])
        S.barrier()


def phase_outproj(g, l, xin, xout):
    nc, S, D, T = g.nc, g.S, g.D, g.T
    with ExitStack() as st:
        sbf = lambda name, shape, dt=F32: st.enter_context(nc.sbuf_tensor(g.nm(name), list(shape), dt))
        psf = lambda name, shape, dt=F32: st.enter_context(nc.psum_tensor(g.nm(name), list(shape), dt))
        Wo = sbf("Wo", [128, 8, DM], BF16); b_W = Buf()
        stage = [sbf(f"wst{i}", [128, 1024]) for i in range(2)]; b_stage = [Buf() for _ in range(2)]
        load_w_bf16(g, Wo, b_W, D["w_out"][l], 8, DM, stage, b_stage)
        xt = [sbf(f"oxt{i}", [128, 8, 512]) for i in range(2)]; b_xt = [Buf() for _ in range(2)]
        ym = [sbf(f"oym{i}", [128, 8, 512], BF16) for i in range(2)]; b_ym = [Buf() for _ in range(2)]
        xo = [sbf(f"oxo{i}", [128, 8, 512]) for i in range(2)]; b_xo = [Buf() for _ in range(2)]
        pq = [psf(f"opq{i}", [128, 512]) for i in range(4)]; b_pq = [Buf(excl=True) for _ in range(4)]
        xv = xin.rearrange("(c p) t -> p c t", p=128)
        xov = xout.rearrange("(c p) t -> p c t", p=128)
        yv = D["ymixT"].rearrange("(c p) t -> p c t", p=128)
        k = 0
        for tt in range(T // 512):
            t0 = tt * 512
            xi = tt % 2
            S.dma("sp", xt[xi][:], xv[:, :, t0:t0 + 512], writes=[b_xt[xi]])
            S.dma("sp", ym[xi][:], yv[:, :, t0:t0 + 512], writes=[b_ym[xi]])
            for j in range(8):
                pi = k % 4; k += 1
                for c in range(8):
                    S.op("pe", lambda e: e.matmul(pq[pi][:], lhsT=Wo[:, c, j * 128:(j + 1) * 128], rhs=ym[xi][:, c, :], start=(c == 0), stop=(c == 7)),
                         reads=[b_W, b_ym[xi]], writes=[b_pq[pi]])
                S.op("dve", lambda e: e.tensor_tensor(out=xo[xi][:, j, :], in0=pq[pi][:], in1=xt[xi][:, j, :], op=ALU.add),
                     reads=[b_pq[pi], b_xt[xi]], writes=[b_xo[xi]])
            S.dma("sp", xov[:, :, t0:t0 + 512], xo[xi][:], reads=[b_xo[xi]])
        S.barrier()


def phase_ffn(g, l, xin, xout):
    nc, S, D, T = g.nc, g.S, g.D, g.T
    N = 512
    with ExitStack() as st:
        sbf = lambda name, shape, dt=F32: st.enter_context(nc.sbuf_tensor(g.nm(name), list(shape), dt))
        psf = lambda name, shape, dt=F32: st.enter_context(nc.psum_tensor(g.nm(name), list(shape), dt))
        Wu = sbf("Wu", [128, 8, 2 * DFF], BF16)
        Wd = sbf("Wd", [128, 22, DM], BF16)
        CHW = 512
        NCW = (2 * DFF) // CHW
        b_Wu = [Buf() for _ in range(NCW)]
        b_Wd = [Buf() for _ in range(22)]
        stage = [sbf(f"wst{i}", [128, CHW]) for i in range(2)]; b_stage = [Buf() for _ in range(2)]
        xt = sbf("fxt", [128, 8, N]); b_xt = Buf()
        xv = xin.rearrange("(c p) t -> p c t", p=128)
        xov = xout.rearrange("(c p) t -> p c t", p=128)
        S.dma("sp", xt[:], xv[:, :, 0:N], writes=[b_xt])
        wuv = D["ffn_w_up"][l].rearrange("(c p) n -> p c n", p=128)
        wdv = D["ffn_w_down"][l].rearrange("(c p) n -> p c n", p=128)
        wu_done = set()
        wd_done = set()

        def emit_wu(nchunk):
            if nchunk in wu_done:
                return
            wu_done.add(nchunk)
            for c in range(8):
                i = g.wctr % 2; g.wctr += 1
                S.dma("sp", stage[i][:, :], wuv[:, c, nchunk * CHW:(nchunk + 1) * CHW], writes=[b_stage[i]])
                cp(S, ("act", "dve", "pool")[g.wctr % 3], Wu[:, c, nchunk * CHW:(nchunk + 1) * CHW], stage[i][:, :], [b_stage[i]], [b_Wu[nchunk]])

        def emit_wd(c):
            if c in wd_done:
                return
            wd_done.add(c)
            for n0 in range(0, DM, CHW):
                n1 = min(DM, n0 + CHW)
                i = g.wctr % 2; g.wctr += 1
                S.dma("sp", stage[i][:, 0:n1 - n0], wdv[:, c, n0:n1], writes=[b_stage[i]])
                cp(S, ("act", "dve", "pool")[g.wctr % 3], Wd[:, c, n0:n1], stage[i][:, 0:n1 - n0], [b_stage[i]], [b_Wd[c]])
        sm = sbf("smf", [128, 8]); b_sm = Buf()
        S.dma("sp", sm[:], D["smalls"][l][:, 16:24], writes=[b_sm])
        cw = sbf("cw", [128, 44, 4]); b_cw = Buf()
        S.dma("sp", cw[:], D["convp"][l], writes=[b_cw])
        gated = sbf("gated", [128, 22, N], BF16); b_gt = Buf()
        R = {}
        R["sq"] = gated[:, 0:8, :]; R["b_sq"] = b_gt
        R["rstd"] = sbf("rstd", [128, N]); R["b_rstd"] = Buf()
        R["p_rms"] = psf("p_rms", [128, 512]); R["b_prms"] = Buf(excl=True)
        R["ones"] = sbf("ones", [128, 128], BF16); R["b_ones"] = Buf()
        R["eps"] = sbf("epsb", [128, 1]); R["b_eps"] = Buf()
        S.op("pool", lambda e: e.memset(R["ones"][:], 1.0), writes=[R["b_ones"]])
        S.op("pool", lambda e: e.memset(R["eps"][:], 1e-6), writes=[R["b_eps"]])
        hT = sbf("fhT", [128, 8, N], BF16); b_h = Buf()
        carry = sbf("carry", [128, 44, 2]); b_carry = Buf()
        S.op("pool", lambda e: e.memset(carry[:], 0.0), writes=[b_carry])
        NU = 3
        U = [sbf(f"U{i}", [128, N + 2]) for i in range(NU)]; b_U = [Buf() for _ in range(NU)]
        cg = [sbf(f"cg{i}", [128, N]) for i in range(2)]; b_cg = [Buf() for _ in range(2)]
        cv = [sbf(f"cv{i}", [128, N]) for i in range(2)]; b_cv = [Buf() for _ in range(2)]
        gi = [sbf(f"gi{i}", [128, N]) for i in range(2)]; b_gi = [Buf() for _ in range(2)]
        pu = [psf(f"fpu{i}", [128, 512]) for i in range(4)]; b_pu = [Buf(excl=True) for _ in range(4)]
        pd = [psf(f"fpd{i}", [128, 512]) for i in range(2)]; b_pd = [Buf(excl=True) for _ in range(2)]
        uc = 0
        pc_ = 0
        for tt in range(T // N):
            t0 = tt * N
            if tt > 0:
                S.dma("sp", xt[:], xv[:, :, t0:t0 + N], writes=[b_xt])
            rmsnorm_tile(g, xt, b_xt, hT, b_h, sm, b_sm, N, R)
            for i in range(22):
                k2 = i % 2
                for which, ch in ((0, i), (1, 22 + i)):
                    ui = uc % NU; uc += 1
                    pi = pc_ % 4; pc_ += 1
                    emit_wu((ch * 128) // CHW)
                    for c in range(8):
                        S.op("pe", lambda e: e.matmul(pu[pi][:, 0:N], lhsT=Wu[:, c, ch * 128:(ch + 1) * 128], rhs=hT[:, c, :], start=(c == 0), stop=(c == 7)),
                             reads=[b_Wu[(ch * 128) // CHW], b_h], writes=[b_pu[pi]])
                    S.op("act", lambda e: e.copy(out=U[ui][:, 2:N + 2], in_=pu[pi][:, 0:N]), reads=[b_pu[pi]], writes=[b_U[ui]])
                    S.op("act", lambda e: e.copy(out=U[ui][:, 0:2], in_=carry[:, ch, :]), reads=[b_carry], writes=[b_U[ui]])
                    dst, bd_ = (cg[k2], b_cg[k2]) if which == 0 else (cv[k2], b_cv[k2])
                    S.op("dve", lambda e: e.tensor_scalar(out=dst[:], in0=U[ui][:, 0:N], scalar1=cw[:, ch, 0:1], scalar2=cw[:, ch, 3:4], op0=ALU.mult, op1=ALU.add),
                         reads=[b_U[ui], b_cw], writes=[bd_])
                    S.op("dve", lambda e: e.scalar_tensor_tensor(out=dst[:], in0=U[ui][:, 1:N + 1], scalar=cw[:, ch, 1:2], in1=dst[:], op0=ALU.mult, op1=ALU.add),
                         reads=[b_U[ui], b_cw, bd_], writes=[bd_])
                    S.op("dve", lambda e: e.scalar_tensor_tensor(out=dst[:], in0=U[ui][:, 2:N + 2], scalar=cw[:, ch, 2:3], in1=dst[:], op0=ALU.mult, op1=ALU.add),
                         reads=[b_U[ui], b_cw, bd_], writes=[bd_])
                    S.op("act", lambda e: e.copy(out=carry[:, ch, :], in_=U[ui][:, N:N + 2]), reads=[b_U[ui]], writes=[b_carry])
                S.op("pool", lambda e: e.tensor_tensor(out=gi[k2][:], in0=cg[k2][:], in1=cg[k2][:], op=ALU.mult), reads=[b_cg[k2]], writes=[b_gi[k2]])
                S.op("pool", lambda e: e.tensor_scalar(out=gi[k2][:], in0=gi[k2][:], scalar1=0.044715, scalar2=1.0, op0=ALU.mult, op1=ALU.add), reads=[b_gi[k2]], writes=[b_gi[k2]])
                S.op("pool", lambda e: e.tensor_tensor(out=gi[k2][:], in0=gi[k2][:], in1=cg[k2][:], op=ALU.mult), reads=[b_gi[k2], b_cg[k2]], writes=[b_gi[k2]])
                S.op("act", lambda e: e.activation(out=gi[k2][:], in_=gi[k2][:], func=AF.Sigmoid, scale=1.5957691216057308), reads=[b_gi[k2]], writes=[b_gi[k2]])
                S.op("pool", lambda e: e.tensor_tensor(out=gi[k2][:], in0=gi[k2][:], in1=cg[k2][:], op=ALU.mult), reads=[b_gi[k2], b_cg[k2]], writes=[b_gi[k2]])
                S.op("dve", lambda e: e.tensor_tensor(out=gated[:, i, :], in0=gi[k2][:], in1=cv[k2][:], op=ALU.mult), reads=[b_gi[k2], b_cv[k2]], writes=[b_gt])
                emit_wd(i)
            for j in range(8):
                pi = j % 2
                for i in range(22):
                    S.op("pe", lambda e: e.matmul(pd[pi][:, 0:N], lhsT=Wd[:, i, j * 128:(j + 1) * 128], rhs=gated[:, i, :], start=(i == 0), stop=(i == 21)),
                         reads=[b_Wd[i], b_gt], writes=[b_pd[pi]])
                S.op("dve", lambda e: e.tensor_tensor(out=xt[:, j, :], in0=pd[pi][:, 0:N], in1=xt[:, j, :], op=ALU.add),
                     reads=[b_pd[pi], b_xt], writes=[b_xt])
            S.dma("sp", xov[:, :, t0:t0 + N], xt[:], reads=[b_xt])
        S.barrier()


def phase_ple(g, l, xin, xout, final):
    nc, S, D, T = g.nc, g.S, g.D, g.T
    N = 512
    with ExitStack() as st:
        sbf = lambda name, shape, dt=F32: st.enter_context(nc.sbuf_tensor(g.nm(name), list(shape), dt))
        psf = lambda name, shape, dt=F32: st.enter_context(nc.psum_tensor(g.nm(name), list(shape), dt))
        Wg = sbf("Wg", [128, 8, DM], BF16); b_Wg = Buf()
        Wp = sbf("Wp", [128, 2, DM], BF16); b_Wp = Buf()
        stage = [sbf(f"wst{i}", [128, 1024]) for i in range(2)]; b_stage = [Buf() for _ in range(2)]
        load_w_bf16(g, Wg, b_Wg, D["ple_w_gate"][l], 8, DM, stage, b_stage)
        load_w_bf16(g, Wp, b_Wp, D["ple_w_proj"][l], 2, DM, stage, b_stage)
        sm = sbf("smq", [128, 16]); b_sm = Buf()
        S.dma("sp", sm[:, 0:8], D["smalls"][l][:, 24:32], writes=[b_sm])
        S.dma("sp", sm[:, 8:16], D["smalls"][l][:, 32:40], writes=[b_sm])
        R = rms_shared(g, sbf, psf, N)
        xt = [sbf(f"pxt{i}", [128, 8, N]) for i in range(2)]; b_xt = [Buf() for _ in range(2)]
        pt = [sbf(f"ppt{i}", [128, 2, N]) for i in range(2)]; b_pt = [Buf() for _ in range(2)]
        ptbs = [sbf(f"pptb{i}", [128, 2, N], BF16) for i in range(2)]; b_ptbs = [Buf() for _ in range(2)]
        hTs = [sbf(f"phT{i}", [128, 8, N], BF16) for i in range(2)]; b_hs = [Buf() for _ in range(2)]
        xos = [sbf(f"pxo{i}", [128, 8, N]) for i in range(2)]; b_xos = [Buf() for _ in range(2)]
        xf = sbf("pxf", [128, 8, N]); b_xf = Buf()
        gs = [sbf(f"pgs{i}", [128, N]) for i in range(2)]; b_gs = [Buf() for _ in range(2)]
        pg = [psf(f"ppg{i}", [128, 512]) for i in range(2)]; b_pg = [Buf(excl=True) for _ in range(2)]
        pq = [psf(f"ppq{i}", [128, 512]) for i in range(2)]; b_pq = [Buf(excl=True) for _ in range(2)]
        xv = xin.rearrange("(c p) t -> p c t", p=128)
        xov = xout.rearrange("(c p) t -> p c t", p=128)
        pv = D["pT"][l].rearrange("(c p) t -> p c t", p=128)
        NTL = T // N

        def prep(tt):
            xi = tt % 2
            t0 = tt * N
            S.dma("sp", xt[xi][:], xv[:, :, t0:t0 + N], writes=[b_xt[xi]])
            S.dma("sp", pt[xi][:], pv[:, :, t0:t0 + N], writes=[b_pt[xi]])
            S.op("pool", lambda e: e.tensor_copy(out=ptbs[xi][:], in_=pt[xi][:]), reads=[b_pt[xi]], writes=[b_ptbs[xi]])
            rmsnorm_tile(g, xt[xi], b_xt[xi], hTs[xi], b_hs[xi], sm, b_sm, N, R)

        prep(0)
        for tt in range(NTL):
            t0 = tt * N
            xi = tt % 2
            if tt + 1 < NTL:
                prep(tt + 1)
            hT, b_h = hTs[xi], b_hs[xi]
            ptb, b_ptb = ptbs[xi], b_ptbs[xi]
            xo, b_xo = xos[xi], b_xos[xi]
            for j in range(8):
                pi = j % 2
                for c in range(8):
                    S.op("pe", lambda e: e.matmul(pg[pi][:], lhsT=Wg[:, c, j * 128:(j + 1) * 128], rhs=hT[:, c, :], start=(c == 0), stop=(c == 7)),
                         reads=[b_Wg, b_h], writes=[b_pg[pi]])
                for c in range(2):
                    S.op("pe", lambda e: e.matmul(pq[pi][:], lhsT=Wp[:, c, j * 128:(j + 1) * 128], rhs=ptb[:, c, :], start=(c == 0), stop=(c == 1)),
                         reads=[b_Wp, b_ptb], writes=[b_pq[pi]])
                S.op("act", lambda e: e.activation(out=gs[pi][:], in_=pg[pi][:], func=AF.Sigmoid), reads=[b_pg[pi]], writes=[b_gs[pi]])
                S.op("dve", lambda e: e.tensor_tensor(out=gs[pi][:], in0=pq[pi][:], in1=gs[pi][:], op=ALU.mult), reads=[b_pq[pi], b_gs[pi]], writes=[b_gs[pi]])
                S.op("pool", lambda e: e.tensor_tensor(out=xo[:, j, :], in0=gs[pi][:], in1=xt[xi][:, j, :], op=ALU.add), reads=[b_gs[pi], b_xt[xi]], writes=[b_xo])
            if not final:
                S.dma("sp", xov[:, :, t0:t0 + N], xo[:], reads=[b_xo])
            else:
                S.op("act", lambda e: e.activation(out=R["sq"][:], in_=xo[:], func=AF.Square), reads=[b_xo], writes=[R["b_sq"]])
                for c in range(8):
                    S.op("pe", lambda e: e.matmul(R["p_rms"][:], lhsT=R["ones"][:], rhs=R["sq"][:, c, :], start=(c == 0), stop=(c == 7)),
                         reads=[R["b_ones"], R["b_sq"]], writes=[R["b_prms"]])
                S.op("act", lambda e: e.activation(out=R["rstd"][:], in_=R["p_rms"][:], func=AF.Sqrt, bias=R["eps"][:, 0:1], scale=1.0 / DM),
                     reads=[R["b_prms"], R["b_eps"]], writes=[R["b_rstd"]])
                S.op("dve", lambda e: e.reciprocal(out=R["rstd"][:], in_=R["rstd"][:]), reads=[R["b_rstd"]], writes=[R["b_rstd"]])
                for c in range(8):
                    S.op("dve", lambda e: e.scalar_tensor_tensor(out=xf[:, c, :], in0=xo[:, c, :], scalar=sm[:, 8 + c:9 + c], in1=R["rstd"][:],
                                                                   op0=ALU.mult, op1=ALU.mult),
                         reads=[b_xo, b_sm, R["b_rstd"]], writes=[b_xf])
                S.dma("sp", xov[:, :, t0:t0 + N], xf[:], reads=[b_xf])
        S.barrier()


def nsa_consts(T):
    import ml_dtypes
    bf = ml_dtypes.bfloat16
    f = np.float32
    c = {}
    nl = np.arange(128)[:, None]
    ql = np.arange(128)[None, :]
    masks = np.zeros((19, 128, 128), f)
    for i in range(17):
        masks[i] = np.where(16 * nl + 31 - ql <= 128 * i, 0.0, -BIG)
    masks[17] = np.where(nl <= ql, 0.0, -BIG)
    masks[18] = np.where(nl > ql, 0.0, -BIG)
    m4 = np.tile(masks, (1, 1, 4))
    c["nsa_masks"] = np.ascontiguousarray(np.transpose(m4, (1, 0, 2))).astype(bf)
    c["identb"] = np.eye(128, dtype=f).astype(bf)
    c["identf"] = np.eye(128, dtype=f)
    key = np.arange(T)[None, :]
    c["E_all"] = (key // 64 == np.arange(128)[:, None]).astype(f).astype(bf)
    n_cmp = (T - 32) // 16 + 1
    ntc = (n_cmp + 127) // 128
    cs = 16 * np.arange(n_cmp)
    ce = cs + 31
    ss = 64 * np.arange(128)
    ov = np.minimum(ce[:, None], ss[None] + 63) - np.maximum(cs[:, None], ss[None]) + 1
    mcs = np.zeros((ntc * 128, 128), f)
    mcs[:n_cmp] = np.clip(ov, 0, 32).astype(f) / 32
    c["mcs"] = np.ascontiguousarray(mcs.reshape(ntc, 128, 128).transpose(1, 0, 2)).astype(bf)
    keep = np.zeros((128, 256), f)
    add = np.zeros((128, 256), f)
    for q in range(128):
        jc = 126 if q < 64 else 127
        cc = np.arange(256)
        keep[q] = (cc < jc - 1)
        add[q] = np.where(cc == jc - 1, 1.1e9, np.where(cc == jc, 1.2e9, np.where(cc > jc, -1e30, 0.0)))
    c["keepw"] = keep
    c["addw"] = add
    return c


def phase_nsa(g, l):
    nc, S, D, T = g.nc, g.S, g.D, g.T
    NQB = T // 128
    NCMP = (T - 32) // 16 + 1
    NTC = (NCMP + 127) // 128
    SK = dict(skip_group_check=True)
    with ExitStack() as st:
        sbf = lambda name, shape, dt=F32: st.enter_context(nc.sbuf_tensor(g.nm(name), list(shape), dt))
        psf = lambda name, shape, dt=F32: st.enter_context(nc.psum_tensor(g.nm(name), list(shape), dt))
        stp = [psf(f"nst{i}", [128, 512]) for i in range(3)]; b_stp = [Buf(excl=True) for _ in range(3)]
        acc = [psf(f"nacc{i}", [128, 512]) for i in range(3)]; b_acc = [Buf(excl=True) for _ in range(3)]
        imp = psf("nimp", [128, 512]); b_imp = Buf(excl=True)
        msc = psf("nmsc", [128, 512]); b_msc = Buf(excl=True)
        mscb = msc[:, 384:448].bitcast(BF16); b_mscb = b_msc
        KcT = sbf("KcT", [64, 2, NTC * 128], BF16); b_Kc = Buf()
        Vc = sbf("Vc", [128, NTC, 2, 128], BF16); b_Vc = Buf()
        S.op("pool", lambda e: e.memset(KcT[:], 0.0), writes=[b_Kc])
        S.op("pool", lambda e: e.memset(Vc[:], 0.0), writes=[b_Vc])
        with ExitStack() as st2:
            sb2 = lambda name, shape, dt=F32: st2.enter_context(nc.sbuf_tensor(g.nm(name), list(shape), dt))
            kc = sb2("kc", [64, 2, T], BF16); b_kc = Buf()
            vc = sb2("vc", [64, 2, T], BF16); b_vc = Buf()
            S.dma("sp", kc[:], D["kT"][0:128, :].rearrange("(h d) t -> d h t", d=64), writes=[b_kc])
            S.dma("sp", vc[:], D["vcT"].rearrange("(h d) t -> d h t", d=64), writes=[b_vc])
            wst = sb2("wckst", [64, 32, 64]); b_wst = Buf()
            wck = sb2("wck", [64, 32, 64], BF16); b_wck = Buf()
            wcv = sb2("wcv", [64, 32, 64], BF16); b_wcv = Buf()
            S.dma("sp", wst[:], D["nsa_w_ck"][l].rearrange("l d e -> d l e"), writes=[b_wst])
            cp(S, "dve", wck[:], wst[:], [b_wst], [b_wck])
            S.dma("sp", wst[:], D["nsa_w_cv"][l].rearrange("l d e -> d l e"), writes=[b_wst])
            cp(S, "dve", wcv[:], wst[:], [b_wst], [b_wcv])
            pest = sb2("pest", [64, 2, 32]); b_pest = Buf()
            peb = sb2("peb", [64, 2, 32], BF16); b_peb = Buf()
            S.dma("sp", pest[:], D["nsa_peT"][l].rearrange("w d l -> d w l"), writes=[b_pest])
            cp(S, "dve", peb[:], pest[:], [b_pest], [b_peb])
            biask = sb2("biask", [64, 1]); b_bk = Buf()
            biasv = sb2("biasv", [1, 64], BF16); b_bv = Buf()
            onesr = sb2("onesr", [1, 128], BF16); b_or = Buf()
            S.op("pool", lambda e: e.memset(onesr[:], 1.0), writes=[b_or])
            for i_ in range(32):
                S.op("pe", lambda e: e.matmul(msc[0:64, 0:1], lhsT=wck[:, i_, :], rhs=peb[:, 0, i_:i_ + 1], start=(i_ == 0), stop=(i_ == 31)),
                     reads=[b_wck, b_peb], writes=[b_msc])
            cp(S, "dve", biask[:], msc[0:64, 0:1], [b_msc], [b_bk])
            for i_ in range(32):
                S.op("pe", lambda e: e.matmul(msc[0:1, 0:64], lhsT=peb[:, 1, i_:i_ + 1], rhs=wcv[:, i_, :], start=(i_ == 0), stop=(i_ == 31)),
                     reads=[b_wcv, b_peb], writes=[b_msc])
            cp(S, "dve", biasv[:], msc[0:1, 0:64], [b_msc], [b_bv])
            span = 16 * (NCMP - 1) + 1
            for h in range(2):
                pk = stp[h]
                for i_ in range(32):
                    S.op("pe", lambda e: e.matmul(pk[0:64, 0:NCMP], lhsT=wck[:, i_, :], rhs=kc[:, h, i_:i_ + span:16], start=(i_ == 0), stop=(i_ == 31)),
                         reads=[b_wck, b_kc], writes=[b_stp[h]])
                S.op("act", lambda e: e.activation(out=KcT[:, h, 0:NCMP], in_=pk[0:64, 0:NCMP], func=AF.Identity, bias=biask[:, 0:1], scale=1.0),
                     reads=[b_stp[h], b_bk], writes=[b_Kc])
            k_ = 0
            for h in range(2):
                for nt in range(NTC):
                    nn = min(128, NCMP - nt * 128)
                    pv_ = acc[k_ % 3]; bpv = b_acc[k_ % 3]; k_ += 1
                    base = 16 * 128 * nt
                    sp_ = 16 * (nn - 1) + 1
                    for i_ in range(32):
                        S.op("pe", lambda e: e.matmul(pv_[0:nn, 0:64], lhsT=vc[:, h, base + i_:base + i_ + sp_:16], rhs=wcv[:, i_, :], start=(i_ == 0), stop=False),
                             reads=[b_wcv, b_vc], writes=[bpv])
                    S.op("pe", lambda e: e.matmul(pv_[0:nn, 0:64], lhsT=onesr[0:1, 0:nn], rhs=biasv[0:1, :], start=False, stop=True),
                         reads=[b_or, b_bv], writes=[bpv])
                    cp(S, "dve", Vc[0:nn, nt, h, 0:64], pv_[0:nn, 0:64], [bpv], [b_Vc])
                    S.op("dve", lambda e: e.memset(Vc[0:nn, nt, h, 64:65], 1.0), writes=[b_Vc])
            S.barrier()
        masks = sbf("masks", [128, 19, 512], BF16); b_masks = Buf()
        S.dma("sp", masks[:], D["nsa_masks"], writes=[b_masks])
        identb = sbf("identb", [128, 128], BF16); b_idb = Buf()
        S.dma("sp", identb[:], D["identb"], writes=[b_idb])
        identf = sbf("identf", [128, 128]); b_idf = Buf()
        S.dma("sp", identf[:], D["identf"], writes=[b_idf])
        MCS = sbf("MCS", [128, NTC, 128], BF16); b_mcs = Buf()
        S.dma("sp", MCS[:], D["mcs"], writes=[b_mcs])
        keepw = sbf("keepw", [128, 256]); b_kw_ = Buf()
        S.dma("sp", keepw[:], D["keepw"], writes=[b_kw_])
        addw = sbf("addw", [128, 256]); b_aw = Buf()
        S.dma("sp", addw[:], D["addw"], writes=[b_aw])
        LH = sbf("LH", [128, 2, T], BF16); b_LH = Buf()
        KwT = sbf("KwT", [64, 2, T], BF16); b_Kw = Buf()
        TH = min(T, 4096)
        ksv = D["kT"][128:256, :].rearrange("(h d) t -> d h t", d=64)
        S.dma("sp", LH[64:128, :, 0:TH], ksv[:, :, 0:TH], writes=[b_LH])
        for h_ in range(2):
            S.dma("sp", LH[0:64, h_, 0:TH], D["E_all"][0:64, 0:TH], writes=[b_LH])
        if T > TH:
            S.dma("sp", LH[0:64, :, TH:T], ksv[:, :, TH:T], writes=[b_LH])
            for h_ in range(2):
                S.dma("sp", LH[64:128, h_, TH:T], D["E_all"][64:128, TH:T], writes=[b_LH])
        S.dma("sp", KwT[:], D["kT"][256:384, :].rearrange("(h d) t -> d h t", d=64), writes=[b_Kw])
        VW = 128
        Vs = sbf("Vs", [128, NQB, 2, VW], BF16); b_Vs = Buf()
        Vw = sbf("Vw", [128, NQB, 2, VW], BF16); b_Vw = Buf()
        S.op("pool", lambda e: e.memset(Vs[:], 0.0), writes=[b_Vs])
        S.op("pool", lambda e: e.memset(Vw[:], 0.0), writes=[b_Vw])
        S.op("pool", lambda e: e.memset(Vs[:, :, :, 64:65], 1.0), writes=[b_Vs])
        S.op("pool", lambda e: e.memset(Vw[:, :, :, 64:65], 1.0), writes=[b_Vw])
        vtv = D["vtok"].rearrange("(kt p) (w h d) -> p kt w h d", p=128, w=2, h=2)
        for h in range(2):
            S.dma("sp", Vs[:, :, h, 0:64], vtv[:, :, 0, h, :], writes=[b_Vs])
            S.dma("sp", Vw[:, :, h, 0:64], vtv[:, :, 1, h, :], writes=[b_Vw])
        Qg = [sbf(f"Qg{i}", [64, 4, 128], BF16) for i in range(2)]; b_Qg = [Buf() for _ in range(2)]
        gt = [sbf(f"ngt{i}", [128, 24]) for i in range(2)]; b_gt = [Buf() for _ in range(2)]
        Pc = [sbf(f"Pc{i}", [128, 512], BF16) for i in range(max(NTC, 1))]; b_Pc = [Buf() for _ in range(max(NTC, 1))]
        NP = 3
        Pb = [sbf(f"Pb{i}", [128, 512], BF16) for i in range(NP)]; b_Pb = [Buf() for _ in range(NP)]
        zz = sbf("zz", [128, 3, 4]); b_zz = Buf()
        coef = sbf("coef", [128, 3, 4]); b_coef = Buf()
        impS = sbf("impS", [128, 128]); b_impS = Buf()
        imp2 = sbf("imp2", [128, 128]); b_imp2 = Buf()
        mx = sbf("mx", [128, 16]); b_mx = Buf()
        thr = sbf("thr", [128, 1]); b_thr = Buf()
        MBf = sbf("MBf", [128, 128]); b_MBf = Buf()
        MBb = sbf("MBb", [128, 128], BF16); b_MBb = Buf()
        MBT4 = sbf("MBT4", [128, 4, 128], BF16); b_MBT4 = Buf()
        yc = [sbf(f"yc{i}", [128, 512]) for i in range(2)]; b_yc = [Buf() for _ in range(2)]
        ycT = [sbf(f"ycT{i}", [128, 4, 128], BF16) for i in range(2)]; b_ycT = [Buf() for _ in range(2)]
        stc = 0
        pbc = 0
        qc = 0
        qv = D["qT"].rearrange("(hq d) t -> d hq t", d=64)
        ymv = D["ymixT"][512:1024, :].rearrange("(c p) t -> p c t", p=128)
        aS = sbf("aS", [65, 512]); b_aS = Buf()
        R0 = [sbf(f"R0_{i}", [128, 512], BF16) for i in range(2)]; b_R0 = [Buf() for _ in range(2)]
        R1 = [sbf(f"R1_{i}", [128, 512], BF16) for i in range(2)]; b_R1 = [Buf() for _ in range(2)]

        def score_tile(KT, bK, h, kt, Q, bQ, extra):
            nonlocal stc
            si = stc % 3; stc += 1
            n_mm = 1 + len(extra)
            rhs_ap = Q[:].rearrange("d g q -> d (g q)") if len(Q.shape) == 3 else Q[:]
            S.op("pe", lambda e: e.matmul(stp[si][:], lhsT=KT[:, h, kt * 128:(kt + 1) * 128], rhs=rhs_ap, start=True, stop=(n_mm == 1)),
                 reads=[bK, bQ], writes=[b_stp[si]])
            for i_, (la, ra, bufs) in enumerate(extra):
                S.op("pe", lambda e: e.matmul(stp[si][:], lhsT=la, rhs=ra, start=False, stop=(i_ == len(extra) - 1)),
                     reads=bufs, writes=[b_stp[si]])
            return si

        def run_branch(br, tiles, h, Q, bQ, V, bV, Pbufs=None, after_exp=None):
            nonlocal pbc
            n = len(tiles)
            tiles = [tl if len(tl) == 6 else tl + (Q, bQ) for tl in tiles]
            DEPTH = 2
            issued = []
            nxt = 0
            for i_ in range(n):
                while nxt < n and nxt <= i_ + DEPTH - 1 + (0 if i_ else 0):
                    KT2, bK2, kt2, extra2, Qx2, bQx2 = tiles[nxt]
                    issued.append(score_tile(KT2, bK2, h, kt2, Qx2, bQx2, extra2))
                    nxt += 1
                si = issued[i_]
                kt = tiles[i_][2]
                if Pbufs is None:
                    pi = pbc % NP; pbc += 1
                    P, bP = Pb[pi], b_Pb[pi]
                else:
                    P, bP = Pbufs[i_]
                S.op("act", lambda e: e.activation(out=P[:], in_=stp[si][:], func=AF.Exp, scale=0.125), reads=[b_stp[si]], writes=[bP])
                if nxt < n:
                    KT2, bK2, kt2, extra2, Qx2, bQx2 = tiles[nxt]
                    issued.append(score_tile(KT2, bK2, h, kt2, Qx2, bQx2, extra2))
                    nxt += 1
                S.op("pe", lambda e: e.matmul(acc[br][:, :], lhsT=V[:, kt, h, :], rhs=P[:], start=(i_ == 0), stop=(i_ == n - 1)),
                     reads=[bP, bV], writes=[b_acc[br]])
                if after_exp is not None:
                    after_exp(i_, P, bP)

        def combine(br, h, yi):
            cp(S, "act", aS[:], acc[br][0:65, :], [b_acc[br]], [b_aS])
            for gq in range(4):
                S.op("pe", lambda e: e.transpose(out=msc[:, gq * 65:(gq + 1) * 65], in_=aS[0:65, gq * 128:(gq + 1) * 128], identity=identf[0:65, 0:65]),
                     reads=[b_aS, b_idf], writes=[b_msc])
            a3 = msc[:, 0:260].rearrange("p (g c) -> p g c", c=65)
            g3 = gt[yi][:].rearrange("p (hg b) -> p hg b", b=3)
            S.op("dve", lambda e: e.tensor_scalar(out=zz[:, br, :], in0=a3[:, :, 64], scalar1=1e-30, scalar2=None, op0=ALU.max), reads=[b_msc], writes=[b_zz])
            S.op("dve", lambda e: e.reciprocal(out=zz[:, br, :], in_=zz[:, br, :]), reads=[b_zz], writes=[b_zz])
            S.op("dve", lambda e: e.tensor_tensor(out=coef[:, br, :], in0=zz[:, br, :], in1=g3[:, h * 4:(h + 1) * 4, br], op=ALU.mult), reads=[b_zz, b_gt[yi]], writes=[b_coef])
            for gq in range(4):
                o_ = yc[yi][:, (h * 4 + gq) * 64:(h * 4 + gq + 1) * 64]
                if br == 0:
                    S.op("dve", lambda e: e.tensor_scalar(out=o_, in0=a3[:, gq, 0:64], scalar1=coef[:, br, gq:gq + 1], scalar2=None, op0=ALU.mult),
                         reads=[b_msc, b_coef], writes=[b_yc[yi]])
                else:
                    S.op("dve", lambda e: e.scalar_tensor_tensor(out=o_, in0=a3[:, gq, 0:64], scalar=coef[:, br, gq:gq + 1], in1=o_, op0=ALU.mult, op1=ALU.add),
                         reads=[b_msc, b_coef, b_yc[yi]], writes=[b_yc[yi]])

        items = [(qb, h) for qb in range(NQB) for h in range(2)]
        qis = {}

        def stage1(qb, h):
            nonlocal qc
            q0 = qb * 128
            yi = qb % 2
            if h == 0:
                S.dma("sp", gt[yi][:], D["gates"][q0:q0 + 128, :], writes=[b_gt[yi]])
            qi = qc % 2; qc += 1
            qis[(qb, h)] = qi
            Q, bQ = Qg[qi], b_Qg[qi]
            S.dma("sp", Q[:], qv[:, h * 4:(h + 1) * 4, q0:q0 + 128], writes=[bQ])
            S.dma("sp", R0[qi][64:128, :].rearrange("d (g q) -> d g q", g=4), qv[:, h * 4:(h + 1) * 4, q0:q0 + 128], writes=[b_R0[qi]])
            if qb >= 32:
                S.dma("sp", R1[qi][0:64, :].rearrange("d (g q) -> d g q", g=4), qv[:, h * 4:(h + 1) * 4, q0:q0 + 128], writes=[b_R1[qi]])
            ntc = min(NTC, (8 * qb + 6) // 128 + 1)
            S.op("dve", lambda e: e.memset(imp[:], 0.0), writes=[b_imp])
            tiles = []
            for nt in range(ntc):
                delta = 128 * qb - 2048 * nt
                extra = []
                if delta < 2064:
                    extra.append((identb[:], masks[:, delta // 128, :], [b_idb, b_masks]))
                tiles.append((KcT, b_Kc, nt, extra))

            def imp_mm(i_, P, bP):
                for gq in range(4):
                    S.op("pe", lambda e: e.matmul(imp[:, gq * 128:(gq + 1) * 128], lhsT=P[:, gq * 128:(gq + 1) * 128], rhs=MCS[:, i_, :], start=False, stop=(i_ == ntc - 1), **SK),
                         reads=[bP, b_mcs], writes=[b_imp])
            run_branch(0, tiles, h, Q, bQ, Vc, b_Vc, Pbufs=[(Pc[i_], b_Pc[i_]) for i_ in range(ntc)], after_exp=imp_mm)
            combine(0, h, yi)
            S.op("dve", lambda e: e.tensor_scalar(out=impS[:], in0=imp[:, 0:128], scalar1=zz[:, 0, 0:1], scalar2=None, op0=ALU.mult), reads=[b_imp, b_zz], writes=[b_impS])
            for gq in range(1, 4):
                S.op("dve", lambda e: e.scalar_tensor_tensor(out=impS[:], in0=imp[:, gq * 128:(gq + 1) * 128], scalar=zz[:, 0, gq:gq + 1], in1=impS[:], op0=ALU.mult, op1=ALU.add),
                     reads=[b_imp, b_zz, b_impS], writes=[b_impS])
            c0 = 126 - 2 * qb
            S.op("dve", lambda e: e.tensor_tensor(out=impS[:], in0=impS[:], in1=keepw[:, c0:c0 + 128], op=ALU.mult), reads=[b_impS, b_kw_], writes=[b_impS])
            S.op("dve", lambda e: e.tensor_tensor(out=impS[:], in0=impS[:], in1=addw[:, c0:c0 + 128], op=ALU.add), reads=[b_impS, b_aw], writes=[b_impS])
            S.op("dve", lambda e: e.memset(impS[:, 0:1], 1.0e9), writes=[b_impS])
            S.op("dve", lambda e: e.max(out=mx[:, 0:8], in_=impS[:]), reads=[b_impS], writes=[b_mx])
            S.op("dve", lambda e: e.match_replace(out=imp2[:], in_to_replace=mx[:, 0:8], in_values=impS[:], imm_value=-3.0e38), reads=[b_mx, b_impS], writes=[b_imp2])
            S.op("dve", lambda e: e.max(out=mx[:, 8:16], in_=imp2[:]), reads=[b_imp2], writes=[b_mx])
            S.op("dve", lambda e: e.tensor_reduce(out=thr[:], in_=mx[:, 8:16], axis=AX.X, op=ALU.min), reads=[b_mx], writes=[b_thr])
            S.op("dve", lambda e: e.tensor_scalar(out=MBf[:], in0=impS[:], scalar1=thr[:, 0:1], scalar2=None, op0=ALU.is_ge), reads=[b_impS, b_thr], writes=[b_MBf])
            S.op("dve", lambda e: e.tensor_scalar(out=MBb[:], in0=MBf[:], scalar1=1.0, scalar2=BIG, op0=ALU.subtract, op1=ALU.mult), reads=[b_MBf], writes=[b_MBb])
            S.op("pe", lambda e: e.transpose(out=mscb[:], in_=MBb[:], identity=identb[:]), reads=[b_MBb, b_idb], writes=[b_mscb])
            for gq in range(4):
                cp(S, "act" if gq % 2 else "dve", R0[qi][0:64, gq * 128:(gq + 1) * 128], mscb[0:64, :], [b_mscb], [b_R0[qi]])
                if qb >= 32:
                    cp(S, "dve" if gq % 2 else "act", R1[qi][64:128, gq * 128:(gq + 1) * 128], mscb[64:128, :], [b_mscb], [b_R1[qi]])

        def stage2(qb, h):
            q0 = qb * 128
            yi = qb % 2
            qi = qis[(qb, h)]
            Q, bQ = Qg[qi], b_Qg[qi]
            tiles = []
            for kt in range(max(0, qb - 4), qb + 1):
                extra = []
                if kt == qb:
                    extra.append((identb[:], masks[:, 17, :], [b_idb, b_masks]))
                elif kt == qb - 4:
                    extra.append((identb[:], masks[:, 18, :], [b_idb, b_masks]))
                tiles.append((KwT, b_Kw, kt, extra))
            run_branch(2, tiles, h, Q, bQ, Vw, b_Vw)
            combine(2, h, yi)
            tiles = []
            for kt in range(qb + 1):
                extra = []
                if kt == qb:
                    extra.append((identb[:], masks[:, 17, :], [b_idb, b_masks]))
                if kt < 32:
                    tiles.append((LH, b_LH, kt, extra, R0[qi], b_R0[qi]))
                else:
                    tiles.append((LH, b_LH, kt, extra, R1[qi], b_R1[qi]))
            run_branch(1, tiles, h, Q, bQ, Vs, b_Vs)
            combine(1, h, yi)
            if h == 1:
                for c in range(4):
                    S.op("pe", lambda e: e.transpose(out=msc[:, c * 128:(c + 1) * 128], in_=yc[yi][:, c * 128:(c + 1) * 128], identity=identf[:]),
                         reads=[b_yc[yi], b_idf], writes=[b_msc])
                cp(S, "act", ycT[yi][:].rearrange("p c q -> p (c q)"), msc[:], [b_msc], [b_ycT[yi]])
                S.dma("sp", ymv[:, :, q0:q0 + 128], ycT[yi][:], reads=[b_ycT[yi]])

        stage1(*items[0])
        for n_ in range(len(items)):
            if n_ + 1 < len(items):
                stage1(*items[n_ + 1])
            stage2(*items[n_])
        S.barrier()


def rwkv_consts():
    f = np.float32
    c = {}
    hs = np.arange(128) // 64
    tt = np.arange(128) % 64
    same = hs[:, None] == hs[None, :]
    c["rw_msu"] = (same & (tt[:, None] < tt[None, :])).astype(f)
    c["rw_mu"] = (same & (tt[:, None] <= tt[None, :])).astype(f)
    c["rw_msl"] = (same & (tt[:, None] > tt[None, :])).astype(f)
    il = np.zeros((64, 128), f); il[np.arange(64), np.arange(64)] = 1
    ir = np.zeros((64, 128), f); ir[np.arange(64), 64 + np.arange(64)] = 1
    c["rw_il"] = il
    c["rw_ir"] = ir
    return c


def phase_rwkv(g, l):
    nc, S, D, T = g.nc, g.S, g.D, g.T
    TB = 256
    NCH = TB // 64
    SK = dict(skip_group_check=True)
    with ExitStack() as st:
        sbf = lambda name, shape, dt=F32: st.enter_context(nc.sbuf_tensor(g.nm(name), list(shape), dt))
        psf = lambda name, shape, dt=F32: st.enter_context(nc.psum_tensor(g.nm(name), list(shape), dt))
        NB = 8
        bank = [psf(f"rb{i}", [128, 512]) for i in range(NB)]; b_bank = [Buf(excl=True) for _ in range(NB)]
        bctr = [0]

        busy = [False] * NB
        F32R = mybir.dt.float32r
        MT = F32R if g.use_f32r else F32
        RR = lambda ap: ap
        AS32 = (lambda ap: ap.bitcast(F32)) if g.use_f32r else (lambda ap: ap)

        def nb():
            for k_ in range(NB):
                i = (bctr[0] + k_) % NB
                if not busy[i]:
                    bctr[0] = i + 1
                    busy[i] = True
                    return bank[i], b_bank[i]
            raise AssertionError("rwkv: no free PSUM bank (too many live tiles across a yield)")

        def rel(bbuf):
            busy[b_bank.index(bbuf)] = False

        def nbx():
            i = bctr[0] % NB
            bctr[0] += 1
            return bank[i], b_bank[i]
        def const(name, shape, src):
            t_ = sbf(name, shape); b_ = Buf()
            S.dma("sp", t_[:], src, writes=[b_])
            return t_, b_
        msu, b_msu = const("msu", [128, 128], D["rw_msu"])
        mu_, b_mu = const("mu", [128, 128], D["rw_mu"])
        msl, b_msl = const("msl", [128, 128], D["rw_msl"])
        il32, b_il32 = const("il32", [64, 128], D["rw_il"])
        ir32, b_ir32 = const("ir32", [64, 128], D["rw_ir"])
        idf, b_idf = const("idf", [128, 128], D["identf"])
        il = sbf("il", [64, 128], MT); b_il = Buf()
        ir = sbf("ir", [64, 128], MT); b_ir = Buf()
        cp(S, "dve", il[:], il32[:], [b_il32], [b_il])
        cp(S, "dve", ir[:], ir32[:], [b_ir32], [b_ir])
        rp, b_rp = const("rp", [128, 64], D["rwp"][l])
        wup32, b_wup32 = const("wup32", [64, 256], D["rw_w_up"][l])
        aup32, b_aup32 = const("aup32", [64, 256], D["rw_a_up"][l])
        gup32, b_gup32 = const("gup32", [128, 256], D["rw_g_up"][l])
        wup = sbf("wup", [64, 256], MT); b_wup = Buf()
        aup = sbf("aup", [64, 256], MT); b_aup = Buf()
        gup = sbf("gup", [128, 256], MT); b_gup = Buf()
        cp(S, "dve", wup[:], wup32[:], [b_wup32], [b_wup])
        cp(S, "dve", aup[:], aup32[:], [b_aup32], [b_aup])
        cp(S, "dve", gup[:], gup32[:], [b_gup32], [b_gup])
        ones32 = sbf("ones32", [64, 64]); b_o32 = Buf()
        S.op("pool", lambda e: e.memset(ones32[:], 1.0), writes=[b_o32])
        ones64 = sbf("ones64", [64, 64], MT); b_o64 = Buf()
        cp(S, "dve", ones64[:], ones32[:], [b_o32], [b_o64])
        omka = sbf("omka", [64, 4]); b_omka = Buf()
        S.op("dve", lambda e: e.tensor_scalar(out=omka[:], in0=rp[0:64, 28:32], scalar1=-1.0, scalar2=1.0, op0=ALU.mult, op1=ALU.add), reads=[b_rp], writes=[b_omka])
        cst = sbf("rcst", [128, 2]); b_cst = Buf()
        S.op("pool", lambda e: e.memset(cst[:, 0:1], 64e-5), writes=[b_cst])
        ST = [[sbf(f"ST{p}_{i}", [128, 64], MT) for i in range(2)] for p in range(2)]
        b_ST = [[Buf() for i in range(2)] for p in range(2)]
        for p in range(2):
            S.op("pool", lambda e: e.memset(AS32(ST[p][0][:]), 0.0), writes=[b_ST[p][0]])
        sidx = [0, 0]
        def arr(name, shape=None, dt=F32):
            return sbf(name, shape or [64, 4, TB], dt), Buf()
        Z3, b_Z3 = arr("Z3", [64, 12, TB + 1])
        ZL, b_ZL = arr("ZL", [64, 2, TB + 1])
        ZG, b_ZG = arr("ZG", [128, TB + 1])
        X3, b_X3 = arr("X3", [64, 12, TB])
        XL, b_XL = arr("XL", [64, 2, TB], MT)
        XG, b_XG = arr("XG", [128, TB], MT)
        Dt, b_Dt = arr("Dt", [128, 12, TB])
        lw, b_lw = arr("lw")
        cl2, b_cl2 = arr("cl2")
        aa, b_aa = arr("aa")
        kkn, b_kkn = arr("kkn")
        tmp, b_tmp = arr("tmp", None, MT)
        kfin, b_kfin = arr("kfin")
        epos, b_epos = arr("epos")
        eneg, b_eneg = arr("eneg")
        eprev, b_eprev = arr("eprev")
        eC, b_eC = arr("eC")
        AR, b_AR = arr("AR", [64, NCH, 2, 2, 2, 64], MT)
        Bt, b_Bt = arr("Bt", [64, NCH, 4, 64], MT)
        Kt, b_Kt = arr("Kt", [64, NCH, 4, 64], MT)
        Bh, b_Bh = arr("Bh", [64, NCH, 4, 64])
        Kh, b_Kh = arr("Kh", [64, NCH, 4, 64])
        Vc_, b_Vc_ = arr("Vcm", [64, NCH, 4, 64])
        hm = lambda a: a[:].rearrange("k h (c t) -> k h c t", t=64)
        cm = lambda a: a[:].rearrange("k c h t -> k h c t")
        arv = lambda ty: AR[:, :, :, ty, :, :].rearrange("k c p hh t -> k p hh c t")
        hm5 = lambda a: a[:].rearrange("k (p hh) (c t) -> k p hh c t", hh=2, t=64)
        bv, b_bv = arr("bv")
        gT, b_gT = arr("gT")
        YN, b_YN = arr("YN")
        PCf, b_PCf = arr("PCf", [64, 4, NCH], MT)
        PCc = sbf("PCc", [128, 2, NCH]); b_PCc = Buf()
        yo = sbf("yo", [64, 4, TB], BF16); b_yo = Buf()
        NTMP = 100
        tm = [sbf(f"tm{i}", [128, 128], MT) for i in range(NTMP)]; b_tm = [Buf() for _ in range(NTMP)]
        NTF = 16
        tf = [sbf(f"tf{i}", [128, 128]) for i in range(NTF)]; b_tf = [Buf() for _ in range(NTF)]
        fctr = [0]

        def ntf_():
            i = fctr[0] % NTF
            fctr[0] += 1
            return tf[i], b_tf[i]
        tctr = [0]

        def nt_():
            i = tctr[0] % NTMP
            tctr[0] += 1
            return tm[i], b_tm[i]
        NBD = 5
        bd = [[sbf(f"bd{k_}_{i}", [128, 128], MT) for i in range(NBD)] for k_ in range(3)]
        b_bd = [[Buf() for i in range(NBD)] for k_ in range(3)]
        for k_ in range(3):
            for i in range(NBD):
                S.op("pool", lambda e: e.memset(AS32(bd[k_][i][:]), 0.0), writes=[b_bd[k_][i]])
        bdc = [0]
        zav = D["zaT"]
        ev_ctr = [0]

        def evac(out, in_, reads, writes):
            ek = "act" if ev_ctr[0] % 2 == 0 else "dve"
            ev_ctr[0] += 1
            cp(S, ek, out, in_, reads, writes)

        for tt in range(T // TB):
            t0 = tt * TB
            if tt == 0:
                S.op("pool", lambda e: e.memset(Z3[:, :, 0:1], 0.0), writes=[b_Z3])
                S.op("pool", lambda e: e.memset(ZL[:, :, 0:1], 0.0), writes=[b_ZL])
                S.op("pool", lambda e: e.memset(ZG[:, 0:1], 0.0), writes=[b_ZG])
                S.dma("sp", Z3[:, :, 1:TB + 1], zav[0:768, 0:TB].rearrange("(gh k) t -> k gh t", k=64), writes=[b_Z3])
                S.dma("sp", ZL[:, :, 1:TB + 1], zav[768:896, 0:TB].rearrange("(g j) t -> j g t", j=64), writes=[b_ZL])
                S.dma("sp", ZG[:, 1:TB + 1], zav[896:1024, 0:TB], writes=[b_ZG])
            else:
                S.dma("sp", Z3[:], zav[0:768, t0 - 1:t0 + TB].rearrange("(gh k) t -> k gh t", k=64), writes=[b_Z3])
                S.dma("sp", ZL[:], zav[768:896, t0 - 1:t0 + TB].rearrange("(g j) t -> j g t", j=64), writes=[b_ZL])
                S.dma("sp", ZG[:], zav[896:1024, t0 - 1:t0 + TB], writes=[b_ZG])
            S.op("dve", lambda e: e.tensor_tensor(out=Dt[0:64, :, :], in0=Z3[:, :, 0:TB], in1=Z3[:, :, 1:TB + 1], op=ALU.subtract), reads=[b_Z3], writes=[b_Dt])
            for j in range(12):
                if j % 3 == 2:
                    S.op("act", lambda e: e.activation(out=Dt[0:64, j, :], in_=Dt[0:64, j, :], func=AF.Copy, scale=rp[0:64, j:j + 1]), reads=[b_Dt, b_rp], writes=[b_Dt])
                else:
                    S.op("dve", lambda e: e.tensor_scalar(out=Dt[0:64, j, :], in0=Dt[0:64, j, :], scalar1=rp[0:64, j:j + 1], scalar2=None, op0=ALU.mult), reads=[b_Dt, b_rp], writes=[b_Dt])
            S.op("dve", lambda e: e.tensor_tensor(out=X3[:], in0=Dt[0:64, :, :], in1=Z3[:, :, 1:TB + 1], op=ALU.add), reads=[b_Dt, b_Z3], writes=[b_X3])
            S.op("dve", lambda e: e.tensor_tensor(out=Dt[0:64, 0:2, :], in0=ZL[:, :, 0:TB], in1=ZL[:, :, 1:TB + 1], op=ALU.subtract), reads=[b_ZL], writes=[b_Dt])
            for j in range(2):
                S.op("dve", lambda e: e.scalar_tensor_tensor(out=XL[:, j, :], in0=Dt[0:64, j, :], scalar=rp[0:64, 12 + j:13 + j], in1=ZL[:, j, 1:TB + 1], op0=ALU.mult, op1=ALU.add),
                     reads=[b_Dt, b_rp, b_ZL], writes=[b_XL])
            S.op("dve", lambda e: e.tensor_tensor(out=Dt[:, 2, :], in0=ZG[:, 0:TB], in1=ZG[:, 1:TB + 1], op=ALU.subtract), reads=[b_ZG], writes=[b_Dt])
            S.op("dve", lambda e: e.scalar_tensor_tensor(out=XG[:], in0=Dt[:, 2, :], scalar=rp[:, 14:15], in1=ZG[:, 1:TB + 1], op0=ALU.mult, op1=ALU.add),
                 reads=[b_Dt, b_rp, b_ZG], writes=[b_XG])
            r_ = lambda h: X3[:, h, :]
            k_ = lambda h: X3[:, 4 + h, :]
            v_ = lambda h: X3[:, 8 + h, :]
            S.op("act", lambda e: e.activation(out=XL[:, 0, :], in_=XL[:, 0, :], func=AF.Tanh), reads=[b_XL], writes=[b_XL])
            S.op("act", lambda e: e.activation(out=XG[:], in_=XG[:], func=AF.Sigmoid), reads=[b_XG], writes=[b_XG])
            for h in range(4):
                pb, bpb = nbx()
                S.op("pe", lambda e: e.matmul(pb[0:64, 0:TB], lhsT=wup[:, h * 64:(h + 1) * 64], rhs=XL[:, 0, :], start=True, stop=True), reads=[b_wup, b_XL], writes=[bpb])
                S.op("act", lambda e: e.activation(out=lw[:, h, :], in_=pb[0:64, 0:TB], func=AF.Sigmoid, bias=rp[0:64, 16 + h:17 + h], scale=1.0), reads=[bpb, b_rp], writes=[b_lw])
                pb, bpb = nbx()
                S.op("pe", lambda e: e.matmul(pb[0:64, 0:TB], lhsT=aup[:, h * 64:(h + 1) * 64], rhs=XL[:, 1, :], start=True, stop=True), reads=[b_aup, b_XL], writes=[bpb])
                S.op("act", lambda e: e.activation(out=aa[:, h, :], in_=pb[0:64, 0:TB], func=AF.Sigmoid, bias=rp[0:64, 20 + h:21 + h], scale=1.0), reads=[bpb, b_rp], writes=[b_aa])
                pb, bpb = nbx()
                S.op("pe", lambda e: e.matmul(pb[0:64, 0:TB], lhsT=gup[:, h * 64:(h + 1) * 64], rhs=XG[:], start=True, stop=True), reads=[b_gup, b_XG], writes=[bpb])
                evac(gT[:, h, :], pb[0:64, 0:TB], [bpb], [b_gT])
            S.op("dve", lambda e: e.tensor_scalar(out=lw[:], in0=lw[:], scalar1=-0.6065306597126334, scalar2=None, op0=ALU.mult), reads=[b_lw], writes=[b_lw])
            for h in range(4):
                S.op("dve", lambda e: e.tensor_scalar(out=kkn[:, h, :], in0=k_(h), scalar1=rp[0:64, 24 + h:25 + h], scalar2=None, op0=ALU.mult), reads=[b_X3, b_rp], writes=[b_kkn])
            S.op("act", lambda e: e.activation(out=tmp[:], in_=kkn[:], func=AF.Square), reads=[b_kkn], writes=[b_tmp])
            for h in range(4):
                pb, bpb = nbx()
                S.op("pe", lambda e: e.matmul(pb[0:64, 0:TB], lhsT=ones64[:], rhs=tmp[:, h, :], start=True, stop=True), reads=[b_o64, b_tmp], writes=[bpb])
                S.op("act", lambda e: e.activation(out=eC[:, h, :], in_=pb[0:64, 0:TB], func=AF.Sqrt), reads=[bpb], writes=[b_eC])
            S.op("dve", lambda e: e.tensor_scalar(out=eC[:], in0=eC[:], scalar1=1e-12, scalar2=None, op0=ALU.max), reads=[b_eC], writes=[b_eC])
            S.op("dve", lambda e: e.reciprocal(out=eC[:], in_=eC[:]), reads=[b_eC], writes=[b_eC])
            S.op("dve", lambda e: e.tensor_tensor(out=kkn[:], in0=kkn[:], in1=eC[:], op=ALU.mult), reads=[b_kkn, b_eC], writes=[b_kkn])
            for h in range(4):
                S.op("dve", lambda e: e.tensor_scalar(out=tmp[:, h, :], in0=aa[:, h, :], scalar1=rp[0:64, 28 + h:29 + h], scalar2=omka[:, h:h + 1], op0=ALU.mult, op1=ALU.add),
                     reads=[b_aa, b_rp, b_omka], writes=[b_tmp])
            S.op("dve", lambda e: e.tensor_tensor(out=kfin[:], in0=X3[:, 4:8, :], in1=tmp[:], op=ALU.mult), reads=[b_X3, b_tmp], writes=[b_kfin])
            for h in range(4):
                S.op("dve", lambda e: e.scalar_tensor_tensor(out=tmp[:, h, :], in0=r_(h), scalar=rp[0:64, 32 + h:33 + h], in1=kfin[:, h, :], op0=ALU.mult, op1=ALU.mult),
                     reads=[b_X3, b_kfin, b_rp], writes=[b_tmp])
                pb, bpb = nbx()
                S.op("pe", lambda e: e.matmul(pb[0:64, 0:TB], lhsT=ones64[:], rhs=tmp[:, h, :], start=True, stop=True), reads=[b_o64, b_tmp], writes=[bpb])
                S.op("dve", lambda e: e.tensor_tensor(out=bv[:, h, :], in0=pb[0:64, 0:TB], in1=v_(h), op=ALU.mult), reads=[bpb, b_X3], writes=[b_bv])
            src, bsrc, dst, bdst = lw, b_lw, cl2, b_cl2
            cp(S, "act", eprev[:], lw[:], [b_lw], [b_eprev])
            for sft in (1, 2, 4, 8, 16, 32):
                s5 = src[:].rearrange("k h (c t) -> k h c t", t=64)
                d5 = dst[:].rearrange("k h (c t) -> k h c t", t=64)
                S.op("dve", lambda e: e.tensor_tensor(out=d5[:, :, :, sft:64], in0=s5[:, :, :, sft:64], in1=s5[:, :, :, 0:64 - sft], op=ALU.add), reads=[bsrc], writes=[bdst])
                cp(S, "act", d5[:, :, :, 0:sft], s5[:, :, :, 0:sft], [bsrc], [bdst])
                src, bsrc, dst, bdst = dst, bdst, src, bsrc
            cl, b_cl = src, bsrc
            S.op("act", lambda e: e.activation(out=epos[:], in_=cl[:], func=AF.Exp), reads=[b_cl], writes=[b_epos])
            S.op("act", lambda e: e.activation(out=eneg[:], in_=cl[:], func=AF.Exp, scale=-1.0), reads=[b_cl], writes=[b_eneg])
            S.op("dve", lambda e: e.tensor_tensor(out=eprev[:], in0=cl[:], in1=eprev[:], op=ALU.subtract), reads=[b_cl, b_eprev], writes=[b_eprev])
            S.op("act", lambda e: e.activation(out=eprev[:], in_=eprev[:], func=AF.Exp), reads=[b_eprev], writes=[b_eprev])
            ep5 = epos[:].rearrange("k h (c t) -> k h c t", t=64)
            S.op("dve", lambda e: e.tensor_copy(out=PCf[:], in_=ep5[:, :, :, 63]), reads=[b_epos], writes=[b_PCf])
            en5 = eneg[:].rearrange("k h (c t) -> k h c t", t=64)
            ec5 = eC[:].rearrange("k h (c t) -> k h c t", t=64)
            for h in range(4):
                for c in range(NCH):
                    if (h + c) % 2:
                        S.op("dve", lambda e: e.tensor_scalar(out=ec5[:, h, c, :], in0=en5[:, h, c, :], scalar1=PCf[:, h, c:c + 1], scalar2=None, op0=ALU.mult),
                             reads=[b_eneg, b_PCf], writes=[b_eC])
                    else:
                        S.op("act", lambda e: e.activation(out=ec5[:, h, c, :], in_=en5[:, h, c, :], func=AF.Copy, scale=AS32(PCf[:, h, c:c + 1])),
                             reads=[b_eneg, b_PCf], writes=[b_eC])
            for h in range(4):
                S.op("dve", lambda e: e.scalar_tensor_tensor(out=AR[:, :, h // 2, 0, h % 2, :], in0=hm(kkn)[:, h], scalar=-1.0, in1=hm(eprev)[:, h], op0=ALU.mult, op1=ALU.mult), reads=[b_kkn, b_eprev], writes=[b_AR])
                S.op("dve", lambda e: e.tensor_tensor(out=AR[:, :, h // 2, 1, h % 2, :], in0=X3[:, h, :].rearrange("k (c t) -> k c t", t=64), in1=hm(epos)[:, h], op=ALU.mult), reads=[b_X3, b_epos], writes=[b_AR])
            S.op("dve", lambda e: e.tensor_tensor(out=tmp[:], in0=kkn[:], in1=aa[:], op=ALU.mult), reads=[b_kkn, b_aa], writes=[b_tmp])
            for h in range(4):
                S.op("dve", lambda e: e.tensor_tensor(out=cm(Bt)[:, h], in0=hm(tmp)[:, h], in1=hm(eneg)[:, h], op=ALU.mult), reads=[b_tmp, b_eneg], writes=[b_Bt])
                S.op("pool", lambda e: e.tensor_tensor(out=cm(Bh)[:, h], in0=hm(tmp)[:, h], in1=hm(eC)[:, h], op=ALU.mult), reads=[b_tmp, b_eC], writes=[b_Bh])
                S.op("dve", lambda e: e.tensor_tensor(out=cm(Kt)[:, h], in0=hm(kfin)[:, h], in1=hm(eneg)[:, h], op=ALU.mult), reads=[b_kfin, b_eneg], writes=[b_Kt])
                S.op("pool", lambda e: e.tensor_tensor(out=cm(Kh)[:, h], in0=hm(kfin)[:, h], in1=hm(eC)[:, h], op=ALU.mult), reads=[b_kfin, b_eC], writes=[b_Kh])
                cp(S, "act", cm(Vc_)[:, h], X3[:, 8 + h, :].rearrange("k (c t) -> k c t", t=64), [b_X3], [b_Vc_])
            for p in range(2):
                pb, bpb = nbx()
                S.op("pe", lambda e: e.matmul(pb[:, 0:NCH], lhsT=il[:], rhs=PCf[:, 2 * p, :], start=True, stop=False), reads=[b_il, b_PCf], writes=[bpb])
                S.op("pe", lambda e: e.matmul(pb[:, 0:NCH], lhsT=ir[:], rhs=PCf[:, 2 * p + 1, :], start=False, stop=True), reads=[b_ir, b_PCf], writes=[bpb])
                evac(PCc[:, p, :], pb[:, 0:NCH], [bpb], [b_PCc])
            if tt == 0:
                g.dbg("d_X3", X3[:], b_X3, [64, 12, TB]); g.dbg("d_cl", cl[:], b_cl, [64, 4, TB]); g.dbg("d_aa", aa[:], b_aa, [64, 4, TB])
                g.dbg("d_kkn", kkn[:], b_kkn, [64, 4, TB]); g.dbg("d_kfin", kfin[:], b_kfin, [64, 4, TB]); g.dbg("d_bv", bv[:], b_bv, [64, 4, TB])
                g.dbg("d_gT", gT[:], b_gT, [64, 4, TB]); g.dbg("d_eC", eC[:], b_eC, [64, 4, TB])
                g.dbg("d_PCc", PCc[:], b_PCc, [128, 2, NCH])
            def unit(c, p):
                tc = slice(c * 64, (c + 1) * 64)
                hp = slice(2 * p, 2 * p + 2)
                fl = lambda ap: ap.rearrange("k h t -> k (h t)")
                At_ = fl(AR[:, c, p, 0, :, :]); Bt_ = fl(Bt[:, c, hp, :]); Kt_ = fl(Kt[:, c, hp, :])
                ARp = AR[:, c, p, :, :, :].rearrange("k a h t -> k (a h t)")
                p1, bp1 = nb(); p2, bp2 = nb()
                S.op("pe", lambda e: e.matmul(p1[:, 0:256], lhsT=RR(Bt_), rhs=RR(ARp), start=True, stop=True), reads=[b_Bt, b_AR], writes=[bp1])
                S.op("pe", lambda e: e.matmul(p2[:, 0:256], lhsT=RR(Kt_), rhs=RR(ARp), start=True, stop=True), reads=[b_Kt, b_AR], writes=[bp2])
                yield
                N0, bN0 = nt_(); ArbT, bArbT = nt_(); AakT, bAakT = nt_(); ArkT, bArkT = nt_()
                S.op("dve", lambda e: e.tensor_tensor(out=N0[:], in0=p1[:, 0:128], in1=msu[:], op=ALU.mult), reads=[bp1, b_msu], writes=[bN0])
                S.op("dve", lambda e: e.tensor_tensor(out=ArbT[:], in0=p1[:, 128:256], in1=mu_[:], op=ALU.mult), reads=[bp1, b_mu], writes=[bArbT])
                rel(bp1)
                S.op("dve", lambda e: e.tensor_tensor(out=AakT[:], in0=p2[:, 0:128], in1=msu[:], op=ALU.mult), reads=[bp2, b_msu], writes=[bAakT])
                S.op("dve", lambda e: e.tensor_tensor(out=ArkT[:], in0=p2[:, 128:256], in1=mu_[:], op=ALU.mult), reads=[bp2, b_mu], writes=[bArkT])
                rel(bp2)
                p3, bp3 = nb()
                S.op("pe", lambda e: e.matmul(p3[:, 0:128], lhsT=RR(At_), rhs=RR(Bt_), start=True, stop=True), reads=[b_AR, b_Bt], writes=[bp3])
                ptr, bptr = nb()
                srcs = [(At_, b_AR), (fl(Vc_[:, c, hp, :]), b_Vc_), (fl(Bh[:, c, hp, :]), b_Bh), (fl(Kh[:, c, hp, :]), b_Kh)]
                for i_, (sap, sb_) in enumerate(srcs):
                    S.op("pe", lambda e: e.transpose(out=ptr[:, i_ * 64:(i_ + 1) * 64], in_=AS32(sap) if i_ == 0 else sap, identity=idf[0:64, 0:64]), reads=[sb_, b_idf], writes=[bptr])
                yield
                NT0, bNT0 = nt_()
                S.op("dve", lambda e: e.tensor_tensor(out=NT0[:], in0=p3[:, 0:128], in1=msl[:], op=ALU.mult), reads=[bp3, b_msl], writes=[bNT0])
                rel(bp3)
                Z, bZ = nt_()
                S.op("pool", lambda e: e.tensor_tensor(out=Z[:], in0=N0[:], in1=idf[:], op=ALU.add), reads=[bN0, b_idf], writes=[bZ])
                TA, bTA = nt_()
                Vt, bVt = nt_()
                cp(S, "act", TA[:, 0:64], ptr[:, 0:64], [bptr], [bTA])
                cp(S, "act", Vt[:, 0:64], ptr[:, 64:128], [bptr], [bVt])
                bi = bdc[0] % NBD; bdc[0] += 1
                Bbd, bBbd = bd[0][bi], b_bd[0][bi]
                Kbd, bKbd = bd[1][bi], b_bd[1][bi]
                Apb, bApb = bd[2][bi], b_bd[2][bi]
                for hh in range(2):
                    rs = slice(hh * 64, (hh + 1) * 64)
                    cp(S, "act", Bbd[rs, rs], ptr[rs, 128:192], [bptr], [bBbd])
                    cp(S, "act", Kbd[rs, rs], ptr[rs, 192:256], [bptr], [bKbd])
                rel(bptr)
                yield
                X, bX, XT, bXT = N0, bN0, NT0, bNT0
                pw, bpw = nb()
                S.op("pe", lambda e: e.matmul(pw[:, 0:64], lhsT=RR(AakT[:]), rhs=RR(Vt[:, 0:64]), start=True, stop=True), reads=[bAakT, bVt], writes=[bpw])
                yield
                cp(S, "act", TA[:, 64:128], pw[:, 0:64], [bpw], [bTA])
                rel(bpw)
                for j in range(1, 6):
                    if j <= 4:
                        px, bpx = nb()
                        S.op("pe", lambda e: e.matmul(px[:, 0:128], lhsT=RR(XT[:]), rhs=RR(X[:]), start=True, stop=True), reads=[bXT, bX], writes=[bpx])
                    pxt, bpxt = nb()
                    S.op("pe", lambda e: e.matmul(pxt[:, 0:128], lhsT=RR(X[:]), rhs=RR(XT[:]), start=True, stop=True), reads=[bXT, bX], writes=[bpxt])
                    yield
                    if j <= 4:
                        Xn, bXn = nt_()
                        cp(S, "act", Xn[:], px[:, 0:128], [bpx], [bXn])
                        rel(bpx)
                    XTn, bXTn = nt_()
                    cp(S, "act" if j > 4 else "dve", XTn[:], pxt[:, 0:128], [bpxt], [bXTn])
                    rel(bpxt)
                    pz, bpz = nb()
                    S.op("pe", lambda e: e.matmul(pz[:, 0:128], lhsT=RR(XTn[:]), rhs=RR(Z[:]), start=True, stop=True), reads=[bXTn, bZ], writes=[bpz])
                    yield
                    Zn, bZn = nt_()
                    S.op("dve", lambda e: e.tensor_tensor(out=Zn[:], in0=pz[:, 0:128], in1=Z[:], op=ALU.add), reads=[bpz, bZ], writes=[bZn])
                    rel(bpz)
                    Z, bZ = Zn, bZn
                    if j <= 4:
                        X, bX = Xn, bXn
                    XT, bXT = XTn, bXTn
                pu, bpu = nb()
                S.op("pe", lambda e: e.matmul(pu[:, 0:128], lhsT=RR(Z[:]), rhs=RR(TA[:]), start=True, stop=True), reads=[bZ, bTA], writes=[bpu])
                yield
                U0, bU0 = nt_()
                cp(S, "dve", U0[:, 0:64], pu[:, 64:128], [bpu], [bU0])
                for hh in range(2):
                    rs = slice(hh * 64, (hh + 1) * 64)
                    cp(S, "act", Apb[rs, rs], pu[rs, 0:64], [bpu], [bApb])
                rel(bpu)
                pg, bpg = nb()
                S.op("pe", lambda e: e.matmul(pg[:, 0:64], lhsT=RR(Bbd[:]), rhs=RR(U0[:, 0:64]), start=True, stop=False), reads=[bBbd, bU0], writes=[bpg])
                S.op("pe", lambda e: e.matmul(pg[:, 0:64], lhsT=RR(Kbd[:]), rhs=RR(Vt[:, 0:64]), start=False, stop=True), reads=[bKbd, bVt], writes=[bpg])
                pf, bpf = nb()
                S.op("pe", lambda e: e.matmul(pf[:, 0:128], lhsT=RR(Apb[:]), rhs=RR(Bbd[:]), start=True, stop=True), reads=[bApb, bBbd], writes=[bpf])
                yield
                Gs, bGs = ntf_()
                cp(S, "act", Gs[:, 0:64], pg[:, 0:64], [bpg], [bGs])
                rel(bpg)
                PhiT, bPhiT = nt_()
                S.op("dve", lambda e: e.scalar_tensor_tensor(out=PhiT[:], in0=idf[:], scalar=PCc[:, p, c:c + 1], in1=pf[:, 0:128], op0=ALU.mult, op1=ALU.add),
                     reads=[b_idf, b_PCc, bpf], writes=[bPhiT])
                rel(bpf)
                pr, bpr = nb()
                S.op("pe", lambda e: e.matmul(pr[:, 0:64], lhsT=RR(il[:]), rhs=RR(AR[:, c, p, 1, 0, :]), start=True, stop=False, **SK), reads=[b_il, b_AR], writes=[bpr])
                S.op("pe", lambda e: e.matmul(pr[:, 64:128], lhsT=RR(ir[:]), rhs=RR(AR[:, c, p, 1, 1, :]), start=False, stop=False, **SK), reads=[b_ir, b_AR], writes=[bpr])
                S.op("pe", lambda e: e.matmul(pr[:, 0:128], lhsT=RR(Apb[:]), rhs=RR(ArbT[:]), start=False, stop=True, **SK), reads=[bApb, bArbT], writes=[bpr])
                yield
                RpT, bRpT = nt_()
                cp(S, "act", RpT[:], pr[:, 0:128], [bpr], [bRpT])
                rel(bpr)
                Scur, bScur = ST[p][sidx[p]], b_ST[p][sidx[p]]
                py, bpy = nb()
                S.op("pe", lambda e: e.matmul(py[:, 0:64], lhsT=RR(ArbT[:]), rhs=RR(U0[:, 0:64]), start=True, stop=False), reads=[bArbT, bU0], writes=[bpy])
                S.op("pe", lambda e: e.matmul(py[:, 0:64], lhsT=RR(ArkT[:]), rhs=RR(Vt[:, 0:64]), start=False, stop=False), reads=[bArkT, bVt], writes=[bpy])
                S.op("pe", lambda e: e.matmul(py[:, 0:64], lhsT=RR(RpT[:]), rhs=RR(Scur[:]), start=False, stop=True), reads=[bRpT, bScur], writes=[bpy])
                ps_, bps = nb()
                S.op("pe", lambda e: e.matmul(ps_[:, 0:64], lhsT=RR(PhiT[:]), rhs=RR(Scur[:]), start=True, stop=True), reads=[bPhiT, bScur], writes=[bps])
                sidx[p] ^= 1
                Snew, bSnew = ST[p][sidx[p]], b_ST[p][sidx[p]]
                S.op("dve", lambda e: e.tensor_tensor(out=Snew[:], in0=ps_[:, 0:64], in1=Gs[:, 0:64], op=ALU.add), reads=[bps, bGs], writes=[bSnew])
                rel(bps)
                Yt, bYt = ntf_()
                st_, bst = ntf_()
                cp(S, "act", Yt[:, 0:64], py[:, 0:64], [bpy], [bYt])
                rel(bpy)
                yield
                S.op("dve", lambda e: e.bn_stats(out=st_[:, 0:6], in_=Yt[:, 0:64]), reads=[bYt], writes=[bst])
                yield
                S.op("dve", lambda e: e.bn_aggr(out=st_[:, 8:10], in_=st_[:, 0:6]), reads=[bst], writes=[bst])
                yield
                S.op("act", lambda e: e.activation(out=st_[:, 10:11], in_=st_[:, 9:10], func=AF.Sqrt, bias=cst[:, 0:1], scale=1.0), reads=[bst, b_cst], writes=[bst])
                yield
                S.op("dve", lambda e: e.reciprocal(out=st_[:, 11:12], in_=st_[:, 10:11]), reads=[bst], writes=[bst])
                yield
                S.op("dve", lambda e: e.tensor_scalar(out=Yt[:, 0:64], in0=Yt[:, 0:64], scalar1=st_[:, 8:9], scalar2=st_[:, 11:12], op0=ALU.subtract, op1=ALU.mult),
                     reads=[bYt, bst], writes=[bYt])
                pyt, bpyt = nb()
                S.op("pe", lambda e: e.transpose(out=pyt[0:64, 0:128], in_=Yt[:, 0:64], identity=idf[:]), reads=[bYt, b_idf], writes=[bpyt])
                yield
                cp(S, "act", YN[:, hp, tc], pyt[0:64, 0:128].rearrange("v (h t) -> v h t", h=2), [bpyt], [b_YN])
                rel(bpyt)

            GRP = 2
            for c0_ in range(0, NCH, GRP):
                for k_ in range(NB):
                    busy[k_] = False
                gens = [unit(c_, p_) for c_ in range(c0_, min(NCH, c0_ + GRP)) for p_ in range(2)]
                alive = list(gens)
                while alive:
                    nxt = []
                    for gn_ in alive:
                        try:
                            next(gn_)
                            nxt.append(gn_)
                        except StopIteration:
                            pass
                    alive = nxt
            if tt == 0:
                g.dbg("d_YN", YN[:], b_YN, [64, 4, TB])
            for h in range(4):
                S.op("dve", lambda e: e.tensor_scalar(out=YN[:, h, :], in0=YN[:, h, :], scalar1=rp[0:64, 36 + h:37 + h], scalar2=rp[0:64, 40 + h:41 + h], op0=ALU.mult, op1=ALU.add),
                     reads=[b_YN, b_rp], writes=[b_YN])
            S.op("dve", lambda e: e.tensor_tensor(out=YN[:], in0=YN[:], in1=bv[:], op=ALU.add), reads=[b_YN, b_bv], writes=[b_YN])
            S.op("dve", lambda e: e.tensor_tensor(out=yo[:], in0=YN[:], in1=gT[:], op=ALU.mult), reads=[b_YN, b_gT], writes=[b_yo])
            S.dma("sp", D["ymixT"][0:256, t0:t0 + TB].rearrange("(h v) t -> v h t", v=64), yo[:], reads=[b_yo])
        S.barrier()


def build(T=8192, debug=False, phases=None, nlayers=2):
    nc = bass.Bass("TRN2", target_bir_lowering=False)
    g = G()
    g.nc, g.T, g.wctr = nc, T, 0
    g.debug = debug
    D = {}
    g.D = D

    def din(name, shape, dt=F32):
        D[name] = nc.dram_tensor(name, list(shape), dt, kind="ExternalInput").ap()

    def dscr(name, shape, dt=F32, out=False):
        D[name] = nc.dram_tensor(name, list(shape), dt, kind=("ExternalOutput" if (out or debug) else "Internal")).ap()

    din("xT", [DM, T]); din("pT", [2, 256, T]); din("pos", [1, T], I32); din("invf", [128, 1])
    din("w_in", [2, DM, NEXT]); din("smalls", [2, 128, 64]); din("w_out", [2, DM, DM])
    din("ffn_w_up", [2, DM, 2 * DFF]); din("ffn_w_down", [2, DFF, DM]); din("convp", [2, 128, 44, 4])
    din("ple_w_gate", [2, DM, DM]); din("ple_w_proj", [2, 256, DM])
    din("pool_wbd", [2, 128, 2, 128]); din("pool_fix", [128, 2, 16])
    for nm, shp, dt in g_extra_inputs(T):
        din(nm, shp, dt)
    dscr("cosT", [128, T]); dscr("sinT", [128, T])
    dscr("zaT", [1024, T]); dscr("zbT", [256, T]); dscr("qT", [512, T], BF16); dscr("kT", [384, T], BF16)
    dscr("vcT", [128, T], BF16); dscr("vtok", [T, 256], BF16); dscr("gates", [T, 24])
    dscr("ymixT", [1024, T], BF16)
    for nm, shp, dt in g_extra_scratch(T):
        dscr(nm, shp, dt)
    dscr("xs0", [DM, T]); dscr("xs1", [DM, T]); dscr("xs2", [DM, T])
    dscr("outT", [DM, T], out=True)
    with ExitStack() as stack:
        g.S = Sched(nc, stack)
        if phases is None:
            phases = ("rope", "inproj", "rwkv", "pool", "nsa", "outproj", "ffn", "ple")
        if "rope" in phases:
            phase_rope(g)
        xcur = D["xT"]
        for l in range(nlayers):
            if "inproj" in phases:
                phase_inproj(g, l, xcur)
            if "rwkv" in phases:
                phase_rwkv(g, l)
            if "pool" in phases:
                phase_pool(g, l)
            if "nsa" in phases:
                phase_nsa(g, l)
            if "outproj" in phases:
                phase_outproj(g, l, xcur, D["xs0"])
            if "ffn" in phases:
                phase_ffn(g, l, D["xs0"], D["xs1"])
            if "ple" in phases:
                last = (l == nlayers - 1)
                phase_ple(g, l, D["xs1"], D["outT"] if last else D["xs2"], last)
            xcur = D["xs2"]
        g.S.barrier()
        g.ninstr = g.S.ninstr
    return nc, g


def g_extra_inputs(T):
    NCMP = (T - 32) // 16 + 1
    NTC = (NCMP + 127) // 128
    return [("nsa_masks", [128, 19, 512], BF16), ("identb", [128, 128], BF16), ("identf", [128, 128], F32),
            ("E_all", [128, T], BF16), ("mcs", [128, NTC, 128], BF16), ("keepw", [128, 256], F32), ("addw", [128, 256], F32),
            ("permb", [128, 128], BF16), ("rw_msu", [128, 128], F32), ("rw_mu", [128, 128], F32), ("rw_msl", [128, 128], F32), ("rw_il", [64, 128], F32), ("rw_ir", [64, 128], F32),
            ("rwp", [2, 128, 64], F32), ("rw_w_up", [2, 64, 256], F32), ("rw_a_up", [2, 64, 256], F32), ("rw_g_up", [2, 128, 256], F32),
            ("nsa_w_ck", [2, 32, 64, 64], F32), ("nsa_w_cv", [2, 32, 64, 64], F32), ("nsa_peT", [2, 2, 64, 32], F32)]


def g_extra_scratch(T):
    return []


def host_prep(inp, T=8192):
    f = np.float32
    cols = inproj_cols()
    shared = {}
    shared["w_in"] = np.ascontiguousarray(inp["w_in"][:, :, cols])
    sm = np.zeros((2, 128, 64), f)
    for l in range(2):
        sm[l, :, 0:8] = inp["g_mix"][l].reshape(8, 128).T
        sm[l, :, 8:10] = inp["pool_scale"][l].reshape(2, 128).T
        sm[l, :, 16:24] = inp["g_ffn"][l].reshape(8, 128).T
        sm[l, :, 24:32] = inp["g_ple"][l].reshape(8, 128).T
        sm[l, :, 32:40] = inp["g_final"].reshape(8, 128).T
    shared["smalls"] = sm
    shared["w_out"] = np.ascontiguousarray(inp["w_out"])
    shared["ffn_w_up"] = np.ascontiguousarray(inp["ffn_w_up"])
    shared["ffn_w_down"] = np.ascontiguousarray(inp["ffn_w_down"])
    cp_ = np.zeros((2, 128, 44, 4), f)
    for l in range(2):
        cw = inp["ffn_conv_w"][l][:, 0, :]
        for i in range(3):
            cp_[l, :, :, i] = cw[i].reshape(44, 128).T
        cp_[l, :, :, 3] = inp["ffn_conv_b"][l].reshape(44, 128).T
    shared["convp"] = cp_
    shared["ple_w_gate"] = np.ascontiguousarray(inp["ple_w_gate"])
    shared["ple_w_proj"] = np.ascontiguousarray(inp["ple_w_proj"])
    pw = np.zeros((2, 128, 2, 128), f)
    for l in range(2):
        for gi in range(4):
            t_, h_ = gi // 2, gi % 2
            pw[l, h_ * 64:(h_ + 1) * 64, t_, h_ * 64:(h_ + 1) * 64] = inp["pool_w"][l, gi]
    shared["pool_wbd"] = pw
    fix = np.zeros((128, 2, 16), f)
    for gi, win in enumerate((2, 4, 8, 16)):
        t_, h_ = gi // 2, gi % 2
        fix[h_ * 64:(h_ + 1) * 64, t_, :] = 1.0 / np.minimum(np.arange(16) + 1, win)
    shared["pool_fix"] = fix
    shared.update(nsa_consts(T))
    shared.update(rwkv_consts())
    import ml_dtypes
    pm = np.zeros((128, 128), f)
    mm_ = np.arange(128)
    pm[(mm_ // 64) * 64 + ((mm_ % 64) + 32) % 64, mm_] = 1.0
    shared["permb"] = pm.astype(ml_dtypes.bfloat16)
    rwp = np.zeros((2, 128, 64), f)
    for l in range(2):
        mu = inp["rw_mu"][l]
        rwp[l, 0:64, 0:12] = mu[0:768].reshape(12, 64).T
        rwp[l, 0:64, 12:14] = mu[768:896].reshape(2, 64).T
        rwp[l, :, 14] = mu[896:1024]
        for j, nm in enumerate(("rw_w0", "rw_a0", "rw_k_k", "rw_k_a")):
            rwp[l, 0:64, 16 + 4 * j:20 + 4 * j] = inp[nm][l].reshape(4, 64).T
        rwp[l, 0:64, 32:36] = inp["rw_r_k"][l].T
        rwp[l, 0:64, 36:40] = inp["rw_gn_g"][l].reshape(4, 64).T
        rwp[l, 0:64, 40:44] = inp["rw_gn_b"][l].reshape(4, 64).T
    shared["rwp"] = rwp
    shared["rw_w_up"] = np.ascontiguousarray(inp["rw_w_up"])
    shared["rw_a_up"] = np.ascontiguousarray(inp["rw_a_up"])
    shared["rw_g_up"] = np.ascontiguousarray(inp["rw_g_up"])
    shared["nsa_w_ck"] = np.ascontiguousarray(inp["nsa_w_ck"])
    shared["nsa_w_cv"] = np.ascontiguousarray(inp["nsa_w_cv"])
    shared["nsa_peT"] = np.ascontiguousarray(np.stack([np.transpose(inp["nsa_pe_k"], (0, 2, 1)), np.transpose(inp["nsa_pe_v"], (0, 2, 1))], axis=1))
    inv = (10000.0 ** (-np.arange(32, dtype=f) / 32)).astype(f)
    shared["invf"] = np.tile(inv, 4).reshape(128, 1).astype(f)
    return shared


def per_core(inp, b, T=8192):
    return {"xT": np.ascontiguousarray(inp["x"][b, :T].T),
            "pT": np.ascontiguousarray(np.transpose(inp["p"][:, b, :T], (0, 2, 1))),
            "pos": np.ascontiguousarray(inp["positions"][b:b + 1, :T]).astype(np.int32)}


def kernel(**inputs):
    T = 8192
    inp = {k: np.asarray(v) for k, v in inputs.items()}
    nc, g = build(T)
    shared = host_prep(inp, T)
    in_maps = []
    for b in range(8):
        m = dict(shared)
        m.update(per_core(inp, b, T))
        in_maps.append(m)
    res = run_bass_kernel_spmd(nc, in_maps, core_ids=list(range(8)))
    out = np.stack([np.ascontiguousarray(res.results[b]["outT"].T) for b in range(8)], axis=0)
    return out.astype(np.float32)
```

```python
import os
import numpy as np
from contextlib import ExitStack
import concourse.bass as bass
import concourse.mybir as mybir
from concourse.bass_utils import run_bass_kernel_spmd

F32 = mybir.dt.float32
BF16 = mybir.dt.bfloat16
I32 = mybir.dt.int32
AF = mybir.ActivationFunctionType
ALU = mybir.AluOpType
AX = mybir.AxisListType

DM = 1024
PI = float(np.pi)
BIG = 30000.0
NFEAT = 2304
NTOKC = 280
NEXT = NFEAT + NTOKC
DFF = 2816


class Buf:
    __slots__ = ("w", "rs", "rd", "excl")

    def __init__(self, excl=False):
        self.w = None
        self.rs = {}
        self.rd = []
        self.excl = excl


class Sched:
    EPOCH = 30000
    NDMA = 6

    def __init__(self, nc, stack):
        self.nc = nc
        self.stack = stack
        self.engs = {"pe": nc.tensor, "act": nc.scalar, "dve": nc.vector, "pool": nc.gpsimd, "sp": nc.sync}
        self.nsem = 0
        self.sem = {}
        self.cnt = {}
        self.seen = {k: {} for k in self.engs}
        for k in self.engs:
            self.sem[k] = self._newsem(k)
            self.cnt[k] = 0
        self.dsem = {}
        self.dpos = {}
        for k in ("sp", "act", "pool"):
            self.dsem[k] = [[self._newsem("d" + k), 0] for _ in range(self.NDMA)]
            self.dpos[k] = 0
        self.last = {}
        self.ninstr = 0
        self.store_q = "pool"

    def _newsem(self, name):
        self.nsem += 1
        return self.stack.enter_context(self.nc.semaphore(f"s_{name}_{self.nsem}"))

    def _wait(self, ek, tok):
        sem, val, src = tok
        d = self.seen[ek]
        key = id(sem)
        if d.get(key, 0) >= val:
            return
        self.engs[ek].wait_ge(sem, val)
        d[key] = val

    def _deps(self, ek, reads, writes):
        toks = []
        same_ok = (ek == "pe")
        for b in reads:
            if b.w is not None and not (b.w[2] == ek and same_ok):
                toks.append(b.w)
            if b.excl:
                for e, t in b.rs.items():
                    if e != ek:
                        toks.append(t)
        for b in writes:
            if b.w is not None and not (b.w[2] == ek and same_ok):
                toks.append(b.w)
            for e, t in b.rs.items():
                if not (e == ek and same_ok):
                    toks.append(t)
            toks.extend(b.rd)
        for t in toks:
            self._wait(ek, t)

    def _record(self, tok, reads, writes, is_dma):
        for b in reads:
            if is_dma:
                b.rd.append(tok)
                if len(b.rd) > 24:
                    del b.rd[0]
            else:
                b.rs[tok[2]] = tok
        for b in writes:
            b.w = tok
            b.rs = {}
            b.rd = []

    def op(self, ek, fn, reads=(), writes=()):
        self._deps(ek, reads, writes)
        ins = fn(self.engs[ek])
        self.cnt[ek] += 1
        ins.then_inc(self.sem[ek], 1)
        tok = (self.sem[ek], self.cnt[ek], ek)
        self.last[ek] = tok
        self._record(tok, reads, writes, False)
        self.ninstr += 1
        if self.cnt[ek] >= self.EPOCH:
            self.sem[ek] = self._newsem(ek)
            self.cnt[ek] = 0
        return tok

    def dma(self, qk, out, in_, reads=(), writes=(), **kw):
        if qk == "sp" and len(writes) == 0 and self.store_q is not None:
            qk = self.store_q
        self._deps(qk, reads, writes)
        slots = self.dsem[qk]
        i = self.dpos[qk]
        self.dpos[qk] = (i + 1) % len(slots)
        sem, val = slots[i]
        if val > 0:
            self._wait(qk, (sem, val, "dma"))
        if val + 16 > 60000:
            sem = self._newsem("d" + qk)
            val = 0
            slots[i][0] = sem
        ins = self.engs[qk].dma_start(out=out, in_=in_, **kw)
        val += 16
        ins.then_inc(sem, 16)
        slots[i][1] = val
        tok = (sem, val, "dma")
        self._record(tok, reads, writes, True)
        self.ninstr += 1
        return tok

    def barrier(self):
        toks = list(self.last.values())
        for qk in self.dsem:
            for sem, val in self.dsem[qk]:
                if val > 0:
                    toks.append((sem, val, "dma"))
        for ek in self.engs:
            for t in toks:
                self._wait(ek, t)


def inproj_cols():
    qb = 1280
    cols = list(range(0, 1280))
    cols += list(range(qb, qb + 512))
    for off in (512, 768, 1024):
        cols += list(range(qb + off, qb + off + 128))
    cols += list(range(qb + 640, qb + 768))
    assert len(cols) == NFEAT
    cols += list(range(qb + 896, qb + 1024)) + list(range(qb + 1152, qb + 1280))
    cols += list(range(qb + 1280, qb + 1304))
    assert len(cols) == NEXT
    return np.array(cols)


class G:
    uid = 0
    debug = False
    use_f32r = True

    def dbg(self, name, ap, buf, shape):
        if not self.debug or name in self.D:
            return
        self.D[name] = self.nc.dram_tensor(name, list(shape), F32, kind="ExternalOutput").ap()
        self.S.dma("sp", self.D[name], ap, reads=[buf])

    def nm(self, name):
        self.uid += 1
        return f"{name}_{self.uid}"


def cp(S, ek, out, in_, reads, writes):
    if ek == "act":
        return S.op("act", lambda e: e.copy(out=out, in_=in_), reads=reads, writes=writes)
    return S.op(ek, lambda e: e.tensor_copy(out=out, in_=in_), reads=reads, writes=writes)


def load_w_bf16(g, dst, b_dst, src, KC, N, stage, b_stage, rows=128):
    S = g.S
    CH = stage[0].shape[-1]
    srcv = src.rearrange("(c p) n -> p c n", p=rows)
    for c in range(KC):
        for n0 in range(0, N, CH):
            n1 = min(N, n0 + CH)
            i = g.wctr % len(stage)
            g.wctr += 1
            S.dma("sp", stage[i][0:rows, 0:n1 - n0], srcv[:, c, n0:n1], writes=[b_stage[i]])
            ek = ("act", "dve", "pool")[g.wctr % 3]
            cp(S, ek, dst[0:rows, c, n0:n1], stage[i][0:rows, 0:n1 - n0], [b_stage[i]], [b_dst])


def rmsnorm_tile(g, xt, b_x, hT, b_h, gcol, b_g, N, R):
    S = g.S
    S.op("act", lambda e: e.activation(out=R["sq"][:, :, 0:N], in_=xt[:, :, 0:N], func=AF.Square), reads=[b_x], writes=[R["b_sq"]])
    for c in range(8):
        S.op("pe", lambda e: e.matmul(R["p_rms"][:, 0:N], lhsT=R["ones"][:], rhs=R["sq"][:, c, 0:N], start=(c == 0), stop=(c == 7)),
             reads=[R["b_ones"], R["b_sq"]], writes=[R["b_prms"]])
    S.op("act", lambda e: e.activation(out=R["rstd"][:, 0:N], in_=R["p_rms"][:, 0:N], func=AF.Sqrt, bias=R["eps"][:, 0:1], scale=1.0 / DM),
         reads=[R["b_prms"], R["b_eps"]], writes=[R["b_rstd"]])
    S.op("dve", lambda e: e.reciprocal(out=R["rstd"][:, 0:N], in_=R["rstd"][:, 0:N]), reads=[R["b_rstd"]], writes=[R["b_rstd"]])
    for c in range(8):
        S.op("dve", lambda e: e.scalar_tensor_tensor(out=hT[:, c, 0:N], in0=xt[:, c, 0:N], scalar=gcol[:, c:c + 1], in1=R["rstd"][:, 0:N],
                                                       op0=ALU.mult, op1=ALU.mult),
             reads=[b_x, b_g, R["b_rstd"]], writes=[b_h])


def rms_shared(g, sbf, psf, N):
    S = g.S
    R = {}
    R["sq"] = sbf("rsq", [128, 8, N], BF16); R["b_sq"] = Buf()
    R["rstd"] = sbf("rstd", [128, N]); R["b_rstd"] = Buf()
    R["p_rms"] = psf("p_rms", [128, 512]); R["b_prms"] = Buf(excl=True)
    R["ones"] = sbf("ones", [128, 128], BF16); R["b_ones"] = Buf()
    R["eps"] = sbf("epsb", [128, 1]); R["b_eps"] = Buf()
    S.op("pool", lambda e: e.memset(R["ones"][:], 1.0), writes=[R["b_ones"]])
    S.op("pool", lambda e: e.memset(R["eps"][:], 1e-6), writes=[R["b_eps"]])
    return R


def phase_rope(g):
    nc, S, D, T = g.nc, g.S, g.D, g.T
    with ExitStack() as st:
        sbf = lambda name, shape, dt=F32: st.enter_context(nc.sbuf_tensor(g.nm(name), list(shape), dt))
        CH = min(T, 2048)
        posi = sbf("posi", [128, CH], I32); b_posi = Buf()
        posf = sbf("posf", [128, CH]); b_posf = Buf()
        ang = sbf("ang", [128, CH]); b_ang = Buf()
        tab = sbf("tab", [128, CH]); b_tab = Buf()
        ki = sbf("ki", [128, CH], I32); b_ki = Buf()
        kf = sbf("kf", [128, CH]); b_kf = Buf()
        inv_sb = sbf("inv_sb", [128, 1]); b_inv = Buf()
        S.dma("sp", inv_sb[:], D["invf"], writes=[b_inv])
        C1 = 6.28125
        C2 = 2 * np.pi - 6.28125
        for c0 in range(0, T, CH):
            S.dma("sp", posi[:], D["pos"][:, c0:c0 + CH].partition_broadcast(128), writes=[b_posi])
            S.op("dve", lambda e: e.tensor_copy(out=posf[:], in_=posi[:]), reads=[b_posi], writes=[b_posf])
            S.op("dve", lambda e: e.tensor_scalar(out=posf[:], in0=posf[:], scalar1=inv_sb[:, 0:1], scalar2=None, op0=ALU.mult),
                 reads=[b_posf, b_inv], writes=[b_posf])
            S.op("dve", lambda e: e.tensor_scalar(out=kf[:], in0=posf[:], scalar1=float(1.0 / (2 * np.pi)), scalar2=None, op0=ALU.mult),
                 reads=[b_posf], writes=[b_kf])
            S.op("dve", lambda e: e.tensor_copy(out=ki[:], in_=kf[:]), reads=[b_kf], writes=[b_ki])
            S.op("dve", lambda e: e.tensor_copy(out=kf[:], in_=ki[:]), reads=[b_ki], writes=[b_kf])
            S.op("dve", lambda e: e.scalar_tensor_tensor(out=posf[:], in0=kf[:], scalar=-C1, in1=posf[:], op0=ALU.mult, op1=ALU.add),
                 reads=[b_kf, b_posf], writes=[b_posf])
            S.op("dve", lambda e: e.scalar_tensor_tensor(out=posf[:], in0=kf[:], scalar=-C2, in1=posf[:], op0=ALU.mult, op1=ALU.add),
                 reads=[b_kf, b_posf], writes=[b_posf])
            for which, shift, dst in (("sin", 0.0, D["sinT"]), ("cos", PI / 2, D["cosT"])):
                S.op("dve", lambda e: e.tensor_scalar(out=ang[:], in0=posf[:], scalar1=shift, scalar2=None, op0=ALU.add),
                     reads=[b_posf], writes=[b_ang])
                S.op("dve", lambda e: e.tensor_scalar(out=kf[:], in0=ang[:], scalar1=PI, scalar2=-2 * PI, op0=ALU.is_gt, op1=ALU.mult),
                     reads=[b_ang], writes=[b_kf])
                S.op("dve", lambda e: e.tensor_tensor(out=ang[:], in0=ang[:], in1=kf[:], op=ALU.add), reads=[b_ang, b_kf], writes=[b_ang])
                S.op("dve", lambda e: e.tensor_scalar(out=ang[:], in0=ang[:], scalar1=3.141592, scalar2=-3.141592, op0=ALU.min, op1=ALU.max),
                     reads=[b_ang], writes=[b_ang])
                S.op("act", lambda e: e.activation(out=tab[:], in_=ang[:], func=AF.Sin), reads=[b_ang], writes=[b_tab])
                if which == "sin":
                    for base in (0, 64):
                        S.op("dve", lambda e: e.tensor_scalar(out=tab[base:base + 32, :], in0=tab[base:base + 32, :], scalar1=-1.0, scalar2=None, op0=ALU.mult),
                             reads=[b_tab], writes=[b_tab])
                S.dma("sp", dst[:, c0:c0 + CH], tab[:], reads=[b_tab])
        S.barrier()


def phase_inproj(g, l, xin):
    nc, S, D, T = g.nc, g.S, g.D, g.T
    with ExitStack() as st:
        sbf = lambda name, shape, dt=F32: st.enter_context(nc.sbuf_tensor(g.nm(name), list(shape), dt))
        psf = lambda name, shape, dt=F32: st.enter_context(nc.psum_tensor(g.nm(name), list(shape), dt))
        Wb = sbf("Wb", [128, 8, NEXT], BF16); b_W = Buf()
        stage = [sbf(f"wst{i}", [128, 1740]) for i in range(2)]; b_stage = [Buf() for _ in range(2)]
        load_w_bf16(g, Wb, b_W, D["w_in"][l], 8, NEXT, stage, b_stage)
        sm = sbf("sm", [128, 8]); b_sm = Buf()
        S.dma("sp", sm[:], D["smalls"][l][:, 0:8], writes=[b_sm])
        R = rms_shared(g, sbf, psf, 512)
        permb = sbf("permb", [128, 128], BF16); b_perm = Buf()
        S.dma("sp", permb[:], D["permb"], writes=[b_perm])
        qbf = [sbf(f"qbf{i}", [128, 512], BF16) for i in range(2)]; b_qbf = [Buf() for _ in range(2)]
        xt = [sbf(f"xt{i}", [128, 8, 512]) for i in range(2)]; b_xt = [Buf() for _ in range(2)]
        hTs = [sbf(f"hT{i}", [128, 8, 512], BF16) for i in range(2)]; b_hs = [Buf() for _ in range(2)]
        cs = [sbf(f"cs{i}", [128, 512]) for i in range(2)]; b_cs = [Buf() for _ in range(2)]
        sn = [sbf(f"sn{i}", [128, 512]) for i in range(2)]; b_sn = [Buf() for _ in range(2)]
        NEV = 4
        ev = [sbf(f"ev{i}", [128, 512]) for i in range(NEV)]; b_ev = [Buf() for _ in range(NEV)]
        evb = [sbf(f"evb{i}", [128, 512], BF16) for i in range(NEV)]; b_evb = [Buf() for _ in range(NEV)]
        t1 = [sbf(f"t1_{i}", [128, 512]) for i in range(2)]; b_t1 = [Buf() for _ in range(2)]
        t2 = [sbf(f"t2_{i}", [128, 512]) for i in range(2)]; b_t2 = [Buf() for _ in range(2)]
        vt = [sbf(f"vt{i}", [128, 256], BF16) for i in range(2)]; b_vt = [Buf() for _ in range(2)]
        gt = [sbf(f"gt{i}", [128, 24]) for i in range(2)]; b_gt = [Buf() for _ in range(2)]
        NPF = 5
        p_f = [psf(f"p_f{i}", [128, 512]) for i in range(NPF)]; b_pf = [Buf(excl=True) for _ in range(NPF)]
        p_t = [psf(f"p_t{i}", [128, 512]) for i in range(2)]; b_pt = [Buf(excl=True) for _ in range(2)]
        xv = xin.rearrange("(c p) t -> p c t", p=128)
        evc = 0
        pfc = 0

        NTL = T // 512

        def prep(tt):
            xi = tt % 2
            t0 = tt * 512
            S.dma("sp", xt[xi][:], xv[:, :, t0:t0 + 512], writes=[b_xt[xi]])
            S.dma("sp", cs[xi][:], D["cosT"][:, t0:t0 + 512], writes=[b_cs[xi]])
            S.dma("sp", sn[xi][:], D["sinT"][:, t0:t0 + 512], writes=[b_sn[xi]])
            rmsnorm_tile(g, xt[xi], b_xt[xi], hTs[xi], b_hs[xi], sm, b_sm, 512, R)

        prep(0)
        for tt in range(NTL):
            t0 = tt * 512
            xi = tt % 2
            if tt + 1 < NTL:
                prep(tt + 1)
            hT, b_h = hTs[xi], b_hs[xi]

            def mm_feat(f, pf, bpf):
                for c in range(8):
                    S.op("pe", lambda e: e.matmul(pf[:], lhsT=Wb[:, c, f * 128:(f + 1) * 128], rhs=hT[:, c, :], start=(c == 0), stop=(c == 7)),
                         reads=[b_W, b_h], writes=[bpf])
            plain = [(f, D["zaT"][f * 128:(f + 1) * 128, t0:t0 + 512], False) for f in range(8)]
            plain += [(8 + f, D["zbT"][f * 128:(f + 1) * 128, t0:t0 + 512], False) for f in range(2)]
            plain += [(17, D["vcT"][:, t0:t0 + 512], True)]
            for k_, (f, dst, isb) in enumerate(plain):
                pi = pfc % NPF; pfc += 1
                mm_feat(f, p_f[pi], b_pf[pi])
                ei = evc % NEV; evc += 1
                ek = "act" if k_ % 2 == 0 else "dve"
                if isb:
                    cp(S, ek, evb[ei][:], p_f[pi][:], [b_pf[pi]], [b_evb[ei]])
                    S.dma("sp", dst, evb[ei][:], reads=[b_evb[ei]])
                else:
                    cp(S, ek, ev[ei][:], p_f[pi][:], [b_pf[pi]], [b_ev[ei]])
                    S.dma("sp", dst, ev[ei][:], reads=[b_ev[ei]])
            pend = None
            for j in range(8):
                cur = None
                if j < 7:
                    if j < 4:
                        f_a = 10 + j
                        dst = D["qT"][j * 128:(j + 1) * 128, t0:t0 + 512]
                    else:
                        f_a = 14 + (j - 4)
                        dst = D["kT"][(j - 4) * 128:(j - 3) * 128, t0:t0 + 512]
                    pa = pfc % NPF; pfc += 1
                    mm_feat(f_a, p_f[pa], b_pf[pa])
                    qi_ = j % 2
                    cp(S, "act", qbf[qi_][:], p_f[pa][:], [b_pf[pa]], [b_qbf[qi_]])
                    cur = (pa, qi_, dst, j)
                if pend is not None:
                    pa_, qj_, dst_, jj = pend
                    pb = pfc % NPF; pfc += 1
                    S.op("pe", lambda e: e.matmul(p_f[pb][:], lhsT=permb[:], rhs=qbf[qj_][:], start=True, stop=True),
                         reads=[b_perm, b_qbf[qj_]], writes=[b_pf[pb]])
                    ti = jj % 2
                    S.op("dve", lambda e: e.tensor_tensor(out=t1[ti][:], in0=p_f[pa_][:], in1=cs[xi][:], op=ALU.mult),
                         reads=[b_pf[pa_], b_cs[xi]], writes=[b_t1[ti]])
                    S.op("dve", lambda e: e.tensor_tensor(out=t2[ti][:], in0=p_f[pb][:], in1=sn[xi][:], op=ALU.mult),
                         reads=[b_pf[pb], b_sn[xi]], writes=[b_t2[ti]])
                    ei = evc % NEV; evc += 1
                    S.op("pool", lambda e: e.tensor_tensor(out=evb[ei][:], in0=t1[ti][:], in1=t2[ti][:], op=ALU.add),
                         reads=[b_t1[ti], b_t2[ti]], writes=[b_evb[ei]])
                    S.dma("sp", dst_, evb[ei][:], reads=[b_evb[ei]])
                pend = cur
            for s4 in range(4):
                pi = s4 % 2
                for c in range(8):
                    S.op("pe", lambda e: e.matmul(p_t[pi][:, 0:NTOKC], lhsT=hT[:, c, s4 * 128:(s4 + 1) * 128], rhs=Wb[:, c, NFEAT:NEXT],
                                                  start=(c == 0), stop=(c == 7)),
                         reads=[b_W, b_h], writes=[b_pt[pi]])
                S.op("dve", lambda e: e.tensor_copy(out=vt[pi][:], in_=p_t[pi][:, 0:256]), reads=[b_pt[pi]], writes=[b_vt[pi]])
                S.op("act", lambda e: e.activation(out=gt[pi][:], in_=p_t[pi][:, 256:280], func=AF.Sigmoid), reads=[b_pt[pi]], writes=[b_gt[pi]])
                S.dma("sp", D["vtok"][t0 + s4 * 128:t0 + (s4 + 1) * 128, :], vt[pi][:], reads=[b_vt[pi]])
                S.dma("sp", D["gates"][t0 + s4 * 128:t0 + (s4 + 1) * 128, :], gt[pi][:], reads=[b_gt[pi]])
        S.barrier()


def phase_pool(g, l):
    nc, S, D, T = g.nc, g.S, g.D, g.T
    with ExitStack() as st:
        sbf = lambda name, shape, dt=F32: st.enter_context(nc.sbuf_tensor(g.nm(name), list(shape), dt))
        psf = lambda name, shape, dt=F32: st.enter_context(nc.psum_tensor(g.nm(name), list(shape), dt))
        CH = 512
        PAD = 16
        wp = sbf("wp", [128, 2, 128]); b_wp = Buf()
        S.dma("sp", wp[:], D["pool_wbd"][l], writes=[b_wp])
        sm = sbf("smp", [128, 2]); b_sm = Buf()
        S.dma("sp", sm[:], D["smalls"][l][:, 8:10], writes=[b_sm])
        fix = sbf("fix", [128, 2, 16]); b_fix = Buf()
        S.dma("sp", fix[:], D["pool_fix"], writes=[b_fix])
        z = [[sbf(f"pz{i}_{f}", [128, PAD + CH]) for f in range(2)] for i in range(2)]
        b_z = [[Buf() for f in range(2)] for i in range(2)]
        s_a = sbf("ps_a", [128, PAD + CH]); b_sa = Buf()
        s_b = sbf("ps_b", [128, PAD + CH]); b_sb = Buf()
        pl = sbf("ppl", [128, CH]); b_pl = Buf()
        ob = [sbf(f"pob{i}", [128, CH], BF16) for i in range(2)]; b_ob = [Buf() for _ in range(2)]
        pp = [psf(f"ppp{i}", [128, 512]) for i in range(2)]; b_pp = [Buf(excl=True) for _ in range(2)]
        k = 0
        for tt in range(T // CH):
            t0 = tt * CH
            zi = tt % 2
            for f in range(2):
                zt, bz = z[zi][f], b_z[zi][f]
                if tt == 0:
                    S.op("pool", lambda e: e.memset(zt[:, 0:PAD], 0.0), writes=[bz])
                    S.dma("sp", zt[:, PAD:PAD + CH], D["zbT"][f * 128:(f + 1) * 128, 0:CH], writes=[bz])
                else:
                    S.dma("sp", zt[:], D["zbT"][f * 128:(f + 1) * 128, t0 - PAD:t0 + CH], writes=[bz])
                W = PAD + CH
                S.op("dve", lambda e: e.tensor_tensor(out=s_a[:, 1:W], in0=zt[:, 1:W], in1=zt[:, 0:W - 1], op=ALU.add), reads=[bz], writes=[b_sa])
                S.op("dve", lambda e: e.tensor_tensor(out=s_b[:, 3:W], in0=s_a[:, 3:W], in1=s_a[:, 1:W - 2], op=ALU.add), reads=[b_sa], writes=[b_sb])
                if f == 0:
                    lo, hi = s_a, s_b
                    blo, bhi = b_sa, b_sb
                    wl, wh = 2, 4
                else:
                    S.op("dve", lambda e: e.tensor_tensor(out=s_a[:, 7:W], in0=s_b[:, 7:W], in1=s_b[:, 3:W - 4], op=ALU.add), reads=[b_sb], writes=[b_sa])
                    S.op("dve", lambda e: e.tensor_tensor(out=s_b[:, 15:W], in0=s_a[:, 15:W], in1=s_a[:, 7:W - 8], op=ALU.add), reads=[b_sa], writes=[b_sb])
                    lo, hi = s_a, s_b
                    blo, bhi = b_sa, b_sb
                    wl, wh = 8, 16
                S.op("dve", lambda e: e.scalar_tensor_tensor(out=pl[0:64, :], in0=lo[0:64, PAD:W], scalar=1.0 / wl, in1=zt[0:64, PAD:W], op0=ALU.mult, op1=ALU.subtract),
                     reads=[blo, bz], writes=[b_pl])
                S.op("dve", lambda e: e.scalar_tensor_tensor(out=pl[64:128, :], in0=hi[64:128, PAD:W], scalar=1.0 / wh, in1=zt[64:128, PAD:W], op0=ALU.mult, op1=ALU.subtract),
                     reads=[bhi, bz], writes=[b_pl])
                if tt == 0:
                    S.op("dve", lambda e: e.tensor_tensor(out=pl[0:64, 0:16], in0=lo[0:64, PAD:PAD + 16], in1=fix[0:64, f, :], op=ALU.mult), reads=[blo, b_fix], writes=[b_pl])
                    S.op("dve", lambda e: e.tensor_tensor(out=pl[64:128, 0:16], in0=hi[64:128, PAD:PAD + 16], in1=fix[64:128, f, :], op=ALU.mult), reads=[bhi, b_fix], writes=[b_pl])
                    S.op("dve", lambda e: e.tensor_tensor(out=pl[:, 0:16], in0=pl[:, 0:16], in1=zt[:, PAD:PAD + 16], op=ALU.subtract), reads=[b_pl, bz], writes=[b_pl])
                pi = k % 2; k += 1
                S.op("pe", lambda e: e.matmul(pp[pi][:, 0:CH], lhsT=wp[:, f, :], rhs=pl[:], start=True, stop=True), reads=[b_wp, b_pl], writes=[b_pp[pi]])
                S.op("act", lambda e: e.activation(out=ob[pi][:], in_=pp[pi][:, 0:CH], func=AF.Copy, scale=sm[:, f:f + 1]), reads=[b_pp[pi], b_sm], writes=[b_ob[pi]])
                S.dma("sp", D["ymixT"][256 + f * 128:256 + (f + 1) * 128, t0:t0 + CH], ob[pi][:], reads=[b_ob[pi]])
        S.barrier()


def phase_outproj(g, l, xin, xout):
    nc, S, D, T = g.nc, g.S, g.D, g.T
    with ExitStack() as st:
        sbf = lambda name, shape, dt=F32: st.enter_context(nc.sbuf_tensor(g.nm(name), list(shape), dt))
        psf = lambda name, shape, dt=F32: st.enter_context(nc.psum_tensor(g.nm(name), list(shape), dt))
        Wo = sbf("Wo", [128, 8, DM], BF16); b_W = Buf()
        stage = [sbf(f"wst{i}", [128, 1024]) for i in range(2)]; b_stage = [Buf() for _ in range(2)]
        load_w_bf16(g, Wo, b_W, D["w_out"][l], 8, DM, stage, b_stage)
        xt = [sbf(f"oxt{i}", [128, 8, 512]) for i in range(2)]; b_xt = [Buf() for _ in range(2)]
        ym = [sbf(f"oym{i}", [128, 8, 512], BF16) for i in range(2)]; b_ym = [Buf() for _ in range(2)]
        xo = [sbf(f"oxo{i}", [128, 8, 512]) for i in range(2)]; b_xo = [Buf() for _ in range(2)]
        pq = [psf(f"opq{i}", [128, 512]) for i in range(4)]; b_pq = [Buf(excl=True) for _ in range(4)]
        xv = xin.rearrange("(c p) t -> p c t", p=128)
        xov = xout.rearrange("(c p) t -> p c t", p=128)
        yv = D["ymixT"].rearrange("(c p) t -> p c t", p=128)
        k = 0
        for tt in range(T // 512):
            t0 = tt * 512
            xi = tt % 2
            S.dma("sp", xt[xi][:], xv[:, :, t0:t0 + 512], writes=[b_xt[xi]])
            S.dma("sp", ym[xi][:], yv[:, :, t0:t0 + 512], writes=[b_ym[xi]])
            for j in range(8):
                pi = k % 4; k += 1
                for c in range(8):
                    S.op("pe", lambda e: e.matmul(pq[pi][:], lhsT=Wo[:, c, j * 128:(j + 1) * 128], rhs=ym[xi][:, c, :], start=(c == 0), stop=(c == 7)),
                         reads=[b_W, b_ym[xi]], writes=[b_pq[pi]])
                S.op("dve", lambda e: e.tensor_tensor(out=xo[xi][:, j, :], in0=pq[pi][:], in1=xt[xi][:, j, :], op=ALU.add),
                     reads=[b_pq[pi], b_xt[xi]], writes=[b_xo[xi]])
            S.dma("sp", xov[:, :, t0:t0 + 512], xo[xi][:], reads=[b_xo[xi]])
        S.barrier()


def phase_ffn(g, l, xin, xout):
    nc, S, D, T = g.nc, g.S, g.D, g.T
    N = 512
    with ExitStack() as st:
        sbf = lambda name, shape, dt=F32: st.enter_context(nc.sbuf_tensor(g.nm(name), list(shape), dt))
        psf = lambda name, shape, dt=F32: st.enter_context(nc.psum_tensor(g.nm(name), list(shape), dt))
        Wu = sbf("Wu", [128, 8, 2 * DFF], BF16)
        Wd = sbf("Wd", [128, 22, DM], BF16)
        CHW = 512
        NCW = (2 * DFF) // CHW
        b_Wu = [Buf() for _ in range(NCW)]
        b_Wd = [Buf() for _ in range(22)]
        stage = [sbf(f"wst{i}", [128, CHW]) for i in range(2)]; b_stage = [Buf() for _ in range(2)]
        xt = sbf("fxt", [128, 8, N]); b_xt = Buf()
        xv = xin.rearrange("(c p) t -> p c t", p=128)
        xov = xout.rearrange("(c p) t -> p c t", p=128)
        S.dma("sp", xt[:], xv[:, :, 0:N], writes=[b_xt])
        wuv = D["ffn_w_up"][l].rearrange("(c p) n -> p c n", p=128)
        wdv = D["ffn_w_down"][l].rearrange("(c p) n -> p c n", p=128)
        wu_done = set()
        wd_done = set()

        def emit_wu(nchunk):
            if nchunk in wu_done:
                return
            wu_done.add(nchunk)
            for c in range(8):
                i = g.wctr % 2; g.wctr += 1
                S.dma("sp", stage[i][:, :], wuv[:, c, nchunk * CHW:(nchunk + 1) * CHW], writes=[b_stage[i]])
                cp(S, ("act", "dve", "pool")[g.wctr % 3], Wu[:, c, nchunk * CHW:(nchunk + 1) * CHW], stage[i][:, :], [b_stage[i]], [b_Wu[nchunk]])

        def emit_wd(c):
            if c in wd_done:
                return
            wd_done.add(c)
            for n0 in range(0, DM, CHW):
                n1 = min(DM, n0 + CHW)
                i = g.wctr % 2; g.wctr += 1
                S.dma("sp", stage[i][:, 0:n1 - n0], wdv[:, c, n0:n1], writes=[b_stage[i]])
                cp(S, ("act", "dve", "pool")[g.wctr % 3], Wd[:, c, n0:n1], stage[i][:, 0:n1 - n0], [b_stage[i]], [b_Wd[c]])
        sm = sbf("smf", [128, 8]); b_sm = Buf()
        S.dma("sp", sm[:], D["smalls"][l][:, 16:24], writes=[b_sm])
        cw = sbf("cw", [128, 44, 4]); b_cw = Buf()
        S.dma("sp", cw[:], D["convp"][l], writes=[b_cw])
        gated = sbf("gated", [128, 22, N], BF16); b_gt = Buf()
        R = {}
        R["sq"] = gated[:, 0:8, :]; R["b_sq"] = b_gt
        R["rstd"] = sbf("rstd", [128, N]); R["b_rstd"] = Buf()
        R["p_rms"] = psf("p_rms", [128, 512]); R["b_prms"] = Buf(excl=True)
        R["ones"] = sbf("ones", [128, 128], BF16); R["b_ones"] = Buf()
        R["eps"] = sbf("epsb", [128, 1]); R["b_eps"] = Buf()
        S.op("pool", lambda e: e.memset(R["ones"][:], 1.0), writes=[R["b_ones"]])
        S.op("pool", lambda e: e.memset(R["eps"][:], 1e-6), writes=[R["b_eps"]])
        hT = sbf("fhT", [128, 8, N], BF16); b_h = Buf()
        carry = sbf("carry", [128, 44, 2]); b_carry = Buf()
        S.op("pool", lambda e: e.memset(carry[:], 0.0), writes=[b_carry])
        NU = 3
        U = [sbf(f"U{i}", [128, N + 2]) for i in range(NU)]; b_U = [Buf() for _ in range(NU)]
        cg = [sbf(f"cg{i}", [128, N]) for i in range(2)]; b_cg = [Buf() for _ in range(2)]
        cv = [sbf(f"cv{i}", [128, N]) for i in range(2)]; b_cv = [Buf() for _ in range(2)]
        gi = [sbf(f"gi{i}", [128, N]) for i in range(2)]; b_gi = [Buf() for _ in range(2)]
        pu = [psf(f"fpu{i}", [128, 512]) for i in range(4)]; b_pu = [Buf(excl=True) for _ in range(4)]
        pd = [psf(f"fpd{i}", [128, 512]) for i in range(2)]; b_pd = [Buf(excl=True) for _ in range(2)]
        uc = 0
        pc_ = 0
        for tt in range(T // N):
            t0 = tt * N
            if tt > 0:
                S.dma("sp", xt[:], xv[:, :, t0:t0 + N], writes=[b_xt])
            rmsnorm_tile(g, xt, b_xt, hT, b_h, sm, b_sm, N, R)
            for i in range(22):
                k2 = i % 2
                for which, ch in ((0, i), (1, 22 + i)):
                    ui = uc % NU; uc += 1
                    pi = pc_ % 4; pc_ += 1
                    emit_wu((ch * 128) // CHW)
                    for c in range(8):
                        S.op("pe", lambda e: e.matmul(pu[pi][:, 0:N], lhsT=Wu[:, c, ch * 128:(ch + 1) * 128], rhs=hT[:, c, :], start=(c == 0), stop=(c == 7)),
                             reads=[b_Wu[(ch * 128) // CHW], b_h], writes=[b_pu[pi]])
                    S.op("act", lambda e: e.copy(out=U[ui][:, 2:N + 2], in_=pu[pi][:, 0:N]), reads=[b_pu[pi]], writes=[b_U[ui]])
                    S.op("act", lambda e: e.copy(out=U[ui][:, 0:2], in_=carry[:, ch, :]), reads=[b_carry], writes=[b_U[ui]])
                    dst, bd_ = (cg[k2], b_cg[k2]) if which == 0 else (cv[k2], b_cv[k2])
                    S.op("dve", lambda e: e.tensor_scalar(out=dst[:], in0=U[ui][:, 0:N], scalar1=cw[:, ch, 0:1], scalar2=cw[:, ch, 3:4], op0=ALU.mult, op1=ALU.add),
                         reads=[b_U[ui], b_cw], writes=[bd_])
                    S.op("dve", lambda e: e.scalar_tensor_tensor(out=dst[:], in0=U[ui][:, 1:N + 1], scalar=cw[:, ch, 1:2], in1=dst[:], op0=ALU.mult, op1=ALU.add),
                         reads=[b_U[ui], b_cw, bd_], writes=[bd_])
                    S.op("dve", lambda e: e.scalar_tensor_tensor(out=dst[:], in0=U[ui][:, 2:N + 2], scalar=cw[:, ch, 2:3], in1=dst[:], op0=ALU.mult, op1=ALU.add),
                         reads=[b_U[ui], b_cw, bd_], writes=[bd_])
                    S.op("act", lambda e: e.copy(out=carry[:, ch, :], in_=U[ui][:, N:N + 2]), reads=[b_U[ui]], writes=[b_carry])
                S.op("pool", lambda e: e.tensor_tensor(out=gi[k2][:], in0=cg[k2][:], in1=cg[k2][:], op=ALU.mult), reads=[b_cg[k2]], writes=[b_gi[k2]])
                S.op("pool", lambda e: e.tensor_scalar(out=gi[k2][:], in0=gi[k2][:], scalar1=0.044715, scalar2=1.0, op0=ALU.mult, op1=ALU.add), reads=[b_gi[k2]], writes=[b_gi[k2]])
                S.op("pool", lambda e: e.tensor_tensor(out=gi[k2][:], in0=gi[k2][:], in1=cg[k2][:], op=ALU.mult), reads=[b_gi[k2], b_cg[k2]], writes=[b_gi[k2]])
                S.op("act", lambda e: e.activation(out=gi[k2][:], in_=gi[k2][:], func=AF.Sigmoid, scale=1.5957691216057308), reads=[b_gi[k2]], writes=[b_gi[k2]])
                S.op("pool", lambda e: e.tensor_tensor(out=gi[k2][:], in0=gi[k2][:], in1=cg[k2][:], op=ALU.mult), reads=[b_gi[k2], b_cg[k2]], writes=[b_gi[k2]])
                S.op("dve", lambda e: e.tensor_tensor(out=gated[:, i, :], in0=gi[k2][:], in1=cv[k2][:], op=ALU.mult), reads=[b_gi[k2], b_cv[k2]], writes=[b_gt])
                emit_wd(i)
            for j in range(8):
                pi = j % 2
                for i in range(22):
                    S.op("pe", lambda e: e.matmul(pd[pi][:, 0:N], lhsT=Wd[:, i, j * 128:(j + 1) * 128], rhs=gated[:, i, :], start=(i == 0), stop=(i == 21)),
                         reads=[b_Wd[i], b_gt], writes=[b_pd[pi]])
                S.op("dve", lambda e: e.tensor_tensor(out=xt[:, j, :], in0=pd[pi][:, 0:N], in1=xt[:, j, :], op=ALU.add),
                     reads=[b_pd[pi], b_xt], writes=[b_xt])
            S.dma("sp", xov[:, :, t0:t0 + N], xt[:], reads=[b_xt])
        S.barrier()


def phase_ple(g, l, xin, xout, final):
    nc, S, D, T = g.nc, g.S, g.D, g.T
    N = 512
    with ExitStack() as st:
        sbf = lambda name, shape, dt=F32: st.enter_context(nc.sbuf_tensor(g.nm(name), list(shape), dt))
        psf = lambda name, shape, dt=F32: st.enter_context(nc.psum_tensor(g.nm(name), list(shape), dt))
        Wg = sbf("Wg", [128, 8, DM], BF16); b_Wg = Buf()
        Wp = sbf("Wp", [128, 2, DM], BF16); b_Wp = Buf()
        stage = [sbf(f"wst{i}", [128, 1024]) for i in range(2)]; b_stage = [Buf() for _ in range(2)]
        load_w_bf16(g, Wg, b_Wg, D["ple_w_gate"][l], 8, DM, stage, b_stage)
        load_w_bf16(g, Wp, b_Wp, D["ple_w_proj"][l], 2, DM, stage, b_stage)
        sm = sbf("smq", [128, 16]); b_sm = Buf()
        S.dma("sp", sm[:, 0:8], D["smalls"][l][:, 24:32], writes=[b_sm])
        S.dma("sp", sm[:, 8:16], D["smalls"][l][:, 32:40], writes=[b_sm])
        R = rms_shared(g, sbf, psf, N)
        xt = [sbf(f"pxt{i}", [128, 8, N]) for i in range(2)]; b_xt = [Buf() for _ in range(2)]
        pt = [sbf(f"ppt{i}", [128, 2, N]) for i in range(2)]; b_pt = [Buf() for _ in range(2)]
        ptbs = [sbf(f"pptb{i}", [128, 2, N], BF16) for i in range(2)]; b_ptbs = [Buf() for _ in range(2)]
        hTs = [sbf(f"phT{i}", [128, 8, N], BF16) for i in range(2)]; b_hs = [Buf() for _ in range(2)]
        xos = [sbf(f"pxo{i}", [128, 8, N]) for i in range(2)]; b_xos = [Buf() for _ in range(2)]
        xf = sbf("pxf", [128, 8, N]); b_xf = Buf()
        gs = [sbf(f"pgs{i}", [128, N]) for i in range(2)]; b_gs = [Buf() for _ in range(2)]
        pg = [psf(f"ppg{i}", [128, 512]) for i in range(2)]; b_pg = [Buf(excl=True) for _ in range(2)]
        pq = [psf(f"ppq{i}", [128, 512]) for i in range(2)]; b_pq = [Buf(excl=True) for _ in range(2)]
        xv = xin.rearrange("(c p) t -> p c t", p=128)
        xov = xout.rearrange("(c p) t -> p c t", p=128)
        pv = D["pT"][l].rearrange("(c p) t -> p c t", p=128)
        NTL = T // N

        def prep(tt):
            xi = tt % 2
            t0 = tt * N
            S.dma("sp", xt[xi][:], xv[:, :, t0:t0 + N], writes=[b_xt[xi]])
            S.dma("sp", pt[xi][:], pv[:, :, t0:t0 + N], writes=[b_pt[xi]])
            S.op("pool", lambda e: e.tensor_copy(out=ptbs[xi][:], in_=pt[xi][:]), reads=[b_pt[xi]], writes=[b_ptbs[xi]])
            rmsnorm_tile(g, xt[xi], b_xt[xi], hTs[xi], b_hs[xi], sm, b_sm, N, R)

        prep(0)
        for tt in range(NTL):
            t0 = tt * N
            xi = tt % 2
            if tt + 1 < NTL:
                prep(tt + 1)
            hT, b_h = hTs[xi], b_hs[xi]
            ptb, b_ptb = ptbs[xi], b_ptbs[xi]
            xo, b_xo = xos[xi], b_xos[xi]
            for j in range(8):
                pi = j % 2
                for c in range(8):
                    S.op("pe", lambda e: e.matmul(pg[pi][:], lhsT=Wg[:, c, j * 128:(j + 1) * 128], rhs=hT[:, c, :], start=(c == 0), stop=(c == 7)),
                         reads=[b_Wg, b_h], writes=[b_pg[pi]])
                for c in range(2):
                    S.op("pe", lambda e: e.matmul(pq[pi][:], lhsT=Wp[:, c, j * 128:(j + 1) * 128], rhs=ptb[:, c, :], start=(c == 0), stop=(c == 1)),
                         reads=[b_Wp, b_ptb], writes=[b_pq[pi]])
                S.op("act", lambda e: e.activation(out=gs[pi][:], in_=pg[pi][:], func=AF.Sigmoid), reads=[b_pg[pi]], writes=[b_gs[pi]])
                S.op("dve", lambda e: e.tensor_tensor(out=gs[pi][:], in0=pq[pi][:], in1=gs[pi][:], op=ALU.mult), reads=[b_pq[pi], b_gs[pi]], writes=[b_gs[pi]])
                S.op("pool", lambda e: e.tensor_tensor(out=xo[:, j, :], in0=gs[pi][:], in1=xt[xi][:, j, :], op=ALU.add), reads=[b_gs[pi], b_xt[xi]], writes=[b_xo])
            if not final:
                S.dma("sp", xov[:, :, t0:t0 + N], xo[:], reads=[b_xo])
            else:
                S.op("act", lambda e: e.activation(out=R["sq"][:], in_=xo[:], func=AF.Square), reads=[b_xo], writes=[R["b_sq"]])
                for c in range(8):
                    S.op("pe", lambda e: e.matmul(R["p_rms"][:], lhsT=R["ones"][:], rhs=R["sq"][:, c, :], start=(c == 0), stop=(c == 7)),
                         reads=[R["b_ones"], R["b_sq"]], writes=[R["b_prms"]])
                S.op("act", lambda e: e.activation(out=R["rstd"][:], in_=R["p_rms"][:], func=AF.Sqrt, bias=R["eps"][:, 0:1], scale=1.0 / DM),
                     reads=[R["b_prms"], R["b_eps"]], writes=[R["b_rstd"]])
                S.op("dve", lambda e: e.reciprocal(out=R["rstd"][:], in_=R["rstd"][:]), reads=[R["b_rstd"]], writes=[R["b_rstd"]])
                for c in range(8):
                    S.op("dve", lambda e: e.scalar_tensor_tensor(out=xf[:, c, :], in0=xo[:, c, :], scalar=sm[:, 8 + c:9 + c], in1=R["rstd"][:],
                                                                   op0=ALU.mult, op1=ALU.mult),
                         reads=[b_xo, b_sm, R["b_rstd"]], writes=[b_xf])
                S.dma("sp", xov[:, :, t0:t0 + N], xf[:], reads=[b_xf])
        S.barrier()


def nsa_consts(T):
    import ml_dtypes
    bf = ml_dtypes.bfloat16
    f = np.float32
    c = {}
    nl = np.arange(128)[:, None]
    ql = np.arange(128)[None, :]
    masks = np.zeros((19, 128, 128), f)
    for i in range(17):
        masks[i] = np.where(16 * nl + 31 - ql <= 128 * i, 0.0, -BIG)
    masks[17] = np.where(nl <= ql, 0.0, -BIG)
    masks[18] = np.where(nl > ql, 0.0, -BIG)
    m4 = np.tile(masks, (1, 1, 4))
    c["nsa_masks"] = np.ascontiguousarray(np.transpose(m4, (1, 0, 2))).astype(bf)
    c["identb"] = np.eye(128, dtype=f).astype(bf)
    c["identf"] = np.eye(128, dtype=f)
    key = np.arange(T)[None, :]
    c["E_all"] = (key // 64 == np.arange(128)[:, None]).astype(f).astype(bf)
    n_cmp = (T - 32) // 16 + 1
    ntc = (n_cmp + 127) // 128
    cs = 16 * np.arange(n_cmp)
    ce = cs + 31
    ss = 64 * np.arange(128)
    ov = np.minimum(ce[:, None], ss[None] + 63) - np.maximum(cs[:, None], ss[None]) + 1
    mcs = np.zeros((ntc * 128, 128), f)
    mcs[:n_cmp] = np.clip(ov, 0, 32).astype(f) / 32
    c["mcs"] = np.ascontiguousarray(mcs.reshape(ntc, 128, 128).transpose(1, 0, 2)).astype(bf)
    keep = np.zeros((128, 256), f)
    add = np.zeros((128, 256), f)
    for q in range(128):
        jc = 126 if q < 64 else 127
        cc = np.arange(256)
        keep[q] = (cc < jc - 1)
        add[q] = np.where(cc == jc - 1, 1.1e9, np.where(cc == jc, 1.2e9, np.where(cc > jc, -1e30, 0.0)))
    c["keepw"] = keep
    c["addw"] = add
    return c


def phase_nsa(g, l):
    nc, S, D, T = g.nc, g.S, g.D, g.T
    NQB = T // 128
    NCMP = (T - 32) // 16 + 1
    NTC = (NCMP + 127) // 128
    SK = dict(skip_group_check=True)
    with ExitStack() as st:
        sbf = lambda name, shape, dt=F32: st.enter_context(nc.sbuf_tensor(g.nm(name), list(shape), dt))
        psf = lambda name, shape, dt=F32: st.enter_context(nc.psum_tensor(g.nm(name), list(shape), dt))
        stp = [psf(f"nst{i}", [128, 512]) for i in range(3)]; b_stp = [Buf(excl=True) for _ in range(3)]
        acc = [psf(f"nacc{i}", [128, 512]) for i in range(3)]; b_acc = [Buf(excl=True) for _ in range(3)]
        imp = psf("nimp", [128, 512]); b_imp = Buf(excl=True)
        msc = psf("nmsc", [128, 512]); b_msc = Buf(excl=True)
        mscb = msc[:, 384:448].bitcast(BF16); b_mscb = b_msc
        KcT = sbf("KcT", [64, 2, NTC * 128], BF16); b_Kc = Buf()
        Vc = sbf("Vc", [128, NTC, 2, 128], BF16); b_Vc = Buf()
        S.op("pool", lambda e: e.memset(KcT[:], 0.0), writes=[b_Kc])
        S.op("pool", lambda e: e.memset(Vc[:], 0.0), writes=[b_Vc])
        with ExitStack() as st2:
            sb2 = lambda name, shape, dt=F32: st2.enter_context(nc.sbuf_tensor(g.nm(name), list(shape), dt))
            kc = sb2("kc", [64, 2, T], BF16); b_kc = Buf()
            vc = sb2("vc", [64, 2, T], BF16); b_vc = Buf()
            S.dma("sp", kc[:], D["kT"][0:128, :].rearrange("(h d) t -> d h t", d=64), writes=[b_kc])
            S.dma("sp", vc[:], D["vcT"].rearrange("(h d) t -> d h t", d=64), writes=[b_vc])
            wst = sb2("wckst", [64, 32, 64]); b_wst = Buf()
            wck = sb2("wck", [64, 32, 64], BF16); b_wck = Buf()
            wcv = sb2("wcv", [64, 32, 64], BF16); b_wcv = Buf()
            S.dma("sp", wst[:], D["nsa_w_ck"][l].rearrange("l d e -> d l e"), writes=[b_wst])
            cp(S, "dve", wck[:], wst[:], [b_wst], [b_wck])
            S.dma("sp", wst[:], D["nsa_w_cv"][l].rearrange("l d e -> d l e"), writes=[b_wst])
            cp(S, "dve", wcv[:], wst[:], [b_wst], [b_wcv])
            pest = sb2("pest", [64, 2, 32]); b_pest = Buf()
            peb = sb2("peb", [64, 2, 32], BF16); b_peb = Buf()
            S.dma("sp", pest[:], D["nsa_peT"][l].rearrange("w d l -> d w l"), writes=[b_pest])
            cp(S, "dve", peb[:], pest[:], [b_pest], [b_peb])
            biask = sb2("biask", [64, 1]); b_bk = Buf()
            biasv = sb2("biasv", [1, 64], BF16); b_bv = Buf()
            onesr = sb2("onesr", [1, 128], BF16); b_or = Buf()
            S.op("pool", lambda e: e.memset(onesr[:], 1.0), writes=[b_or])
            for i_ in range(32):
                S.op("pe", lambda e: e.matmul(msc[0:64, 0:1], lhsT=wck[:, i_, :], rhs=peb[:, 0, i_:i_ + 1], start=(i_ == 0), stop=(i_ == 31)),
                     reads=[b_wck, b_peb], writes=[b_msc])
            cp(S, "dve", biask[:], msc[0:64, 0:1], [b_msc], [b_bk])
            for i_ in range(32):
                S.op("pe", lambda e: e.matmul(msc[0:1, 0:64], lhsT=peb[:, 1, i_:i_ + 1], rhs=wcv[:, i_, :], start=(i_ == 0), stop=(i_ == 31)),
                     reads=[b_wcv, b_peb], writes=[b_msc])
            cp(S, "dve", biasv[:], msc[0:1, 0:64], [b_msc], [b_bv])
            span = 16 * (NCMP - 1) + 1
            for h in range(2):
                pk = stp[h]
                for i_ in range(32):
                    S.op("pe", lambda e: e.matmul(pk[0:64, 0:NCMP], lhsT=wck[:, i_, :], rhs=kc[:, h, i_:i_ + span:16], start=(i_ == 0), stop=(i_ == 31)),
                         reads=[b_wck, b_kc], writes=[b_stp[h]])
                S.op("act", lambda e: e.activation(out=KcT[:, h, 0:NCMP], in_=pk[0:64, 0:NCMP], func=AF.Identity, bias=biask[:, 0:1], scale=1.0),
                     reads=[b_stp[h], b_bk], writes=[b_Kc])
            k_ = 0
            for h in range(2):
                for nt in range(NTC):
                    nn = min(128, NCMP - nt * 128)
                    pv_ = acc[k_ % 3]; bpv = b_acc[k_ % 3]; k_ += 1
                    base = 16 * 128 * nt
                    sp_ = 16 * (nn - 1) + 1
                    for i_ in range(32):
                        S.op("pe", lambda e: e.matmul(pv_[0:nn, 0:64], lhsT=vc[:, h, base + i_:base + i_ + sp_:16], rhs=wcv[:, i_, :], start=(i_ == 0), stop=False),
                             reads=[b_wcv, b_vc], writes=[bpv])
                    S.op("pe", lambda e: e.matmul(pv_[0:nn, 0:64], lhsT=onesr[0:1, 0:nn], rhs=biasv[0:1, :], start=False, stop=True),
                         reads=[b_or, b_bv], writes=[bpv])
                    cp(S, "dve", Vc[0:nn, nt, h, 0:64], pv_[0:nn, 0:64], [bpv], [b_Vc])
                    S.op("dve", lambda e: e.memset(Vc[0:nn, nt, h, 64:65], 1.0), writes=[b_Vc])
            S.barrier()
        masks = sbf("masks", [128, 19, 512], BF16); b_masks = Buf()
        S.dma("sp", masks[:], D["nsa_masks"], writes=[b_masks])
        identb = sbf("identb", [128, 128], BF16); b_idb = Buf()
        S.dma("sp", identb[:], D["identb"], writes=[b_idb])
        identf = sbf("identf", [128, 128]); b_idf = Buf()
        S.dma("sp", identf[:], D["identf"], writes=[b_idf])
        MCS = sbf("MCS", [128, NTC, 128], BF16); b_mcs = Buf()
        S.dma("sp", MCS[:], D["mcs"], writes=[b_mcs])
        keepw = sbf("keepw", [128, 256]); b_kw_ = Buf()
        S.dma("sp", keepw[:], D["keepw"], writes=[b_kw_])
        addw = sbf("addw", [128, 256]); b_aw = Buf()
        S.dma("sp", addw[:], D["addw"], writes=[b_aw])
        LH = sbf("LH", [128, 2, T], BF16); b_LH = Buf()
        KwT = sbf("KwT", [128, 2, T], BF16); b_Kw = Buf()
        S.op("pool", lambda e: e.memset(KwT[0:64, :, :], 0.0), writes=[b_Kw])
        TH = min(T, 4096)
        ksv = D["kT"][128:256, :].rearrange("(h d) t -> d h t", d=64)
        S.dma("sp", LH[64:128, :, 0:TH], ksv[:, :, 0:TH], writes=[b_LH])
        for h_ in range(2):
            S.dma("sp", LH[0:64, h_, 0:TH], D["E_all"][0:64, 0:TH], writes=[b_LH])
        if T > TH:
            S.dma("sp", LH[0:64, :, TH:T], ksv[:, :, TH:T], writes=[b_LH])
            for h_ in range(2):
                S.dma("sp", LH[64:128, h_, TH:T], D["E_all"][64:128, TH:T], writes=[b_LH])
        S.dma("sp", KwT[64:128, :, :], D["kT"][256:384, :].rearrange("(h d) t -> d h t", d=64), writes=[b_Kw])
        VW = 128
        Vs = sbf("Vs", [128, NQB, 2, VW], BF16); b_Vs = Buf()
        Vw = sbf("Vw", [128, NQB, 2, VW], BF16); b_Vw = Buf()
        S.op("pool", lambda e: e.memset(Vs[:], 0.0), writes=[b_Vs])
        S.op("pool", lambda e: e.memset(Vw[:], 0.0), writes=[b_Vw])
        S.op("pool", lambda e: e.memset(Vs[:, :, :, 64:65], 1.0), writes=[b_Vs])
        S.op("pool", lambda e: e.memset(Vw[:, :, :, 64:65], 1.0), writes=[b_Vw])
        vtv = D["vtok"].rearrange("(kt p) (w h d) -> p kt w h d", p=128, w=2, h=2)
        for h in range(2):
            S.dma("sp", Vs[:, :, h, 0:64], vtv[:, :, 0, h, :], writes=[b_Vs])
            S.dma("sp", Vw[:, :, h, 0:64], vtv[:, :, 1, h, :], writes=[b_Vw])
        Qg = [sbf(f"Qg{i}", [64, 4, 128], BF16) for i in range(2)]; b_Qg = [Buf() for _ in range(2)]
        gt = [sbf(f"ngt{i}", [128, 24]) for i in range(2)]; b_gt = [Buf() for _ in range(2)]
        Pc = [sbf(f"Pc{i}", [128, 512], BF16) for i in range(max(NTC, 1))]; b_Pc = [Buf() for _ in range(max(NTC, 1))]
        NP = 3
        Pb = [sbf(f"Pb{i}", [128, 512], BF16) for i in range(NP)]; b_Pb = [Buf() for _ in range(NP)]
        zz = sbf("zz", [128, 3, 4]); b_zz = Buf()
        coef = sbf("coef", [128, 3, 4]); b_coef = Buf()
        impS = sbf("impS", [128, 128]); b_impS = Buf()
        imp2 = sbf("imp2", [128, 128]); b_imp2 = Buf()
        mx = sbf("mx", [128, 16]); b_mx = Buf()
        thr = sbf("thr", [128, 1]); b_thr = Buf()
        MBf = sbf("MBf", [128, 128]); b_MBf = Buf()
        MBb = sbf("MBb", [128, 128], BF16); b_MBb = Buf()
        MBT4 = sbf("MBT4", [128, 4, 128], BF16); b_MBT4 = Buf()
        yc = [sbf(f"yc{i}", [128, 512]) for i in range(2)]; b_yc = [Buf() for _ in range(2)]
        ycT = [sbf(f"ycT{i}", [128, 4, 128], BF16) for i in range(2)]; b_ycT = [Buf() for _ in range(2)]
        stc = 0
        pbc = 0
        qc = 0
        qv = D["qT"].rearrange("(hq d) t -> d hq t", d=64)
        ymv = D["ymixT"][512:1024, :].rearrange("(c p) t -> p c t", p=128)
        aS = sbf("aS", [65, 512]); b_aS = Buf()
        R0 = [sbf(f"R0_{i}", [128, 512], BF16) for i in range(2)]; b_R0 = [Buf() for _ in range(2)]
        R1 = [sbf(f"R1_{i}", [128, 512], BF16) for i in range(2)]; b_R1 = [Buf() for _ in range(2)]

        def score_tile(KT, bK, h, kt, Q, bQ, extra):
            nonlocal stc
            si = stc % 3; stc += 1
            n_mm = 1 + len(extra)
            rhs_ap = Q[:].rearrange("d g q -> d (g q)") if len(Q.shape) == 3 else Q[:]
            S.op("pe", lambda e: e.matmul(stp[si][:], lhsT=KT[:, h, kt * 128:(kt + 1) * 128], rhs=rhs_ap, start=True, stop=(n_mm == 1)),
                 reads=[bK, bQ], writes=[b_stp[si]])
            for i_, (la, ra, bufs) in enumerate(extra):
                S.op("pe", lambda e: e.matmul(stp[si][:], lhsT=la, rhs=ra, start=False, stop=(i_ == len(extra) - 1)),
                     reads=bufs, writes=[b_stp[si]])
            return si

        def run_branch(br, tiles, h, Q, bQ, V, bV, Pbufs=None, after_exp=None):
            nonlocal pbc
            n = len(tiles)
            tiles = [tl if len(tl) == 6 else tl + (Q, bQ) for tl in tiles]
            DEPTH = 2
            issued = []
            nxt = 0
            for i_ in range(n):
                while nxt < n and nxt <= i_ + DEPTH - 1 + (0 if i_ else 0):
                    KT2, bK2, kt2, extra2, Qx2, bQx2 = tiles[nxt]
                    issued.append(score_tile(KT2, bK2, h, kt2, Qx2, bQx2, extra2))
                    nxt += 1
                si = issued[i_]
                kt = tiles[i_][2]
                if Pbufs is None:
                    pi = pbc % NP; pbc += 1
                    P, bP = Pb[pi], b_Pb[pi]
                else:
                    P, bP = Pbufs[i_]
                S.op("act", lambda e: e.activation(out=P[:], in_=stp[si][:], func=AF.Exp, scale=0.125), reads=[b_stp[si]], writes=[bP])
                if nxt < n:
                    KT2, bK2, kt2, extra2, Qx2, bQx2 = tiles[nxt]
                    issued.append(score_tile(KT2, bK2, h, kt2, Qx2, bQx2, extra2))
                    nxt += 1
                S.op("pe", lambda e: e.matmul(acc[br][:, :], lhsT=V[:, kt, h, :], rhs=P[:], start=(i_ == 0), stop=(i_ == n - 1)),
                     reads=[bP, bV], writes=[b_acc[br]])
                if after_exp is not None:
                    after_exp(i_, P, bP)

        def combine(br, h, yi):
            cp(S, "act", aS[:], acc[br][0:65, :], [b_acc[br]], [b_aS])
            for gq in range(4):
                S.op("pe", lambda e: e.transpose(out=msc[:, gq * 65:(gq + 1) * 65], in_=aS[0:65, gq * 128:(gq + 1) * 128], identity=identf[0:65, 0:65]),
                     reads=[b_aS, b_idf], writes=[b_msc])
            a3 = msc[:, 0:260].rearrange("p (g c) -> p g c", c=65)
            g3 = gt[yi][:].rearrange("p (hg b) -> p hg b", b=3)
            S.op("dve", lambda e: e.tensor_scalar(out=zz[:, br, :], in0=a3[:, :, 64], scalar1=1e-30, scalar2=None, op0=ALU.max), reads=[b_msc], writes=[b_zz])
            S.op("dve", lambda e: e.reciprocal(out=zz[:, br, :], in_=zz[:, br, :]), reads=[b_zz], writes=[b_zz])
            S.op("dve", lambda e: e.tensor_tensor(out=coef[:, br, :], in0=zz[:, br, :], in1=g3[:, h * 4:(h + 1) * 4, br], op=ALU.mult), reads=[b_zz, b_gt[yi]], writes=[b_coef])
            for gq in range(4):
                o_ = yc[yi][:, (h * 4 + gq) * 64:(h * 4 + gq + 1) * 64]
                if br == 0:
                    S.op("dve", lambda e: e.tensor_scalar(out=o_, in0=a3[:, gq, 0:64], scalar1=coef[:, br, gq:gq + 1], scalar2=None, op0=ALU.mult),
                         reads=[b_msc, b_coef], writes=[b_yc[yi]])
                else:
                    S.op("dve", lambda e: e.scalar_tensor_tensor(out=o_, in0=a3[:, gq, 0:64], scalar=coef[:, br, gq:gq + 1], in1=o_, op0=ALU.mult, op1=ALU.add),
                         reads=[b_msc, b_coef, b_yc[yi]], writes=[b_yc[yi]])

        items = [(qb, h) for qb in range(NQB) for h in range(2)]
        qis = {}

        def stage1(qb, h):
            nonlocal qc
            q0 = qb * 128
            yi = qb % 2
            if h == 0:
                S.dma("sp", gt[yi][:], D["gates"][q0:q0 + 128, :], writes=[b_gt[yi]])
            qi = qc % 2; qc += 1
            qis[(qb, h)] = qi
            Q, bQ = Qg[qi], b_Qg[qi]
            S.dma("sp", Q[:], qv[:, h * 4:(h + 1) * 4, q0:q0 + 128], writes=[bQ])
            S.dma("sp", R0[qi][64:128, :].rearrange("d (g q) -> d g q", g=4), qv[:, h * 4:(h + 1) * 4, q0:q0 + 128], writes=[b_R0[qi]])
            if qb >= 32:
                S.dma("sp", R1[qi][0:64, :].rearrange("d (g q) -> d g q", g=4), qv[:, h * 4:(h + 1) * 4, q0:q0 + 128], writes=[b_R1[qi]])
            ntc = min(NTC, (8 * qb + 6) // 128 + 1)
            S.op("dve", lambda e: e.memset(imp[:], 0.0), writes=[b_imp])
            tiles = []
            for nt in range(ntc):
                delta = 128 * qb - 2048 * nt
                extra = []
                if delta < 2064:
                    extra.append((identb[:], masks[:, delta // 128, :], [b_idb, b_masks]))
                tiles.append((KcT, b_Kc, nt, extra))

            def imp_mm(i_, P, bP):
                for gq in range(4):
                    S.op("pe", lambda e: e.matmul(imp[:, gq * 128:(gq + 1) * 128], lhsT=P[:, gq * 128:(gq + 1) * 128], rhs=MCS[:, i_, :], start=False, stop=(i_ == ntc - 1), **SK),
                         reads=[bP, b_mcs], writes=[b_imp])
            run_branch(0, tiles, h, Q, bQ, Vc, b_Vc, Pbufs=[(Pc[i_], b_Pc[i_]) for i_ in range(ntc)], after_exp=imp_mm)
            combine(0, h, yi)
            S.op("dve", lambda e: e.tensor_scalar(out=impS[:], in0=imp[:, 0:128], scalar1=zz[:, 0, 0:1], scalar2=None, op0=ALU.mult), reads=[b_imp, b_zz], writes=[b_impS])
            for gq in range(1, 4):
                S.op("dve", lambda e: e.scalar_tensor_tensor(out=impS[:], in0=imp[:, gq * 128:(gq + 1) * 128], scalar=zz[:, 0, gq:gq + 1], in1=impS[:], op0=ALU.mult, op1=ALU.add),
                     reads=[b_imp, b_zz, b_impS], writes=[b_impS])
            c0 = 126 - 2 * qb
            S.op("dve", lambda e: e.tensor_tensor(out=impS[:], in0=impS[:], in1=keepw[:, c0:c0 + 128], op=ALU.mult), reads=[b_impS, b_kw_], writes=[b_impS])
            S.op("dve", lambda e: e.tensor_tensor(out=impS[:], in0=impS[:], in1=addw[:, c0:c0 + 128], op=ALU.add), reads=[b_impS, b_aw], writes=[b_impS])
            S.op("dve", lambda e: e.memset(impS[:, 0:1], 1.0e9), writes=[b_impS])
            S.op("dve", lambda e: e.max(out=mx[:, 0:8], in_=impS[:]), reads=[b_impS], writes=[b_mx])
            S.op("dve", lambda e: e.match_replace(out=imp2[:], in_to_replace=mx[:, 0:8], in_values=impS[:], imm_value=-3.0e38), reads=[b_mx, b_impS], writes=[b_imp2])
            S.op("dve", lambda e: e.max(out=mx[:, 8:16], in_=imp2[:]), reads=[b_imp2], writes=[b_mx])
            S.op("dve", lambda e: e.tensor_reduce(out=thr[:], in_=mx[:, 8:16], axis=AX.X, op=ALU.min), reads=[b_mx], writes=[b_thr])
            S.op("dve", lambda e: e.tensor_scalar(out=MBf[:], in0=impS[:], scalar1=thr[:, 0:1], scalar2=None, op0=ALU.is_ge), reads=[b_impS, b_thr], writes=[b_MBf])
            S.op("dve", lambda e: e.tensor_scalar(out=MBb[:], in0=MBf[:], scalar1=1.0, scalar2=BIG, op0=ALU.subtract, op1=ALU.mult), reads=[b_MBf], writes=[b_MBb])
            S.op("pe", lambda e: e.transpose(out=mscb[:], in_=MBb[:], identity=identb[:]), reads=[b_MBb, b_idb], writes=[b_mscb])
            for gq in range(4):
                cp(S, "act" if gq % 2 else "dve", R0[qi][0:64, gq * 128:(gq + 1) * 128], mscb[0:64, :], [b_mscb], [b_R0[qi]])
                if qb >= 32:
                    cp(S, "dve" if gq % 2 else "act", R1[qi][64:128, gq * 128:(gq + 1) * 128], mscb[64:128, :], [b_mscb], [b_R1[qi]])

        def stage2(qb, h):
            q0 = qb * 128
            yi = qb % 2
            qi = qis[(qb, h)]
            Q, bQ = Qg[qi], b_Qg[qi]
            tiles = []
            for kt in range(max(0, qb - 4), qb + 1):
                extra = []
                if kt == qb:
                    extra.append((identb[:], masks[:, 17, :], [b_idb, b_masks]))
                elif kt == qb - 4:
                    extra.append((identb[:], masks[:, 18, :], [b_idb, b_masks]))
                tiles.append((KwT, b_Kw, kt, extra, R0[qi], b_R0[qi]))
            run_branch(2, tiles, h, Q, bQ, Vw, b_Vw)
            combine(2, h, yi)
            tiles = []
            for kt in range(qb + 1):
                extra = []
                if kt == qb:
                    extra.append((identb[:], masks[:, 17, :], [b_idb, b_masks]))
                if kt < 32:
                    tiles.append((LH, b_LH, kt, extra, R0[qi], b_R0[qi]))
                else:
                    tiles.append((LH, b_LH, kt, extra, R1[qi], b_R1[qi]))
            run_branch(1, tiles, h, Q, bQ, Vs, b_Vs)
            combine(1, h, yi)
            if h == 1:
                for c in range(4):
                    S.op("pe", lambda e: e.transpose(out=msc[:, c * 128:(c + 1) * 128], in_=yc[yi][:, c * 128:(c + 1) * 128], identity=identf[:]),
                         reads=[b_yc[yi], b_idf], writes=[b_msc])
                cp(S, "act", ycT[yi][:].rearrange("p c q -> p (c q)"), msc[:], [b_msc], [b_ycT[yi]])
                S.dma("sp", ymv[:, :, q0:q0 + 128], ycT[yi][:], reads=[b_ycT[yi]])

        stage1(*items[0])
        for n_ in range(len(items)):
            if n_ + 1 < len(items):
                stage1(*items[n_ + 1])
            stage2(*items[n_])
        S.barrier()


def rwkv_consts():
    f = np.float32
    c = {}
    hs = np.arange(128) // 64
    tt = np.arange(128) % 64
    same = hs[:, None] == hs[None, :]
    c["rw_msu"] = (same & (tt[:, None] < tt[None, :])).astype(f)
    c["rw_mu"] = (same & (tt[:, None] <= tt[None, :])).astype(f)
    c["rw_msl"] = (same & (tt[:, None] > tt[None, :])).astype(f)
    il = np.zeros((64, 128), f); il[np.arange(64), np.arange(64)] = 1
    ir = np.zeros((64, 128), f); ir[np.arange(64), 64 + np.arange(64)] = 1
    c["rw_il"] = il
    c["rw_ir"] = ir
    return c


def phase_rwkv(g, l):
    nc, S, D, T = g.nc, g.S, g.D, g.T
    TB = 256
    NCH = TB // 64
    SK = dict(skip_group_check=True)
    with ExitStack() as st:
        sbf = lambda name, shape, dt=F32: st.enter_context(nc.sbuf_tensor(g.nm(name), list(shape), dt))
        psf = lambda name, shape, dt=F32: st.enter_context(nc.psum_tensor(g.nm(name), list(shape), dt))
        NB = 8
        bank = [psf(f"rb{i}", [128, 512]) for i in range(NB)]; b_bank = [Buf(excl=True) for _ in range(NB)]
        bctr = [0]

        busy = [False] * NB
        F32R = mybir.dt.float32r
        MT = F32R if g.use_f32r else F32
        RR = lambda ap: ap
        AS32 = (lambda ap: ap.bitcast(F32)) if g.use_f32r else (lambda ap: ap)

        def nb():
            for k_ in range(NB):
                i = (bctr[0] + k_) % NB
                if not busy[i]:
                    bctr[0] = i + 1
                    busy[i] = True
                    return bank[i], b_bank[i]
            raise AssertionError("rwkv: no free PSUM bank (too many live tiles across a yield)")

        def rel(bbuf):
            busy[b_bank.index(bbuf)] = False

        def nbx():
            i = bctr[0] % NB
            bctr[0] += 1
            return bank[i], b_bank[i]
        def const(name, shape, src):
            t_ = sbf(name, shape); b_ = Buf()
            S.dma("sp", t_[:], src, writes=[b_])
            return t_, b_
        msu, b_msu = const("msu", [128, 128], D["rw_msu"])
        mu_, b_mu = const("mu", [128, 128], D["rw_mu"])
        msl, b_msl = const("msl", [128, 128], D["rw_msl"])
        il32, b_il32 = const("il32", [64, 128], D["rw_il"])
        ir32, b_ir32 = const("ir32", [64, 128], D["rw_ir"])
        idf, b_idf = const("idf", [128, 128], D["identf"])
        il = sbf("il", [64, 128], MT); b_il = Buf()
        ir = sbf("ir", [64, 128], MT); b_ir = Buf()
        cp(S, "dve", il[:], il32[:], [b_il32], [b_il])
        cp(S, "dve", ir[:], ir32[:], [b_ir32], [b_ir])
        rp, b_rp = const("rp", [128, 64], D["rwp"][l])
        wup32, b_wup32 = const("wup32", [64, 256], D["rw_w_up"][l])
        aup32, b_aup32 = const("aup32", [64, 256], D["rw_a_up"][l])
        gup32, b_gup32 = const("gup32", [128, 256], D["rw_g_up"][l])
        wup = sbf("wup", [64, 256], MT); b_wup = Buf()
        aup = sbf("aup", [64, 256], MT); b_aup = Buf()
        gup = sbf("gup", [128, 256], MT); b_gup = Buf()
        cp(S, "dve", wup[:], wup32[:], [b_wup32], [b_wup])
        cp(S, "dve", aup[:], aup32[:], [b_aup32], [b_aup])
        cp(S, "dve", gup[:], gup32[:], [b_gup32], [b_gup])
        ones32 = sbf("ones32", [64, 64]); b_o32 = Buf()
        S.op("pool", lambda e: e.memset(ones32[:], 1.0), writes=[b_o32])
        ones64 = sbf("ones64", [64, 64], MT); b_o64 = Buf()
        cp(S, "dve", ones64[:], ones32[:], [b_o32], [b_o64])
        omka = sbf("omka", [64, 4]); b_omka = Buf()
        S.op("dve", lambda e: e.tensor_scalar(out=omka[:], in0=rp[0:64, 28:32], scalar1=-1.0, scalar2=1.0, op0=ALU.mult, op1=ALU.add), reads=[b_rp], writes=[b_omka])
        cst = sbf("rcst", [128, 2]); b_cst = Buf()
        S.op("pool", lambda e: e.memset(cst[:, 0:1], 64e-5), writes=[b_cst])
        ST = [[sbf(f"ST{p}_{i}", [128, 64], MT) for i in range(2)] for p in range(2)]
        b_ST = [[Buf() for i in range(2)] for p in range(2)]
        for p in range(2):
            S.op("pool", lambda e: e.memset(AS32(ST[p][0][:]), 0.0), writes=[b_ST[p][0]])
        sidx = [0, 0]
        def arr(name, shape=None, dt=F32):
            return sbf(name, shape or [64, 4, TB], dt), Buf()
        Z3, b_Z3 = arr("Z3", [64, 12, TB + 1])
        ZL, b_ZL = arr("ZL", [64, 2, TB + 1])
        ZG, b_ZG = arr("ZG", [128, TB + 1])
        X3, b_X3 = arr("X3", [64, 12, TB])
        XL, b_XL = arr("XL", [64, 2, TB], MT)
        XG, b_XG = arr("XG", [128, TB], MT)
        Dt, b_Dt = arr("Dt", [128, 12, TB])
        lw, b_lw = arr("lw")
        cl2, b_cl2 = arr("cl2")
        aa, b_aa = arr("aa")
        kkn, b_kkn = arr("kkn")
        tmp, b_tmp = arr("tmp", None, MT)
        kfin, b_kfin = arr("kfin")
        epos, b_epos = arr("epos")
        eneg, b_eneg = arr("eneg")
        eprev, b_eprev = arr("eprev")
        eC, b_eC = arr("eC")
        AR, b_AR = arr("AR", [64, NCH, 2, 2, 2, 64], MT)
        Bt, b_Bt = arr("Bt", [64, NCH, 4, 64], MT)
        Kt, b_Kt = arr("Kt", [64, NCH, 4, 64], MT)
        Bh, b_Bh = arr("Bh", [64, NCH, 4, 64])
        Kh, b_Kh = arr("Kh", [64, NCH, 4, 64])
        Vc_, b_Vc_ = arr("Vcm", [64, NCH, 4, 64])
        hm = lambda a: a[:].rearrange("k h (c t) -> k h c t", t=64)
        cm = lambda a: a[:].rearrange("k c h t -> k h c t")
        arv = lambda ty: AR[:, :, :, ty, :, :].rearrange("k c p hh t -> k p hh c t")
        hm5 = lambda a: a[:].rearrange("k (p hh) (c t) -> k p hh c t", hh=2, t=64)
        bv, b_bv = arr("bv")
        gT, b_gT = arr("gT")
        YN, b_YN = arr("YN")
        PCf, b_PCf = arr("PCf", [64, 4, NCH], MT)
        PCc = sbf("PCc", [128, 2, NCH]); b_PCc = Buf()
        yo = sbf("yo", [64, 4, TB], BF16); b_yo = Buf()
        NTMP = 100
        tm = [sbf(f"tm{i}", [128, 128], MT) for i in range(NTMP)]; b_tm = [Buf() for _ in range(NTMP)]
        NTF = 16
        tf = [sbf(f"tf{i}", [128, 128]) for i in range(NTF)]; b_tf = [Buf() for _ in range(NTF)]
        fctr = [0]

        def ntf_():
            i = fctr[0] % NTF
            fctr[0] += 1
            return tf[i], b_tf[i]
        tctr = [0]

        def nt_():
            i = tctr[0] % NTMP
            tctr[0] += 1
            return tm[i], b_tm[i]
        NBD = 5
        bd = [[sbf(f"bd{k_}_{i}", [128, 128], MT) for i in range(NBD)] for k_ in range(3)]
        b_bd = [[Buf() for i in range(NBD)] for k_ in range(3)]
        for k_ in range(3):
            for i in range(NBD):
                S.op("pool", lambda e: e.memset(AS32(bd[k_][i][:]), 0.0), writes=[b_bd[k_][i]])
        bdc = [0]
        zav = D["zaT"]
        ev_ctr = [0]

        def evac(out, in_, reads, writes):
            ek = "act" if ev_ctr[0] % 2 == 0 else "dve"
            ev_ctr[0] += 1
            cp(S, ek, out, in_, reads, writes)

        for tt in range(T // TB):
            t0 = tt * TB
            if tt == 0:
                S.op("pool", lambda e: e.memset(Z3[:, :, 0:1], 0.0), writes=[b_Z3])
                S.op("pool", lambda e: e.memset(ZL[:, :, 0:1], 0.0), writes=[b_ZL])
                S.op("pool", lambda e: e.memset(ZG[:, 0:1], 0.0), writes=[b_ZG])
                S.dma("sp", Z3[:, :, 1:TB + 1], zav[0:768, 0:TB].rearrange("(gh k) t -> k gh t", k=64), writes=[b_Z3])
                S.dma("sp", ZL[:, :, 1:TB + 1], zav[768:896, 0:TB].rearrange("(g j) t -> j g t", j=64), writes=[b_ZL])
                S.dma("sp", ZG[:, 1:TB + 1], zav[896:1024, 0:TB], writes=[b_ZG])
            else:
                S.dma("sp", Z3[:], zav[0:768, t0 - 1:t0 + TB].rearrange("(gh k) t -> k gh t", k=64), writes=[b_Z3])
                S.dma("sp", ZL[:], zav[768:896, t0 - 1:t0 + TB].rearrange("(g j) t -> j g t", j=64), writes=[b_ZL])
                S.dma("sp", ZG[:], zav[896:1024, t0 - 1:t0 + TB], writes=[b_ZG])
            S.op("dve", lambda e: e.tensor_tensor(out=Dt[0:64, :, :], in0=Z3[:, :, 0:TB], in1=Z3[:, :, 1:TB + 1], op=ALU.subtract), reads=[b_Z3], writes=[b_Dt])
            for j in range(12):
                if j % 3 == 2:
                    S.op("act", lambda e: e.activation(out=Dt[0:64, j, :], in_=Dt[0:64, j, :], func=AF.Copy, scale=rp[0:64, j:j + 1]), reads=[b_Dt, b_rp], writes=[b_Dt])
                else:
                    S.op("dve", lambda e: e.tensor_scalar(out=Dt[0:64, j, :], in0=Dt[0:64, j, :], scalar1=rp[0:64, j:j + 1], scalar2=None, op0=ALU.mult), reads=[b_Dt, b_rp], writes=[b_Dt])
            S.op("dve", lambda e: e.tensor_tensor(out=X3[:], in0=Dt[0:64, :, :], in1=Z3[:, :, 1:TB + 1], op=ALU.add), reads=[b_Dt, b_Z3], writes=[b_X3])
            S.op("dve", lambda e: e.tensor_tensor(out=Dt[0:64, 0:2, :], in0=ZL[:, :, 0:TB], in1=ZL[:, :, 1:TB + 1], op=ALU.subtract), reads=[b_ZL], writes=[b_Dt])
            for j in range(2):
                S.op("dve", lambda e: e.scalar_tensor_tensor(out=XL[:, j, :], in0=Dt[0:64, j, :], scalar=rp[0:64, 12 + j:13 + j], in1=ZL[:, j, 1:TB + 1], op0=ALU.mult, op1=ALU.add),
                     reads=[b_Dt, b_rp, b_ZL], writes=[b_XL])
            S.op("dve", lambda e: e.tensor_tensor(out=Dt[:, 2, :], in0=ZG[:, 0:TB], in1=ZG[:, 1:TB + 1], op=ALU.subtract), reads=[b_ZG], writes=[b_Dt])
            S.op("dve", lambda e: e.scalar_tensor_tensor(out=XG[:], in0=Dt[:, 2, :], scalar=rp[:, 14:15], in1=ZG[:, 1:TB + 1], op0=ALU.mult, op1=ALU.add),
                 reads=[b_Dt, b_rp, b_ZG], writes=[b_XG])
            r_ = lambda h: X3[:, h, :]
            k_ = lambda h: X3[:, 4 + h, :]
            v_ = lambda h: X3[:, 8 + h, :]
            S.op("act", lambda e: e.activation(out=XL[:, 0, :], in_=XL[:, 0, :], func=AF.Tanh), reads=[b_XL], writes=[b_XL])
            S.op("act", lambda e: e.activation(out=XG[:], in_=XG[:], func=AF.Sigmoid), reads=[b_XG], writes=[b_XG])
            for h in range(4):
                pb, bpb = nbx()
                S.op("pe", lambda e: e.matmul(pb[0:64, 0:TB], lhsT=wup[:, h * 64:(h + 1) * 64], rhs=XL[:, 0, :], start=True, stop=True), reads=[b_wup, b_XL], writes=[bpb])
                S.op("act", lambda e: e.activation(out=lw[:, h, :], in_=pb[0:64, 0:TB], func=AF.Sigmoid, bias=rp[0:64, 16 + h:17 + h], scale=1.0), reads=[bpb, b_rp], writes=[b_lw])
                pb, bpb = nbx()
                S.op("pe", lambda e: e.matmul(pb[0:64, 0:TB], lhsT=aup[:, h * 64:(h + 1) * 64], rhs=XL[:, 1, :], start=True, stop=True), reads=[b_aup, b_XL], writes=[bpb])
                S.op("act", lambda e: e.activation(out=aa[:, h, :], in_=pb[0:64, 0:TB], func=AF.Sigmoid, bias=rp[0:64, 20 + h:21 + h], scale=1.0), reads=[bpb, b_rp], writes=[b_aa])
                pb, bpb = nbx()
                S.op("pe", lambda e: e.matmul(pb[0:64, 0:TB], lhsT=gup[:, h * 64:(h + 1) * 64], rhs=XG[:], start=True, stop=True), reads=[b_gup, b_XG], writes=[bpb])
                evac(gT[:, h, :], pb[0:64, 0:TB], [bpb], [b_gT])
            S.op("dve", lambda e: e.tensor_scalar(out=lw[:], in0=lw[:], scalar1=-0.6065306597126334, scalar2=None, op0=ALU.mult), reads=[b_lw], writes=[b_lw])
            for h in range(4):
                S.op("dve", lambda e: e.tensor_scalar(out=kkn[:, h, :], in0=k_(h), scalar1=rp[0:64, 24 + h:25 + h], scalar2=None, op0=ALU.mult), reads=[b_X3, b_rp], writes=[b_kkn])
            S.op("act", lambda e: e.activation(out=tmp[:], in_=kkn[:], func=AF.Square), reads=[b_kkn], writes=[b_tmp])
            for h in range(4):
                pb, bpb = nbx()
                S.op("pe", lambda e: e.matmul(pb[0:64, 0:TB], lhsT=ones64[:], rhs=tmp[:, h, :], start=True, stop=True), reads=[b_o64, b_tmp], writes=[bpb])
                S.op("act", lambda e: e.activation(out=eC[:, h, :], in_=pb[0:64, 0:TB], func=AF.Sqrt), reads=[bpb], writes=[b_eC])
            S.op("dve", lambda e: e.tensor_scalar(out=eC[:], in0=eC[:], scalar1=1e-12, scalar2=None, op0=ALU.max), reads=[b_eC], writes=[b_eC])
            S.op("dve", lambda e: e.reciprocal(out=eC[:], in_=eC[:]), reads=[b_eC], writes=[b_eC])
            S.op("dve", lambda e: e.tensor_tensor(out=kkn[:], in0=kkn[:], in1=eC[:], op=ALU.mult), reads=[b_kkn, b_eC], writes=[b_kkn])
            for h in range(4):
                S.op("dve", lambda e: e.tensor_scalar(out=tmp[:, h, :], in0=aa[:, h, :], scalar1=rp[0:64, 28 + h:29 + h], scalar2=omka[:, h:h + 1], op0=ALU.mult, op1=ALU.add),
                     reads=[b_aa, b_rp, b_omka], writes=[b_tmp])
            S.op("dve", lambda e: e.tensor_tensor(out=kfin[:], in0=X3[:, 4:8, :], in1=tmp[:], op=ALU.mult), reads=[b_X3, b_tmp], writes=[b_kfin])
            for h in range(4):
                S.op("dve", lambda e: e.scalar_tensor_tensor(out=tmp[:, h, :], in0=r_(h), scalar=rp[0:64, 32 + h:33 + h], in1=kfin[:, h, :], op0=ALU.mult, op1=ALU.mult),
                     reads=[b_X3, b_kfin, b_rp], writes=[b_tmp])
                pb, bpb = nbx()
                S.op("pe", lambda e: e.matmul(pb[0:64, 0:TB], lhsT=ones64[:], rhs=tmp[:, h, :], start=True, stop=True), reads=[b_o64, b_tmp], writes=[bpb])
                S.op("dve", lambda e: e.tensor_tensor(out=bv[:, h, :], in0=pb[0:64, 0:TB], in1=v_(h), op=ALU.mult), reads=[bpb, b_X3], writes=[b_bv])
            src, bsrc, dst, bdst = lw, b_lw, cl2, b_cl2
            cp(S, "act", eprev[:], lw[:], [b_lw], [b_eprev])
            for sft in (1, 2, 4, 8, 16, 32):
                s5 = src[:].rearrange("k h (c t) -> k h c t", t=64)
                d5 = dst[:].rearrange("k h (c t) -> k h c t", t=64)
                S.op("dve", lambda e: e.tensor_tensor(out=d5[:, :, :, sft:64], in0=s5[:, :, :, sft:64], in1=s5[:, :, :, 0:64 - sft], op=ALU.add), reads=[bsrc], writes=[bdst])
                cp(S, "act", d5[:, :, :, 0:sft], s5[:, :, :, 0:sft], [bsrc], [bdst])
                src, bsrc, dst, bdst = dst, bdst, src, bsrc
            cl, b_cl = src, bsrc
            S.op("act", lambda e: e.activation(out=epos[:], in_=cl[:], func=AF.Exp), reads=[b_cl], writes=[b_epos])
            S.op("act", lambda e: e.activation(out=eneg[:], in_=cl[:], func=AF.Exp, scale=-1.0), reads=[b_cl], writes=[b_eneg])
            S.op("dve", lambda e: e.tensor_tensor(out=eprev[:], in0=cl[:], in1=eprev[:], op=ALU.subtract), reads=[b_cl, b_eprev], writes=[b_eprev])
            S.op("act", lambda e: e.activation(out=eprev[:], in_=eprev[:], func=AF.Exp), reads=[b_eprev], writes=[b_eprev])
            ep5 = epos[:].rearrange("k h (c t) -> k h c t", t=64)
            S.op("dve", lambda e: e.tensor_copy(out=PCf[:], in_=ep5[:, :, :, 63]), reads=[b_epos], writes=[b_PCf])
            en5 = eneg[:].rearrange("k h (c t) -> k h c t", t=64)
            ec5 = eC[:].rearrange("k h (c t) -> k h c t", t=64)
            for h in range(4):
                for c in range(NCH):
                    if (h + c) % 2:
                        S.op("dve", lambda e: e.tensor_scalar(out=ec5[:, h, c, :], in0=en5[:, h, c, :], scalar1=PCf[:, h, c:c + 1], scalar2=None, op0=ALU.mult),
                             reads=[b_eneg, b_PCf], writes=[b_eC])
                    else:
                        S.op("act", lambda e: e.activation(out=ec5[:, h, c, :], in_=en5[:, h, c, :], func=AF.Copy, scale=AS32(PCf[:, h, c:c + 1])),
                             reads=[b_eneg, b_PCf], writes=[b_eC])
            for h in range(4):
                S.op("dve", lambda e: e.scalar_tensor_tensor(out=AR[:, :, h // 2, 0, h % 2, :], in0=hm(kkn)[:, h], scalar=-1.0, in1=hm(eprev)[:, h], op0=ALU.mult, op1=ALU.mult), reads=[b_kkn, b_eprev], writes=[b_AR])
                S.op("dve", lambda e: e.tensor_tensor(out=AR[:, :, h // 2, 1, h % 2, :], in0=X3[:, h, :].rearrange("k (c t) -> k c t", t=64), in1=hm(epos)[:, h], op=ALU.mult), reads=[b_X3, b_epos], writes=[b_AR])
            S.op("dve", lambda e: e.tensor_tensor(out=tmp[:], in0=kkn[:], in1=aa[:], op=ALU.mult), reads=[b_kkn, b_aa], writes=[b_tmp])
            for h in range(4):
                S.op("dve", lambda e: e.tensor_tensor(out=cm(Bt)[:, h], in0=hm(tmp)[:, h], in1=hm(eneg)[:, h], op=ALU.mult), reads=[b_tmp, b_eneg], writes=[b_Bt])
                S.op("pool", lambda e: e.tensor_tensor(out=cm(Bh)[:, h], in0=hm(tmp)[:, h], in1=hm(eC)[:, h], op=ALU.mult), reads=[b_tmp, b_eC], writes=[b_Bh])
                S.op("dve", lambda e: e.tensor_tensor(out=cm(Kt)[:, h], in0=hm(kfin)[:, h], in1=hm(eneg)[:, h], op=ALU.mult), reads=[b_kfin, b_eneg], writes=[b_Kt])
                S.op("pool", lambda e: e.tensor_tensor(out=cm(Kh)[:, h], in0=hm(kfin)[:, h], in1=hm(eC)[:, h], op=ALU.mult), reads=[b_kfin, b_eC], writes=[b_Kh])
                cp(S, "act", cm(Vc_)[:, h], X3[:, 8 + h, :].rearrange("k (c t) -> k c t", t=64), [b_X3], [b_Vc_])
            for p in range(2):
                pb, bpb = nbx()
                S.op("pe", lambda e: e.matmul(pb[:, 0:NCH], lhsT=il[:], rhs=PCf[:, 2 * p, :], start=True, stop=False), reads=[b_il, b_PCf], writes=[bpb])
                S.op("pe", lambda e: e.matmul(pb[:, 0:NCH], lhsT=ir[:], rhs=PCf[:, 2 * p + 1, :], start=False, stop=True), reads=[b_ir, b_PCf], writes=[bpb])
                evac(PCc[:, p, :], pb[:, 0:NCH], [bpb], [b_PCc])
            if tt == 0:
                g.dbg("d_X3", X3[:], b_X3, [64, 12, TB]); g.dbg("d_cl", cl[:], b_cl, [64, 4, TB]); g.dbg("d_aa", aa[:], b_aa, [64, 4, TB])
                g.dbg("d_kkn", kkn[:], b_kkn, [64, 4, TB]); g.dbg("d_kfin", kfin[:], b_kfin, [64, 4, TB]); g.dbg("d_bv", bv[:], b_bv, [64, 4, TB])
                g.dbg("d_gT", gT[:], b_gT, [64, 4, TB]); g.dbg("d_eC", eC[:], b_eC, [64, 4, TB])
                g.dbg("d_PCc", PCc[:], b_PCc, [128, 2, NCH])
            def unit(c, p):
                tc = slice(c * 64, (c + 1) * 64)
                hp = slice(2 * p, 2 * p + 2)
                fl = lambda ap: ap.rearrange("k h t -> k (h t)")
                At_ = fl(AR[:, c, p, 0, :, :]); Bt_ = fl(Bt[:, c, hp, :]); Kt_ = fl(Kt[:, c, hp, :])
                ARp = AR[:, c, p, :, :, :].rearrange("k a h t -> k (a h t)")
                p1, bp1 = nb(); p2, bp2 = nb()
                S.op("pe", lambda e: e.matmul(p1[:, 0:256], lhsT=RR(Bt_), rhs=RR(ARp), start=True, stop=True), reads=[b_Bt, b_AR], writes=[bp1])
                S.op("pe", lambda e: e.matmul(p2[:, 0:256], lhsT=RR(Kt_), rhs=RR(ARp), start=True, stop=True), reads=[b_Kt, b_AR], writes=[bp2])
                yield
                N0, bN0 = nt_(); ArbT, bArbT = nt_(); AakT, bAakT = nt_(); ArkT, bArkT = nt_()
                S.op("dve", lambda e: e.tensor_tensor(out=N0[:], in0=p1[:, 0:128], in1=msu[:], op=ALU.mult), reads=[bp1, b_msu], writes=[bN0])
                S.op("dve", lambda e: e.tensor_tensor(out=ArbT[:], in0=p1[:, 128:256], in1=mu_[:], op=ALU.mult), reads=[bp1, b_mu], writes=[bArbT])
                rel(bp1)
                S.op("dve", lambda e: e.tensor_tensor(out=AakT[:], in0=p2[:, 0:128], in1=msu[:], op=ALU.mult), reads=[bp2, b_msu], writes=[bAakT])
                S.op("dve", lambda e: e.tensor_tensor(out=ArkT[:], in0=p2[:, 128:256], in1=mu_[:], op=ALU.mult), reads=[bp2, b_mu], writes=[bArkT])
                rel(bp2)
                p3, bp3 = nb()
                S.op("pe", lambda e: e.matmul(p3[:, 0:128], lhsT=RR(At_), rhs=RR(Bt_), start=True, stop=True), reads=[b_AR, b_Bt], writes=[bp3])
                ptr, bptr = nb()
                srcs = [(At_, b_AR), (fl(Vc_[:, c, hp, :]), b_Vc_), (fl(Bh[:, c, hp, :]), b_Bh), (fl(Kh[:, c, hp, :]), b_Kh)]
                for i_, (sap, sb_) in enumerate(srcs):
                    S.op("pe", lambda e: e.transpose(out=ptr[:, i_ * 64:(i_ + 1) * 64], in_=AS32(sap) if i_ == 0 else sap, identity=idf[0:64, 0:64]), reads=[sb_, b_idf], writes=[bptr])
                yield
                NT0, bNT0 = nt_()
                S.op("dve", lambda e: e.tensor_tensor(out=NT0[:], in0=p3[:, 0:128], in1=msl[:], op=ALU.mult), reads=[bp3, b_msl], writes=[bNT0])
                rel(bp3)
                Z, bZ = nt_()
                S.op("pool", lambda e: e.tensor_tensor(out=Z[:], in0=N0[:], in1=idf[:], op=ALU.add), reads=[bN0, b_idf], writes=[bZ])
                TA, bTA = nt_()
                Vt, bVt = nt_()
                cp(S, "act", TA[:, 0:64], ptr[:, 0:64], [bptr], [bTA])
                cp(S, "act", Vt[:, 0:64], ptr[:, 64:128], [bptr], [bVt])
                bi = bdc[0] % NBD; bdc[0] += 1
                Bbd, bBbd = bd[0][bi], b_bd[0][bi]
                Kbd, bKbd = bd[1][bi], b_bd[1][bi]
                Apb, bApb = bd[2][bi], b_bd[2][bi]
                for hh in range(2):
                    rs = slice(hh * 64, (hh + 1) * 64)
                    cp(S, "act", Bbd[rs, rs], ptr[rs, 128:192], [bptr], [bBbd])
                    cp(S, "act", Kbd[rs, rs], ptr[rs, 192:256], [bptr], [bKbd])
                rel(bptr)
                yield
                X, bX, XT, bXT = N0, bN0, NT0, bNT0
                pw, bpw = nb()
                S.op("pe", lambda e: e.matmul(pw[:, 0:64], lhsT=RR(AakT[:]), rhs=RR(Vt[:, 0:64]), start=True, stop=True), reads=[bAakT, bVt], writes=[bpw])
                yield
                cp(S, "act", TA[:, 64:128], pw[:, 0:64], [bpw], [bTA])
                rel(bpw)
                for j in range(1, 6):
                    if j <= 4:
                        px, bpx = nb()
                        S.op("pe", lambda e: e.matmul(px[:, 0:128], lhsT=RR(XT[:]), rhs=RR(X[:]), start=True, stop=True), reads=[bXT, bX], writes=[bpx])
                    pxt, bpxt = nb()
                    S.op("pe", lambda e: e.matmul(pxt[:, 0:128], lhsT=RR(X[:]), rhs=RR(XT[:]), start=True, stop=True), reads=[bXT, bX], writes=[bpxt])
                    yield
                    if j <= 4:
                        Xn, bXn = nt_()
                        cp(S, "act", Xn[:], px[:, 0:128], [bpx], [bXn])
                        rel(bpx)
                    XTn, bXTn = nt_()
                    cp(S, "act" if j > 4 else "dve", XTn[:], pxt[:, 0:128], [bpxt], [bXTn])
                    rel(bpxt)
                    pz, bpz = nb()
                    S.op("pe", lambda e: e.matmul(pz[:, 0:128], lhsT=RR(XTn[:]), rhs=RR(Z[:]), start=True, stop=True), reads=[bXTn, bZ], writes=[bpz])
                    yield
                    Zn, bZn = nt_()
                    S.op("dve", lambda e: e.tensor_tensor(out=Zn[:], in0=pz[:, 0:128], in1=Z[:], op=ALU.add), reads=[bpz, bZ], writes=[bZn])
                    rel(bpz)
                    Z, bZ = Zn, bZn
                    if j <= 4:
                        X, bX = Xn, bXn
                    XT, bXT = XTn, bXTn
                pu, bpu = nb()
                S.op("pe", lambda e: e.matmul(pu[:, 0:128], lhsT=RR(Z[:]), rhs=RR(TA[:]), start=True, stop=True), reads=[bZ, bTA], writes=[bpu])
                yield
                U0, bU0 = nt_()
                cp(S, "dve", U0[:, 0:64], pu[:, 64:128], [bpu], [bU0])
                for hh in range(2):
                    rs = slice(hh * 64, (hh + 1) * 64)
                    cp(S, "act", Apb[rs, rs], pu[rs, 0:64], [bpu], [bApb])
                rel(bpu)
                pg, bpg = nb()
                S.op("pe", lambda e: e.matmul(pg[:, 0:64], lhsT=RR(Bbd[:]), rhs=RR(U0[:, 0:64]), start=True, stop=False), reads=[bBbd, bU0], writes=[bpg])
                S.op("pe", lambda e: e.matmul(pg[:, 0:64], lhsT=RR(Kbd[:]), rhs=RR(Vt[:, 0:64]), start=False, stop=True), reads=[bKbd, bVt], writes=[bpg])
                pf, bpf = nb()
                S.op("pe", lambda e: e.matmul(pf[:, 0:128], lhsT=RR(Apb[:]), rhs=RR(Bbd[:]), start=True, stop=True), reads=[bApb, bBbd], writes=[bpf])
                yield
                Gs, bGs = ntf_()
                cp(S, "act", Gs[:, 0:64], pg[:, 0:64], [bpg], [bGs])
                rel(bpg)
                PhiT, bPhiT = nt_()
                S.op("dve", lambda e: e.scalar_tensor_tensor(out=PhiT[:], in0=idf[:], scalar=PCc[:, p, c:c + 1], in1=pf[:, 0:128], op0=ALU.mult, op1=ALU.add),
                     reads=[b_idf, b_PCc, bpf], writes=[bPhiT])
                rel(bpf)
                pr, bpr = nb()
                S.op("pe", lambda e: e.matmul(pr[:, 0:64], lhsT=RR(il[:]), rhs=RR(AR[:, c, p, 1, 0, :]), start=True, stop=False, **SK), reads=[b_il, b_AR], writes=[bpr])
                S.op("pe", lambda e: e.matmul(pr[:, 64:128], lhsT=RR(ir[:]), rhs=RR(AR[:, c, p, 1, 1, :]), start=False, stop=False, **SK), reads=[b_ir, b_AR], writes=[bpr])
                S.op("pe", lambda e: e.matmul(pr[:, 0:128], lhsT=RR(Apb[:]), rhs=RR(ArbT[:]), start=False, stop=True, **SK), reads=[bApb, bArbT], writes=[bpr])
                yield
                RpT, bRpT = nt_()
                cp(S, "act", RpT[:], pr[:, 0:128], [bpr], [bRpT])
                rel(bpr)
                Scur, bScur = ST[p][sidx[p]], b_ST[p][sidx[p]]
                py, bpy = nb()
                S.op("pe", lambda e: e.matmul(py[:, 0:64], lhsT=RR(ArbT[:]), rhs=RR(U0[:, 0:64]), start=True, stop=False), reads=[bArbT, bU0], writes=[bpy])
                S.op("pe", lambda e: e.matmul(py[:, 0:64], lhsT=RR(ArkT[:]), rhs=RR(Vt[:, 0:64]), start=False, stop=False), reads=[bArkT, bVt], writes=[bpy])
                S.op("pe", lambda e: e.matmul(py[:, 0:64], lhsT=RR(RpT[:]), rhs=RR(Scur[:]), start=False, stop=True), reads=[bRpT, bScur], writes=[bpy])
                ps_, bps = nb()
                S.op("pe", lambda e: e.matmul(ps_[:, 0:64], lhsT=RR(PhiT[:]), rhs=RR(Scur[:]), start=True, stop=True), reads=[bPhiT, bScur], writes=[bps])
                sidx[p] ^= 1
                Snew, bSnew = ST[p][sidx[p]], b_ST[p][sidx[p]]
                S.op("dve", lambda e: e.tensor_tensor(out=Snew[:], in0=ps_[:, 0:64], in1=Gs[:, 0:64], op=ALU.add), reads=[bps, bGs], writes=[bSnew])
                rel(bps)
                Yt, bYt = ntf_()
                st_, bst = ntf_()
                cp(S, "act", Yt[:, 0:64], py[:, 0:64], [bpy], [bYt])
                rel(bpy)
                yield
                S.op("dve", lambda e: e.bn_stats(out=st_[:, 0:6], in_=Yt[:, 0:64]), reads=[bYt], writes=[bst])
                yield
                S.op("dve", lambda e: e.bn_aggr(out=st_[:, 8:10], in_=st_[:, 0:6]), reads=[bst], writes=[bst])
                yield
                S.op("act", lambda e: e.activation(out=st_[:, 10:11], in_=st_[:, 9:10], func=AF.Sqrt, bias=cst[:, 0:1], scale=1.0), reads=[bst, b_cst], writes=[bst])
                yield
                S.op("dve", lambda e: e.reciprocal(out=st_[:, 11:12], in_=st_[:, 10:11]), reads=[bst], writes=[bst])
                yield
                S.op("dve", lambda e: e.tensor_scalar(out=Yt[:, 0:64], in0=Yt[:, 0:64], scalar1=st_[:, 8:9], scalar2=st_[:, 11:12], op0=ALU.subtract, op1=ALU.mult),
                     reads=[bYt, bst], writes=[bYt])
                pyt, bpyt = nb()
                S.op("pe", lambda e: e.transpose(out=pyt[0:64, 0:128], in_=Yt[:, 0:64], identity=idf[:]), reads=[bYt, b_idf], writes=[bpyt])
                yield
                cp(S, "act", YN[:, hp, tc], pyt[0:64, 0:128].rearrange("v (h t) -> v h t", h=2), [bpyt], [b_YN])
                rel(bpyt)

            GRP = 2
            for c0_ in range(0, NCH, GRP):
                for k_ in range(NB):
                    busy[k_] = False
                gens = [unit(c_, p_) for c_ in range(c0_, min(NCH, c0_ + GRP)) for p_ in range(2)]
                alive = list(gens)
                while alive:
                    nxt = []
                    for gn_ in alive:
                        try:
                            next(gn_)
                            nxt.append(gn_)
                        except StopIteration:
                            pass
                    alive = nxt
            if tt == 0:
                g.dbg("d_YN", YN[:], b_YN, [64, 4, TB])
            for h in range(4):
                S.op("dve", lambda e: e.tensor_scalar(out=YN[:, h, :], in0=YN[:, h, :], scalar1=rp[0:64, 36 + h:37 + h], scalar2=rp[0:64, 40 + h:41 + h], op0=ALU.mult, op1=ALU.add),
                     reads=[b_YN, b_rp], writes=[b_YN])
            S.op("dve", lambda e: e.tensor_tensor(out=YN[:], in0=YN[:], in1=bv[:], op=ALU.add), reads=[b_YN, b_bv], writes=[b_YN])
            S.op("dve", lambda e: e.tensor_tensor(out=yo[:], in0=YN[:], in1=gT[:], op=ALU.mult), reads=[b_YN, b_gT], writes=[b_yo])
            S.dma("sp", D["ymixT"][0:256, t0:t0 + TB].rearrange("(h v) t -> v h t", v=64), yo[:], reads=[b_yo])
        S.barrier()


def build(T=8192, debug=False, phases=None, nlayers=2):
    nc = bass.Bass("TRN2", target_bir_lowering=False)
    g = G()
    g.nc, g.T, g.wctr = nc, T, 0
    g.debug = debug
    D = {}
    g.D = D

    def din(name, shape, dt=F32):
        D[name] = nc.dram_tensor(name, list(shape), dt, kind="ExternalInput").ap()

    def dscr(name, shape, dt=F32, out=False):
        D[name] = nc.dram_tensor(name, list(shape), dt, kind=("ExternalOutput" if (out or debug) else "Internal")).ap()

    din("xT", [DM, T]); din("pT", [2, 256, T]); din("pos", [1, T], I32); din("invf", [128, 1])
    din("w_in", [2, DM, NEXT]); din("smalls", [2, 128, 64]); din("w_out", [2, DM, DM])
    din("ffn_w_up", [2, DM, 2 * DFF]); din("ffn_w_down", [2, DFF, DM]); din("convp", [2, 128, 44, 4])
    din("ple_w_gate", [2, DM, DM]); din("ple_w_proj", [2, 256, DM])
    din("pool_wbd", [2, 128, 2, 128]); din("pool_fix", [128, 2, 16])
    for nm, shp, dt in g_extra_inputs(T):
        din(nm, shp, dt)
    dscr("cosT", [128, T]); dscr("sinT", [128, T])
    dscr("zaT", [1024, T]); dscr("zbT", [256, T]); dscr("qT", [512, T], BF16); dscr("kT", [384, T], BF16)
    dscr("vcT", [128, T], BF16); dscr("vtok", [T, 256], BF16); dscr("gates", [T, 24])
    dscr("ymixT", [1024, T], BF16)
    for nm, shp, dt in g_extra_scratch(T):
        dscr(nm, shp, dt)
    dscr("xs0", [DM, T]); dscr("xs1", [DM, T]); dscr("xs2", [DM, T])
    dscr("outT", [DM, T], out=True)
    with ExitStack() as stack:
        g.S = Sched(nc, stack)
        if phases is None:
            phases = ("rope", "inproj", "rwkv", "pool", "nsa", "outproj", "ffn", "ple")
        if "rope" in phases:
            phase_rope(g)
        xcur = D["xT"]
        for l in range(nlayers):
            if "inproj" in phases:
                phase_inproj(g, l, xcur)
            if "rwkv" in phases:
                phase_rwkv(g, l)
            if "pool" in phases:
                phase_pool(g, l)
            if "nsa" in phases:
                phase_nsa(g, l)
            if "outproj" in phases:
                phase_outproj(g, l, xcur, D["xs0"])
            if "ffn" in phases:
                phase_ffn(g, l, D["xs0"], D["xs1"])
            if "ple" in phases:
                last = (l == nlayers - 1)
                phase_ple(g, l, D["xs1"], D["outT"] if last else D["xs2"], last)
            xcur = D["xs2"]
        g.S.barrier()
        g.ninstr = g.S.ninstr
    return nc, g


def g_extra_inputs(T):
    NCMP = (T - 32) // 16 + 1
    NTC = (NCMP + 127) // 128
    return [("nsa_masks", [128, 19, 512], BF16), ("identb", [128, 128], BF16), ("identf", [128, 128], F32),
            ("E_all", [128, T], BF16), ("mcs", [128, NTC, 128], BF16), ("keepw", [128, 256], F32), ("addw", [128, 256], F32),
            ("permb", [128, 128], BF16), ("rw_msu", [128, 128], F32), ("rw_mu", [128, 128], F32), ("rw_msl", [128, 128], F32), ("rw_il", [64, 128], F32), ("rw_ir", [64, 128], F32),
            ("rwp", [2, 128, 64], F32), ("rw_w_up", [2, 64, 256], F32), ("rw_a_up", [2, 64, 256], F32), ("rw_g_up", [2, 128, 256], F32),
            ("nsa_w_ck", [2, 32, 64, 64], F32), ("nsa_w_cv", [2, 32, 64, 64], F32), ("nsa_peT", [2, 2, 64, 32], F32)]


def g_extra_scratch(T):
    return []


def host_prep(inp, T=8192):
    f = np.float32
    cols = inproj_cols()
    shared = {}
    shared["w_in"] = np.ascontiguousarray(inp["w_in"][:, :, cols])
    sm = np.zeros((2, 128, 64), f)
    for l in range(2):
        sm[l, :, 0:8] = inp["g_mix"][l].reshape(8, 128).T
        sm[l, :, 8:10] = inp["pool_scale"][l].reshape(2, 128).T
        sm[l, :, 16:24] = inp["g_ffn"][l].reshape(8, 128).T
        sm[l, :, 24:32] = inp["g_ple"][l].reshape(8, 128).T
        sm[l, :, 32:40] = inp["g_final"].reshape(8, 128).T
    shared["smalls"] = sm
    shared["w_out"] = np.ascontiguousarray(inp["w_out"])
    shared["ffn_w_up"] = np.ascontiguousarray(inp["ffn_w_up"])
    shared["ffn_w_down"] = np.ascontiguousarray(inp["ffn_w_down"])
    cp_ = np.zeros((2, 128, 44, 4), f)
    for l in range(2):
        cw = inp["ffn_conv_w"][l][:, 0, :]
        for i in range(3):
            cp_[l, :, :, i] = cw[i].reshape(44, 128).T
        cp_[l, :, :, 3] = inp["ffn_conv_b"][l].reshape(44, 128).T
    shared["convp"] = cp_
    shared["ple_w_gate"] = np.ascontiguousarray(inp["ple_w_gate"])
    shared["ple_w_proj"] = np.ascontiguousarray(inp["ple_w_proj"])
    pw = np.zeros((2, 128, 2, 128), f)
    for l in range(2):
        for gi in range(4):
            t_, h_ = gi // 2, gi % 2
            pw[l, h_ * 64:(h_ + 1) * 64, t_, h_ * 64:(h_ + 1) * 64] = inp["pool_w"][l, gi]
    shared["pool_wbd"] = pw
    fix = np.zeros((128, 2, 16), f)
    for gi, win in enumerate((2, 4, 8, 16)):
        t_, h_ = gi // 2, gi % 2
        fix[h_ * 64:(h_ + 1) * 64, t_, :] = 1.0 / np.minimum(np.arange(16) + 1, win)
    shared["pool_fix"] = fix
    shared.update(nsa_consts(T))
    shared.update(rwkv_consts())
    import ml_dtypes
    pm = np.zeros((128, 128), f)
    mm_ = np.arange(128)
    pm[(mm_ // 64) * 64 + ((mm_ % 64) + 32) % 64, mm_] = 1.0
    shared["permb"] = pm.astype(ml_dtypes.bfloat16)
    rwp = np.zeros((2, 128, 64), f)
    for l in range(2):
        mu = inp["rw_mu"][l]
        rwp[l, 0:64, 0:12] = mu[0:768].reshape(12, 64).T
        rwp[l, 0:64, 12:14] = mu[768:896].reshape(2, 64).T
        rwp[l, :, 14] = mu[896:1024]
        for j, nm in enumerate(("rw_w0", "rw_a0", "rw_k_k", "rw_k_a")):
            rwp[l, 0:64, 16 + 4 * j:20 + 4 * j] = inp[nm][l].reshape(4, 64).T
        rwp[l, 0:64, 32:36] = inp["rw_r_k"][l].T
        rwp[l, 0:64, 36:40] = inp["rw_gn_g"][l].reshape(4, 64).T
        rwp[l, 0:64, 40:44] = inp["rw_gn_b"][l].reshape(4, 64).T
    shared["rwp"] = rwp
    shared["rw_w_up"] = np.ascontiguousarray(inp["rw_w_up"])
    shared["rw_a_up"] = np.ascontiguousarray(inp["rw_a_up"])
    shared["rw_g_up"] = np.ascontiguousarray(inp["rw_g_up"])
    shared["nsa_w_ck"] = np.ascontiguousarray(inp["nsa_w_ck"])
    shared["nsa_w_cv"] = np.ascontiguousarray(inp["nsa_w_cv"])
    shared["nsa_peT"] = np.ascontiguousarray(np.stack([np.transpose(inp["nsa_pe_k"], (0, 2, 1)), np.transpose(inp["nsa_pe_v"], (0, 2, 1))], axis=1))
    inv = (10000.0 ** (-np.arange(32, dtype=f) / 32)).astype(f)
    shared["invf"] = np.tile(inv, 4).reshape(128, 1).astype(f)
    return shared


def per_core(inp, b, T=8192):
    return {"xT": np.ascontiguousarray(inp["x"][b, :T].T),
            "pT": np.ascontiguousarray(np.transpose(inp["p"][:, b, :T], (0, 2, 1))),
            "pos": np.ascontiguousarray(inp["positions"][b:b + 1, :T]).astype(np.int32)}


def kernel(**inputs):
    T = 8192
    inp = {k: np.asarray(v) for k, v in inputs.items()}
    nc, g = build(T)
    shared = host_prep(inp, T)
    in_maps = []
    for b in range(8):
        m = dict(shared)
        m.update(per_core(inp, b, T))
        in_maps.append(m)
    res = run_bass_kernel_spmd(nc, in_maps, core_ids=list(range(8)))
    out = np.stack([np.ascontiguousarray(res.results[b]["outT"].T) for b in range(8)], axis=0)
    return out.astype(np.float32)
```
